# Optimizing a Trainium2 kernel written in Bass

```python
import math
import jax, jax.numpy as jnp
from jax import lax
import numpy as np

D_MODEL = 2048
BATCH = 16
SEQ = 2048
DEPTH = 4

CHUNK = 64
N_META = 16
D_A = D_MODEL // 2
CONV_WIDTH = 31
H_B = 8
DH_B = D_MODEL // (2 * H_B)
D_C = D_MODEL // 8
H_IDX = 16
D_IDX = 64
TOPK_MAX = 256
Q_BLOCK = 128
N_BUCKETS = 32
MAX_DISTANCE = 128
D_MIX_EVEN = D_A + H_B * DH_B
EVEN_SPLITS = (D_A, D_A, D_A, H_B * DH_B, D_C, H_B * DH_B, H_IDX * D_IDX, D_IDX, H_IDX)
P_EVEN = sum(EVEN_SPLITS)
H_C = 16
DK_C = D_MODEL // H_C
DV_C = D_MODEL // H_C
D_REC = H_C * DK_C
N_EVEN = (DEPTH + 1) // 2
N_ODD = DEPTH // 2
RMS_EPS = 1e-6
LN_EPS = 1e-5
NEG_INF = -1e30

kernel_name = 'chunk_causal_hybrid_conv_dsa_hgrn2'


def rms_norm(x, g):
    xf = x.astype(jnp.float32)
    y = xf * lax.rsqrt(jnp.mean(jnp.square(xf), -1, keepdims=True) + RMS_EPS)
    return (y * g.astype(jnp.float32)).astype(x.dtype)


def layer_norm(x, g, b):
    xf = x.astype(jnp.float32)
    mu = jnp.mean(xf, -1, keepdims=True)
    var = jnp.mean(jnp.square(xf - mu), -1, keepdims=True)
    y = (xf - mu) * lax.rsqrt(var + LN_EPS) * g.astype(jnp.float32) + b.astype(jnp.float32)
    return y.astype(x.dtype)


def split_last(a, sizes):
    out, start = [], 0
    for s in sizes:
        out.append(a[..., start:start + s])
        start += s
    return out


def chunk_ids(pos):
    return jnp.where(pos < N_META, 0, 1 + (pos - N_META) // CHUNK)


def t5_bucket(rel):
    nb = N_BUCKETS // 2
    ret = jnp.where(rel > 0, nb, 0)
    n = jnp.abs(rel)
    max_exact = nb // 2
    nf = jnp.maximum(n, 1).astype(jnp.float32)
    large = max_exact + (jnp.log(nf / max_exact) / math.log(MAX_DISTANCE / max_exact)
                         * (nb - max_exact)).astype(jnp.int32)
    large = jnp.minimum(large, nb - 1)
    return ret + jnp.where(n < max_exact, n, large)


def dsa_attention(q_lat, c, qi, ki, wi, chunk_id, rel_bias_table, topk):
    bsz, t_len = c.shape[:2]
    n_blk = -(-t_len // Q_BLOCK)
    t_pad = n_blk * Q_BLOCK

    def blocks(a):
        a = jnp.pad(a, [(0, 0), (0, t_pad - t_len)] + [(0, 0)] * (a.ndim - 2))
        return jnp.moveaxis(a.reshape((bsz, n_blk, Q_BLOCK) + a.shape[2:]), 1, 0)

    qpos = jnp.arange(t_pad, dtype=jnp.int32).reshape(n_blk, Q_BLOCK)
    qchunk = chunk_ids(qpos)

    def one_block(args):
        q_b, qi_b, wi_b, qpos_b, qchunk_b = args
        dots = jnp.einsum('bqhd,bsd->bqhs', qi_b, ki)
        score = jnp.einsum('bqh,bqhs->bqs', wi_b, jax.nn.relu(dots)).astype(jnp.float32)
        admissible = chunk_id[None, :] <= qchunk_b[:, None]
        score = jnp.where(admissible[None], score, NEG_INF)
        _, idx = lax.top_k(score, topk)
        c_sel = jax.vmap(lambda cb, ib: cb[ib])(c, idx)
        valid = chunk_id[idx] <= qchunk_b[None, :, None]
        bias = rel_bias_table[t5_bucket(idx - qpos_b[None, :, None])]
        logits = (jnp.einsum('bqhc,bqkc->bhqk', q_b, c_sel).astype(jnp.float32)
                  + jnp.moveaxis(bias, -1, 1).astype(jnp.float32))
        logits = jnp.where(valid[:, None], logits, NEG_INF)
        p = jax.nn.softmax(logits, axis=-1).astype(c.dtype)
        return jnp.einsum('bhqk,bqkc->bqhc', p, c_sel)

    out = lax.map(one_block, (blocks(q_lat), blocks(qi), blocks(wi), qpos, qchunk))
    out = jnp.moveaxis(out, 0, 1).reshape((bsz, t_pad) + out.shape[3:])
    return out[:, :t_len]


def even_mixer(hn, w_in, conv_w, conv_b, ln_g, ln_b, kv_g, w_uk, w_uv, w_out,
               rel_bias_table, chunk_id, topk):
    bsz, t_len, _ = hn.shape
    glu_v, glu_g, gate_a, q, c, gate_b, qi, ki, wi = split_last(hn @ w_in, EVEN_SPLITS)
    u = glu_v * jax.nn.sigmoid(glu_g)
    u = lax.conv_general_dilated(u, conv_w[:, None, :], (1,), [(CONV_WIDTH - 1, 0)],
                                 dimension_numbers=('NWC', 'WIO', 'NWC'),
                                 feature_group_count=D_A) + conv_b
    a_out = jax.nn.silu(layer_norm(u, ln_g, ln_b)) * jax.nn.silu(gate_a)
    c = rms_norm(c, kv_g)
    q_lat = jnp.einsum('bthd,hcd->bthc', q.reshape(bsz, t_len, H_B, DH_B), w_uk) * (DH_B ** -0.5)
    qi = qi.reshape(bsz, t_len, H_IDX, D_IDX) * (D_IDX ** -0.5)
    wi = wi * (H_IDX ** -0.5)
    o_lat = dsa_attention(q_lat, c, qi, ki, wi, chunk_id, rel_bias_table, topk)
    b_out = jnp.einsum('bthc,hcd->bthd', o_lat, w_uv).reshape(bsz, t_len, H_B * DH_B)
    b_out = b_out * jax.nn.silu(gate_b)
    return jnp.concatenate([a_out, b_out], axis=-1) @ w_out


def hgrn2_chunkwise(q, log_f, k, v):
    bsz, t_len = q.shape[:2]
    lpad = (-N_META) % CHUNK
    rpad = (-(t_len + lpad)) % CHUNK
    n_chunks = (t_len + lpad + rpad) // CHUNK

    def chunks(a):
        a = jnp.pad(a.astype(jnp.float32), ((0, 0), (lpad, rpad), (0, 0), (0, 0)))
        return a.reshape(bsz, n_chunks, CHUNK, H_C, a.shape[-1]).transpose(1, 0, 3, 2, 4)

    causal = jnp.tril(jnp.ones((CHUNK, CHUNK), dtype=bool))

    def step(state, inp):
        q_c, g_c, k_c, v_c = inp
        b = jnp.cumsum(g_c, axis=2)
        o_inter = jnp.einsum('bhtk,bhkv->bhtv', q_c * jnp.exp(b), state)
        diff = jnp.where(causal[:, :, None], b[:, :, :, None, :] - b[:, :, None, :, :], -jnp.inf)
        attn = jnp.einsum('bhtk,bhtsk,bhsk->bhts', q_c, jnp.exp(diff), k_c)
        o_intra = jnp.einsum('bhts,bhsv->bhtv', attn, v_c)
        b_last = b[:, :, -1]
        state = (jnp.exp(b_last)[..., None] * state
                 + jnp.einsum('bhsk,bhsv->bhkv', k_c * jnp.exp(b_last[:, :, None] - b), v_c))
        return state, o_inter + o_intra

    s0 = jnp.zeros((bsz, H_C, DK_C, DV_C), jnp.float32)
    _, out = lax.scan(step, s0, (chunks(q), chunks(log_f), chunks(k), chunks(v)))
    out = out.transpose(1, 0, 3, 2, 4).reshape(bsz, n_chunks * CHUNK, H_C, DV_C)
    return out[:, lpad:lpad + t_len]


def odd_mixer(hn, w_in, lb, norm_g, w_out):
    bsz, t_len, _ = hn.shape
    q, fz, i, g = split_last(hn @ w_in, (D_REC, D_REC, D_REC, D_REC))
    heads = lambda a: a.reshape(bsz, t_len, H_C, -1)
    fz = fz.astype(jnp.float32)
    log_f = jnp.logaddexp(jnp.log(lb), jnp.log1p(-lb) + jax.nn.log_sigmoid(fz))
    k = (1.0 - lb) * jax.nn.sigmoid(-fz)
    o = hgrn2_chunkwise(heads(jax.nn.silu(q)), heads(log_f), heads(k), heads(i))
    o = o * lax.rsqrt(jnp.mean(jnp.square(o), -1, keepdims=True) + RMS_EPS)
    o = o * norm_g.reshape(H_C, DV_C).astype(jnp.float32)
    o = o.reshape(bsz, t_len, D_REC).astype(hn.dtype) * jax.nn.silu(g)
    return o @ w_out


def setup_inputs(seed: int = 0) -> dict:
    key = jax.random.key(seed)
    ks = jax.random.split(key, 20)
    nrm = lambda k, shape, s: jax.random.normal(k, shape, jnp.float32) * s
    return {
        'x': nrm(ks[0], (BATCH, SEQ, D_MODEL), 1.0),
        'meta_tokens': nrm(ks[1], (N_META, D_MODEL), 1.0),
        'norm_gain': 1.0 + nrm(ks[2], (DEPTH, D_MODEL), 0.02),
        'final_norm_gain': 1.0 + nrm(ks[3], (D_MODEL,), 0.02),
        'rel_bias_table': nrm(ks[4], (N_BUCKETS, H_B), 0.5),
        'w_in_even': nrm(ks[5], (N_EVEN, D_MODEL, P_EVEN), D_MODEL ** -0.5),
        'conv_w': nrm(ks[6], (N_EVEN, CONV_WIDTH, D_A), CONV_WIDTH ** -0.5),
        'conv_b': nrm(ks[7], (N_EVEN, D_A), 0.01),
        'conv_ln_gain': 1.0 + nrm(ks[8], (N_EVEN, D_A), 0.02),
        'conv_ln_bias': nrm(ks[9], (N_EVEN, D_A), 0.01),
        'kv_norm_gain': 1.0 + nrm(ks[10], (N_EVEN, D_C), 0.02),
        'w_uk': nrm(ks[11], (N_EVEN, H_B, D_C, DH_B), D_C ** -0.5),
        'w_uv': nrm(ks[12], (N_EVEN, H_B, D_C, DH_B), D_C ** -0.5),
        'w_out_even': nrm(ks[13], (N_EVEN, D_MIX_EVEN, D_MODEL), D_MIX_EVEN ** -0.5),
        'w_in_odd': nrm(ks[14], (N_ODD, D_MODEL, 4 * D_REC), D_MODEL ** -0.5),
        'lb_logits': nrm(ks[15], (DEPTH, D_REC), 0.1),
        'rec_norm_gain': 1.0 + nrm(ks[16], (N_ODD, D_REC), 0.02),
        'w_out_odd': nrm(ks[17], (N_ODD, D_REC, D_MODEL), D_REC ** -0.5),
    }


def reference(x, meta_tokens, norm_gain, final_norm_gain, rel_bias_table, w_in_even, conv_w,
              conv_b, conv_ln_gain, conv_ln_bias, kv_norm_gain, w_uk, w_uv, w_out_even,
              w_in_odd, lb_logits, rec_norm_gain, w_out_odd):
    bsz, seq, d = x.shape
    t_len = seq + N_META
    topk = min(TOPK_MAX, seq // 4)
    h = jnp.concatenate([jnp.broadcast_to(meta_tokens.astype(x.dtype), (bsz, N_META, d)), x], axis=1)
    chunk_id = chunk_ids(jnp.arange(t_len, dtype=jnp.int32))
    lb_soft = jax.nn.softmax(lb_logits.astype(jnp.float32), axis=0)
    lower_bounds = jnp.cumsum(lb_soft, axis=0) - lb_soft[0]
    for layer in range(DEPTH):
        hn = rms_norm(h, norm_gain[layer])
        if layer % 2 == 0:
            e = layer // 2
            y = even_mixer(hn, w_in_even[e], conv_w[e], conv_b[e], conv_ln_gain[e], conv_ln_bias[e],
                           kv_norm_gain[e], w_uk[e], w_uv[e], w_out_even[e], rel_bias_table,
                           chunk_id, topk)
        else:
            o = layer // 2
            y = odd_mixer(hn, w_in_odd[o], lower_bounds[layer], rec_norm_gain[o], w_out_odd[o])
        h = h + y
    h = rms_norm(h, final_norm_gain)
    return h[:, N_META:]
```

```python
import math
from contextlib import ExitStack

import numpy as np
import concourse.bass as bass
import concourse.mybir as mybir
from concourse.bass_utils import run_bass_kernel_spmd

F32 = mybir.dt.float32
BF16 = mybir.dt.bfloat16
AF = mybir.ActivationFunctionType
ALU = mybir.AluOpType
AX = mybir.AxisListType

NCORES = 8
SEQ_PER_CORE = 2
D = 2048
SEQ = 2048
NMETA = 16
T = SEQ + NMETA
NT = 17
DEPTH = 4
P_EVEN = 6480
RMS_EPS = 1e-6
LN_EPS = 1e-5
NEG = -1.0e30
NEG2 = -3.0e38
TOPK = 256


def trng(i):
    return (0, 16) if i == 0 else (16 + 128 * (i - 1), 128)


CCH = [(0, 16)] + [(16 + 512 * j, 512) for j in range(4)]


class Tk:
    __slots__ = ("name", "t", "w", "r", "dkey", "bank")

    def __init__(self, name, t=None):
        self.name = name
        self.t = t
        self.w = {}
        self.r = {}
        self.dkey = None
        self.bank = None

    def __getitem__(self, idx):
        return self.t[idx]


class Ring:
    def __init__(self, bufs):
        self.bufs = bufs
        self.i = 0

    def next(self):
        b = self.bufs[self.i % len(self.bufs)]
        self.i += 1
        return b


class Em:
    ENG = ("pe", "act", "dve", "pool", "sp")

    def __init__(self, nc, n_dsem=78):
        self.nc = nc
        self.top = ExitStack()
        self.eng = dict(pe=nc.tensor, act=nc.scalar, dve=nc.vector, pool=nc.gpsimd, sp=nc.sync)
        self.sems = {}
        self.val = {}
        for e in self.ENG:
            self.sems[e] = self.top.enter_context(nc.semaphore("es_" + e))
            self.val[e] = 0
        self.bar = self.top.enter_context(nc.semaphore("bar"))
        self.nbar = 0
        self.free_d = []
        for i in range(n_dsem):
            k = "D%d" % i
            self.sems[k] = self.top.enter_context(nc.semaphore("ds_%d" % i))
            self.val[k] = 0
            self.free_d.append(k)
        self.free_sw = []
        for i in range(10):
            k = "DS%d" % i
            self.sems[k] = self.top.enter_context(nc.semaphore("dsw_%d" % i))
            self.val[k] = 0
            self.free_sw.append(k)
        self.stage_sw = []
        self.seen = {e: {} for e in self.ENG}
        self.stage = None
        self.stage_d = []
        self.uid = 0
        self.nins = 0
        self.reg = {}

    def begin(self):
        self.stage = ExitStack()
        self.stage_d = []

    def end(self):
        self.barrier()
        self.stage.close()
        self.stage = None
        self.free_d.extend(self.stage_d)
        self.stage_d = []
        self.free_sw.extend(self.stage_sw)
        self.stage_sw = []

    def _nm(self, name):
        self.uid += 1
        return "%s_%d" % (name, self.uid)

    def sb(self, name, shape, dt=F32, dma=False, top=False):
        st = self.top if top else self.stage
        nm = self._nm(name)
        t = st.enter_context(self.nc.sbuf_tensor(nm, list(shape), dt))
        self.reg[name] = nm
        tk = Tk(name, t)
        if dma == "sw":
            k = self.free_sw.pop()
            tk.dkey = k
            if not top:
                self.stage_sw.append(k)
        elif dma:
            k = self.free_d.pop()
            tk.dkey = k
            if not top:
                self.stage_d.append(k)
        return tk

    def ring(self, name, n, shape, dt=F32, dma=False):
        return Ring([self.sb("%s%d" % (name, i), shape, dt, dma=dma) for i in range(n)])

    def ps(self, name, shape, dt=F32):
        t = self.stage.enter_context(self.nc.psum_tensor(self._nm(name), list(shape), dt))
        return Tk(name, t)

    def ps_views(self, name, n, sub_shape, dt=F32):
        t = self.stage.enter_context(self.nc.psum_tensor(self._nm(name), [128, n] + list(sub_shape), dt))
        bank = Tk(name + "_bank")
        views = [Tk("%s%d" % (name, i), t[:, i]) for i in range(n)]
        for v in views:
            v.bank = bank
        return views

    def psring(self, name, n, shape, dt=F32):
        return Ring([self.ps("%s%d" % (name, i), shape, dt) for i in range(n)])

    def _wait(self, eng, toks):
        need = {}
        for d in toks:
            for k, v in d.items():
                if v > need.get(k, 0):
                    need[k] = v
        seen = self.seen[eng]
        for k, v in need.items():
            if k == eng and eng == "pe":
                continue
            if seen.get(k, 0) >= v:
                continue
            self.eng[eng].wait_ge(self.sems[k], v)
            seen[k] = v

    @staticmethod
    def _deps(reads, writes):
        toks = []
        for t in reads:
            toks.append(t.w)
        for t in writes:
            toks.append(t.w)
            toks.append(t.r)
            if t.bank is not None:
                toks.append(t.bank.r)
        return toks

    @staticmethod
    def _mark(k, v, reads, writes):
        for t in reads:
            if t.r.get(k, 0) < v:
                t.r[k] = v
            if t.bank is not None and t.bank.r.get(k, 0) < v:
                t.bank.r[k] = v
        for t in writes:
            t.w = {k: v}
            t.r = {}

    def op(self, eng, fn, reads=(), writes=()):
        self._wait(eng, self._deps(reads, writes))
        ins = fn(self.eng[eng])
        self.val[eng] += 1
        ins.then_inc(self.sems[eng], 1)
        self._mark(eng, self.val[eng], reads, writes)
        self.nins += 1
        return ins

    def dma(self, q, out, in_, reads=(), writes=(), owner=None, **kw):
        self._wait(q, self._deps(reads, writes))
        ins = self.eng[q].dma_start(out=out, in_=in_, **kw)
        k = owner.dkey
        self.val[k] += 16
        ins.then_inc(self.sems[k], 16)
        self._mark(k, self.val[k], reads, writes)
        self.nins += 1
        return ins

    def barrier(self):
        self.nbar += 1
        for e in self.ENG:
            g = self.eng[e]
            if self.val[e] > 0:
                g.wait_ge(self.sems[e], self.val[e])
            if e == "sp":
                for k, v in self.val.items():
                    if k[0] == "D" and v > 0 and self.seen["sp"].get(k, 0) < v:
                        g.wait_ge(self.sems[k], v)
            g.sem_inc(self.bar, 1)
        for e in self.ENG:
            self.eng[e].wait_ge(self.bar, 5 * self.nbar)
        for e in self.ENG:
            for k, v in self.val.items():
                self.seen[e][k] = v

    def close(self):
        self.top.close()


def _t5_bucket_np(rel):
    rel = np.asarray(rel, dtype=np.int32)
    nb = 16
    ret = np.where(rel > 0, nb, 0).astype(np.int32)
    n = np.abs(rel)
    max_exact = nb // 2
    nf = np.maximum(n, 1).astype(np.float32)
    large = max_exact + (np.log(nf / np.float32(max_exact)) / np.float32(math.log(128 / max_exact))
                         * np.float32(nb - max_exact)).astype(np.int32)
    large = np.minimum(large, nb - 1)
    return ret + np.where(n < max_exact, n, large)


def _bucket_onehot():
    rel = 255 - np.arange(512)
    b = _t5_bucket_np(rel)
    oh = np.zeros((32, 512), np.float32)
    oh[b, np.arange(512)] = 1.0
    return oh


class Prog:
    def __init__(self, nseq=SEQ_PER_CORE, layers=DEPTH, dbg=False, stop=10 ** 9):
        self.stop = stop
        self.nstage = 0
        self.nseq = nseq
        self.layers = layers
        self.dbg = dbg if dbg else ()
        ne = max(1, (layers + 1) // 2)
        no = layers // 2
        od = (lambda *sh: [max(no, 1)] + ([1] * len(sh) if no == 0 else list(sh)))
        nc = bass.Bass("TRN2", target_bir_lowering=False)
        self.nc = nc
        dt = nc.dram_tensor
        I = "ExternalInput"
        self.x = dt("x", [nseq, SEQ, D], F32, kind=I).ap()
        self.meta = dt("meta_tokens", [NMETA, D], F32, kind=I).ap()
        self.norm_gain = dt("norm_gain", [4, D], F32, kind=I).ap()
        self.final_gain = dt("final_norm_gain", [D], F32, kind=I).ap()
        self.relb = dt("rel_bias_table", [32, 8], F32, kind=I).ap()
        self.w_in_even = dt("w_in_even", [ne, D, P_EVEN], F32, kind=I).ap()
        self.conv_w = dt("conv_w", [2, 31, 1024], F32, kind=I).ap()
        self.conv_b = dt("conv_b", [2, 1024], F32, kind=I).ap()
        self.ln_g = dt("conv_ln_gain", [2, 1024], F32, kind=I).ap()
        self.ln_b = dt("conv_ln_bias", [2, 1024], F32, kind=I).ap()
        self.kv_g = dt("kv_norm_gain", [2, 256], F32, kind=I).ap()
        self.w_uk = dt("w_uk", [ne, 8, 256, 128], F32, kind=I).ap()
        self.w_uv = dt("w_uv", [ne, 8, 256, 128], F32, kind=I).ap()
        self.w_out_even = dt("w_out_even", [ne, D, D], F32, kind=I).ap()
        self.w_in_odd = dt("w_in_odd", od(D, 4 * D), F32, kind=I).ap()
        self.lb_logits = dt("lb_logits", [4, D], F32, kind=I).ap()
        self.rec_g = dt("rec_norm_gain", [2, D], F32, kind=I).ap()
        self.w_out_odd = dt("w_out_odd", od(D, D), F32, kind=I).ap()
        self.c_oh = dt("c_oh", [32, 512], F32, kind=I).ap()
        self.out = dt("out", [nseq, SEQ, D], F32, kind="ExternalOutput").ap()
        sk = lambda n: "ExternalOutput" if n in self.dbg else "Internal"
        self.H = dt("Hs", [T, D], F32, kind=sk("Hs")).ap()
        self.ZT = dt("ZTs", [4 * D, T], F32, kind=sk("ZTs")).ap()
        self.VTOK = dt("VTOKs", [T, D], BF16, kind=sk("VTOKs")).ap()
        self.ZB = dt("ZBs", [2176, T], BF16, kind=sk("ZBs")).ap()
        self.MIXT = dt("MIXTs", [D, T], BF16, kind=sk("MIXTs")).ap()
        self.FD = dt("FDs", [8, 512], F32, kind=sk("FDs")).ap()
        self.em = Em(nc)

    def build(self):
        em = self.em

        def run(f, *a, **k):
            if self.nstage < self.stop:
                f(*a, **k)
            self.nstage += 1

        run(self.setup_consts)
        for s in range(self.nseq):
            for l in range(self.layers):
                last = (l == self.layers - 1)
                if l % 2 == 0:
                    run(self.stage_norm_proj, s, l, even=True)
                    run(self.stage_conv, l // 2)
                    run(self.stage_attn, l // 2)
                    run(self.stage_out, s, l, self.w_out_even[l // 2], last)
                else:
                    run(self.stage_norm_proj, s, l, even=False)
                    run(self.stage_rec, l)
                    run(self.stage_out, s, l, self.w_out_odd[l // 2], last)
        em.close()
        return self.nc

    def vecT(self, dst, dst_cols, src_rows_ap, nrows, ps, stg):
        em = self.em
        em.dma("sp", stg[0:nrows, :], src_rows_ap, writes=[stg], owner=stg)
        em.op("pe", lambda e: e.transpose(ps[:, 0:nrows], stg[0:nrows, :], self.idf[0:nrows, 0:nrows]),
              reads=[stg, self.idf], writes=[ps])
        em.op("act", lambda e: e.activation(out=dst[:, dst_cols:dst_cols + nrows], in_=ps[:, 0:nrows], func=AF.Copy),
              reads=[ps], writes=[dst])

    def setup_consts(self):
        em = self.em
        nc = self.nc
        self.idf = em.sb("idf", [128, 128], F32, top=True)
        self.idb = em.sb("idb", [128, 128], BF16, top=True)
        self.onesb = em.sb("onesb", [128, 128], BF16, top=True)
        self.avg = {}
        for n in (128, 256, 1024):
            self.avg[n] = em.sb("avg%d" % n, [128, 128], F32, top=True)
        self.cmask = em.sb("cmask", [128, 128], F32, top=True)
        self.EB = em.sb("EB", [128, 3, 8, 128], F32, top=True)
        self.bfar = em.sb("bfar", [128, 8], F32, top=True, dma=True)
        self.lbv = em.sb("lbv", [128, 2, 16], F32, top=True)
        self.oml = em.sb("oml", [128, 2, 16], F32, top=True)
        self.noml = em.sb("noml", [128, 2, 16], F32, top=True)

        em.begin()
        P = lambda f, **k: em.op("pool", f, **k)
        P(lambda e: e.memset(self.idf[:], 1.0), writes=[self.idf])
        P(lambda e: e.affine_select(out=self.idf[:], in_=self.idf[:], pattern=[[-1, 128]], compare_op=ALU.is_equal,
                                    fill=0.0, base=0, channel_multiplier=1), reads=[self.idf], writes=[self.idf])
        P(lambda e: e.tensor_copy(self.idb[:], self.idf[:]), reads=[self.idf], writes=[self.idb])
        P(lambda e: e.memset(self.onesb[:], 1.0), writes=[self.onesb])
        for n in (128, 256, 1024):
            P(lambda e, n=n: e.memset(self.avg[n][:], 1.0 / n), writes=[self.avg[n]])
        P(lambda e: e.memset(self.cmask[:], 1.0), writes=[self.cmask])
        P(lambda e: e.affine_select(out=self.cmask[:], in_=self.cmask[:], pattern=[[1, 128]], compare_op=ALU.is_ge,
                                    fill=0.0, base=0, channel_multiplier=-1), reads=[self.cmask], writes=[self.cmask])
        P(lambda e: e.memset(self.cmask[0:64, 64:128], 0.0), writes=[self.cmask])

        anti = em.sb("anti", [128, 128], F32)
        P(lambda e: e.memset(anti[:], 1.0), writes=[anti])
        P(lambda e: e.affine_select(out=anti[:], in_=anti[:], pattern=[[1, 128]], compare_op=ALU.is_equal,
                                    fill=0.0, base=-127, channel_multiplier=1), reads=[anti], writes=[anti])
        tab = em.sb("tab", [32, 8], F32, dma=True)
        oh = em.sb("oh", [32, 512], F32, dma=True)
        fsb = em.sb("fsb", [8, 512], F32, dma=True)
        nbfar = em.sb("nbfar", [128, 8], F32)
        psA = em.ps("psA", [128, 512], F32)
        psB = em.ps("psB", [128, 128], F32)
        em.dma("sp", tab[:], self.relb[:, :], writes=[tab], owner=tab)
        em.dma("sp", oh[:], self.c_oh[:, :], writes=[oh], owner=oh)
        em.dma("sp", self.bfar[:], bass.AP(tensor=self.relb.tensor, offset=15 * 8, ap=[[0, 128], [1, 8]]),
               writes=[self.bfar], owner=self.bfar)
        em.op("dve", lambda e: e.tensor_scalar(out=nbfar[:], in0=self.bfar[:], scalar1=-1.0, scalar2=None, op0=ALU.mult),
              reads=[self.bfar], writes=[nbfar])
        em.op("pe", lambda e: e.matmul(psA[0:8, :], tab[0:32, 0:8], oh[0:32, :], start=True, stop=True),
              reads=[tab, oh], writes=[psA])
        em.op("act", lambda e: e.activation(out=fsb[:], in_=psA[0:8, :], func=AF.Copy), reads=[psA], writes=[fsb])
        fd_tk = Tk("FD")
        em.dma("sp", self.FD[:, :], fsb[:], reads=[fsb], writes=[fd_tk], owner=fsb)
        hk = em.ring("hk", 2, [128, 128], F32, dma=True)
        for ty, c0 in enumerate((128, 256, 144)):
            for h in range(8):
                hb = hk.next()
                src = bass.AP(tensor=self.FD.tensor, offset=h * 512 + c0, ap=[[1, 128], [1, 128]])
                em.dma("sp", hb[:], src, reads=[fd_tk], writes=[hb], owner=hb)
                em.op("pe", lambda e, hb=hb: e.matmul(psB[:], anti[:], hb[:], start=True, stop=True),
                      reads=[anti, hb], writes=[psB])
                em.op("act", lambda e, ty=ty, h=h: e.activation(out=self.EB[:, ty, h, :], in_=psB[:], func=AF.Exp,
                                                                bias=nbfar[:, h:h + 1]),
                      reads=[psB, nbfar], writes=[self.EB])

        stg = em.sb("stg", [128, 128], F32, dma=True)
        lbT = em.sb("lbT", [128, 64], F32)
        self.vecT(lbT, 0, self.lb_logits.rearrange("l (c p) -> (l c) p", p=128), 64, psB, stg)
        mx = em.sb("mx", [128, 16], F32)
        ex = em.sb("ex", [128, 4, 16], F32)
        sm = em.sb("sm", [128, 16], F32)
        rs = em.sb("rs", [128, 16], F32)
        c1 = em.sb("c1", [128, 16], F32)
        V = lambda f, **k: em.op("dve", f, **k)
        V(lambda e: e.tensor_max(out=mx[:], in0=lbT[:, 0:16], in1=lbT[:, 16:32]), reads=[lbT], writes=[mx])
        V(lambda e: e.tensor_max(out=mx[:], in0=mx[:], in1=lbT[:, 32:48]), reads=[lbT, mx], writes=[mx])
        V(lambda e: e.tensor_max(out=mx[:], in0=mx[:], in1=lbT[:, 48:64]), reads=[lbT, mx], writes=[mx])
        for l in range(4):
            V(lambda e, l=l: e.tensor_sub(out=ex[:, l, :], in0=lbT[:, 16 * l:16 * l + 16], in1=mx[:]),
              reads=[lbT, mx], writes=[ex])
        em.op("act", lambda e: e.activation(out=ex[:], in_=ex[:], func=AF.Exp), reads=[ex], writes=[ex])
        V(lambda e: e.tensor_add(out=sm[:], in0=ex[:, 0, :], in1=ex[:, 1, :]), reads=[ex], writes=[sm])
        V(lambda e: e.tensor_add(out=sm[:], in0=sm[:], in1=ex[:, 2, :]), reads=[ex, sm], writes=[sm])
        V(lambda e: e.tensor_add(out=sm[:], in0=sm[:], in1=ex[:, 3, :]), reads=[ex, sm], writes=[sm])
        V(lambda e: e.reciprocal(out=rs[:], in_=sm[:]), reads=[sm], writes=[rs])
        V(lambda e: e.tensor_mul(out=self.lbv[:, 0, :], in0=ex[:, 1, :], in1=rs[:]), reads=[ex, rs], writes=[self.lbv])
        V(lambda e: e.tensor_add(out=c1[:], in0=ex[:, 1, :], in1=ex[:, 2, :]), reads=[ex], writes=[c1])
        V(lambda e: e.tensor_add(out=c1[:], in0=c1[:], in1=ex[:, 3, :]), reads=[ex, c1], writes=[c1])
        V(lambda e: e.tensor_mul(out=self.lbv[:, 1, :], in0=c1[:], in1=rs[:]), reads=[c1, rs], writes=[self.lbv])
        V(lambda e: e.tensor_scalar(out=self.oml[:], in0=self.lbv[:], scalar1=-1.0, scalar2=1.0, op0=ALU.mult, op1=ALU.add),
          reads=[self.lbv], writes=[self.oml])
        V(lambda e: e.tensor_scalar(out=self.noml[:], in0=self.oml[:], scalar1=-1.0, scalar2=None, op0=ALU.mult),
          reads=[self.oml], writes=[self.noml])
        em.end()

    def h_src(self, s, l, i):
        r0, n = trng(i)
        if l == 0:
            if i == 0:
                return self.meta[:, :]
            return self.x[s, r0 - 16:r0 - 16 + n, :]
        return self.H[r0:r0 + n, :]

    def rstd_rows(self, ss, n, dim, eps, tmp):
        em = self.em
        em.op("dve", lambda e: e.tensor_scalar(out=tmp[0:n, 0:1], in0=ss[0:n, 0:1], scalar1=1.0 / dim, scalar2=eps,
                                               op0=ALU.mult, op1=ALU.add), reads=[ss], writes=[tmp])
        em.op("act", lambda e: e.activation(out=tmp[0:n, 1:2], in_=tmp[0:n, 0:1], func=AF.Sqrt), reads=[tmp], writes=[tmp])
        em.op("dve", lambda e: e.reciprocal(out=ss[0:n, 1:2], in_=tmp[0:n, 1:2]), reads=[tmp], writes=[ss])

    def stage_norm_proj(self, s, l, even):
        em = self.em
        em.begin()
        hnT = em.sb("hnT", [128, 16, T], BF16)
        outer = em.stage
        nrm = ExitStack()
        em.stage = nrm
        gbc = em.sb("gbc", [128, D], F32, dma=True)
        em.dma("sp", gbc[:], self.norm_gain[l, :].partition_broadcast(128), writes=[gbc], owner=gbc)
        htr = em.ring("ht", 4, [128, D], F32, dma=True)
        hsr = em.ring("hs", 3, [128, D], BF16)
        junk = em.sb("junk", [128, D], BF16)
        ssr = em.ring("ss", 4, [128, 2], F32)
        tmr = em.ring("tm", 4, [128, 2], F32)
        ptr = em.psring("ptr", 2, [128, 16, 128], BF16)
        for i in range(NT):
            r0, n = trng(i)
            ht = htr.next()
            hs = hsr.next()
            ss = ssr.next()
            tm = tmr.next()
            pt = ptr.next()
            em.dma("sp", ht[0:n, :], self.h_src(s, l, i), writes=[ht], owner=ht)
            em.op("pool", lambda e: e.memset(ss[:], 0.0), writes=[ss])
            em.op("act", lambda e: e.activation(out=junk[0:n, :], in_=ht[0:n, :], func=AF.Square, accum_out=ss[0:n, 0:1]),
                  reads=[ht], writes=[junk, ss])
            self.rstd_rows(ss, n, D, RMS_EPS, tm)
            em.op("dve", lambda e: e.scalar_tensor_tensor(out=hs[0:n, :], in0=ht[0:n, :], scalar=ss[0:n, 1:2],
                                                          in1=gbc[0:n, :], op0=ALU.mult, op1=ALU.mult),
                  reads=[ht, ss, gbc], writes=[hs])
            for k in range(16):
                em.op("pe", lambda e, k=k: e.transpose(pt[:, k, 0:n], hs[0:n, k * 128:(k + 1) * 128], self.idb[0:n, 0:n]),
                      reads=[hs, self.idb], writes=[pt])
            em.op("act", lambda e: e.activation(out=hnT[:, :, r0:r0 + n], in_=pt[:, :, 0:n], func=AF.Copy),
                  reads=[pt], writes=[hnT])
        self.hnT = hnT
        em.barrier()
        nrm.close()
        em.stage = outer
        if even:
            self.proj_even(l // 2)
        else:
            self.proj_odd(l // 2)
        em.end()

    def proj_fm(self, W, chunks, wtr32, wtr, pmm, otr, otbr=None):
        em = self.em
        hnT = self.hnT
        blocks = []
        for ch in chunks:
            col0, ncols, func, scale, dst0, dup = ch
            if (blocks and not dup and ncols == 128 and len(blocks[-1]) < 4 and not blocks[-1][-1][5]
                    and blocks[-1][-1][1] == 128 and blocks[-1][-1][0] + 128 == col0):
                blocks[-1].append(ch)
            else:
                blocks.append([ch])
        for blk in blocks:
            w32 = wtr32.next()
            wb = wtr.next()
            bcol0 = blk[0][0]
            bn = sum(c[1] for c in blk)
            em.dma("sp", w32[:, :, 0:bn], W[:, bcol0:bcol0 + bn].rearrange("(kc p) m -> p kc m", p=128),
                   writes=[w32], owner=w32)
            if blk[0][5]:
                em.dma("sp", w32[:, :, bn:2 * bn], W[:, bcol0:bcol0 + bn].rearrange("(kc p) m -> p kc m", p=128),
                       writes=[w32], owner=w32)
            for bi, (col0, ncols, func, scale, dst0, dup) in enumerate(blk):
                o = col0 - bcol0
                nm = ncols * (2 if dup else 1)
                em.op("pool", lambda e, o=o, nm=nm: e.tensor_copy(wb[:, :, o:o + nm], w32[:, :, o:o + nm]), reads=[w32], writes=[wb])
            for bi, (col0, ncols, func, scale, dst0, dup) in enumerate(blk):
                o = col0 - bcol0
                nm = ncols * (2 if dup else 1)
                tobf = dst0 < 0
                ot = otbr.next() if tobf else otr.next()
                for (c0, n) in CCH:
                    ps = pmm.next()
                    for k in range(16):
                        em.op("pe", lambda e, k=k: e.matmul(ps[0:nm, 0:n], wb[:, k, o:o + nm], hnT[:, k, c0:c0 + n],
                                                            start=(k == 0), stop=(k == 15)),
                              reads=[wb, hnT], writes=[ps])
                    em.op("act", lambda e: e.activation(out=ot[0:nm, c0:c0 + n], in_=ps[0:nm, 0:n], func=func, scale=scale),
                          reads=[ps], writes=[ot])
                if tobf:
                    r0 = -dst0 - 1
                    em.dma("act", self.ZB[r0:r0 + nm, :], ot[0:nm, :], reads=[ot], owner=ot)
                else:
                    em.dma("act", self.ZT[dst0:dst0 + nm, :], ot[0:nm, :], reads=[ot], owner=ot)

    ZE = dict(glu_v=0, glu_g=1024, gate_a=2048, q=3072, c=4096, gate_b=4352, qi=5376, ki=6400, wi=6528)

    def proj_even(self, e_):
        em = self.em
        W = self.w_in_even[e_]
        chunks = []
        for m in range(50):
            col0 = m * 128
            if col0 < 1024:
                f, sc = AF.Copy, 1.0
            elif col0 < 2048:
                f, sc = AF.Sigmoid, 1.0
            elif col0 < 3072:
                f, sc = AF.Silu, 1.0
            elif col0 < 4352:
                f, sc = AF.Copy, 1.0
            elif col0 < 5376:
                f, sc = AF.Silu, 1.0
            else:
                f, sc = AF.Copy, 0.125
            dst = col0
            if 3072 <= col0 < 4096:
                dst = -(col0 - 3072) - 1
            elif 5376 <= col0 < 6400:
                dst = -(1024 + col0 - 5376) - 1
            chunks.append((col0, 128, f, sc, dst, False))
        chunks.append((6400, 64, AF.Copy, 1.0, -2048 - 1, True))
        chunks.append((6464, 16, AF.Copy, 0.25, 6528, False))
        wtr32 = em.ring("w32", 2, [128, 16, 512], F32, dma=True)
        wtr = em.ring("wb", 2, [128, 16, 512], BF16)
        pmm = em.psring("pmm", 4, [128, 512], F32)
        otr = em.ring("ot", 2, [128, T], F32, dma=True)
        otbr = em.ring("otb", 2, [128, T], BF16, dma=True)
        self.proj_fm(W, chunks, wtr32, wtr, pmm, otr, otbr)

    def proj_odd(self, o_):
        em = self.em
        W = self.w_in_odd[o_]
        chunks = []
        for m in range(16):
            chunks.append((m * 128, 128, AF.Silu, 1.0, m * 128, False))
        for m in range(16):
            chunks.append((2048 + m * 128, 128, AF.Sigmoid, 1.0, 2048 + m * 128, False))
        for m in range(16):
            chunks.append((6144 + m * 128, 128, AF.Silu, 1.0, 4096 + m * 128, False))
        wtr32 = em.ring("w32", 2, [128, 16, 512], F32, dma=True)
        wtr = em.ring("wb", 2, [128, 16, 512], BF16)
        pmm = em.psring("pmm", 4, [128, 512], F32)
        otr = em.ring("ot", 2, [128, T], F32, dma=True)
        self.proj_fm(W, chunks, wtr32, wtr, pmm, otr)
        hnT = self.hnT
        vo = em.ring("vo", 2, [128, 512], BF16, dma=True)
        for g in range(4):
            w32 = wtr32.next()
            wb = wtr.next()
            em.dma("sp", w32[:], W[:, 4096 + g * 512:4096 + (g + 1) * 512].rearrange("(kc p) m -> p kc m", p=128),
                   writes=[w32], owner=w32)
            for k4 in range(4):
                em.op("pool", lambda e, k4=k4: e.tensor_copy(wb[:, 4 * k4:4 * k4 + 4, :], w32[:, 4 * k4:4 * k4 + 4, :]), reads=[w32], writes=[wb])
            for i in range(NT):
                r0, n = trng(i)
                ps = pmm.next()
                for k in range(16):
                    em.op("pe", lambda e, k=k: e.matmul(ps[0:n, :], hnT[:, k, r0:r0 + n], wb[:, k, :],
                                                        start=(k == 0), stop=(k == 15)),
                          reads=[wb, hnT], writes=[ps])
                v = vo.next()
                em.op("act", lambda e: e.activation(out=v[0:n, :], in_=ps[0:n, :], func=AF.Copy), reads=[ps], writes=[v])
                em.dma("act", self.VTOK[r0:r0 + n, g * 512:(g + 1) * 512], v[0:n, :], reads=[v], owner=v)

    def stage_conv(self, e_):
        em = self.em
        ZE = self.ZE
        em.begin()
        stg = em.sb("stg", [128, 128], F32, dma=True)
        pst = em.ps("pst", [128, 128], F32)
        cw = em.sb("cw", [128, 248], F32)
        cv = em.sb("cv", [128, 24], F32)
        cwr = self.conv_w[e_].rearrange("j (cc p) -> (j cc) p", p=128)
        self.vecT(cw, 0, cwr[0:124, :], 124, pst, stg)
        self.vecT(cw, 124, cwr[124:248, :], 124, pst, stg)
        self.vecT(cv, 0, self.conv_b[e_].rearrange("(cc p) -> cc p", p=128), 8, pst, stg)
        self.vecT(cv, 8, self.ln_g[e_].rearrange("(cc p) -> cc p", p=128), 8, pst, stg)
        self.vecT(cv, 16, self.ln_b[e_].rearrange("(cc p) -> cc p", p=128), 8, pst, stg)
        import os
        KCUT = int(os.environ.get("KCUT", "99"))
        if KCUT <= 1:
            em.end()
            return
        uall = em.sb("uall", [128, 8, T], F32)
        acc1 = em.sb("acc1", [128, T], F32)
        acc2 = em.sb("acc2", [128, T], F32)
        gsr = em.ring("gs", 1, [128, T], F32, dma=True)
        upr = em.ring("up", 2, [128, 30 + T], F32, dma=True)
        pa = em.ring("pa", 1, [128, T], F32)
        pb = em.ring("pb", 1, [128, T], F32)
        tmpr = em.ring("ctmp", 2, [128, T], F32)
        sq = em.ring("sq", 1, [128, T], F32)
        for b in upr.bufs:
            em.op("pool", lambda e, b=b: e.memset(b[:, 0:30], 0.0), writes=[b])
        for cc in range(8):
            gs = gsr.next()
            up = upr.next()
            A = pa.next()
            B = pb.next()
            em.dma("sp", up[:, 30:30 + T], self.ZT[ZE["glu_v"] + cc * 128:ZE["glu_v"] + (cc + 1) * 128, :], writes=[up], owner=up)
            em.dma("sp", gs[:], self.ZT[ZE["glu_g"] + cc * 128:ZE["glu_g"] + (cc + 1) * 128, :], writes=[gs], owner=gs)
            em.op("pool", lambda e: e.tensor_tensor(out=up[:, 30:30 + T], in0=up[:, 30:30 + T], in1=gs[:], op=ALU.mult),
                  reads=[up, gs], writes=[up])
            w = lambda j: cw[:, j * 8 + cc:j * 8 + cc + 1]
            em.op("dve", lambda e: e.tensor_scalar(out=A[:], in0=up[:, 0:T], scalar1=w(0), scalar2=cv[:, cc:cc + 1],
                                                   op0=ALU.mult, op1=ALU.add), reads=[up, cw, cv], writes=[A])
            for j in range(1, 17):
                em.op("dve", lambda e, j=j: e.scalar_tensor_tensor(out=A[:], in0=up[:, j:j + T], scalar=w(j), in1=A[:],
                                                                   op0=ALU.mult, op1=ALU.add), reads=[up, cw, A], writes=[A])
            em.op("act", lambda e: e.activation(out=B[:], in_=up[:, 17:17 + T], func=AF.Copy, scale=w(17)),
                  reads=[up, cw], writes=[B])
            for j in range(18, 31):
                tp = tmpr.next()
                em.op("act", lambda e, j=j, tp=tp: e.activation(out=tp[:], in_=up[:, j:j + T], func=AF.Copy, scale=w(j)),
                      reads=[up, cw], writes=[tp])
                em.op("pool", lambda e, tp=tp: e.tensor_add(out=B[:], in0=B[:], in1=tp[:]), reads=[tp, B], writes=[B])
            em.op("dve", lambda e: e.tensor_add(out=uall[:, cc, :], in0=A[:], in1=B[:]), reads=[A, B], writes=[uall])
            s2 = sq.next()
            em.op("act", lambda e: e.activation(out=s2[:], in_=uall[:, cc, :], func=AF.Square), reads=[uall], writes=[s2])
            if cc == 0:
                em.op("pool", lambda e: e.tensor_copy(acc1[:], uall[:, cc, :]), reads=[uall], writes=[acc1])
                em.op("pool", lambda e: e.tensor_copy(acc2[:], s2[:]), reads=[s2], writes=[acc2])
            else:
                em.op("pool", lambda e: e.tensor_add(out=acc1[:], in0=acc1[:], in1=uall[:, cc, :]), reads=[uall, acc1], writes=[acc1])
                em.op("pool", lambda e: e.tensor_add(out=acc2[:], in0=acc2[:], in1=s2[:]), reads=[s2, acc2], writes=[acc2])
        if KCUT <= 2:
            em.end()
            return
        mean = em.sb("mean", [128, T], F32)
        rstd = em.sb("rstd", [128, T], F32)
        p1 = em.ps("p1", [128, 512], F32)
        p2 = em.ps("p2", [128, 512], F32)
        avg = self.avg[1024]
        for (c0, n) in CCH:
            em.op("pe", lambda e: e.matmul(p1[:, 0:n], avg[:], acc1[:, c0:c0 + n], start=True, stop=True),
                  reads=[avg, acc1], writes=[p1])
            em.op("pe", lambda e: e.matmul(p2[:, 0:n], avg[:], acc2[:, c0:c0 + n], start=True, stop=True),
                  reads=[avg, acc2], writes=[p2])
            em.op("act", lambda e: e.activation(out=mean[:, c0:c0 + n], in_=p1[:, 0:n], func=AF.Copy), reads=[p1], writes=[mean])
            em.op("dve", lambda e: e.tensor_tensor(out=rstd[:, c0:c0 + n], in0=mean[:, c0:c0 + n], in1=mean[:, c0:c0 + n], op=ALU.mult),
                  reads=[mean], writes=[rstd])
            em.op("dve", lambda e: e.tensor_sub(out=rstd[:, c0:c0 + n], in0=p2[:, 0:n], in1=rstd[:, c0:c0 + n]),
                  reads=[p2, rstd], writes=[rstd])
            em.op("dve", lambda e: e.tensor_scalar(out=rstd[:, c0:c0 + n], in0=rstd[:, c0:c0 + n], scalar1=LN_EPS, scalar2=None, op0=ALU.add),
                  reads=[rstd], writes=[rstd])
        em.op("act", lambda e: e.activation(out=rstd[:], in_=rstd[:], func=AF.Sqrt), reads=[rstd], writes=[rstd])
        em.op("dve", lambda e: e.reciprocal(out=rstd[:], in_=rstd[:]), reads=[rstd], writes=[rstd])
        if KCUT <= 3:
            em.end()
            return
        mxr = em.ring("mx", 2, [128, T], BF16, dma=True)
        for cc in range(8):
            ga = gsr.next()
            t1 = pa.next()
            t2 = pb.next()
            mx = mxr.next()
            em.dma("sp", ga[:], self.ZT[ZE["gate_a"] + cc * 128:ZE["gate_a"] + (cc + 1) * 128, :], writes=[ga], owner=ga)
            em.op("dve", lambda e: e.tensor_sub(out=t1[:], in0=uall[:, cc, :], in1=mean[:]), reads=[uall, mean], writes=[t1])
            em.op("pool", lambda e: e.tensor_mul(out=t1[:], in0=t1[:], in1=rstd[:]), reads=[t1, rstd], writes=[t1])
            em.op("act", lambda e: e.activation(out=t2[:], in_=t1[:], func=AF.Silu, scale=cv[:, 8 + cc:9 + cc], bias=cv[:, 16 + cc:17 + cc]),
                  reads=[t1, cv], writes=[t2])
            em.op("dve", lambda e: e.tensor_mul(out=mx[:], in0=t2[:], in1=ga[:]), reads=[t2, ga], writes=[mx])
            em.dma("sp", self.MIXT[cc * 128:(cc + 1) * 128, :], mx[:], reads=[mx], owner=mx)
        em.end()

    def stage_attn(self, e_):
        em = self.em
        ZE = self.ZE
        ZT = self.ZT
        em.begin()
        cnT = em.sb("cnT", [128, 2, T], BF16)
        cnk = em.sb("cnk", [128, NT, 256], BF16)
        wukT = em.sb("wukT", [128, 8, 256], BF16)
        wuvb = em.sb("wuvb", [128, 8, 2, 128], BF16)
        kiT2 = em.sb("kiT2", [128, T], BF16, dma=True)
        wiTok = em.sb("wiTok", [128, NT, 16], F32)
        kvg = em.sb("kvg", [128, 2], F32)
        stg = em.sb("stg", [128, 128], F32, dma=True)

        prep = ExitStack()
        stage_outer = em.stage
        em.stage = prep
        pA = em.ps("pA", [128, 512], F32)
        pB = em.ps("pB", [128, 512], F32)
        pT = em.ps("pT", [128, 128], F32)
        pTb = em.ps("pTb", [128, 2, 128], BF16)
        big = em.sb("big", [128, 2, T], F32, dma=True)
        big2 = em.sb("big2", [128, T], F32)
        rsd = em.sb("rsd", [128, T], F32)
        self.vecT(kvg, 0, self.kv_g[e_].rearrange("(cc p) -> cc p", p=128), 2, pT, stg)
        em.dma("sp", big[:], ZT[ZE["c"]:ZE["c"] + 256, :].rearrange("(cc p) t -> p cc t", p=128), writes=[big], owner=big)
        em.op("act", lambda e: e.activation(out=big2[:], in_=big[:, 0, :], func=AF.Square), reads=[big], writes=[big2])
        em.op("act", lambda e: e.activation(out=rsd[:], in_=big[:, 1, :], func=AF.Square), reads=[big], writes=[rsd])
        em.op("dve", lambda e: e.tensor_add(out=big2[:], in0=big2[:], in1=rsd[:]), reads=[big2, rsd], writes=[big2])
        avg = self.avg[256]
        for (c0, n) in CCH:
            em.op("pe", lambda e: e.matmul(pA[:, 0:n], avg[:], big2[:, c0:c0 + n], start=True, stop=True),
                  reads=[avg, big2], writes=[pA])
            em.op("dve", lambda e: e.tensor_scalar(out=rsd[:, c0:c0 + n], in0=pA[:, 0:n], scalar1=RMS_EPS, scalar2=None, op0=ALU.add),
                  reads=[pA], writes=[rsd])
        em.op("act", lambda e: e.activation(out=rsd[:], in_=rsd[:], func=AF.Sqrt), reads=[rsd], writes=[rsd])
        em.op("dve", lambda e: e.reciprocal(out=rsd[:], in_=rsd[:]), reads=[rsd], writes=[rsd])
        for cc in range(2):
            em.op("dve", lambda e, cc=cc: e.scalar_tensor_tensor(out=cnT[:, cc, :], in0=big[:, cc, :], scalar=kvg[:, cc:cc + 1],
                                                                 in1=rsd[:], op0=ALU.mult, op1=ALU.mult),
                  reads=[big, kvg, rsd], writes=[cnT])
        for i in range(NT):
            r0, n = trng(i)
            for cc in range(2):
                em.op("pe", lambda e, cc=cc: e.transpose(pTb[0:n, cc, :], cnT[:, cc, r0:r0 + n], self.idb[:, :]),
                      reads=[cnT, self.idb], writes=[pTb])
            em.op("act", lambda e: e.activation(out=cnk[0:n, i, :], in_=pTb[0:n, :, :], func=AF.Copy), reads=[pTb], writes=[cnk])
        wld = em.ring("wld", 2, [128, 2, 128], F32, dma=True)
        for h in range(8):
            w = wld.next()
            em.dma("sp", w[:], self.w_uk[e_, h].rearrange("(cc p) d -> p cc d", p=128), writes=[w], owner=w)
            for cc in range(2):
                em.op("pe", lambda e, cc=cc: e.transpose(pT[:, :], w[:, cc, :], self.idf[:, :]), reads=[w, self.idf], writes=[pT])
                em.op("act", lambda e, cc=cc: e.activation(out=wukT[:, h, cc * 128:(cc + 1) * 128], in_=pT[:, :], func=AF.Copy,
                                                           scale=128.0 ** -0.5), reads=[pT], writes=[wukT])
            w2 = wld.next()
            em.dma("sp", w2[:], self.w_uv[e_, h].rearrange("(cc p) d -> p cc d", p=128), writes=[w2], owner=w2)
            em.op("pool", lambda e: e.tensor_copy(wuvb[:, h, :, :], w2[:]), reads=[w2], writes=[wuvb])
        em.dma("sp", kiT2[:], self.ZB[2048:2176, :], writes=[kiT2], owner=kiT2)
        wiT = em.sb("wiT", [16, T], F32, dma=True)
        em.dma("sp", wiT[:], ZT[ZE["wi"]:ZE["wi"] + 16, :], writes=[wiT], owner=wiT)
        for i in range(NT):
            r0, n = trng(i)
            em.op("pe", lambda e: e.transpose(pT[0:n, 0:16], wiT[0:16, r0:r0 + n], self.idf[0:16, 0:16]),
                  reads=[wiT, self.idf], writes=[pT])
            em.op("act", lambda e: e.activation(out=wiTok[0:n, i, :], in_=pT[0:n, 0:16], func=AF.Copy), reads=[pT], writes=[wiTok])
        em.barrier()
        prep.close()
        em.stage = stage_outer

        import os
        AQT = int(os.environ.get("AQT", "99"))
        ASUB = int(os.environ.get("ASUB", "99"))
        pdot = em.psring("pdot", 2, [128, 512], F32)
        pmt = em.ps("pmt", [128, 8, 128], BF16)
        plog = em.psring("plog", 2, [128, 4, 128], F32)
        pso_r = em.psring("pso", 1, [128, 3, 128], F32)
        psb = em.ps("psb", [128, 128], F32)
        pql = em.ps("pql", [128, 2, 128], F32)
        qi_r = em.ring("qiT", 2, [128, 8, 128], BF16, dma=True)
        qh_r = em.ring("qhT", 2, [128, 8, 128], BF16, dma=True)
        ql_r = em.ring("qlat", 2, [128, 8, 2, 128], BF16)
        score_r = em.ring("score", 2, [128, T], F32)
        work_r = em.ring("work", 1, [128, T], F32)
        m8 = em.sb("m8", [128, 8], F32)
        NIT = 24
        blo = em.sb("blo", [128, 1], F32)
        brg = em.sb("brg", [128, 1], F32)
        bthr = em.sb("bthr", [128, 1], F32)
        bcnt = em.sb("bcnt", [128, 1], F32)
        btq = em.sb("btq", [128, 1], F32)
        stab = em.sb("stab", [128, NIT], F32)
        pw2 = em.sb("pw2", [128, NIT], F32)
        for k in range(NIT):
            em.op("pool", lambda e, k=k: e.memset(pw2[:, k:k + 1], 2.0 ** -(k + 1)), writes=[pw2])
        rl_r = em.ring("rl", 3, [128, 512], F32)
        mask_r = em.ring("mask", 2, [128, T], BF16)
        maskT_r = em.ring("maskT", 2, [128, NT, 128], BF16)
        cm_r = em.ring("cm", 2, [128, 8, 2, 128], F32)
        ex_r = em.ring("ex", 2, [128, 4, 128], F32)
        pt_r = em.ring("ptile", 10, [128, 4, 128], BF16)
        osb_r = em.ring("osb", 2, [128, 2, 128], BF16)
        rden_r = em.ring("rden", 2, [128, 128], F32)
        dsb_r = em.ring("dsb", 2, [128, 128], F32)
        gb_r = em.ring("gb", 2, [128, 8, 128], F32, dma=True)
        tb_r = em.ring("tb", 2, [128, 128], F32)
        mixb_r = em.ring("mixb", 2, [128, 8, 128], BF16, dma=True)
        EB = self.EB
        def pre(qt):
            q0, nq = trng(qt)
            nk = q0 + nq
            score = score_r.next()
            gb = gb_r.next()
            em.dma("sp", gb[:, :, 0:nq], ZT[ZE["gate_b"]:ZE["gate_b"] + 1024, q0:q0 + nq].rearrange("(h p) t -> p h t", p=128),
                   writes=[gb], owner=gb)
            qiT = qi_r.next()
            em.dma("sp", qiT[:, :, 0:nq], self.ZB[1024:2048, q0:q0 + nq].rearrange("(c p) t -> p c t", p=128),
                   writes=[qiT], owner=qiT)
            qhT = qh_r.next()
            em.dma("sp", qhT[:, :, 0:nq], self.ZB[0:1024, q0:q0 + nq].rearrange("(h p) t -> p h t", p=128),
                   writes=[qhT], owner=qhT)
            qlat = ql_r.next()
            for h in range(8):
                for cc in range(2):
                    em.op("pe", lambda e, h=h, cc=cc: e.matmul(pql[:, cc, 0:nq], wukT[:, h, cc * 128:(cc + 1) * 128], qhT[:, h, 0:nq],
                                                               start=True, stop=True), reads=[wukT, qhT], writes=[pql])
                em.op("act", lambda e, h=h: e.activation(out=qlat[:, h, :, 0:nq], in_=pql[:, :, 0:nq], func=AF.Copy),
                      reads=[pql], writes=[qlat])
            yield
            kch = [(k0, min(512, nk - k0)) for k0 in range(0, nk, 512)]
            for h16 in range(16):
                c_, po = h16 // 2, (h16 % 2) * 64
                for (k0, n) in kch:
                    ps = pdot.next()
                    rl = rl_r.next()
                    em.op("pe", lambda e: e.matmul(ps[0:nq, 0:n], qiT[po:po + 64, c_, 0:nq], kiT2[po:po + 64, k0:k0 + n],
                                                   start=True, stop=True), reads=[qiT, kiT2], writes=[ps])
                    em.op("act", lambda e: e.activation(out=rl[0:nq, 0:n], in_=ps[0:nq, 0:n], func=AF.Relu), reads=[ps], writes=[rl])
                    if h16 == 0:
                        em.op("dve", lambda e: e.tensor_scalar(out=score[0:nq, k0:k0 + n], in0=rl[0:nq, 0:n],
                                                               scalar1=wiTok[0:nq, qt, 0:1], scalar2=None, op0=ALU.mult),
                              reads=[rl, wiTok], writes=[score])
                    else:
                        em.op("dve", lambda e: e.scalar_tensor_tensor(out=score[0:nq, k0:k0 + n], in0=rl[0:nq, 0:n],
                                                                      scalar=wiTok[0:nq, qt, h16:h16 + 1],
                                                                      in1=score[0:nq, k0:k0 + n], op0=ALU.mult, op1=ALU.add),
                              reads=[rl, wiTok, score], writes=[score])
                yield
            mask = mask_r.next()
            if qt >= 1:
                em.op("pool", lambda e: e.memset(score[0:64, nk - 64:nk], NEG), reads=[], writes=[score])
            if nk > TOPK and nk - 64 < TOPK:
                work = work_r.next()
                src = score
                for r in range(TOPK // 8):
                    em.op("dve", lambda e, src=src: e.max(out=m8[0:nq, :], in_=src[0:nq, 0:nk]), reads=[src], writes=[m8])
                    em.op("dve", lambda e, src=src: e.match_replace(out=work[0:nq, 0:nk], in_to_replace=m8[0:nq, :],
                                                                    in_values=src[0:nq, 0:nk], imm_value=NEG2),
                          reads=[src, m8], writes=[work])
                    src = work
                    yield
                em.op("dve", lambda e: e.tensor_scalar(out=mask[0:nq, 0:nk], in0=work[0:nq, 0:nk], scalar1=-2.0e38, scalar2=None,
                                                       op0=ALU.is_lt), reads=[work], writes=[mask])
            elif nk > TOPK:
                nlo = nk - 64
                em.op("dve", lambda e: e.max(out=m8[0:nq, :], in_=score[0:nq, 0:nk]), reads=[score], writes=[m8])
                em.op("dve", lambda e: e.tensor_reduce(out=blo[0:nq, :], in_=score[0:nq, 0:nlo], axis=AX.X, op=ALU.min),
                      reads=[score], writes=[blo])
                em.op("dve", lambda e: e.tensor_sub(out=brg[0:nq, :], in0=m8[0:nq, 0:1], in1=blo[0:nq, :]), reads=[m8, blo], writes=[brg])
                em.op("dve", lambda e: e.tensor_scalar(out=stab[0:nq, :], in0=pw2[0:nq, :], scalar1=brg[0:nq, 0:1], scalar2=None, op0=ALU.mult),
                      reads=[pw2, brg], writes=[stab])
                yield
                for k in range(NIT):
                    em.op("dve", lambda e, k=k: e.tensor_add(out=bthr[0:nq, :], in0=blo[0:nq, :], in1=stab[0:nq, k:k + 1]),
                          reads=[blo, stab], writes=[bthr])
                    em.op("dve", lambda e: e.tensor_scalar(out=mask[0:nq, 0:nk], in0=score[0:nq, 0:nk], scalar1=bthr[0:nq, 0:1], scalar2=0.0,
                                                           op0=ALU.is_ge, op1=ALU.add, accum_out=bcnt[0:nq, 0:1]),
                          reads=[score, bthr], writes=[mask, bcnt])
                    em.op("dve", lambda e, k=k: e.scalar_tensor_tensor(out=btq[0:nq, :], in0=bcnt[0:nq, :], scalar=TOPK - 0.5,
                                                                       in1=stab[0:nq, k:k + 1], op0=ALU.is_ge, op1=ALU.mult),
                          reads=[bcnt, stab], writes=[btq])
                    em.op("dve", lambda e: e.tensor_add(out=blo[0:nq, :], in0=blo[0:nq, :], in1=btq[0:nq, :]), reads=[blo, btq], writes=[blo])
                    yield
                em.op("dve", lambda e: e.tensor_scalar(out=mask[0:nq, 0:nk], in0=score[0:nq, 0:nk], scalar1=blo[0:nq, 0:1], scalar2=None,
                                                       op0=ALU.is_ge), reads=[score, blo], writes=[mask])
            else:
                em.op("dve", lambda e: e.tensor_scalar(out=mask[0:nq, 0:nk], in0=score[0:nq, 0:nk], scalar1=-1.0e29, scalar2=None,
                                                       op0=ALU.is_gt), reads=[score], writes=[mask])
            if qt >= 1:
                em.op("pool", lambda e: e.memset(mask[0:64, nk - 64:nk], 0.0), writes=[mask])
            maskT = maskT_r.next()
            for kb0 in range(0, qt + 1, 8):
                kbs = list(range(kb0, min(qt + 1, kb0 + 8)))
                for kb in kbs:
                    k0, nkb = trng(kb)
                    em.op("pe", lambda e, kb=kb, k0=k0, nkb=nkb: e.transpose(pmt[0:nkb, kb - kb0, 0:nq], mask[0:nq, k0:k0 + nkb],
                                                                            self.idb[0:nq, 0:nq]),
                          reads=[mask, self.idb], writes=[pmt])
                if kb0 == 0:
                    em.op("act", lambda e: e.activation(out=maskT[0:16, 0, 0:nq], in_=pmt[0:16, 0, 0:nq], func=AF.Copy),
                          reads=[pmt], writes=[maskT])
                    if len(kbs) > 1:
                        em.op("act", lambda e: e.activation(out=maskT[:, 1:len(kbs), 0:nq], in_=pmt[:, 1:len(kbs), 0:nq], func=AF.Copy),
                              reads=[pmt], writes=[maskT])
                else:
                    em.op("act", lambda e: e.activation(out=maskT[:, kb0:kb0 + len(kbs), 0:nq], in_=pmt[:, 0:len(kbs), 0:nq], func=AF.Copy),
                          reads=[pmt], writes=[maskT])
            yield
            cm = cm_r.next()
            if qt == 0:
                near = {0: (0, 0, 16)}
            elif qt == 1:
                near = {1: (0, 0, 128), 0: (1, 2, 16)}
            else:
                near = {qt: (0, 0, 128), qt - 1: (1, 1, 128)}
            for kb, (slot, ty, rows) in near.items():
                for h in range(8):
                    em.op("pool", lambda e, kb=kb, slot=slot, ty=ty, rows=rows, h=h: e.tensor_tensor(
                        out=cm[0:rows, h, slot, 0:nq], in0=EB[0:rows, ty, h, 0:nq], in1=maskT[0:rows, kb, 0:nq], op=ALU.mult),
                        reads=[EB, maskT], writes=[cm])
            self._pre[qt] = dict(gb=gb, qlat=qlat, maskT=maskT, cm=cm, near=near)
            yield

        def head(qt, h, st, mixb):
            q0, nq = trng(qt)
            gb, qlat, maskT, cm, near = st["gb"], st["qlat"], st["maskT"], st["cm"], st["near"]
            groups = [[0]] + [list(range(a, min(qt + 1, a + 4))) for a in range(1, qt + 1, 4)]
            pso = pso_r.next()
            ptiles = {}
            for grp in groups:
                pl = plog.next()
                rows = 16 if grp[0] == 0 else 128
                for gi, kb in enumerate(grp):
                    k0, nkb = trng(kb)
                    for cc in range(2):
                        em.op("pe", lambda e, gi=gi, k0=k0, nkb=nkb, cc=cc: e.matmul(
                            pl[0:nkb, gi, 0:nq], cnT[:, cc, k0:k0 + nkb], qlat[:, h, cc, 0:nq],
                            start=(cc == 0), stop=(cc == 1)), reads=[cnT, qlat], writes=[pl])
                ex = ex_r.next()
                g_n = len(grp)
                em.op("act", lambda e, rows=rows, g_n=g_n: e.activation(out=ex[0:rows, 0:g_n, 0:nq], in_=pl[0:rows, 0:g_n, 0:nq],
                                                                        func=AF.Exp, bias=self.bfar[0:rows, h:h + 1]),
                      reads=[pl, self.bfar], writes=[ex])
                ptile = pt_r.next()
                far = [gi for gi, kb in enumerate(grp) if kb not in near]
                if far:
                    a, b = far[0], far[-1] + 1
                    kba = grp[a]
                    em.op("dve", lambda e, a=a, b=b, kba=kba, rows=rows: e.tensor_tensor(
                        out=ptile[0:rows, a:b, 0:nq], in0=ex[0:rows, a:b, 0:nq], in1=maskT[0:rows, kba:kba + (b - a), 0:nq], op=ALU.mult),
                        reads=[ex, maskT], writes=[ptile])
                for gi, kb in enumerate(grp):
                    if kb in near:
                        slot, ty, rws = near[kb]
                        em.op("dve", lambda e, gi=gi, slot=slot, rws=rws: e.tensor_tensor(
                            out=ptile[0:rws, gi, 0:nq], in0=ex[0:rws, gi, 0:nq], in1=cm[0:rws, h, slot, 0:nq], op=ALU.mult),
                            reads=[ex, cm], writes=[ptile])
                for gi, kb in enumerate(grp):
                    ptiles[kb] = (ptile, gi)
            for part in range(3):
                for kb in range(qt + 1):
                    k0, nkb = trng(kb)
                    ptile, gi = ptiles[kb]
                    if part < 2:
                        lhs = cnk[0:nkb, kb, part * 128:(part + 1) * 128]
                        rd = [cnk, ptile]
                    else:
                        lhs = self.onesb[0:nkb, :]
                        rd = [self.onesb, ptile]
                    em.op("pe", lambda e, lhs=lhs, ptile=ptile, gi=gi, nkb=nkb, kb=kb, part=part: e.matmul(
                        pso[:, part, 0:nq], lhs, ptile[0:nkb, gi, 0:nq], start=(kb == 0), stop=(kb == qt)),
                        reads=rd, writes=[pso])
            osb = osb_r.next()
            rden = rden_r.next()
            dsb = dsb_r.next()
            em.op("act", lambda e: e.activation(out=osb[:, :, 0:nq], in_=pso[:, 0:2, 0:nq], func=AF.Copy), reads=[pso], writes=[osb])
            em.op("act", lambda e: e.activation(out=dsb[:, 0:nq], in_=pso[:, 2, 0:nq], func=AF.Copy), reads=[pso], writes=[dsb])
            for cc in range(2):
                em.op("pe", lambda e, cc=cc: e.matmul(psb[:, 0:nq], wuvb[:, h, cc, :], osb[:, cc, 0:nq], start=(cc == 0), stop=(cc == 1)),
                      reads=[wuvb, osb], writes=[psb])
            tb = tb_r.next()
            em.op("dve", lambda e: e.reciprocal(out=rden[:, 0:nq], in_=dsb[:, 0:nq]), reads=[dsb], writes=[rden])
            em.op("dve", lambda e: e.tensor_tensor(out=tb[:, 0:nq], in0=psb[:, 0:nq], in1=rden[:, 0:nq], op=ALU.mult),
                  reads=[psb, rden], writes=[tb])
            em.op("pool", lambda e: e.tensor_tensor(out=mixb[:, h, 0:nq], in0=tb[:, 0:nq], in1=gb[:, h, 0:nq], op=ALU.mult),
                  reads=[tb, gb], writes=[mixb])

        self._pre = {}
        for _ in pre(0):
            pass
        nqt = min(NT, AQT)
        for qt in range(nqt):
            q0, nq = trng(qt)
            st = self._pre.pop(qt)
            gen = pre(qt + 1) if qt + 1 < nqt else None
            nk1 = trng(qt + 1)[0] + trng(qt + 1)[1] if gen is not None else 0
            nsteps = 0 if gen is None else (3 + 16 + (0 if nk1 <= TOPK else (TOPK // 8 if nk1 - 64 < TOPK else NIT + 1)))
            per_head = (nsteps + 7) // 8
            mixb = mixb_r.next()
            for h in range(8):
                head(qt, h, st, mixb)
                if gen is not None:
                    for _ in range(per_head):
                        if next(gen, "done") == "done":
                            gen = None
                            break
            if gen is not None:
                for _ in gen:
                    pass
            em.dma("sp", self.MIXT[1024:2048, q0:q0 + nq].rearrange("(h p) t -> p h t", p=128), mixb[:, :, 0:nq], reads=[mixb], owner=mixb)
        em.end()

    def stage_rec(self, l):
        em = self.em
        ZT = self.ZT
        li = l // 2
        G = 4
        em.begin()
        stg = em.sb("stg", [128, 128], F32, dma=True)
        rng = em.sb("rng", [128, 16], F32)
        self.rmask = em.sb("rmask", [128, T], F32)
        em.op("pool", lambda e: e.memset(self.rmask[:], 1.0), writes=[self.rmask])
        em.op("pool", lambda e: e.memset(self.rmask[:, 0:1], 0.0), writes=[self.rmask])
        em.op("pool", lambda e: e.memset(self.rmask[:, 16:T].rearrange("p (c j) -> p c j", j=64)[:, :, 0:1], 0.0),
              writes=[self.rmask])
        ptk = em.ps("ptk", [128, 8, 128], BF16)
        pn = em.ps("pn", [128, 512], F32)
        self.vecT(rng, 0, self.rec_g[li].rearrange("(c p) -> c p", p=128), 16, pn, stg)
        qs_r = em.ring("qs", 1, [128, T], F32, dma=True)
        sg_r = em.ring("sg", 1, [128, T], F32, dma=True)
        fb_r = em.ring("fb", 1, [128, T], F32)
        b_r = em.ring("b", 1, [128, T], F32)
        d_r = em.ring("d1", 1, [128, T], F32)
        kk_r = em.ring("kk", 1, [128, T], F32)
        mo_r = em.ring("mo", 2, [128, T], BF16, dma=True)
        qt_r = em.ring("qtl", G, [128, T], BF16)
        kt_r = em.ring("ktl", G, [128, T], BF16)
        ktok_r = em.ring("ktok", G, [128, NT, 128], BF16)
        vtok_r = em.ring("vtok", G, [128, NT, 128], BF16, dma=True)
        sc_r = em.ring("sc", G, [128, 4, 33], F32)
        bl_r = em.ring("bl", G, [128, 33], F32)
        oT_r = em.ring("oT", G, [128, T], F32)
        S_rs = [em.ring("S%d" % g, 2, [128, 128], F32) for g in range(G)]
        Sb_r = em.ring("Sb", 2 * G, [128, 128], BF16)
        St_r = em.ring("St", 2 * G, [128, 128], F32)
        am_r = em.ring("am", 2 * G, [128, 128], BF16)
        hbank = [em.ps_views("ph%d" % g, 4, [128], F32) for g in range(G)]
        lb = self.lbv

        def pre_head(h, g):
            qs = qs_r.next(); sg = sg_r.next()
            fb = fb_r.next(); b = b_r.next(); d1 = d_r.next(); kk = kk_r.next()
            qtl = qt_r.next(); ktl = kt_r.next(); ktok = ktok_r.next(); vtok = vtok_r.next()
            sc = sc_r.next(); bl = bl_r.next(); oT = oT_r.next()
            em.dma("sp", qs[:], ZT[h * 128:(h + 1) * 128, :], writes=[qs], owner=qs)
            em.dma("sp", sg[:], ZT[2048 + h * 128:2048 + (h + 1) * 128, :], writes=[sg], owner=sg)
            em.dma("sp", vtok[0:16, 0, :], self.VTOK[0:16, h * 128:(h + 1) * 128], writes=[vtok], owner=vtok)
            em.dma("sp", vtok[:, 1:NT, :], self.VTOK[16:T, h * 128:(h + 1) * 128].rearrange("(i p) v -> p i v", p=128),
                   writes=[vtok], owner=vtok)
            em.op("dve", lambda e: e.tensor_scalar(out=fb[:], in0=sg[:], scalar1=self.oml[:, li, h:h + 1], scalar2=lb[:, li, h:h + 1],
                                                   op0=ALU.mult, op1=ALU.add), reads=[sg, self.oml, lb], writes=[fb])
            em.op("act", lambda e: e.activation(out=fb[:], in_=fb[:], func=AF.Ln), reads=[fb], writes=[fb])
            em.op("dve", lambda e: e.tensor_scalar(out=kk[:], in0=sg[:], scalar1=self.noml[:, li, h:h + 1], scalar2=self.oml[:, li, h:h + 1],
                                                   op0=ALU.mult, op1=ALU.add), reads=[sg, self.noml, self.oml], writes=[kk])
            em.op("dve", lambda e: e.tensor_tensor_scan(b[:], self.rmask[:], fb[:], 0.0, ALU.mult, ALU.add),
                  reads=[self.rmask, fb], writes=[b])
            em.op("pool", lambda e: e.tensor_copy(sc[:, 0, 0:1], b[:, 8:9]), reads=[b], writes=[sc])
            em.op("pool", lambda e: e.tensor_copy(sc[:, 0, 1:33], b[:, 48:T:64]), reads=[b], writes=[sc])
            em.op("pool", lambda e: e.tensor_copy(bl[:, 0:1], b[:, 15:16]), reads=[b], writes=[bl])
            em.op("pool", lambda e: e.tensor_copy(bl[:, 1:33], b[:, 79:T:64]), reads=[b], writes=[bl])
            em.op("dve", lambda e: e.tensor_sub(out=d1[:, 0:16], in0=b[:, 0:16], in1=sc[:, 0, 0:1].to_broadcast([128, 16])),
                  reads=[b, sc], writes=[d1])
            em.op("dve", lambda e: e.tensor_sub(out=d1[:, 16:T].rearrange("p (c j) -> p c j", j=64),
                                                in0=b[:, 16:T].rearrange("p (c j) -> p c j", j=64),
                                                in1=sc[:, 0, 1:33].unsqueeze(2).to_broadcast([128, 32, 64])),
                  reads=[b, sc], writes=[d1])
            em.op("act", lambda e: e.activation(out=sc[:, 1, :], in_=sc[:, 0, :], func=AF.Exp), reads=[sc], writes=[sc])
            em.op("act", lambda e: e.activation(out=sc[:, 3, :], in_=bl[:], func=AF.Exp), reads=[bl, sc], writes=[sc])
            em.op("dve", lambda e: e.tensor_sub(out=bl[:], in0=bl[:], in1=sc[:, 0, :]), reads=[bl, sc], writes=[bl])
            em.op("act", lambda e: e.activation(out=sc[:, 2, :], in_=bl[:], func=AF.Exp), reads=[bl, sc], writes=[sc])
            em.op("act", lambda e: e.activation(out=fb[:], in_=d1[:], func=AF.Exp), reads=[d1, fb], writes=[fb])
            em.op("dve", lambda e: e.tensor_mul(out=qtl[:], in0=qs[:], in1=fb[:]), reads=[qs, fb], writes=[qtl])
            em.op("act", lambda e: e.activation(out=d1[:], in_=d1[:], func=AF.Exp, scale=-1.0), reads=[d1], writes=[d1])
            em.op("pool", lambda e: e.tensor_mul(out=ktl[:], in0=kk[:], in1=d1[:]), reads=[kk, d1], writes=[ktl])
            for i0 in range(0, NT, 8):
                ii = list(range(i0, min(NT, i0 + 8)))
                for i in ii:
                    r0, n = trng(i)
                    em.op("pe", lambda e, i=i, r0=r0, n=n: e.transpose(ptk[0:n, i - i0, :], ktl[:, r0:r0 + n], self.idb[:, :]),
                          reads=[ktl, self.idb], writes=[ptk])
                if i0 == 0:
                    em.op("act", lambda e: e.activation(out=ktok[0:16, 0, :], in_=ptk[0:16, 0, :], func=AF.Copy), reads=[ptk], writes=[ktok])
                    em.op("act", lambda e: e.activation(out=ktok[:, 1:8, :], in_=ptk[:, 1:8, :], func=AF.Copy), reads=[ptk], writes=[ktok])
                else:
                    em.op("act", lambda e, i0=i0, m=len(ii): e.activation(out=ktok[:, i0:i0 + m, :], in_=ptk[:, 0:m, :], func=AF.Copy),
                          reads=[ptk], writes=[ktok])
            S = S_rs[g].next()
            em.op("pool", lambda e: e.memset(S[:], 0.0), writes=[S])
            return dict(h=h, g=g, qtl=qtl, ktl=ktl, ktok=ktok, vtok=vtok, sc=sc, oT=oT, S=S)

        def tile_part(c, i):
            r0, n = trng(i)
            qtl, ktl, vtok, oT = c["qtl"], c["ktl"], c["vtok"], c["oT"]
            psa = hbank[c["g"]][0]
            am = am_r.next()
            pso = hbank[c["g"]][1]
            em.op("pe", lambda e: e.matmul(psa[0:n, 0:n], ktl[:, r0:r0 + n], qtl[:, r0:r0 + n], start=True, stop=True),
                  reads=[ktl, qtl], writes=[psa])
            em.op("dve", lambda e: e.tensor_tensor(out=am[0:n, 0:n], in0=psa[0:n, 0:n], in1=self.cmask[0:n, 0:n], op=ALU.mult),
                  reads=[psa, self.cmask], writes=[am])
            em.op("pe", lambda e: e.matmul(pso[:, 0:n], vtok[0:n, i, :], am[0:n, 0:n], start=True, stop=True),
                  reads=[vtok, am], writes=[pso])
            em.op("act", lambda e: e.activation(out=oT[:, r0:r0 + n], in_=pso[:, 0:n], func=AF.Copy), reads=[pso], writes=[oT])

        def chunk_part(c, i, ci, p0, ncx):
            r0, n = trng(i)
            j = 0 if i == 0 else 1 + 2 * (i - 1) + ci
            c0 = r0 + p0
            qtl, ktok, vtok, oT, sc, S = c["qtl"], c["ktok"], c["vtok"], c["oT"], c["sc"], c["S"]
            Sb = Sb_r.next()
            psi = hbank[c["g"]][3]
            pss = hbank[c["g"]][2]
            St = St_r.next()
            em.op("act", lambda e: e.activation(out=Sb[:], in_=S[:], func=AF.Copy, scale=sc[:, 1, j:j + 1]),
                  reads=[S, sc], writes=[Sb])
            em.op("pe", lambda e: e.matmul(psi[:, 0:ncx], Sb[:], qtl[:, c0:c0 + ncx], start=True, stop=True),
                  reads=[Sb, qtl], writes=[psi])
            em.op("dve", lambda e: e.tensor_add(out=oT[:, c0:c0 + ncx], in0=oT[:, c0:c0 + ncx], in1=psi[:, 0:ncx]),
                  reads=[oT, psi], writes=[oT])
            em.op("pe", lambda e: e.matmul(pss[:], ktok[p0:p0 + ncx, i, :], vtok[p0:p0 + ncx, i, :], start=True, stop=True),
                  reads=[ktok, vtok], writes=[pss])
            em.op("dve", lambda e: e.tensor_scalar(out=St[:], in0=S[:], scalar1=sc[:, 3, j:j + 1], scalar2=None, op0=ALU.mult),
                  reads=[S, sc], writes=[St])
            S2 = S_rs[c["g"]].next()
            em.op("dve", lambda e: e.scalar_tensor_tensor(out=S2[:], in0=pss[:], scalar=sc[:, 2, j:j + 1], in1=St[:],
                                                          op0=ALU.mult, op1=ALU.add), reads=[pss, sc, St], writes=[S2])
            c["S"] = S2

        def post_head(c):
            h, oT = c["h"], c["oT"]
            gs = qs_r.next(); d1 = d_r.next(); kk = kk_r.next()
            em.dma("sp", gs[:], ZT[4096 + h * 128:4096 + (h + 1) * 128, :], writes=[gs], owner=gs)
            em.op("act", lambda e: e.activation(out=d1[:], in_=oT[:], func=AF.Square), reads=[oT], writes=[d1])
            avg = self.avg[128]
            for (c0, n) in CCH:
                em.op("pe", lambda e: e.matmul(pn[:, 0:n], avg[:], d1[:, c0:c0 + n], start=True, stop=True), reads=[avg, d1], writes=[pn])
                em.op("dve", lambda e: e.tensor_scalar(out=kk[:, c0:c0 + n], in0=pn[:, 0:n], scalar1=RMS_EPS, scalar2=None, op0=ALU.add),
                      reads=[pn], writes=[kk])
            em.op("act", lambda e: e.activation(out=kk[:], in_=kk[:], func=AF.Sqrt), reads=[kk], writes=[kk])
            em.op("dve", lambda e: e.reciprocal(out=kk[:], in_=kk[:]), reads=[kk], writes=[kk])
            em.op("dve", lambda e: e.tensor_mul(out=kk[:], in0=kk[:], in1=oT[:]), reads=[kk, oT], writes=[kk])
            mo = mo_r.next()
            em.op("dve", lambda e: e.scalar_tensor_tensor(out=mo[:], in0=kk[:], scalar=rng[:, h:h + 1], in1=gs[:], op0=ALU.mult, op1=ALU.mult),
                  reads=[kk, rng, gs], writes=[mo])
            em.dma("sp", self.MIXT[h * 128:(h + 1) * 128, :], mo[:], reads=[mo], owner=mo)

        for g0 in range(0, 16, G):
            ctx = [pre_head(g0 + g, g) for g in range(G)]
            for i in range(NT):
                for c in ctx:
                    tile_part(c, i)
                chunks = [(0, 16)] if i == 0 else [(0, 64), (64, 64)]
                for ci, (p0, ncx) in enumerate(chunks):
                    for c in ctx:
                        chunk_part(c, i, ci, p0, ncx)
            for c in ctx:
                post_head(c)
        em.end()

    def stage_out(self, s, l, W, last):
        em = self.em
        em.begin()
        Wb = em.sb("Wb", [128, 16, D], BF16)
        if last:
            self.fgain = em.sb("fgain", [128, D], F32, dma=True)
            em.dma("sp", self.fgain[:], self.final_gain.partition_broadcast(128), writes=[self.fgain], owner=self.fgain)
        w32r = em.ring("wo32", 2, [128, D], F32, dma=True)
        for k in range(16):
            w32 = w32r.next()
            em.dma("sp", w32[:], W[k * 128:(k + 1) * 128, :], writes=[w32], owner=w32)
            em.op("pool", lambda e, k=k: e.tensor_copy(Wb[:, k, :], w32[:]), reads=[w32], writes=[Wb])
        mtr = em.ring("mt", 2, [128, 16, 128], BF16, dma=True)
        htr = em.ring("ht", 2, [128, D], F32, dma=True)
        hnr = em.ring("hn", 2, [128, D], F32, dma=True)
        pmm = em.psring("pmm", 4, [128, 512], F32)
        junk = em.sb("junk", [128, D], BF16)
        ssr = em.ring("ss", 2, [128, 2], F32)
        tmr = em.ring("tm", 2, [128, 2], F32)
        for i in range(NT):
            r0, n = trng(i)
            if last and i == 0:
                continue
            mt = mtr.next()
            ht = htr.next()
            hn = hnr.next()
            em.dma("sp", mt[:, :, 0:n], self.MIXT[:, r0:r0 + n].rearrange("(k p) t -> p k t", p=128), writes=[mt], owner=mt)
            em.dma("sp", ht[0:n, :], self.h_src(s, l, i), writes=[ht], owner=ht)
            for c in range(4):
                ps = pmm.next()
                for k in range(16):
                    em.op("pe", lambda e, k=k, c=c: e.matmul(ps[0:n, :], mt[:, k, 0:n], Wb[:, k, c * 512:(c + 1) * 512],
                                                             start=(k == 0), stop=(k == 15)), reads=[mt, Wb], writes=[ps])
                em.op("dve", lambda e, c=c: e.tensor_add(out=hn[0:n, c * 512:(c + 1) * 512], in0=ht[0:n, c * 512:(c + 1) * 512], in1=ps[0:n, :]),
                      reads=[ht, ps], writes=[hn])
            if not last:
                em.dma("sp", self.H[r0:r0 + n, :], hn[0:n, :], reads=[hn], owner=hn)
            else:
                ss = ssr.next()
                tm = tmr.next()
                em.op("pool", lambda e: e.memset(ss[:], 0.0), writes=[ss])
                em.op("act", lambda e: e.activation(out=junk[0:n, :], in_=hn[0:n, :], func=AF.Square, accum_out=ss[0:n, 0:1]),
                      reads=[hn], writes=[junk, ss])
                self.rstd_rows(ss, n, D, RMS_EPS, tm)
                em.op("dve", lambda e: e.scalar_tensor_tensor(out=ht[0:n, :], in0=hn[0:n, :], scalar=ss[0:n, 1:2], in1=self.fgain[0:n, :],
                                                              op0=ALU.mult, op1=ALU.mult), reads=[hn, ss, self.fgain], writes=[ht])
                em.dma("sp", self.out[s, r0 - 16:r0 - 16 + n, :], ht[0:n, :], reads=[ht], owner=ht)
        em.end()


_CACHE = {}


def kernel(**inputs):
    x = np.ascontiguousarray(inputs["x"], dtype=np.float32)
    if "nc" not in _CACHE:
        _CACHE["nc"] = Prog().build()
    nc = _CACHE["nc"]
    oh = _bucket_onehot()
    names = ["meta_tokens", "norm_gain", "final_norm_gain", "rel_bias_table", "w_in_even", "conv_w", "conv_b",
             "conv_ln_gain", "conv_ln_bias", "kv_norm_gain", "w_uk", "w_uv", "w_out_even", "w_in_odd", "lb_logits",
             "rec_norm_gain", "w_out_odd"]
    shared = {k: np.ascontiguousarray(inputs[k], dtype=np.float32) for k in names}
    shared["c_oh"] = oh
    in_maps = []
    for c in range(NCORES):
        m = dict(shared)
        m["x"] = x[c * SEQ_PER_CORE:(c + 1) * SEQ_PER_CORE]
        in_maps.append(m)
    res = run_bass_kernel_spmd(nc, in_maps, core_ids=list(range(NCORES)))
    return np.concatenate([r["out"] for r in res.results], axis=0)
```

```python
import math
from contextlib import ExitStack

import numpy as np
import concourse.bass as bass
import concourse.mybir as mybir
from concourse.bass_utils import run_bass_kernel_spmd

F32 = mybir.dt.float32
BF16 = mybir.dt.bfloat16
AF = mybir.ActivationFunctionType
ALU = mybir.AluOpType
AX = mybir.AxisListType

NCORES = 8
SEQ_PER_CORE = 2
D = 2048
SEQ = 2048
NMETA = 16
T = SEQ + NMETA
NT = 17
DEPTH = 4
P_EVEN = 6480
RMS_EPS = 1e-6
LN_EPS = 1e-5
NEG = -1.0e30
NEG2 = -3.0e38
TOPK = 256


def trng(i):
    return (0, 16) if i == 0 else (16 + 128 * (i - 1), 128)


CCH = [(0, 16)] + [(16 + 512 * j, 512) for j in range(4)]


class Tk:
    __slots__ = ("name", "t", "w", "r", "dkey", "bank")

    def __init__(self, name, t=None):
        self.name = name
        self.t = t
        self.w = {}
        self.r = {}
        self.dkey = None
        self.bank = None

    def __getitem__(self, idx):
        return self.t[idx]


class Ring:
    def __init__(self, bufs):
        self.bufs = bufs
        self.i = 0

    def next(self):
        b = self.bufs[self.i % len(self.bufs)]
        self.i += 1
        return b


import os as _os
NOSELF = _os.environ.get("NOSELF", "0") == "1"


class Em:
    ENG = ("pe", "act", "dve", "pool", "sp")

    def __init__(self, nc, n_dsem=78):
        self.nc = nc
        self.top = ExitStack()
        self.eng = dict(pe=nc.tensor, act=nc.scalar, dve=nc.vector, pool=nc.gpsimd, sp=nc.sync)
        self.sems = {}
        self.val = {}
        for e in self.ENG:
            self.sems[e] = self.top.enter_context(nc.semaphore("es_" + e))
            self.val[e] = 0
        self.bar = self.top.enter_context(nc.semaphore("bar"))
        self.nbar = 0
        self.free_d = []
        for i in range(n_dsem):
            k = "D%d" % i
            self.sems[k] = self.top.enter_context(nc.semaphore("ds_%d" % i))
            self.val[k] = 0
            self.free_d.append(k)
        self.free_sw = []
        for i in range(10):
            k = "DS%d" % i
            self.sems[k] = self.top.enter_context(nc.semaphore("dsw_%d" % i))
            self.val[k] = 0
            self.free_sw.append(k)
        self.stage_sw = []
        self.seen = {e: {} for e in self.ENG}
        self.stage = None
        self.stage_d = []
        self.uid = 0
        self.nins = 0
        self.reg = {}

    def begin(self):
        self.stage = ExitStack()
        self.stage_d = []

    def end(self):
        self.barrier()
        self.stage.close()
        self.stage = None
        self.free_d.extend(self.stage_d)
        self.stage_d = []
        self.free_sw.extend(self.stage_sw)
        self.stage_sw = []

    def _nm(self, name):
        self.uid += 1
        return "%s_%d" % (name, self.uid)

    def sb(self, name, shape, dt=F32, dma=False, top=False):
        st = self.top if top else self.stage
        nm = self._nm(name)
        t = st.enter_context(self.nc.sbuf_tensor(nm, list(shape), dt))
        self.reg[name] = nm
        tk = Tk(name, t)
        if dma == "sw":
            k = self.free_sw.pop()
            tk.dkey = k
            if not top:
                self.stage_sw.append(k)
        elif dma:
            k = self.free_d.pop()
            tk.dkey = k
            if not top:
                self.stage_d.append(k)
        return tk

    def ring(self, name, n, shape, dt=F32, dma=False):
        return Ring([self.sb("%s%d" % (name, i), shape, dt, dma=dma) for i in range(n)])

    def ps(self, name, shape, dt=F32):
        t = self.stage.enter_context(self.nc.psum_tensor(self._nm(name), list(shape), dt))
        return Tk(name, t)

    def ps_views(self, name, n, sub_shape, dt=F32):
        t = self.stage.enter_context(self.nc.psum_tensor(self._nm(name), [128, n] + list(sub_shape), dt))
        bank = Tk(name + "_bank")
        views = [Tk("%s%d" % (name, i), t[:, i]) for i in range(n)]
        for v in views:
            v.bank = bank
        return views

    def psring(self, name, n, shape, dt=F32):
        return Ring([self.ps("%s%d" % (name, i), shape, dt) for i in range(n)])

    def _wait(self, eng, toks):
        need = {}
        for d in toks:
            for k, v in d.items():
                if v > need.get(k, 0):
                    need[k] = v
        seen = self.seen[eng]
        for k, v in need.items():
            if k == eng and (eng == "pe" or NOSELF):
                continue
            if seen.get(k, 0) >= v:
                continue
            self.eng[eng].wait_ge(self.sems[k], v)
            seen[k] = v

    @staticmethod
    def _deps(reads, writes):
        toks = []
        for t in reads:
            toks.append(t.w)
            if t.bank is not None:
                toks.append(t.bank.w)
        for t in writes:
            toks.append(t.w)
            toks.append(t.r)
            if t.bank is not None:
                toks.append(t.bank.r)
        return toks

    @staticmethod
    def _mark(k, v, reads, writes):
        for t in reads:
            if t.r.get(k, 0) < v:
                t.r[k] = v
            if t.bank is not None and t.bank.r.get(k, 0) < v:
                t.bank.r[k] = v
        for t in writes:
            t.w = {k: v}
            t.r = {}
            if t.bank is not None:
                t.bank.w = {k: v}

    def op(self, eng, fn, reads=(), writes=()):
        self._wait(eng, self._deps(reads, writes))
        ins = fn(self.eng[eng])
        self.val[eng] += 1
        ins.then_inc(self.sems[eng], 1)
        self._mark(eng, self.val[eng], reads, writes)
        self.nins += 1
        return ins

    def dma(self, q, out, in_, reads=(), writes=(), owner=None, **kw):
        self._wait(q, self._deps(reads, writes))
        ins = self.eng[q].dma_start(out=out, in_=in_, **kw)
        k = owner.dkey
        self.val[k] += 16
        ins.then_inc(self.sems[k], 16)
        self._mark(k, self.val[k], reads, writes)
        self.nins += 1
        return ins

    def barrier(self):
        self.nbar += 1
        for e in self.ENG:
            g = self.eng[e]
            if self.val[e] > 0:
                g.wait_ge(self.sems[e], self.val[e])
            if e == "sp":
                for k, v in self.val.items():
                    if k[0] == "D" and v > 0 and self.seen["sp"].get(k, 0) < v:
                        g.wait_ge(self.sems[k], v)
            g.sem_inc(self.bar, 1)
        for e in self.ENG:
            self.eng[e].wait_ge(self.bar, 5 * self.nbar)
        for e in self.ENG:
            for k, v in self.val.items():
                self.seen[e][k] = v

    def close(self):
        self.top.close()


def _t5_bucket_np(rel):
    rel = np.asarray(rel, dtype=np.int32)
    nb = 16
    ret = np.where(rel > 0, nb, 0).astype(np.int32)
    n = np.abs(rel)
    max_exact = nb // 2
    nf = np.maximum(n, 1).astype(np.float32)
    large = max_exact + (np.log(nf / np.float32(max_exact)) / np.float32(math.log(128 / max_exact))
                         * np.float32(nb - max_exact)).astype(np.int32)
    large = np.minimum(large, nb - 1)
    return ret + np.where(n < max_exact, n, large)


def _bucket_onehot():
    rel = 255 - np.arange(512)
    b = _t5_bucket_np(rel)
    oh = np.zeros((32, 512), np.float32)
    oh[b, np.arange(512)] = 1.0
    return oh


class Prog:
    def __init__(self, nseq=SEQ_PER_CORE, layers=DEPTH, dbg=False, stop=10 ** 9):
        self.stop = stop
        self.nstage = 0
        self.nseq = nseq
        self.layers = layers
        self.dbg = dbg if dbg else ()
        ne = max(1, (layers + 1) // 2)
        no = layers // 2
        od = (lambda *sh: [max(no, 1)] + ([1] * len(sh) if no == 0 else list(sh)))
        nc = bass.Bass("TRN2", target_bir_lowering=False)
        self.nc = nc
        dt = nc.dram_tensor
        I = "ExternalInput"
        self.x = dt("x", [nseq, SEQ, D], F32, kind=I).ap()
        self.meta = dt("meta_tokens", [NMETA, D], F32, kind=I).ap()
        self.norm_gain = dt("norm_gain", [4, D], F32, kind=I).ap()
        self.final_gain = dt("final_norm_gain", [D], F32, kind=I).ap()
        self.relb = dt("rel_bias_table", [32, 8], F32, kind=I).ap()
        self.w_in_even = dt("w_in_even", [ne, D, P_EVEN], F32, kind=I).ap()
        self.conv_w = dt("conv_w", [2, 31, 1024], F32, kind=I).ap()
        self.conv_b = dt("conv_b", [2, 1024], F32, kind=I).ap()
        self.ln_g = dt("conv_ln_gain", [2, 1024], F32, kind=I).ap()
        self.ln_b = dt("conv_ln_bias", [2, 1024], F32, kind=I).ap()
        self.kv_g = dt("kv_norm_gain", [2, 256], F32, kind=I).ap()
        self.w_uk = dt("w_uk", [ne, 8, 256, 128], F32, kind=I).ap()
        self.w_uv = dt("w_uv", [ne, 8, 256, 128], F32, kind=I).ap()
        self.w_out_even = dt("w_out_even", [ne, D, D], F32, kind=I).ap()
        self.w_in_odd = dt("w_in_odd", od(D, 4 * D), F32, kind=I).ap()
        self.lb_logits = dt("lb_logits", [4, D], F32, kind=I).ap()
        self.rec_g = dt("rec_norm_gain", [2, D], F32, kind=I).ap()
        self.w_out_odd = dt("w_out_odd", od(D, D), F32, kind=I).ap()
        self.c_oh = dt("c_oh", [32, 512], F32, kind=I).ap()
        self.out = dt("out", [nseq, SEQ, D], F32, kind="ExternalOutput").ap()
        sk = lambda n: "ExternalOutput" if n in self.dbg else "Internal"
        self.H = dt("Hs", [T, D], F32, kind=sk("Hs")).ap()
        self.ZT = dt("ZTs", [4 * D, T], F32, kind=sk("ZTs")).ap()
        self.VTOK = dt("VTOKs", [T, D], BF16, kind=sk("VTOKs")).ap()
        self.ZB = dt("ZBs", [2176, T], BF16, kind=sk("ZBs")).ap()
        self.MIXT = dt("MIXTs", [D, T], BF16, kind=sk("MIXTs")).ap()
        self.FD = dt("FDs", [8, 512], F32, kind=sk("FDs")).ap()
        self.em = Em(nc)

    def build(self):
        em = self.em

        def run(f, *a, **k):
            if self.nstage < self.stop:
                f(*a, **k)
            self.nstage += 1

        run(self.setup_consts)
        for s in range(self.nseq):
            for l in range(self.layers):
                last = (l == self.layers - 1)
                if l % 2 == 0:
                    run(self.stage_norm_proj, s, l, even=True)
                    run(self.stage_conv, l // 2)
                    run(self.stage_attn, l // 2)
                    run(self.stage_out, s, l, self.w_out_even[l // 2], last)
                else:
                    run(self.stage_norm_proj, s, l, even=False)
                    run(self.stage_rec, l)
                    run(self.stage_out, s, l, self.w_out_odd[l // 2], last)
        em.close()
        return self.nc

    def vecT(self, dst, dst_cols, src_rows_ap, nrows, ps, stg):
        em = self.em
        em.dma("sp", stg[0:nrows, :], src_rows_ap, writes=[stg], owner=stg)
        em.op("pe", lambda e: e.transpose(ps[:, 0:nrows], stg[0:nrows, :], self.idf[0:nrows, 0:nrows]),
              reads=[stg, self.idf], writes=[ps])
        em.op("act", lambda e: e.activation(out=dst[:, dst_cols:dst_cols + nrows], in_=ps[:, 0:nrows], func=AF.Copy),
              reads=[ps], writes=[dst])

    def setup_consts(self):
        em = self.em
        nc = self.nc
        self.idf = em.sb("idf", [128, 128], F32, top=True)
        self.idb = em.sb("idb", [128, 128], BF16, top=True)
        self.onesb = em.sb("onesb", [128, 128], BF16, top=True)
        self.avg = {}
        for n in (128, 256, 1024):
            self.avg[n] = em.sb("avg%d" % n, [128, 128], F32, top=True)
        self.cmask = em.sb("cmask", [128, 128], F32, top=True)
        self.EB = em.sb("EB", [128, 3, 8, 128], F32, top=True)
        self.bfar = em.sb("bfar", [128, 8], F32, top=True, dma=True)
        self.lbv = em.sb("lbv", [128, 2, 16], F32, top=True)
        self.oml = em.sb("oml", [128, 2, 16], F32, top=True)
        self.noml = em.sb("noml", [128, 2, 16], F32, top=True)

        em.begin()
        P = lambda f, **k: em.op("pool", f, **k)
        P(lambda e: e.memset(self.idf[:], 1.0), writes=[self.idf])
        P(lambda e: e.affine_select(out=self.idf[:], in_=self.idf[:], pattern=[[-1, 128]], compare_op=ALU.is_equal,
                                    fill=0.0, base=0, channel_multiplier=1), reads=[self.idf], writes=[self.idf])
        P(lambda e: e.tensor_copy(self.idb[:], self.idf[:]), reads=[self.idf], writes=[self.idb])
        P(lambda e: e.memset(self.onesb[:], 1.0), writes=[self.onesb])
        for n in (128, 256, 1024):
            P(lambda e, n=n: e.memset(self.avg[n][:], 1.0 / n), writes=[self.avg[n]])
        P(lambda e: e.memset(self.cmask[:], 1.0), writes=[self.cmask])
        P(lambda e: e.affine_select(out=self.cmask[:], in_=self.cmask[:], pattern=[[1, 128]], compare_op=ALU.is_ge,
                                    fill=0.0, base=0, channel_multiplier=-1), reads=[self.cmask], writes=[self.cmask])
        P(lambda e: e.memset(self.cmask[0:64, 64:128], 0.0), writes=[self.cmask])

        anti = em.sb("anti", [128, 128], F32)
        P(lambda e: e.memset(anti[:], 1.0), writes=[anti])
        P(lambda e: e.affine_select(out=anti[:], in_=anti[:], pattern=[[1, 128]], compare_op=ALU.is_equal,
                                    fill=0.0, base=-127, channel_multiplier=1), reads=[anti], writes=[anti])
        tab = em.sb("tab", [32, 8], F32, dma=True)
        oh = em.sb("oh", [32, 512], F32, dma=True)
        fsb = em.sb("fsb", [8, 512], F32, dma=True)
        nbfar = em.sb("nbfar", [128, 8], F32)
        psA = em.ps("psA", [128, 512], F32)
        psB = em.ps("psB", [128, 128], F32)
        em.dma("sp", tab[:], self.relb[:, :], writes=[tab], owner=tab)
        em.dma("sp", oh[:], self.c_oh[:, :], writes=[oh], owner=oh)
        em.dma("sp", self.bfar[:], bass.AP(tensor=self.relb.tensor, offset=15 * 8, ap=[[0, 128], [1, 8]]),
               writes=[self.bfar], owner=self.bfar)
        em.op("dve", lambda e: e.tensor_scalar(out=nbfar[:], in0=self.bfar[:], scalar1=-1.0, scalar2=None, op0=ALU.mult),
              reads=[self.bfar], writes=[nbfar])
        em.op("pe", lambda e: e.matmul(psA[0:8, :], tab[0:32, 0:8], oh[0:32, :], start=True, stop=True),
              reads=[tab, oh], writes=[psA])
        em.op("act", lambda e: e.activation(out=fsb[:], in_=psA[0:8, :], func=AF.Copy), reads=[psA], writes=[fsb])
        fd_tk = Tk("FD")
        em.dma("sp", self.FD[:, :], fsb[:], reads=[fsb], writes=[fd_tk], owner=fsb)
        hk = em.ring("hk", 2, [128, 128], F32, dma=True)
        for ty, c0 in enumerate((128, 256, 144)):
            for h in range(8):
                hb = hk.next()
                src = bass.AP(tensor=self.FD.tensor, offset=h * 512 + c0, ap=[[1, 128], [1, 128]])
                em.dma("sp", hb[:], src, reads=[fd_tk], writes=[hb], owner=hb)
                em.op("pe", lambda e, hb=hb: e.matmul(psB[:], anti[:], hb[:], start=True, stop=True),
                      reads=[anti, hb], writes=[psB])
                em.op("act", lambda e, ty=ty, h=h: e.activation(out=self.EB[:, ty, h, :], in_=psB[:], func=AF.Exp,
                                                                bias=nbfar[:, h:h + 1]),
                      reads=[psB, nbfar], writes=[self.EB])

        stg = em.sb("stg", [128, 128], F32, dma=True)
        lbT = em.sb("lbT", [128, 64], F32)
        self.vecT(lbT, 0, self.lb_logits.rearrange("l (c p) -> (l c) p", p=128), 64, psB, stg)
        mx = em.sb("mx", [128, 16], F32)
        ex = em.sb("ex", [128, 4, 16], F32)
        sm = em.sb("sm", [128, 16], F32)
        rs = em.sb("rs", [128, 16], F32)
        c1 = em.sb("c1", [128, 16], F32)
        V = lambda f, **k: em.op("dve", f, **k)
        V(lambda e: e.tensor_max(out=mx[:], in0=lbT[:, 0:16], in1=lbT[:, 16:32]), reads=[lbT], writes=[mx])
        V(lambda e: e.tensor_max(out=mx[:], in0=mx[:], in1=lbT[:, 32:48]), reads=[lbT, mx], writes=[mx])
        V(lambda e: e.tensor_max(out=mx[:], in0=mx[:], in1=lbT[:, 48:64]), reads=[lbT, mx], writes=[mx])
        for l in range(4):
            V(lambda e, l=l: e.tensor_sub(out=ex[:, l, :], in0=lbT[:, 16 * l:16 * l + 16], in1=mx[:]),
              reads=[lbT, mx], writes=[ex])
        em.op("act", lambda e: e.activation(out=ex[:], in_=ex[:], func=AF.Exp), reads=[ex], writes=[ex])
        V(lambda e: e.tensor_add(out=sm[:], in0=ex[:, 0, :], in1=ex[:, 1, :]), reads=[ex], writes=[sm])
        V(lambda e: e.tensor_add(out=sm[:], in0=sm[:], in1=ex[:, 2, :]), reads=[ex, sm], writes=[sm])
        V(lambda e: e.tensor_add(out=sm[:], in0=sm[:], in1=ex[:, 3, :]), reads=[ex, sm], writes=[sm])
        V(lambda e: e.reciprocal(out=rs[:], in_=sm[:]), reads=[sm], writes=[rs])
        V(lambda e: e.tensor_mul(out=self.lbv[:, 0, :], in0=ex[:, 1, :], in1=rs[:]), reads=[ex, rs], writes=[self.lbv])
        V(lambda e: e.tensor_add(out=c1[:], in0=ex[:, 1, :], in1=ex[:, 2, :]), reads=[ex], writes=[c1])
        V(lambda e: e.tensor_add(out=c1[:], in0=c1[:], in1=ex[:, 3, :]), reads=[ex, c1], writes=[c1])
        V(lambda e: e.tensor_mul(out=self.lbv[:, 1, :], in0=c1[:], in1=rs[:]), reads=[c1, rs], writes=[self.lbv])
        V(lambda e: e.tensor_scalar(out=self.oml[:], in0=self.lbv[:], scalar1=-1.0, scalar2=1.0, op0=ALU.mult, op1=ALU.add),
          reads=[self.lbv], writes=[self.oml])
        V(lambda e: e.tensor_scalar(out=self.noml[:], in0=self.oml[:], scalar1=-1.0, scalar2=None, op0=ALU.mult),
          reads=[self.oml], writes=[self.noml])
        em.end()

    def h_src(self, s, l, i):
        r0, n = trng(i)
        if l == 0:
            if i == 0:
                return self.meta[:, :]
            return self.x[s, r0 - 16:r0 - 16 + n, :]
        return self.H[r0:r0 + n, :]

    def rstd_rows(self, ss, n, dim, eps, tmp):
        em = self.em
        em.op("dve", lambda e: e.tensor_scalar(out=tmp[0:n, 0:1], in0=ss[0:n, 0:1], scalar1=1.0 / dim, scalar2=eps,
                                               op0=ALU.mult, op1=ALU.add), reads=[ss], writes=[tmp])
        em.op("act", lambda e: e.activation(out=tmp[0:n, 1:2], in_=tmp[0:n, 0:1], func=AF.Sqrt), reads=[tmp], writes=[tmp])
        em.op("dve", lambda e: e.reciprocal(out=ss[0:n, 1:2], in_=tmp[0:n, 1:2]), reads=[tmp], writes=[ss])

    def stage_norm_proj(self, s, l, even):
        em = self.em
        em.begin()
        hnT = em.sb("hnT", [128, 16, T], BF16)
        outer = em.stage
        nrm = ExitStack()
        em.stage = nrm
        gbc = em.sb("gbc", [128, D], F32, dma=True)
        em.dma("sp", gbc[:], self.norm_gain[l, :].partition_broadcast(128), writes=[gbc], owner=gbc)
        htr = em.ring("ht", 4, [128, D], F32, dma=True)
        hsr = em.ring("hs", 3, [128, D], BF16)
        junk = em.sb("junk", [128, D], BF16)
        ssr = em.ring("ss", 4, [128, 2], F32)
        tmr = em.ring("tm", 4, [128, 2], F32)
        ptr = em.psring("ptr", 2, [128, 16, 128], BF16)
        for i in range(NT):
            r0, n = trng(i)
            ht = htr.next()
            hs = hsr.next()
            ss = ssr.next()
            tm = tmr.next()
            pt = ptr.next()
            em.dma("sp", ht[0:n, :], self.h_src(s, l, i), writes=[ht], owner=ht)
            em.op("pool", lambda e: e.memset(ss[:], 0.0), writes=[ss])
            em.op("act", lambda e: e.activation(out=junk[0:n, :], in_=ht[0:n, :], func=AF.Square, accum_out=ss[0:n, 0:1]),
                  reads=[ht], writes=[junk, ss])
            self.rstd_rows(ss, n, D, RMS_EPS, tm)
            em.op("dve", lambda e: e.scalar_tensor_tensor(out=hs[0:n, :], in0=ht[0:n, :], scalar=ss[0:n, 1:2],
                                                          in1=gbc[0:n, :], op0=ALU.mult, op1=ALU.mult),
                  reads=[ht, ss, gbc], writes=[hs])
            for k in range(16):
                em.op("pe", lambda e, k=k: e.transpose(pt[:, k, 0:n], hs[0:n, k * 128:(k + 1) * 128], self.idb[0:n, 0:n]),
                      reads=[hs, self.idb], writes=[pt])
            em.op("act", lambda e: e.activation(out=hnT[:, :, r0:r0 + n], in_=pt[:, :, 0:n], func=AF.Copy),
                  reads=[pt], writes=[hnT])
        self.hnT = hnT
        em.barrier()
        nrm.close()
        em.stage = outer
        if even:
            self.proj_even(l // 2)
        else:
            self.proj_odd(l // 2)
        em.end()

    def proj_fm(self, W, chunks, wtr32, wtr, pmm, otr, otbr=None):
        em = self.em
        hnT = self.hnT
        blocks = []
        for ch in chunks:
            col0, ncols, func, scale, dst0, dup = ch
            if (blocks and not dup and ncols == 128 and len(blocks[-1]) < 4 and not blocks[-1][-1][5]
                    and blocks[-1][-1][1] == 128 and blocks[-1][-1][0] + 128 == col0):
                blocks[-1].append(ch)
            else:
                blocks.append([ch])
        for blk in blocks:
            w32 = wtr32.next()
            wb = wtr.next()
            bcol0 = blk[0][0]
            bn = sum(c[1] for c in blk)
            em.dma("sp", w32[:, :, 0:bn], W[:, bcol0:bcol0 + bn].rearrange("(kc p) m -> p kc m", p=128),
                   writes=[w32], owner=w32)
            if blk[0][5]:
                em.dma("sp", w32[:, :, bn:2 * bn], W[:, bcol0:bcol0 + bn].rearrange("(kc p) m -> p kc m", p=128),
                       writes=[w32], owner=w32)
            for bi, (col0, ncols, func, scale, dst0, dup) in enumerate(blk):
                o = col0 - bcol0
                nm = ncols * (2 if dup else 1)
                em.op("pool", lambda e, o=o, nm=nm: e.tensor_copy(wb[:, :, o:o + nm], w32[:, :, o:o + nm]), reads=[w32], writes=[wb])
            for bi, (col0, ncols, func, scale, dst0, dup) in enumerate(blk):
                o = col0 - bcol0
                nm = ncols * (2 if dup else 1)
                tobf = dst0 < 0
                ot = otbr.next() if tobf else otr.next()
                for (c0, n) in CCH:
                    ps = pmm.next()
                    for k in range(16):
                        em.op("pe", lambda e, k=k: e.matmul(ps[0:nm, 0:n], wb[:, k, o:o + nm], hnT[:, k, c0:c0 + n],
                                                            start=(k == 0), stop=(k == 15)),
                              reads=[wb, hnT], writes=[ps])
                    em.op("act", lambda e: e.activation(out=ot[0:nm, c0:c0 + n], in_=ps[0:nm, 0:n], func=func, scale=scale),
                          reads=[ps], writes=[ot])
                if tobf:
                    r0 = -dst0 - 1
                    em.dma("act", self.ZB[r0:r0 + nm, :], ot[0:nm, :], reads=[ot], owner=ot)
                else:
                    em.dma("act", self.ZT[dst0:dst0 + nm, :], ot[0:nm, :], reads=[ot], owner=ot)

    ZE = dict(glu_v=0, glu_g=1024, gate_a=2048, q=3072, c=4096, gate_b=4352, qi=5376, ki=6400, wi=6528)

    def proj_even(self, e_):
        em = self.em
        W = self.w_in_even[e_]
        chunks = []
        for m in range(50):
            col0 = m * 128
            if col0 < 1024:
                f, sc = AF.Copy, 1.0
            elif col0 < 2048:
                f, sc = AF.Sigmoid, 1.0
            elif col0 < 3072:
                f, sc = AF.Silu, 1.0
            elif col0 < 4352:
                f, sc = AF.Copy, 1.0
            elif col0 < 5376:
                f, sc = AF.Silu, 1.0
            else:
                f, sc = AF.Copy, 0.125
            dst = col0
            if 3072 <= col0 < 4096:
                dst = -(col0 - 3072) - 1
            elif 5376 <= col0 < 6400:
                dst = -(1024 + col0 - 5376) - 1
            chunks.append((col0, 128, f, sc, dst, False))
        chunks.append((6400, 64, AF.Copy, 1.0, -2048 - 1, True))
        chunks.append((6464, 16, AF.Copy, 0.25, 6528, False))
        wtr32 = em.ring("w32", 2, [128, 16, 512], F32, dma=True)
        wtr = em.ring("wb", 2, [128, 16, 512], BF16)
        pmm = em.psring("pmm", 4, [128, 512], F32)
        otr = em.ring("ot", 2, [128, T], F32, dma=True)
        otbr = em.ring("otb", 2, [128, T], BF16, dma=True)
        self.proj_fm(W, chunks, wtr32, wtr, pmm, otr, otbr)

    def proj_odd(self, o_):
        em = self.em
        W = self.w_in_odd[o_]
        chunks = []
        for m in range(16):
            chunks.append((m * 128, 128, AF.Silu, 1.0, m * 128, False))
        for m in range(16):
            chunks.append((2048 + m * 128, 128, AF.Sigmoid, 1.0, 2048 + m * 128, False))
        for m in range(16):
            chunks.append((6144 + m * 128, 128, AF.Silu, 1.0, 4096 + m * 128, False))
        wtr32 = em.ring("w32", 2, [128, 16, 512], F32, dma=True)
        wtr = em.ring("wb", 2, [128, 16, 512], BF16)
        pmm = em.psring("pmm", 4, [128, 512], F32)
        otr = em.ring("ot", 2, [128, T], F32, dma=True)
        self.proj_fm(W, chunks, wtr32, wtr, pmm, otr)
        hnT = self.hnT
        vo = em.ring("vo", 2, [128, 512], BF16, dma=True)
        for g in range(4):
            w32 = wtr32.next()
            wb = wtr.next()
            em.dma("sp", w32[:], W[:, 4096 + g * 512:4096 + (g + 1) * 512].rearrange("(kc p) m -> p kc m", p=128),
                   writes=[w32], owner=w32)
            for k4 in range(4):
                em.op("pool", lambda e, k4=k4: e.tensor_copy(wb[:, 4 * k4:4 * k4 + 4, :], w32[:, 4 * k4:4 * k4 + 4, :]), reads=[w32], writes=[wb])
            for i in range(NT):
                r0, n = trng(i)
                ps = pmm.next()
                for k in range(16):
                    em.op("pe", lambda e, k=k: e.matmul(ps[0:n, :], hnT[:, k, r0:r0 + n], wb[:, k, :],
                                                        start=(k == 0), stop=(k == 15)),
                          reads=[wb, hnT], writes=[ps])
                v = vo.next()
                em.op("act", lambda e: e.activation(out=v[0:n, :], in_=ps[0:n, :], func=AF.Copy), reads=[ps], writes=[v])
                em.dma("act", self.VTOK[r0:r0 + n, g * 512:(g + 1) * 512], v[0:n, :], reads=[v], owner=v)

    def stage_conv(self, e_):
        em = self.em
        ZE = self.ZE
        em.begin()
        stg = em.sb("stg", [128, 128], F32, dma=True)
        pst = em.ps("pst", [128, 128], F32)
        cw = em.sb("cw", [128, 248], F32)
        cv = em.sb("cv", [128, 24], F32)
        cwr = self.conv_w[e_].rearrange("j (cc p) -> (j cc) p", p=128)
        self.vecT(cw, 0, cwr[0:124, :], 124, pst, stg)
        self.vecT(cw, 124, cwr[124:248, :], 124, pst, stg)
        self.vecT(cv, 0, self.conv_b[e_].rearrange("(cc p) -> cc p", p=128), 8, pst, stg)
        self.vecT(cv, 8, self.ln_g[e_].rearrange("(cc p) -> cc p", p=128), 8, pst, stg)
        self.vecT(cv, 16, self.ln_b[e_].rearrange("(cc p) -> cc p", p=128), 8, pst, stg)
        import os
        KCUT = int(os.environ.get("KCUT", "99"))
        if KCUT <= 1:
            em.end()
            return
        uall = em.sb("uall", [128, 8, T], F32)
        acc1 = em.sb("acc1", [128, T], F32)
        acc2 = em.sb("acc2", [128, T], F32)
        gsr = em.ring("gs", 1, [128, T], F32, dma=True)
        upr = em.ring("up", 2, [128, 30 + T], F32, dma=True)
        pa = em.ring("pa", 1, [128, T], F32)
        pb = em.ring("pb", 1, [128, T], F32)
        tmpr = em.ring("ctmp", 2, [128, T], F32)
        sq = em.ring("sq", 1, [128, T], F32)
        for b in upr.bufs:
            em.op("pool", lambda e, b=b: e.memset(b[:, 0:30], 0.0), writes=[b])
        for cc in range(8):
            gs = gsr.next()
            up = upr.next()
            A = pa.next()
            B = pb.next()
            em.dma("sp", up[:, 30:30 + T], self.ZT[ZE["glu_v"] + cc * 128:ZE["glu_v"] + (cc + 1) * 128, :], writes=[up], owner=up)
            em.dma("sp", gs[:], self.ZT[ZE["glu_g"] + cc * 128:ZE["glu_g"] + (cc + 1) * 128, :], writes=[gs], owner=gs)
            em.op("pool", lambda e: e.tensor_tensor(out=up[:, 30:30 + T], in0=up[:, 30:30 + T], in1=gs[:], op=ALU.mult),
                  reads=[up, gs], writes=[up])
            w = lambda j: cw[:, j * 8 + cc:j * 8 + cc + 1]
            em.op("dve", lambda e: e.tensor_scalar(out=A[:], in0=up[:, 0:T], scalar1=w(0), scalar2=cv[:, cc:cc + 1],
                                                   op0=ALU.mult, op1=ALU.add), reads=[up, cw, cv], writes=[A])
            for j in range(1, 17):
                em.op("dve", lambda e, j=j: e.scalar_tensor_tensor(out=A[:], in0=up[:, j:j + T], scalar=w(j), in1=A[:],
                                                                   op0=ALU.mult, op1=ALU.add), reads=[up, cw, A], writes=[A])
            em.op("act", lambda e: e.activation(out=B[:], in_=up[:, 17:17 + T], func=AF.Copy, scale=w(17)),
                  reads=[up, cw], writes=[B])
            for j in range(18, 31):
                tp = tmpr.next()
                em.op("act", lambda e, j=j, tp=tp: e.activation(out=tp[:], in_=up[:, j:j + T], func=AF.Copy, scale=w(j)),
                      reads=[up, cw], writes=[tp])
                em.op("pool", lambda e, tp=tp: e.tensor_add(out=B[:], in0=B[:], in1=tp[:]), reads=[tp, B], writes=[B])
            em.op("dve", lambda e: e.tensor_add(out=uall[:, cc, :], in0=A[:], in1=B[:]), reads=[A, B], writes=[uall])
            s2 = sq.next()
            em.op("act", lambda e: e.activation(out=s2[:], in_=uall[:, cc, :], func=AF.Square), reads=[uall], writes=[s2])
            if cc == 0:
                em.op("pool", lambda e: e.tensor_copy(acc1[:], uall[:, cc, :]), reads=[uall], writes=[acc1])
                em.op("pool", lambda e: e.tensor_copy(acc2[:], s2[:]), reads=[s2], writes=[acc2])
            else:
                em.op("pool", lambda e: e.tensor_add(out=acc1[:], in0=acc1[:], in1=uall[:, cc, :]), reads=[uall, acc1], writes=[acc1])
                em.op("pool", lambda e: e.tensor_add(out=acc2[:], in0=acc2[:], in1=s2[:]), reads=[s2, acc2], writes=[acc2])
        if KCUT <= 2:
            em.end()
            return
        mean = em.sb("mean", [128, T], F32)
        rstd = em.sb("rstd", [128, T], F32)
        p1 = em.ps("p1", [128, 512], F32)
        p2 = em.ps("p2", [128, 512], F32)
        avg = self.avg[1024]
        for (c0, n) in CCH:
            em.op("pe", lambda e: e.matmul(p1[:, 0:n], avg[:], acc1[:, c0:c0 + n], start=True, stop=True),
                  reads=[avg, acc1], writes=[p1])
            em.op("pe", lambda e: e.matmul(p2[:, 0:n], avg[:], acc2[:, c0:c0 + n], start=True, stop=True),
                  reads=[avg, acc2], writes=[p2])
            em.op("act", lambda e: e.activation(out=mean[:, c0:c0 + n], in_=p1[:, 0:n], func=AF.Copy), reads=[p1], writes=[mean])
            em.op("dve", lambda e: e.tensor_tensor(out=rstd[:, c0:c0 + n], in0=mean[:, c0:c0 + n], in1=mean[:, c0:c0 + n], op=ALU.mult),
                  reads=[mean], writes=[rstd])
            em.op("dve", lambda e: e.tensor_sub(out=rstd[:, c0:c0 + n], in0=p2[:, 0:n], in1=rstd[:, c0:c0 + n]),
                  reads=[p2, rstd], writes=[rstd])
            em.op("dve", lambda e: e.tensor_scalar(out=rstd[:, c0:c0 + n], in0=rstd[:, c0:c0 + n], scalar1=LN_EPS, scalar2=None, op0=ALU.add),
                  reads=[rstd], writes=[rstd])
        em.op("act", lambda e: e.activation(out=rstd[:], in_=rstd[:], func=AF.Sqrt), reads=[rstd], writes=[rstd])
        em.op("dve", lambda e: e.reciprocal(out=rstd[:], in_=rstd[:]), reads=[rstd], writes=[rstd])
        if KCUT <= 3:
            em.end()
            return
        mxr = em.ring("mx", 2, [128, T], BF16, dma=True)
        for cc in range(8):
            ga = gsr.next()
            t1 = pa.next()
            t2 = pb.next()
            mx = mxr.next()
            em.dma("sp", ga[:], self.ZT[ZE["gate_a"] + cc * 128:ZE["gate_a"] + (cc + 1) * 128, :], writes=[ga], owner=ga)
            em.op("dve", lambda e: e.tensor_sub(out=t1[:], in0=uall[:, cc, :], in1=mean[:]), reads=[uall, mean], writes=[t1])
            em.op("pool", lambda e: e.tensor_mul(out=t1[:], in0=t1[:], in1=rstd[:]), reads=[t1, rstd], writes=[t1])
            em.op("act", lambda e: e.activation(out=t2[:], in_=t1[:], func=AF.Silu, scale=cv[:, 8 + cc:9 + cc], bias=cv[:, 16 + cc:17 + cc]),
                  reads=[t1, cv], writes=[t2])
            em.op("dve", lambda e: e.tensor_mul(out=mx[:], in0=t2[:], in1=ga[:]), reads=[t2, ga], writes=[mx])
            em.dma("sp", self.MIXT[cc * 128:(cc + 1) * 128, :], mx[:], reads=[mx], owner=mx)
        em.end()

    def stage_attn(self, e_):
        em = self.em
        ZE = self.ZE
        ZT = self.ZT
        em.begin()
        cnT = em.sb("cnT", [128, 2, T], BF16)
        cnk = em.sb("cnk", [128, NT, 256], BF16)
        wukT = em.sb("wukT", [128, 8, 256], BF16)
        wuvb = em.sb("wuvb", [128, 8, 2, 128], BF16)
        kiT2 = em.sb("kiT2", [128, T], BF16, dma=True)
        wiTok = em.sb("wiTok", [128, NT, 16], F32)
        kvg = em.sb("kvg", [128, 2], F32)
        stg = em.sb("stg", [128, 128], F32, dma=True)

        prep = ExitStack()
        stage_outer = em.stage
        em.stage = prep
        pA = em.ps("pA", [128, 512], F32)
        pB = em.ps("pB", [128, 512], F32)
        pT = em.ps("pT", [128, 128], F32)
        pTb = em.ps("pTb", [128, 2, 128], BF16)
        big = em.sb("big", [128, 2, T], F32, dma=True)
        big2 = em.sb("big2", [128, T], F32)
        rsd = em.sb("rsd", [128, T], F32)
        self.vecT(kvg, 0, self.kv_g[e_].rearrange("(cc p) -> cc p", p=128), 2, pT, stg)
        em.dma("sp", big[:], ZT[ZE["c"]:ZE["c"] + 256, :].rearrange("(cc p) t -> p cc t", p=128), writes=[big], owner=big)
        em.op("act", lambda e: e.activation(out=big2[:], in_=big[:, 0, :], func=AF.Square), reads=[big], writes=[big2])
        em.op("act", lambda e: e.activation(out=rsd[:], in_=big[:, 1, :], func=AF.Square), reads=[big], writes=[rsd])
        em.op("dve", lambda e: e.tensor_add(out=big2[:], in0=big2[:], in1=rsd[:]), reads=[big2, rsd], writes=[big2])
        avg = self.avg[256]
        for (c0, n) in CCH:
            em.op("pe", lambda e: e.matmul(pA[:, 0:n], avg[:], big2[:, c0:c0 + n], start=True, stop=True),
                  reads=[avg, big2], writes=[pA])
            em.op("dve", lambda e: e.tensor_scalar(out=rsd[:, c0:c0 + n], in0=pA[:, 0:n], scalar1=RMS_EPS, scalar2=None, op0=ALU.add),
                  reads=[pA], writes=[rsd])
        em.op("act", lambda e: e.activation(out=rsd[:], in_=rsd[:], func=AF.Sqrt), reads=[rsd], writes=[rsd])
        em.op("dve", lambda e: e.reciprocal(out=rsd[:], in_=rsd[:]), reads=[rsd], writes=[rsd])
        for cc in range(2):
            em.op("dve", lambda e, cc=cc: e.scalar_tensor_tensor(out=cnT[:, cc, :], in0=big[:, cc, :], scalar=kvg[:, cc:cc + 1],
                                                                 in1=rsd[:], op0=ALU.mult, op1=ALU.mult),
                  reads=[big, kvg, rsd], writes=[cnT])
        for i in range(NT):
            r0, n = trng(i)
            for cc in range(2):
                em.op("pe", lambda e, cc=cc: e.transpose(pTb[0:n, cc, :], cnT[:, cc, r0:r0 + n], self.idb[:, :]),
                      reads=[cnT, self.idb], writes=[pTb])
            em.op("act", lambda e: e.activation(out=cnk[0:n, i, :], in_=pTb[0:n, :, :], func=AF.Copy), reads=[pTb], writes=[cnk])
        wld = em.ring("wld", 2, [128, 2, 128], F32, dma=True)
        for h in range(8):
            w = wld.next()
            em.dma("sp", w[:], self.w_uk[e_, h].rearrange("(cc p) d -> p cc d", p=128), writes=[w], owner=w)
            for cc in range(2):
                em.op("pe", lambda e, cc=cc: e.transpose(pT[:, :], w[:, cc, :], self.idf[:, :]), reads=[w, self.idf], writes=[pT])
                em.op("act", lambda e, cc=cc: e.activation(out=wukT[:, h, cc * 128:(cc + 1) * 128], in_=pT[:, :], func=AF.Copy,
                                                           scale=128.0 ** -0.5), reads=[pT], writes=[wukT])
            w2 = wld.next()
            em.dma("sp", w2[:], self.w_uv[e_, h].rearrange("(cc p) d -> p cc d", p=128), writes=[w2], owner=w2)
            em.op("pool", lambda e: e.tensor_copy(wuvb[:, h, :, :], w2[:]), reads=[w2], writes=[wuvb])
        em.dma("sp", kiT2[:], self.ZB[2048:2176, :], writes=[kiT2], owner=kiT2)
        wiT = em.sb("wiT", [16, T], F32, dma=True)
        em.dma("sp", wiT[:], ZT[ZE["wi"]:ZE["wi"] + 16, :], writes=[wiT], owner=wiT)
        for i in range(NT):
            r0, n = trng(i)
            em.op("pe", lambda e: e.transpose(pT[0:n, 0:16], wiT[0:16, r0:r0 + n], self.idf[0:16, 0:16]),
                  reads=[wiT, self.idf], writes=[pT])
            em.op("act", lambda e: e.activation(out=wiTok[0:n, i, :], in_=pT[0:n, 0:16], func=AF.Copy), reads=[pT], writes=[wiTok])
        em.barrier()
        prep.close()
        em.stage = stage_outer

        import os
        AQT = int(os.environ.get("AQT", "99"))
        ASUB = int(os.environ.get("ASUB", "99"))
        pdot = em.psring("pdot", 2, [128, 512], F32)
        pmt = em.ps("pmt", [128, 8, 128], BF16)
        plog = em.psring("plog", 2, [128, 4, 128], F32)
        pso_r = em.psring("pso", 1, [128, 3, 128], F32)
        psb = em.ps("psb", [128, 128], F32)
        pql = em.ps("pql", [128, 2, 128], F32)
        qi_r = em.ring("qiT", 2, [128, 8, 128], BF16, dma=True)
        qh_r = em.ring("qhT", 2, [128, 8, 128], BF16, dma=True)
        ql_r = em.ring("qlat", 2, [128, 8, 2, 128], BF16)
        score_r = em.ring("score", 2, [128, T], F32)
        work_r = em.ring("work", 1, [128, T], F32)
        m8 = em.sb("m8", [128, 8], F32)
        NIT = 24
        blo = em.sb("blo", [128, 1], F32)
        brg = em.sb("brg", [128, 1], F32)
        bthr = em.sb("bthr", [128, 1], F32)
        bcnt = em.sb("bcnt", [128, 1], F32)
        btq = em.sb("btq", [128, 1], F32)
        stab = em.sb("stab", [128, NIT], F32)
        pw2 = em.sb("pw2", [128, NIT], F32)
        for k in range(NIT):
            em.op("pool", lambda e, k=k: e.memset(pw2[:, k:k + 1], 2.0 ** -(k + 1)), writes=[pw2])
        rl_r = em.ring("rl", 3, [128, 512], F32)
        mask_r = em.ring("mask", 2, [128, T], BF16)
        maskT_r = em.ring("maskT", 2, [128, NT, 128], BF16)
        cm_r = em.ring("cm", 2, [128, 8, 2, 128], F32)
        ex_r = em.ring("ex", 2, [128, 4, 128], F32)
        pt_r = em.ring("ptile", 10, [128, 4, 128], BF16)
        osb_r = em.ring("osb", 2, [128, 2, 128], BF16)
        rden_r = em.ring("rden", 2, [128, 128], F32)
        dsb_r = em.ring("dsb", 2, [128, 128], F32)
        gb_r = em.ring("gb", 2, [128, 8, 128], F32, dma=True)
        tb_r = em.ring("tb", 2, [128, 128], F32)
        mixb_r = em.ring("mixb", 2, [128, 8, 128], BF16, dma=True)
        EB = self.EB
        def pre(qt):
            q0, nq = trng(qt)
            nk = q0 + nq
            score = score_r.next()
            gb = gb_r.next()
            em.dma("sp", gb[:, :, 0:nq], ZT[ZE["gate_b"]:ZE["gate_b"] + 1024, q0:q0 + nq].rearrange("(h p) t -> p h t", p=128),
                   writes=[gb], owner=gb)
            qiT = qi_r.next()
            em.dma("sp", qiT[:, :, 0:nq], self.ZB[1024:2048, q0:q0 + nq].rearrange("(c p) t -> p c t", p=128),
                   writes=[qiT], owner=qiT)
            qhT = qh_r.next()
            em.dma("sp", qhT[:, :, 0:nq], self.ZB[0:1024, q0:q0 + nq].rearrange("(h p) t -> p h t", p=128),
                   writes=[qhT], owner=qhT)
            qlat = ql_r.next()
            for h in range(8):
                for cc in range(2):
                    em.op("pe", lambda e, h=h, cc=cc: e.matmul(pql[:, cc, 0:nq], wukT[:, h, cc * 128:(cc + 1) * 128], qhT[:, h, 0:nq],
                                                               start=True, stop=True), reads=[wukT, qhT], writes=[pql])
                em.op("act", lambda e, h=h: e.activation(out=qlat[:, h, :, 0:nq], in_=pql[:, :, 0:nq], func=AF.Copy),
                      reads=[pql], writes=[qlat])
            yield
            kch = [(k0, min(512, nk - k0)) for k0 in range(0, nk, 512)]
            for h16 in range(16):
                c_, po = h16 // 2, (h16 % 2) * 64
                for (k0, n) in kch:
                    ps = pdot.next()
                    rl = rl_r.next()
                    em.op("pe", lambda e: e.matmul(ps[0:nq, 0:n], qiT[po:po + 64, c_, 0:nq], kiT2[po:po + 64, k0:k0 + n],
                                                   start=True, stop=True), reads=[qiT, kiT2], writes=[ps])
                    em.op("act", lambda e: e.activation(out=rl[0:nq, 0:n], in_=ps[0:nq, 0:n], func=AF.Relu), reads=[ps], writes=[rl])
                    if h16 == 0:
                        em.op("dve", lambda e: e.tensor_scalar(out=score[0:nq, k0:k0 + n], in0=rl[0:nq, 0:n],
                                                               scalar1=wiTok[0:nq, qt, 0:1], scalar2=None, op0=ALU.mult),
                              reads=[rl, wiTok], writes=[score])
                    else:
                        em.op("dve", lambda e: e.scalar_tensor_tensor(out=score[0:nq, k0:k0 + n], in0=rl[0:nq, 0:n],
                                                                      scalar=wiTok[0:nq, qt, h16:h16 + 1],
                                                                      in1=score[0:nq, k0:k0 + n], op0=ALU.mult, op1=ALU.add),
                              reads=[rl, wiTok, score], writes=[score])
                yield
            mask = mask_r.next()
            if qt >= 1:
                em.op("pool", lambda e: e.memset(score[0:64, nk - 64:nk], NEG), reads=[], writes=[score])
            if nk > TOPK and nk - 64 < TOPK:
                work = work_r.next()
                src = score
                for r in range(TOPK // 8):
                    em.op("dve", lambda e, src=src: e.max(out=m8[0:nq, :], in_=src[0:nq, 0:nk]), reads=[src], writes=[m8])
                    em.op("dve", lambda e, src=src: e.match_replace(out=work[0:nq, 0:nk], in_to_replace=m8[0:nq, :],
                                                                    in_values=src[0:nq, 0:nk], imm_value=NEG2),
                          reads=[src, m8], writes=[work])
                    src = work
                    yield
                em.op("dve", lambda e: e.tensor_scalar(out=mask[0:nq, 0:nk], in0=work[0:nq, 0:nk], scalar1=-2.0e38, scalar2=None,
                                                       op0=ALU.is_lt), reads=[work], writes=[mask])
            elif nk > TOPK:
                nlo = nk - 64
                em.op("dve", lambda e: e.max(out=m8[0:nq, :], in_=score[0:nq, 0:nk]), reads=[score], writes=[m8])
                em.op("dve", lambda e: e.tensor_reduce(out=blo[0:nq, :], in_=score[0:nq, 0:nlo], axis=AX.X, op=ALU.min),
                      reads=[score], writes=[blo])
                em.op("dve", lambda e: e.tensor_sub(out=brg[0:nq, :], in0=m8[0:nq, 0:1], in1=blo[0:nq, :]), reads=[m8, blo], writes=[brg])
                em.op("dve", lambda e: e.tensor_scalar(out=stab[0:nq, :], in0=pw2[0:nq, :], scalar1=brg[0:nq, 0:1], scalar2=None, op0=ALU.mult),
                      reads=[pw2, brg], writes=[stab])
                yield
                for k in range(NIT):
                    em.op("dve", lambda e, k=k: e.tensor_add(out=bthr[0:nq, :], in0=blo[0:nq, :], in1=stab[0:nq, k:k + 1]),
                          reads=[blo, stab], writes=[bthr])
                    em.op("dve", lambda e: e.tensor_scalar(out=mask[0:nq, 0:nk], in0=score[0:nq, 0:nk], scalar1=bthr[0:nq, 0:1], scalar2=0.0,
                                                           op0=ALU.is_ge, op1=ALU.add, accum_out=bcnt[0:nq, 0:1]),
                          reads=[score, bthr], writes=[mask, bcnt])
                    em.op("dve", lambda e, k=k: e.scalar_tensor_tensor(out=btq[0:nq, :], in0=bcnt[0:nq, :], scalar=TOPK - 0.5,
                                                                       in1=stab[0:nq, k:k + 1], op0=ALU.is_ge, op1=ALU.mult),
                          reads=[bcnt, stab], writes=[btq])
                    em.op("dve", lambda e: e.tensor_add(out=blo[0:nq, :], in0=blo[0:nq, :], in1=btq[0:nq, :]), reads=[blo, btq], writes=[blo])
                    yield
                em.op("dve", lambda e: e.tensor_scalar(out=mask[0:nq, 0:nk], in0=score[0:nq, 0:nk], scalar1=blo[0:nq, 0:1], scalar2=None,
                                                       op0=ALU.is_ge), reads=[score, blo], writes=[mask])
            else:
                em.op("dve", lambda e: e.tensor_scalar(out=mask[0:nq, 0:nk], in0=score[0:nq, 0:nk], scalar1=-1.0e29, scalar2=None,
                                                       op0=ALU.is_gt), reads=[score], writes=[mask])
            if qt >= 1:
                em.op("pool", lambda e: e.memset(mask[0:64, nk - 64:nk], 0.0), writes=[mask])
            maskT = maskT_r.next()
            for kb0 in range(0, qt + 1, 8):
                kbs = list(range(kb0, min(qt + 1, kb0 + 8)))
                for kb in kbs:
                    k0, nkb = trng(kb)
                    em.op("pe", lambda e, kb=kb, k0=k0, nkb=nkb: e.transpose(pmt[0:nkb, kb - kb0, 0:nq], mask[0:nq, k0:k0 + nkb],
                                                                            self.idb[0:nq, 0:nq]),
                          reads=[mask, self.idb], writes=[pmt])
                if kb0 == 0:
                    em.op("act", lambda e: e.activation(out=maskT[0:16, 0, 0:nq], in_=pmt[0:16, 0, 0:nq], func=AF.Copy),
                          reads=[pmt], writes=[maskT])
                    if len(kbs) > 1:
                        em.op("act", lambda e: e.activation(out=maskT[:, 1:len(kbs), 0:nq], in_=pmt[:, 1:len(kbs), 0:nq], func=AF.Copy),
                              reads=[pmt], writes=[maskT])
                else:
                    em.op("act", lambda e: e.activation(out=maskT[:, kb0:kb0 + len(kbs), 0:nq], in_=pmt[:, 0:len(kbs), 0:nq], func=AF.Copy),
                          reads=[pmt], writes=[maskT])
            yield
            cm = cm_r.next()
            if qt == 0:
                near = {0: (0, 0, 16)}
            elif qt == 1:
                near = {1: (0, 0, 128), 0: (1, 2, 16)}
            else:
                near = {qt: (0, 0, 128), qt - 1: (1, 1, 128)}
            for kb, (slot, ty, rows) in near.items():
                for h in range(8):
                    em.op("pool", lambda e, kb=kb, slot=slot, ty=ty, rows=rows, h=h: e.tensor_tensor(
                        out=cm[0:rows, h, slot, 0:nq], in0=EB[0:rows, ty, h, 0:nq], in1=maskT[0:rows, kb, 0:nq], op=ALU.mult),
                        reads=[EB, maskT], writes=[cm])
            self._pre[qt] = dict(gb=gb, qlat=qlat, maskT=maskT, cm=cm, near=near)
            yield

        def head(qt, h, st, mixb):
            q0, nq = trng(qt)
            gb, qlat, maskT, cm, near = st["gb"], st["qlat"], st["maskT"], st["cm"], st["near"]
            groups = [[0]] + [list(range(a, min(qt + 1, a + 4))) for a in range(1, qt + 1, 4)]
            pso = pso_r.next()
            ptiles = {}
            for grp in groups:
                pl = plog.next()
                rows = 16 if grp[0] == 0 else 128
                for gi, kb in enumerate(grp):
                    k0, nkb = trng(kb)
                    for cc in range(2):
                        em.op("pe", lambda e, gi=gi, k0=k0, nkb=nkb, cc=cc: e.matmul(
                            pl[0:nkb, gi, 0:nq], cnT[:, cc, k0:k0 + nkb], qlat[:, h, cc, 0:nq],
                            start=(cc == 0), stop=(cc == 1)), reads=[cnT, qlat], writes=[pl])
                ex = ex_r.next()
                g_n = len(grp)
                em.op("act", lambda e, rows=rows, g_n=g_n: e.activation(out=ex[0:rows, 0:g_n, 0:nq], in_=pl[0:rows, 0:g_n, 0:nq],
                                                                        func=AF.Exp, bias=self.bfar[0:rows, h:h + 1]),
                      reads=[pl, self.bfar], writes=[ex])
                ptile = pt_r.next()
                far = [gi for gi, kb in enumerate(grp) if kb not in near]
                if far:
                    a, b = far[0], far[-1] + 1
                    kba = grp[a]
                    em.op("dve", lambda e, a=a, b=b, kba=kba, rows=rows: e.tensor_tensor(
                        out=ptile[0:rows, a:b, 0:nq], in0=ex[0:rows, a:b, 0:nq], in1=maskT[0:rows, kba:kba + (b - a), 0:nq], op=ALU.mult),
                        reads=[ex, maskT], writes=[ptile])
                for gi, kb in enumerate(grp):
                    if kb in near:
                        slot, ty, rws = near[kb]
                        em.op("dve", lambda e, gi=gi, slot=slot, rws=rws: e.tensor_tensor(
                            out=ptile[0:rws, gi, 0:nq], in0=ex[0:rws, gi, 0:nq], in1=cm[0:rws, h, slot, 0:nq], op=ALU.mult),
                            reads=[ex, cm], writes=[ptile])
                for gi, kb in enumerate(grp):
                    ptiles[kb] = (ptile, gi)
            for part in range(3):
                for kb in range(qt + 1):
                    k0, nkb = trng(kb)
                    ptile, gi = ptiles[kb]
                    if part < 2:
                        lhs = cnk[0:nkb, kb, part * 128:(part + 1) * 128]
                        rd = [cnk, ptile]
                    else:
                        lhs = self.onesb[0:nkb, :]
                        rd = [self.onesb, ptile]
                    em.op("pe", lambda e, lhs=lhs, ptile=ptile, gi=gi, nkb=nkb, kb=kb, part=part: e.matmul(
                        pso[:, part, 0:nq], lhs, ptile[0:nkb, gi, 0:nq], start=(kb == 0), stop=(kb == qt)),
                        reads=rd, writes=[pso])
            osb = osb_r.next()
            rden = rden_r.next()
            dsb = dsb_r.next()
            em.op("act", lambda e: e.activation(out=osb[:, :, 0:nq], in_=pso[:, 0:2, 0:nq], func=AF.Copy), reads=[pso], writes=[osb])
            em.op("act", lambda e: e.activation(out=dsb[:, 0:nq], in_=pso[:, 2, 0:nq], func=AF.Copy), reads=[pso], writes=[dsb])
            for cc in range(2):
                em.op("pe", lambda e, cc=cc: e.matmul(psb[:, 0:nq], wuvb[:, h, cc, :], osb[:, cc, 0:nq], start=(cc == 0), stop=(cc == 1)),
                      reads=[wuvb, osb], writes=[psb])
            tb = tb_r.next()
            em.op("dve", lambda e: e.reciprocal(out=rden[:, 0:nq], in_=dsb[:, 0:nq]), reads=[dsb], writes=[rden])
            em.op("dve", lambda e: e.tensor_tensor(out=tb[:, 0:nq], in0=psb[:, 0:nq], in1=rden[:, 0:nq], op=ALU.mult),
                  reads=[psb, rden], writes=[tb])
            em.op("pool", lambda e: e.tensor_tensor(out=mixb[:, h, 0:nq], in0=tb[:, 0:nq], in1=gb[:, h, 0:nq], op=ALU.mult),
                  reads=[tb, gb], writes=[mixb])

        self._pre = {}
        for _ in pre(0):
            pass
        nqt = min(NT, AQT)
        for qt in range(nqt):
            q0, nq = trng(qt)
            st = self._pre.pop(qt)
            gen = pre(qt + 1) if qt + 1 < nqt else None
            nk1 = trng(qt + 1)[0] + trng(qt + 1)[1] if gen is not None else 0
            nsteps = 0 if gen is None else (3 + 16 + (0 if nk1 <= TOPK else (TOPK // 8 if nk1 - 64 < TOPK else NIT + 1)))
            per_head = (nsteps + 7) // 8
            mixb = mixb_r.next()
            for h in range(8):
                head(qt, h, st, mixb)
                if gen is not None:
                    for _ in range(per_head):
                        if next(gen, "done") == "done":
                            gen = None
                            break
            if gen is not None:
                for _ in gen:
                    pass
            em.dma("sp", self.MIXT[1024:2048, q0:q0 + nq].rearrange("(h p) t -> p h t", p=128), mixb[:, :, 0:nq], reads=[mixb], owner=mixb)
        em.end()

    def stage_rec(self, l):
        em = self.em
        ZT = self.ZT
        li = l // 2
        G = 4
        em.begin()
        stg = em.sb("stg", [128, 128], F32, dma=True)
        rng = em.sb("rng", [128, 16], F32)
        epsb = em.sb("epsb", [128, 1], F32)
        em.op("pool", lambda e: e.memset(epsb[:], RMS_EPS), writes=[epsb])
        self.rmask = em.sb("rmask", [128, T], F32)
        em.op("pool", lambda e: e.memset(self.rmask[:], 1.0), writes=[self.rmask])
        em.op("pool", lambda e: e.memset(self.rmask[:, 0:1], 0.0), writes=[self.rmask])
        em.op("pool", lambda e: e.memset(self.rmask[:, 16:T].rearrange("p (c j) -> p c j", j=64)[:, :, 0:1], 0.0),
              writes=[self.rmask])
        ptk = em.ps("ptk", [128, 8, 128], BF16)
        pn = em.ps("pn", [128, 512], F32)
        self.vecT(rng, 0, self.rec_g[li].rearrange("(c p) -> c p", p=128), 16, pn, stg)
        qs_r = em.ring("qs", 2, [128, T], F32, dma=True)
        sg_r = em.ring("sg", 2, [128, T], F32, dma=True)
        fb_r = em.ring("fb", 1, [128, T], F32)
        b_r = em.ring("b", 1, [128, T], F32)
        d_r = em.ring("d1", 1, [128, T], F32)
        kk_r = em.ring("kk", 1, [128, T], F32)
        mo_r = em.ring("mo", 1, [128, T], BF16, dma=True)
        qt_r = em.ring("qtl", G, [128, T], BF16)
        kt_r = em.ring("ktl", G, [128, T], BF16)
        ktok_r = em.ring("ktok", G, [128, NT, 128], BF16)
        vtok_r = em.ring("vtok", G, [128, NT, 128], BF16, dma=True)
        sc_r = em.ring("sc", G, [128, 4, 33], F32)
        bl_r = em.ring("bl", G, [128, 33], F32)
        oT_r = em.ring("oT", G, [128, T], F32)
        S_rs = [em.ring("S%d" % g, 2, [128, 128], F32) for g in range(G)]
        Sb_r = em.ring("Sb", 2 * G, [128, 128], BF16)
        St_r = em.ring("St", 2 * G, [128, 128], F32)
        am_r = em.ring("am", 2 * G, [128, 128], BF16)
        hbank = [em.ps_views("ph%d" % g, 4, [128], F32) for g in range(G)]
        lb = self.lbv

        def pre_head(h, g):
            qs = qs_r.next(); sg = sg_r.next()
            fb = fb_r.next(); b = b_r.next(); d1 = d_r.next(); kk = kk_r.next()
            qtl = qt_r.next(); ktl = kt_r.next(); ktok = ktok_r.next(); vtok = vtok_r.next()
            sc = sc_r.next(); bl = bl_r.next(); oT = oT_r.next()
            em.dma("sp", qs[:], ZT[h * 128:(h + 1) * 128, :], writes=[qs], owner=qs)
            em.dma("sp", sg[:], ZT[2048 + h * 128:2048 + (h + 1) * 128, :], writes=[sg], owner=sg)
            em.dma("sp", vtok[0:16, 0, :], self.VTOK[0:16, h * 128:(h + 1) * 128], writes=[vtok], owner=vtok)
            em.dma("sp", vtok[:, 1:NT, :], self.VTOK[16:T, h * 128:(h + 1) * 128].rearrange("(i p) v -> p i v", p=128),
                   writes=[vtok], owner=vtok)
            em.op("dve", lambda e: e.tensor_scalar(out=fb[:], in0=sg[:], scalar1=self.oml[:, li, h:h + 1], scalar2=lb[:, li, h:h + 1],
                                                   op0=ALU.mult, op1=ALU.add), reads=[sg, self.oml, lb], writes=[fb])
            em.op("act", lambda e: e.activation(out=fb[:], in_=fb[:], func=AF.Ln), reads=[fb], writes=[fb])
            em.op("dve", lambda e: e.tensor_scalar(out=kk[:], in0=sg[:], scalar1=self.noml[:, li, h:h + 1], scalar2=self.oml[:, li, h:h + 1],
                                                   op0=ALU.mult, op1=ALU.add), reads=[sg, self.noml, self.oml], writes=[kk])
            em.op("dve", lambda e: e.tensor_tensor_scan(b[:], self.rmask[:], fb[:], 0.0, ALU.mult, ALU.add),
                  reads=[self.rmask, fb], writes=[b])
            em.op("pool", lambda e: e.tensor_copy(sc[:, 0, 0:1], b[:, 8:9]), reads=[b], writes=[sc])
            em.op("pool", lambda e: e.tensor_copy(sc[:, 0, 1:33], b[:, 48:T:64]), reads=[b], writes=[sc])
            em.op("pool", lambda e: e.tensor_copy(bl[:, 0:1], b[:, 15:16]), reads=[b], writes=[bl])
            em.op("pool", lambda e: e.tensor_copy(bl[:, 1:33], b[:, 79:T:64]), reads=[b], writes=[bl])
            em.op("dve", lambda e: e.tensor_sub(out=d1[:, 0:16], in0=b[:, 0:16], in1=sc[:, 0, 0:1].to_broadcast([128, 16])),
                  reads=[b, sc], writes=[d1])
            em.op("dve", lambda e: e.tensor_sub(out=d1[:, 16:T].rearrange("p (c j) -> p c j", j=64),
                                                in0=b[:, 16:T].rearrange("p (c j) -> p c j", j=64),
                                                in1=sc[:, 0, 1:33].unsqueeze(2).to_broadcast([128, 32, 64])),
                  reads=[b, sc], writes=[d1])
            em.op("act", lambda e: e.activation(out=sc[:, 1, :], in_=sc[:, 0, :], func=AF.Exp), reads=[sc], writes=[sc])
            em.op("act", lambda e: e.activation(out=sc[:, 3, :], in_=bl[:], func=AF.Exp), reads=[bl, sc], writes=[sc])
            em.op("dve", lambda e: e.tensor_sub(out=bl[:], in0=bl[:], in1=sc[:, 0, :]), reads=[bl, sc], writes=[bl])
            em.op("act", lambda e: e.activation(out=sc[:, 2, :], in_=bl[:], func=AF.Exp), reads=[bl, sc], writes=[sc])
            em.op("act", lambda e: e.activation(out=fb[:], in_=d1[:], func=AF.Exp), reads=[d1, fb], writes=[fb])
            em.op("dve", lambda e: e.tensor_mul(out=qtl[:], in0=qs[:], in1=fb[:]), reads=[qs, fb], writes=[qtl])
            em.op("act", lambda e: e.activation(out=d1[:], in_=d1[:], func=AF.Exp, scale=-1.0), reads=[d1], writes=[d1])
            em.op("pool", lambda e: e.tensor_mul(out=ktl[:], in0=kk[:], in1=d1[:]), reads=[kk, d1], writes=[ktl])
            for i0 in range(0, NT, 8):
                ii = list(range(i0, min(NT, i0 + 8)))
                for i in ii:
                    r0, n = trng(i)
                    em.op("pe", lambda e, i=i, r0=r0, n=n: e.transpose(ptk[0:n, i - i0, :], ktl[:, r0:r0 + n], self.idb[:, :]),
                          reads=[ktl, self.idb], writes=[ptk])
                if i0 == 0:
                    em.op("act", lambda e: e.activation(out=ktok[0:16, 0, :], in_=ptk[0:16, 0, :], func=AF.Copy), reads=[ptk], writes=[ktok])
                    em.op("act", lambda e: e.activation(out=ktok[:, 1:8, :], in_=ptk[:, 1:8, :], func=AF.Copy), reads=[ptk], writes=[ktok])
                else:
                    em.op("act", lambda e, i0=i0, m=len(ii): e.activation(out=ktok[:, i0:i0 + m, :], in_=ptk[:, 0:m, :], func=AF.Copy),
                          reads=[ptk], writes=[ktok])
            Sb = Sb_r.next()
            St = St_r.next()
            em.op("pool", lambda e: e.memset(Sb[:], 0.0), writes=[Sb])
            em.op("pool", lambda e: e.memset(St[:], 0.0), writes=[St])
            return dict(h=h, g=g, qtl=qtl, ktl=ktl, ktok=ktok, vtok=vtok, sc=sc, oT=oT, Sb=Sb, St=St, pss=None)

        def tile_part(c, i):
            r0, n = trng(i)
            qtl, ktl, vtok, oT = c["qtl"], c["ktl"], c["vtok"], c["oT"]
            psa = hbank[c["g"]][0]
            am = am_r.next()
            pso = hbank[c["g"]][0]
            em.op("pe", lambda e: e.matmul(psa[0:n, 0:n], ktl[:, r0:r0 + n], qtl[:, r0:r0 + n], start=True, stop=True),
                  reads=[ktl, qtl], writes=[psa])
            em.op("dve", lambda e: e.tensor_tensor(out=am[0:n, 0:n], in0=psa[0:n, 0:n], in1=self.cmask[0:n, 0:n], op=ALU.mult),
                  reads=[psa, self.cmask], writes=[am])
            em.op("pe", lambda e: e.matmul(pso[:, 0:n], vtok[0:n, i, :], am[0:n, 0:n], start=True, stop=True),
                  reads=[vtok, am], writes=[pso])
            em.op("act", lambda e: e.activation(out=oT[:, r0:r0 + n], in_=pso[:, 0:n], func=AF.Copy), reads=[pso], writes=[oT])

        def chunk_list():
            out = []
            for i in range(NT):
                r0, n = trng(i)
                for ci, (p0, ncx) in enumerate([(0, 16)] if i == 0 else [(0, 64), (64, 64)]):
                    j = 0 if i == 0 else 1 + 2 * (i - 1) + ci
                    out.append((i, j, p0, ncx, r0 + p0))
            return out

        CH = chunk_list()

        def emit_pss(c, idx):
            i, j, p0, ncx, c0 = CH[idx]
            pss = hbank[c["g"]][1 + (idx % 2)]
            em.op("pe", lambda e: e.matmul(pss[:], c["ktok"][p0:p0 + ncx, i, :], c["vtok"][p0:p0 + ncx, i, :], start=True, stop=True),
                  reads=[c["ktok"], c["vtok"]], writes=[pss])

        def chunk_pe(c, idx):
            i, j, p0, ncx, c0 = CH[idx]
            psi = hbank[c["g"]][3]
            Sb = c["Sb"]
            em.op("pe", lambda e: e.matmul(psi[:, 0:ncx], Sb[:], c["qtl"][:, c0:c0 + ncx], start=True, stop=True),
                  reads=[Sb, c["qtl"]], writes=[psi])
            if idx + 1 < len(CH):
                emit_pss(c, idx + 1)

        def chunk_dve(c, idx):
            i, j, p0, ncx, c0 = CH[idx]
            sc, oT = c["sc"], c["oT"]
            pss = hbank[c["g"]][1 + (idx % 2)]
            psi = hbank[c["g"]][3]
            St = c["St"]
            if idx + 1 < len(CH):
                jn = CH[idx + 1][1]
                S2 = S_rs[c["g"]].next()
                em.op("dve", lambda e: e.scalar_tensor_tensor(out=S2[:], in0=pss[:], scalar=sc[:, 2, j:j + 1], in1=St[:],
                                                              op0=ALU.mult, op1=ALU.add), reads=[pss, sc, St], writes=[S2])
                Sb2 = Sb_r.next()
                St2 = St_r.next()
                em.op("dve", lambda e: e.tensor_scalar(out=Sb2[:], in0=S2[:], scalar1=sc[:, 1, jn:jn + 1], scalar2=None, op0=ALU.mult),
                      reads=[S2, sc], writes=[Sb2])
                em.op("dve", lambda e: e.tensor_scalar(out=St2[:], in0=S2[:], scalar1=sc[:, 3, jn:jn + 1], scalar2=None, op0=ALU.mult),
                      reads=[S2, sc], writes=[St2])
                c["Sb"], c["St"] = Sb2, St2
            em.op("dve", lambda e: e.tensor_add(out=oT[:, c0:c0 + ncx], in0=oT[:, c0:c0 + ncx], in1=psi[:, 0:ncx]),
                  reads=[oT, psi], writes=[oT])

        def post_head(c):
            h, oT = c["h"], c["oT"]
            gs = qs_r.next(); d1 = d_r.next(); kk = kk_r.next()
            em.dma("sp", gs[:], ZT[4096 + h * 128:4096 + (h + 1) * 128, :], writes=[gs], owner=gs)
            em.op("act", lambda e: e.activation(out=d1[:], in_=oT[:], func=AF.Square), reads=[oT], writes=[d1])
            avg = self.avg[128]
            for (c0, n) in CCH:
                em.op("pe", lambda e: e.matmul(pn[:, 0:n], avg[:], d1[:, c0:c0 + n], start=True, stop=True), reads=[avg, d1], writes=[pn])
                em.op("act", lambda e: e.activation(out=kk[:, c0:c0 + n], in_=pn[:, 0:n], func=AF.Ln, bias=epsb[:, 0:1]),
                      reads=[pn, epsb], writes=[kk])
            em.op("act", lambda e: e.activation(out=kk[:], in_=kk[:], func=AF.Exp, scale=-0.5), reads=[kk], writes=[kk])
            em.op("dve", lambda e: e.tensor_mul(out=kk[:], in0=kk[:], in1=oT[:]), reads=[kk, oT], writes=[kk])
            mo = mo_r.next()
            em.op("dve", lambda e: e.scalar_tensor_tensor(out=mo[:], in0=kk[:], scalar=rng[:, h:h + 1], in1=gs[:], op0=ALU.mult, op1=ALU.mult),
                  reads=[kk, rng, gs], writes=[mo])
            em.dma("sp", self.MIXT[h * 128:(h + 1) * 128, :], mo[:], reads=[mo], owner=mo)

        for g0 in range(0, 16, G):
            ctx = [pre_head(g0 + g, g) for g in range(G)]
            for c in ctx:
                emit_pss(c, 0)
            idx = 0
            for i in range(NT):
                for c in ctx:
                    tile_part(c, i)
                for _ in ([0] if i == 0 else [0, 1]):
                    for c in ctx:
                        chunk_pe(c, idx)
                    for c in ctx:
                        chunk_dve(c, idx)
                    idx += 1
            for c in ctx:
                post_head(c)
        em.end()

    def stage_out(self, s, l, W, last):
        em = self.em
        em.begin()
        Wb = em.sb("Wb", [128, 16, D], BF16)
        if last:
            self.fgain = em.sb("fgain", [128, D], F32, dma=True)
            em.dma("sp", self.fgain[:], self.final_gain.partition_broadcast(128), writes=[self.fgain], owner=self.fgain)
        w32r = em.ring("wo32", 2, [128, D], F32, dma=True)
        for k in range(16):
            w32 = w32r.next()
            em.dma("sp", w32[:], W[k * 128:(k + 1) * 128, :], writes=[w32], owner=w32)
            em.op("pool", lambda e, k=k: e.tensor_copy(Wb[:, k, :], w32[:]), reads=[w32], writes=[Wb])
        mtr = em.ring("mt", 2, [128, 16, 128], BF16, dma=True)
        htr = em.ring("ht", 2, [128, D], F32, dma=True)
        hnr = em.ring("hn", 2, [128, D], F32, dma=True)
        pmm = em.psring("pmm", 4, [128, 512], F32)
        junk = em.sb("junk", [128, D], BF16)
        ssr = em.ring("ss", 2, [128, 2], F32)
        tmr = em.ring("tm", 2, [128, 2], F32)
        for i in range(NT):
            r0, n = trng(i)
            if last and i == 0:
                continue
            mt = mtr.next()
            ht = htr.next()
            hn = hnr.next()
            em.dma("sp", mt[:, :, 0:n], self.MIXT[:, r0:r0 + n].rearrange("(k p) t -> p k t", p=128), writes=[mt], owner=mt)
            em.dma("sp", ht[0:n, :], self.h_src(s, l, i), writes=[ht], owner=ht)
            for c in range(4):
                ps = pmm.next()
                for k in range(16):
                    em.op("pe", lambda e, k=k, c=c: e.matmul(ps[0:n, :], mt[:, k, 0:n], Wb[:, k, c * 512:(c + 1) * 512],
                                                             start=(k == 0), stop=(k == 15)), reads=[mt, Wb], writes=[ps])
                em.op("dve", lambda e, c=c: e.tensor_add(out=hn[0:n, c * 512:(c + 1) * 512], in0=ht[0:n, c * 512:(c + 1) * 512], in1=ps[0:n, :]),
                      reads=[ht, ps], writes=[hn])
            if not last:
                em.dma("sp", self.H[r0:r0 + n, :], hn[0:n, :], reads=[hn], owner=hn)
            else:
                ss = ssr.next()
                tm = tmr.next()
                em.op("pool", lambda e: e.memset(ss[:], 0.0), writes=[ss])
                em.op("act", lambda e: e.activation(out=junk[0:n, :], in_=hn[0:n, :], func=AF.Square, accum_out=ss[0:n, 0:1]),
                      reads=[hn], writes=[junk, ss])
                self.rstd_rows(ss, n, D, RMS_EPS, tm)
                em.op("dve", lambda e: e.scalar_tensor_tensor(out=ht[0:n, :], in0=hn[0:n, :], scalar=ss[0:n, 1:2], in1=self.fgain[0:n, :],
                                                              op0=ALU.mult, op1=ALU.mult), reads=[hn, ss, self.fgain], writes=[ht])
                em.dma("sp", self.out[s, r0 - 16:r0 - 16 + n, :], ht[0:n, :], reads=[ht], owner=ht)
        em.end()


_CACHE = {}


def kernel(**inputs):
    x = np.ascontiguousarray(inputs["x"], dtype=np.float32)
    if "nc" not in _CACHE:
        _CACHE["nc"] = Prog().build()
    nc = _CACHE["nc"]
    oh = _bucket_onehot()
    names = ["meta_tokens", "norm_gain", "final_norm_gain", "rel_bias_table", "w_in_even", "conv_w", "conv_b",
             "conv_ln_gain", "conv_ln_bias", "kv_norm_gain", "w_uk", "w_uv", "w_out_even", "w_in_odd", "lb_logits",
             "rec_norm_gain", "w_out_odd"]
    shared = {k: np.ascontiguousarray(inputs[k], dtype=np.float32) for k in names}
    shared["c_oh"] = oh
    in_maps = []
    for c in range(NCORES):
        m = dict(shared)
        m["x"] = x[c * SEQ_PER_CORE:(c + 1) * SEQ_PER_CORE]
        in_maps.append(m)
    res = run_bass_kernel_spmd(nc, in_maps, core_ids=list(range(NCORES)))
    return np.concatenate([r["out"] for r in res.results], axis=0)
```

```python
import math
from contextlib import ExitStack

import numpy as np
import concourse.bass as bass
import concourse.mybir as mybir
from concourse.bass_utils import run_bass_kernel_spmd

F32 = mybir.dt.float32
BF16 = mybir.dt.bfloat16
AF = mybir.ActivationFunctionType
ALU = mybir.AluOpType
AX = mybir.AxisListType

NCORES = 8
SEQ_PER_CORE = 2
D = 2048
SEQ = 2048
NMETA = 16
T = SEQ + NMETA
NT = 17
DEPTH = 4
P_EVEN = 6480
RMS_EPS = 1e-6
LN_EPS = 1e-5
NEG = -1.0e30
NEG2 = -3.0e38
TOPK = 256


def trng(i):
    return (0, 16) if i == 0 else (16 + 128 * (i - 1), 128)


CCH = [(0, 16)] + [(16 + 512 * j, 512) for j in range(4)]


class Tk:
    __slots__ = ("name", "t", "w", "r", "dkey", "bank")

    def __init__(self, name, t=None):
        self.name = name
        self.t = t
        self.w = {}
        self.r = {}
        self.dkey = None
        self.bank = None

    def __getitem__(self, idx):
        return self.t[idx]


class Ring:
    def __init__(self, bufs):
        self.bufs = bufs
        self.i = 0

    def next(self):
        b = self.bufs[self.i % len(self.bufs)]
        self.i += 1
        return b


import os as _os
NOSELF = _os.environ.get("NOSELF", "0") == "1"


class Em:
    ENG = ("pe", "act", "dve", "pool", "sp")

    def __init__(self, nc, n_dsem=78):
        self.nc = nc
        self.top = ExitStack()
        self.eng = dict(pe=nc.tensor, act=nc.scalar, dve=nc.vector, pool=nc.gpsimd, sp=nc.sync)
        self.sems = {}
        self.val = {}
        for e in self.ENG:
            self.sems[e] = self.top.enter_context(nc.semaphore("es_" + e))
            self.val[e] = 0
        self.bar = self.top.enter_context(nc.semaphore("bar"))
        self.nbar = 0
        self.free_d = []
        for i in range(n_dsem):
            k = "D%d" % i
            self.sems[k] = self.top.enter_context(nc.semaphore("ds_%d" % i))
            self.val[k] = 0
            self.free_d.append(k)
        self.free_sw = []
        for i in range(10):
            k = "DS%d" % i
            self.sems[k] = self.top.enter_context(nc.semaphore("dsw_%d" % i))
            self.val[k] = 0
            self.free_sw.append(k)
        self.stage_sw = []
        self.seen = {e: {} for e in self.ENG}
        self.stage = None
        self.stage_d = []
        self.uid = 0
        self.nins = 0
        self.reg = {}

    def begin(self):
        self.stage = ExitStack()
        self.stage_d = []

    def end(self):
        self.barrier()
        self.stage.close()
        self.stage = None
        self.free_d.extend(self.stage_d)
        self.stage_d = []
        self.free_sw.extend(self.stage_sw)
        self.stage_sw = []

    def _nm(self, name):
        self.uid += 1
        return "%s_%d" % (name, self.uid)

    def sb(self, name, shape, dt=F32, dma=False, top=False):
        st = self.top if top else self.stage
        nm = self._nm(name)
        t = st.enter_context(self.nc.sbuf_tensor(nm, list(shape), dt))
        self.reg[name] = nm
        tk = Tk(name, t)
        if dma == "sw":
            k = self.free_sw.pop()
            tk.dkey = k
            if not top:
                self.stage_sw.append(k)
        elif dma:
            k = self.free_d.pop()
            tk.dkey = k
            if not top:
                self.stage_d.append(k)
        return tk

    def ring(self, name, n, shape, dt=F32, dma=False):
        return Ring([self.sb("%s%d" % (name, i), shape, dt, dma=dma) for i in range(n)])

    def ps(self, name, shape, dt=F32):
        t = self.stage.enter_context(self.nc.psum_tensor(self._nm(name), list(shape), dt))
        return Tk(name, t)

    def ps_views(self, name, n, sub_shape, dt=F32):
        t = self.stage.enter_context(self.nc.psum_tensor(self._nm(name), [128, n] + list(sub_shape), dt))
        bank = Tk(name + "_bank")
        views = [Tk("%s%d" % (name, i), t[:, i]) for i in range(n)]
        for v in views:
            v.bank = bank
        return views

    def psring(self, name, n, shape, dt=F32):
        return Ring([self.ps("%s%d" % (name, i), shape, dt) for i in range(n)])

    def _wait(self, eng, toks):
        need = {}
        for d in toks:
            for k, v in d.items():
                if v > need.get(k, 0):
                    need[k] = v
        seen = self.seen[eng]
        for k, v in need.items():
            if k == eng and (eng == "pe" or NOSELF):
                continue
            if seen.get(k, 0) >= v:
                continue
            self.eng[eng].wait_ge(self.sems[k], v)
            seen[k] = v

    @staticmethod
    def _deps(reads, writes):
        toks = []
        for t in reads:
            toks.append(t.w)
            if t.bank is not None:
                toks.append(t.bank.w)
        for t in writes:
            toks.append(t.w)
            toks.append(t.r)
            if t.bank is not None:
                toks.append(t.bank.r)
        return toks

    @staticmethod
    def _mark(k, v, reads, writes):
        for t in reads:
            if t.r.get(k, 0) < v:
                t.r[k] = v
            if t.bank is not None and t.bank.r.get(k, 0) < v:
                t.bank.r[k] = v
        for t in writes:
            t.w = {k: v}
            t.r = {}
            if t.bank is not None:
                t.bank.w = {k: v}

    def op(self, eng, fn, reads=(), writes=()):
        self._wait(eng, self._deps(reads, writes))
        ins = fn(self.eng[eng])
        self.val[eng] += 1
        ins.then_inc(self.sems[eng], 1)
        self._mark(eng, self.val[eng], reads, writes)
        self.nins += 1
        return ins

    def dma(self, q, out, in_, reads=(), writes=(), owner=None, **kw):
        self._wait(q, self._deps(reads, writes))
        ins = self.eng[q].dma_start(out=out, in_=in_, **kw)
        k = owner.dkey
        self.val[k] += 16
        ins.then_inc(self.sems[k], 16)
        self._mark(k, self.val[k], reads, writes)
        self.nins += 1
        return ins

    def barrier(self):
        self.nbar += 1
        for e in self.ENG:
            g = self.eng[e]
            if self.val[e] > 0:
                g.wait_ge(self.sems[e], self.val[e])
            if e == "sp":
                for k, v in self.val.items():
                    if k[0] == "D" and v > 0 and self.seen["sp"].get(k, 0) < v:
                        g.wait_ge(self.sems[k], v)
            g.sem_inc(self.bar, 1)
        for e in self.ENG:
            self.eng[e].wait_ge(self.bar, 5 * self.nbar)
        for e in self.ENG:
            for k, v in self.val.items():
                self.seen[e][k] = v

    def close(self):
        self.top.close()


def _t5_bucket_np(rel):
    rel = np.asarray(rel, dtype=np.int32)
    nb = 16
    ret = np.where(rel > 0, nb, 0).astype(np.int32)
    n = np.abs(rel)
    max_exact = nb // 2
    nf = np.maximum(n, 1).astype(np.float32)
    large = max_exact + (np.log(nf / np.float32(max_exact)) / np.float32(math.log(128 / max_exact))
                         * np.float32(nb - max_exact)).astype(np.int32)
    large = np.minimum(large, nb - 1)
    return ret + np.where(n < max_exact, n, large)


def _bucket_onehot():
    rel = 255 - np.arange(512)
    b = _t5_bucket_np(rel)
    oh = np.zeros((32, 512), np.float32)
    oh[b, np.arange(512)] = 1.0
    return oh


class Prog:
    def __init__(self, nseq=SEQ_PER_CORE, layers=DEPTH, dbg=False, stop=10 ** 9):
        self.stop = stop
        self.nstage = 0
        self.nseq = nseq
        self.layers = layers
        self.dbg = dbg if dbg else ()
        ne = max(1, (layers + 1) // 2)
        no = layers // 2
        od = (lambda *sh: [max(no, 1)] + ([1] * len(sh) if no == 0 else list(sh)))
        nc = bass.Bass("TRN2", target_bir_lowering=False)
        self.nc = nc
        dt = nc.dram_tensor
        I = "ExternalInput"
        self.x = dt("x", [nseq, SEQ, D], F32, kind=I).ap()
        self.meta = dt("meta_tokens", [NMETA, D], F32, kind=I).ap()
        self.norm_gain = dt("norm_gain", [4, D], F32, kind=I).ap()
        self.final_gain = dt("final_norm_gain", [D], F32, kind=I).ap()
        self.relb = dt("rel_bias_table", [32, 8], F32, kind=I).ap()
        self.w_in_even = dt("w_in_even", [ne, D, P_EVEN], F32, kind=I).ap()
        self.conv_w = dt("conv_w", [2, 31, 1024], F32, kind=I).ap()
        self.conv_b = dt("conv_b", [2, 1024], F32, kind=I).ap()
        self.ln_g = dt("conv_ln_gain", [2, 1024], F32, kind=I).ap()
        self.ln_b = dt("conv_ln_bias", [2, 1024], F32, kind=I).ap()
        self.kv_g = dt("kv_norm_gain", [2, 256], F32, kind=I).ap()
        self.w_uk = dt("w_uk", [ne, 8, 256, 128], F32, kind=I).ap()
        self.w_uv = dt("w_uv", [ne, 8, 256, 128], F32, kind=I).ap()
        self.w_out_even = dt("w_out_even", [ne, D, D], F32, kind=I).ap()
        self.w_in_odd = dt("w_in_odd", od(D, 4 * D), F32, kind=I).ap()
        self.lb_logits = dt("lb_logits", [4, D], F32, kind=I).ap()
        self.rec_g = dt("rec_norm_gain", [2, D], F32, kind=I).ap()
        self.w_out_odd = dt("w_out_odd", od(D, D), F32, kind=I).ap()
        self.c_oh = dt("c_oh", [32, 512], F32, kind=I).ap()
        self.out = dt("out", [nseq, SEQ, D], F32, kind="ExternalOutput").ap()
        sk = lambda n: "ExternalOutput" if n in self.dbg else "Internal"
        self.H = dt("Hs", [T, D], F32, kind=sk("Hs")).ap()
        self.ZT = dt("ZTs", [4 * D, T], F32, kind=sk("ZTs")).ap()
        self.VTOK = dt("VTOKs", [T, D], BF16, kind=sk("VTOKs")).ap()
        self.ZB = dt("ZBs", [2176, T], BF16, kind=sk("ZBs")).ap()
        self.MIXT = dt("MIXTs", [D, T], BF16, kind=sk("MIXTs")).ap()
        self.FD = dt("FDs", [8, 512], F32, kind=sk("FDs")).ap()
        self.em = Em(nc)

    def build(self):
        em = self.em

        def run(f, *a, **k):
            if self.nstage < self.stop:
                f(*a, **k)
            self.nstage += 1

        run(self.setup_consts)
        for s in range(self.nseq):
            for l in range(self.layers):
                last = (l == self.layers - 1)
                if l % 2 == 0:
                    run(self.stage_norm_proj, s, l, even=True)
                    run(self.stage_conv, l // 2)
                    run(self.stage_attn, l // 2)
                    run(self.stage_out, s, l, self.w_out_even[l // 2], last)
                else:
                    run(self.stage_norm_proj, s, l, even=False)
                    run(self.stage_rec, l)
                    run(self.stage_out, s, l, self.w_out_odd[l // 2], last)
        em.close()
        return self.nc

    def vecT(self, dst, dst_cols, src_rows_ap, nrows, ps, stg):
        em = self.em
        em.dma("sp", stg[0:nrows, :], src_rows_ap, writes=[stg], owner=stg)
        em.op("pe", lambda e: e.transpose(ps[:, 0:nrows], stg[0:nrows, :], self.idf[0:nrows, 0:nrows]),
              reads=[stg, self.idf], writes=[ps])
        em.op("act", lambda e: e.activation(out=dst[:, dst_cols:dst_cols + nrows], in_=ps[:, 0:nrows], func=AF.Copy),
              reads=[ps], writes=[dst])

    def setup_consts(self):
        em = self.em
        nc = self.nc
        self.idf = em.sb("idf", [128, 128], F32, top=True)
        self.idb = em.sb("idb", [128, 128], BF16, top=True)
        self.onesb = em.sb("onesb", [128, 128], BF16, top=True)
        self.avg = {}
        for n in (128, 256, 1024):
            self.avg[n] = em.sb("avg%d" % n, [128, 128], F32, top=True)
        self.cmask = em.sb("cmask", [128, 128], F32, top=True)
        self.EB = em.sb("EB", [128, 3, 8, 128], F32, top=True)
        self.bfar = em.sb("bfar", [128, 8], F32, top=True, dma=True)
        self.lbv = em.sb("lbv", [128, 2, 16], F32, top=True)
        self.oml = em.sb("oml", [128, 2, 16], F32, top=True)
        self.noml = em.sb("noml", [128, 2, 16], F32, top=True)

        em.begin()
        P = lambda f, **k: em.op("pool", f, **k)
        P(lambda e: e.memset(self.idf[:], 1.0), writes=[self.idf])
        P(lambda e: e.affine_select(out=self.idf[:], in_=self.idf[:], pattern=[[-1, 128]], compare_op=ALU.is_equal,
                                    fill=0.0, base=0, channel_multiplier=1), reads=[self.idf], writes=[self.idf])
        P(lambda e: e.tensor_copy(self.idb[:], self.idf[:]), reads=[self.idf], writes=[self.idb])
        P(lambda e: e.memset(self.onesb[:], 1.0), writes=[self.onesb])
        for n in (128, 256, 1024):
            P(lambda e, n=n: e.memset(self.avg[n][:], 1.0 / n), writes=[self.avg[n]])
        P(lambda e: e.memset(self.cmask[:], 1.0), writes=[self.cmask])
        P(lambda e: e.affine_select(out=self.cmask[:], in_=self.cmask[:], pattern=[[1, 128]], compare_op=ALU.is_ge,
                                    fill=0.0, base=0, channel_multiplier=-1), reads=[self.cmask], writes=[self.cmask])
        P(lambda e: e.memset(self.cmask[0:64, 64:128], 0.0), writes=[self.cmask])

        anti = em.sb("anti", [128, 128], F32)
        P(lambda e: e.memset(anti[:], 1.0), writes=[anti])
        P(lambda e: e.affine_select(out=anti[:], in_=anti[:], pattern=[[1, 128]], compare_op=ALU.is_equal,
                                    fill=0.0, base=-127, channel_multiplier=1), reads=[anti], writes=[anti])
        tab = em.sb("tab", [32, 8], F32, dma=True)
        oh = em.sb("oh", [32, 512], F32, dma=True)
        fsb = em.sb("fsb", [8, 512], F32, dma=True)
        nbfar = em.sb("nbfar", [128, 8], F32)
        psA = em.ps("psA", [128, 512], F32)
        psB = em.ps("psB", [128, 128], F32)
        em.dma("sp", tab[:], self.relb[:, :], writes=[tab], owner=tab)
        em.dma("sp", oh[:], self.c_oh[:, :], writes=[oh], owner=oh)
        em.dma("sp", self.bfar[:], bass.AP(tensor=self.relb.tensor, offset=15 * 8, ap=[[0, 128], [1, 8]]),
               writes=[self.bfar], owner=self.bfar)
        em.op("dve", lambda e: e.tensor_scalar(out=nbfar[:], in0=self.bfar[:], scalar1=-1.0, scalar2=None, op0=ALU.mult),
              reads=[self.bfar], writes=[nbfar])
        em.op("pe", lambda e: e.matmul(psA[0:8, :], tab[0:32, 0:8], oh[0:32, :], start=True, stop=True),
              reads=[tab, oh], writes=[psA])
        em.op("act", lambda e: e.activation(out=fsb[:], in_=psA[0:8, :], func=AF.Copy), reads=[psA], writes=[fsb])
        fd_tk = Tk("FD")
        em.dma("sp", self.FD[:, :], fsb[:], reads=[fsb], writes=[fd_tk], owner=fsb)
        hk = em.ring("hk", 2, [128, 128], F32, dma=True)
        for ty, c0 in enumerate((128, 256, 144)):
            for h in range(8):
                hb = hk.next()
                src = bass.AP(tensor=self.FD.tensor, offset=h * 512 + c0, ap=[[1, 128], [1, 128]])
                em.dma("sp", hb[:], src, reads=[fd_tk], writes=[hb], owner=hb)
                em.op("pe", lambda e, hb=hb: e.matmul(psB[:], anti[:], hb[:], start=True, stop=True),
                      reads=[anti, hb], writes=[psB])
                em.op("act", lambda e, ty=ty, h=h: e.activation(out=self.EB[:, ty, h, :], in_=psB[:], func=AF.Exp,
                                                                bias=nbfar[:, h:h + 1]),
                      reads=[psB, nbfar], writes=[self.EB])

        stg = em.sb("stg", [128, 128], F32, dma=True)
        lbT = em.sb("lbT", [128, 64], F32)
        self.vecT(lbT, 0, self.lb_logits.rearrange("l (c p) -> (l c) p", p=128), 64, psB, stg)
        mx = em.sb("mx", [128, 16], F32)
        ex = em.sb("ex", [128, 4, 16], F32)
        sm = em.sb("sm", [128, 16], F32)
        rs = em.sb("rs", [128, 16], F32)
        c1 = em.sb("c1", [128, 16], F32)
        V = lambda f, **k: em.op("dve", f, **k)
        V(lambda e: e.tensor_max(out=mx[:], in0=lbT[:, 0:16], in1=lbT[:, 16:32]), reads=[lbT], writes=[mx])
        V(lambda e: e.tensor_max(out=mx[:], in0=mx[:], in1=lbT[:, 32:48]), reads=[lbT, mx], writes=[mx])
        V(lambda e: e.tensor_max(out=mx[:], in0=mx[:], in1=lbT[:, 48:64]), reads=[lbT, mx], writes=[mx])
        for l in range(4):
            V(lambda e, l=l: e.tensor_sub(out=ex[:, l, :], in0=lbT[:, 16 * l:16 * l + 16], in1=mx[:]),
              reads=[lbT, mx], writes=[ex])
        em.op("act", lambda e: e.activation(out=ex[:], in_=ex[:], func=AF.Exp), reads=[ex], writes=[ex])
        V(lambda e: e.tensor_add(out=sm[:], in0=ex[:, 0, :], in1=ex[:, 1, :]), reads=[ex], writes=[sm])
        V(lambda e: e.tensor_add(out=sm[:], in0=sm[:], in1=ex[:, 2, :]), reads=[ex, sm], writes=[sm])
        V(lambda e: e.tensor_add(out=sm[:], in0=sm[:], in1=ex[:, 3, :]), reads=[ex, sm], writes=[sm])
        V(lambda e: e.reciprocal(out=rs[:], in_=sm[:]), reads=[sm], writes=[rs])
        V(lambda e: e.tensor_mul(out=self.lbv[:, 0, :], in0=ex[:, 1, :], in1=rs[:]), reads=[ex, rs], writes=[self.lbv])
        V(lambda e: e.tensor_add(out=c1[:], in0=ex[:, 1, :], in1=ex[:, 2, :]), reads=[ex], writes=[c1])
        V(lambda e: e.tensor_add(out=c1[:], in0=c1[:], in1=ex[:, 3, :]), reads=[ex, c1], writes=[c1])
        V(lambda e: e.tensor_mul(out=self.lbv[:, 1, :], in0=c1[:], in1=rs[:]), reads=[c1, rs], writes=[self.lbv])
        V(lambda e: e.tensor_scalar(out=self.oml[:], in0=self.lbv[:], scalar1=-1.0, scalar2=1.0, op0=ALU.mult, op1=ALU.add),
          reads=[self.lbv], writes=[self.oml])
        V(lambda e: e.tensor_scalar(out=self.noml[:], in0=self.oml[:], scalar1=-1.0, scalar2=None, op0=ALU.mult),
          reads=[self.oml], writes=[self.noml])
        em.end()

    def h_src(self, s, l, i):
        r0, n = trng(i)
        if l == 0:
            if i == 0:
                return self.meta[:, :]
            return self.x[s, r0 - 16:r0 - 16 + n, :]
        return self.H[r0:r0 + n, :]

    def rstd_rows(self, ss, n, dim, eps, tmp):
        em = self.em
        em.op("dve", lambda e: e.tensor_scalar(out=tmp[0:n, 0:1], in0=ss[0:n, 0:1], scalar1=1.0 / dim, scalar2=eps,
                                               op0=ALU.mult, op1=ALU.add), reads=[ss], writes=[tmp])
        em.op("act", lambda e: e.activation(out=tmp[0:n, 1:2], in_=tmp[0:n, 0:1], func=AF.Sqrt), reads=[tmp], writes=[tmp])
        em.op("dve", lambda e: e.reciprocal(out=ss[0:n, 1:2], in_=tmp[0:n, 1:2]), reads=[tmp], writes=[ss])

    def stage_norm_proj(self, s, l, even):
        em = self.em
        em.begin()
        hnT = em.sb("hnT", [128, 16, T], BF16)
        outer = em.stage
        nrm = ExitStack()
        em.stage = nrm
        gbc = em.sb("gbc", [128, D], F32, dma=True)
        em.dma("sp", gbc[:], self.norm_gain[l, :].partition_broadcast(128), writes=[gbc], owner=gbc)
        htr = em.ring("ht", 4, [128, D], F32, dma=True)
        hsr = em.ring("hs", 3, [128, D], BF16)
        junk = em.sb("junk", [128, D], BF16)
        ssr = em.ring("ss", 4, [128, 2], F32)
        tmr = em.ring("tm", 4, [128, 2], F32)
        ptr = em.psring("ptr", 2, [128, 16, 128], BF16)
        for i in range(NT):
            r0, n = trng(i)
            ht = htr.next()
            hs = hsr.next()
            ss = ssr.next()
            tm = tmr.next()
            pt = ptr.next()
            em.dma("sp", ht[0:n, :], self.h_src(s, l, i), writes=[ht], owner=ht)
            em.op("pool", lambda e: e.memset(ss[:], 0.0), writes=[ss])
            em.op("act", lambda e: e.activation(out=junk[0:n, :], in_=ht[0:n, :], func=AF.Square, accum_out=ss[0:n, 0:1]),
                  reads=[ht], writes=[junk, ss])
            self.rstd_rows(ss, n, D, RMS_EPS, tm)
            em.op("dve", lambda e: e.scalar_tensor_tensor(out=hs[0:n, :], in0=ht[0:n, :], scalar=ss[0:n, 1:2],
                                                          in1=gbc[0:n, :], op0=ALU.mult, op1=ALU.mult),
                  reads=[ht, ss, gbc], writes=[hs])
            for k in range(16):
                em.op("pe", lambda e, k=k: e.transpose(pt[:, k, 0:n], hs[0:n, k * 128:(k + 1) * 128], self.idb[0:n, 0:n]),
                      reads=[hs, self.idb], writes=[pt])
            em.op("act", lambda e: e.activation(out=hnT[:, :, r0:r0 + n], in_=pt[:, :, 0:n], func=AF.Copy),
                  reads=[pt], writes=[hnT])
        self.hnT = hnT
        em.barrier()
        nrm.close()
        em.stage = outer
        if even:
            self.proj_even(l // 2)
        else:
            self.proj_odd(l // 2)
        em.end()

    def proj_fm(self, W, chunks, wtr32, wtr, pmm, otr, otbr=None):
        em = self.em
        hnT = self.hnT
        blocks = []
        for ch in chunks:
            col0, ncols, func, scale, dst0, dup = ch
            if (blocks and not dup and ncols == 128 and len(blocks[-1]) < 4 and not blocks[-1][-1][5]
                    and blocks[-1][-1][1] == 128 and blocks[-1][-1][0] + 128 == col0):
                blocks[-1].append(ch)
            else:
                blocks.append([ch])
        for blk in blocks:
            w32 = wtr32.next()
            wb = wtr.next()
            bcol0 = blk[0][0]
            bn = sum(c[1] for c in blk)
            em.dma("sp", w32[:, :, 0:bn], W[:, bcol0:bcol0 + bn].rearrange("(kc p) m -> p kc m", p=128),
                   writes=[w32], owner=w32)
            if blk[0][5]:
                em.dma("sp", w32[:, :, bn:2 * bn], W[:, bcol0:bcol0 + bn].rearrange("(kc p) m -> p kc m", p=128),
                       writes=[w32], owner=w32)
            for bi, (col0, ncols, func, scale, dst0, dup) in enumerate(blk):
                o = col0 - bcol0
                nm = ncols * (2 if dup else 1)
                em.op("pool", lambda e, o=o, nm=nm: e.tensor_copy(wb[:, :, o:o + nm], w32[:, :, o:o + nm]), reads=[w32], writes=[wb])
            for bi, (col0, ncols, func, scale, dst0, dup) in enumerate(blk):
                o = col0 - bcol0
                nm = ncols * (2 if dup else 1)
                tobf = dst0 < 0
                ot = otbr.next() if tobf else otr.next()
                for (c0, n) in CCH:
                    ps = pmm.next()
                    for k in range(16):
                        em.op("pe", lambda e, k=k: e.matmul(ps[0:nm, 0:n], wb[:, k, o:o + nm], hnT[:, k, c0:c0 + n],
                                                            start=(k == 0), stop=(k == 15)),
                              reads=[wb, hnT], writes=[ps])
                    em.op("act", lambda e: e.activation(out=ot[0:nm, c0:c0 + n], in_=ps[0:nm, 0:n], func=func, scale=scale),
                          reads=[ps], writes=[ot])
                if tobf:
                    r0 = -dst0 - 1
                    em.dma("act", self.ZB[r0:r0 + nm, :], ot[0:nm, :], reads=[ot], owner=ot)
                else:
                    em.dma("act", self.ZT[dst0:dst0 + nm, :], ot[0:nm, :], reads=[ot], owner=ot)

    ZE = dict(glu_v=0, glu_g=1024, gate_a=2048, q=3072, c=4096, gate_b=4352, qi=5376, ki=6400, wi=6528)

    def proj_even(self, e_):
        em = self.em
        W = self.w_in_even[e_]
        chunks = []
        for m in range(50):
            col0 = m * 128
            if col0 < 1024:
                f, sc = AF.Copy, 1.0
            elif col0 < 2048:
                f, sc = AF.Sigmoid, 1.0
            elif col0 < 3072:
                f, sc = AF.Silu, 1.0
            elif col0 < 4352:
                f, sc = AF.Copy, 1.0
            elif col0 < 5376:
                f, sc = AF.Silu, 1.0
            else:
                f, sc = AF.Copy, 0.125
            dst = col0
            if 3072 <= col0 < 4096:
                dst = -(col0 - 3072) - 1
            elif 5376 <= col0 < 6400:
                dst = -(1024 + col0 - 5376) - 1
            chunks.append((col0, 128, f, sc, dst, False))
        chunks.append((6400, 64, AF.Copy, 1.0, -2048 - 1, True))
        chunks.append((6464, 16, AF.Copy, 0.25, 6528, False))
        wtr32 = em.ring("w32", 2, [128, 16, 512], F32, dma=True)
        wtr = em.ring("wb", 2, [128, 16, 512], BF16)
        pmm = em.psring("pmm", 4, [128, 512], F32)
        otr = em.ring("ot", 2, [128, T], F32, dma=True)
        otbr = em.ring("otb", 2, [128, T], BF16, dma=True)
        self.proj_fm(W, chunks, wtr32, wtr, pmm, otr, otbr)

    def proj_odd(self, o_):
        em = self.em
        W = self.w_in_odd[o_]
        chunks = []
        for m in range(16):
            chunks.append((m * 128, 128, AF.Silu, 1.0, m * 128, False))
        for m in range(16):
            chunks.append((2048 + m * 128, 128, AF.Sigmoid, 1.0, 2048 + m * 128, False))
        for m in range(16):
            chunks.append((6144 + m * 128, 128, AF.Silu, 1.0, 4096 + m * 128, False))
        wtr32 = em.ring("w32", 2, [128, 16, 512], F32, dma=True)
        wtr = em.ring("wb", 2, [128, 16, 512], BF16)
        pmm = em.psring("pmm", 4, [128, 512], F32)
        otr = em.ring("ot", 2, [128, T], F32, dma=True)
        self.proj_fm(W, chunks, wtr32, wtr, pmm, otr)
        hnT = self.hnT
        vo = em.ring("vo", 2, [128, 512], BF16, dma=True)
        for g in range(4):
            w32 = wtr32.next()
            wb = wtr.next()
            em.dma("sp", w32[:], W[:, 4096 + g * 512:4096 + (g + 1) * 512].rearrange("(kc p) m -> p kc m", p=128),
                   writes=[w32], owner=w32)
            for k4 in range(4):
                em.op("pool", lambda e, k4=k4: e.tensor_copy(wb[:, 4 * k4:4 * k4 + 4, :], w32[:, 4 * k4:4 * k4 + 4, :]), reads=[w32], writes=[wb])
            for i in range(NT):
                r0, n = trng(i)
                ps = pmm.next()
                for k in range(16):
                    em.op("pe", lambda e, k=k: e.matmul(ps[0:n, :], hnT[:, k, r0:r0 + n], wb[:, k, :],
                                                        start=(k == 0), stop=(k == 15)),
                          reads=[wb, hnT], writes=[ps])
                v = vo.next()
                em.op("act", lambda e: e.activation(out=v[0:n, :], in_=ps[0:n, :], func=AF.Copy), reads=[ps], writes=[v])
                em.dma("act", self.VTOK[r0:r0 + n, g * 512:(g + 1) * 512], v[0:n, :], reads=[v], owner=v)

    def stage_conv(self, e_):
        em = self.em
        ZE = self.ZE
        em.begin()
        stg = em.sb("stg", [128, 128], F32, dma=True)
        pst = em.ps("pst", [128, 128], F32)
        cw = em.sb("cw", [128, 248], F32)
        cv = em.sb("cv", [128, 24], F32)
        cwr = self.conv_w[e_].rearrange("j (cc p) -> (j cc) p", p=128)
        self.vecT(cw, 0, cwr[0:124, :], 124, pst, stg)
        self.vecT(cw, 124, cwr[124:248, :], 124, pst, stg)
        self.vecT(cv, 0, self.conv_b[e_].rearrange("(cc p) -> cc p", p=128), 8, pst, stg)
        self.vecT(cv, 8, self.ln_g[e_].rearrange("(cc p) -> cc p", p=128), 8, pst, stg)
        self.vecT(cv, 16, self.ln_b[e_].rearrange("(cc p) -> cc p", p=128), 8, pst, stg)
        import os
        KCUT = int(os.environ.get("KCUT", "99"))
        if KCUT <= 1:
            em.end()
            return
        uall = em.sb("uall", [128, 8, T], F32)
        acc1 = em.sb("acc1", [128, T], F32)
        acc2 = em.sb("acc2", [128, T], F32)
        gsr = em.ring("gs", 1, [128, T], F32, dma=True)
        upr = em.ring("up", 2, [128, 30 + T], F32, dma=True)
        pa = em.ring("pa", 1, [128, T], F32)
        pb = em.ring("pb", 1, [128, T], F32)
        tmpr = em.ring("ctmp", 2, [128, T], F32)
        dgr = em.ring("dg", 3, [128, 128], F32)
        pcs = [em.ps("pc%d" % i, [128, 512], F32) for i in range(5)]
        sq = em.ring("sq", 1, [128, T], F32)
        for b in upr.bufs:
            em.op("pool", lambda e, b=b: e.memset(b[:, 0:30], 0.0), writes=[b])
        def load_u(cc):
            gs = gsr.next()
            up = upr.next()
            em.dma("sp", up[:, 30:30 + T], self.ZT[ZE["glu_v"] + cc * 128:ZE["glu_v"] + (cc + 1) * 128, :], writes=[up], owner=up)
            em.dma("sp", gs[:], self.ZT[ZE["glu_g"] + cc * 128:ZE["glu_g"] + (cc + 1) * 128, :], writes=[gs], owner=gs)
            em.op("pool", lambda e: e.tensor_tensor(out=up[:, 30:30 + T], in0=up[:, 30:30 + T], in1=gs[:], op=ALU.mult),
                  reads=[up, gs], writes=[up])
            return up

        up_next = load_u(0)
        for cc in range(8):
            up = up_next
            A = pa.next()
            B = pb.next()
            w = lambda j: cw[:, j * 8 + cc:j * 8 + cc + 1]
            em.op("dve", lambda e: e.tensor_scalar(out=A[:], in0=up[:, 0:T], scalar1=w(0), scalar2=cv[:, cc:cc + 1],
                                                   op0=ALU.mult, op1=ALU.add), reads=[up, cw, cv], writes=[A])
            for j in range(1, 10):
                em.op("dve", lambda e, j=j: e.scalar_tensor_tensor(out=A[:], in0=up[:, j:j + T], scalar=w(j), in1=A[:],
                                                                   op0=ALU.mult, op1=ALU.add), reads=[up, cw, A], writes=[A])
            em.op("act", lambda e: e.activation(out=B[:], in_=up[:, 10:10 + T], func=AF.Copy, scale=w(10)),
                  reads=[up, cw], writes=[B])
            for j in range(11, 17):
                tp = tmpr.next()
                em.op("act", lambda e, j=j, tp=tp: e.activation(out=tp[:], in_=up[:, j:j + T], func=AF.Copy, scale=w(j)),
                      reads=[up, cw], writes=[tp])
                em.op("pool", lambda e, tp=tp: e.tensor_add(out=B[:], in0=B[:], in1=tp[:]), reads=[tp, B], writes=[B])
            if cc + 1 < 8:
                up_next = load_u(cc + 1)
            for j in range(17, 31):
                dg = dgr.next()
                em.op("act", lambda e, j=j, dg=dg: e.activation(out=dg[:], in_=self.idf[:], func=AF.Copy, scale=w(j)),
                      reads=[self.idf, cw], writes=[dg])
                for ci, (c0, n) in enumerate(CCH):
                    em.op("pe", lambda e, j=j, dg=dg, ci=ci, c0=c0, n=n: e.matmul(pcs[ci][:, 0:n], dg[:], up[:, j + c0:j + c0 + n],
                                                                               start=(j == 17), stop=(j == 30)),
                          reads=[dg, up], writes=[pcs[ci]])
            em.op("dve", lambda e: e.tensor_add(out=uall[:, cc, :], in0=A[:], in1=B[:]), reads=[A, B], writes=[uall])
            for ci, (c0, n) in enumerate(CCH):
                em.op("dve", lambda e, ci=ci, c0=c0, n=n: e.tensor_add(out=uall[:, cc, c0:c0 + n], in0=uall[:, cc, c0:c0 + n], in1=pcs[ci][:, 0:n]),
                      reads=[uall, pcs[ci]], writes=[uall])
            s2 = sq.next()
            em.op("act", lambda e: e.activation(out=s2[:], in_=uall[:, cc, :], func=AF.Square), reads=[uall], writes=[s2])
            if cc == 0:
                em.op("pool", lambda e: e.tensor_copy(acc1[:], uall[:, cc, :]), reads=[uall], writes=[acc1])
                em.op("pool", lambda e: e.tensor_copy(acc2[:], s2[:]), reads=[s2], writes=[acc2])
            else:
                em.op("pool", lambda e: e.tensor_add(out=acc1[:], in0=acc1[:], in1=uall[:, cc, :]), reads=[uall, acc1], writes=[acc1])
                em.op("pool", lambda e: e.tensor_add(out=acc2[:], in0=acc2[:], in1=s2[:]), reads=[s2, acc2], writes=[acc2])
        if KCUT <= 2:
            em.end()
            return
        mean = em.sb("mean", [128, T], F32)
        rstd = em.sb("rstd", [128, T], F32)
        p1 = em.ps("p1", [128, 512], F32)
        p2 = em.ps("p2", [128, 512], F32)
        avg = self.avg[1024]
        for (c0, n) in CCH:
            em.op("pe", lambda e: e.matmul(p1[:, 0:n], avg[:], acc1[:, c0:c0 + n], start=True, stop=True),
                  reads=[avg, acc1], writes=[p1])
            em.op("pe", lambda e: e.matmul(p2[:, 0:n], avg[:], acc2[:, c0:c0 + n], start=True, stop=True),
                  reads=[avg, acc2], writes=[p2])
            em.op("act", lambda e: e.activation(out=mean[:, c0:c0 + n], in_=p1[:, 0:n], func=AF.Copy), reads=[p1], writes=[mean])
            em.op("dve", lambda e: e.tensor_tensor(out=rstd[:, c0:c0 + n], in0=mean[:, c0:c0 + n], in1=mean[:, c0:c0 + n], op=ALU.mult),
                  reads=[mean], writes=[rstd])
            em.op("dve", lambda e: e.tensor_sub(out=rstd[:, c0:c0 + n], in0=p2[:, 0:n], in1=rstd[:, c0:c0 + n]),
                  reads=[p2, rstd], writes=[rstd])
            em.op("dve", lambda e: e.tensor_scalar(out=rstd[:, c0:c0 + n], in0=rstd[:, c0:c0 + n], scalar1=LN_EPS, scalar2=None, op0=ALU.add),
                  reads=[rstd], writes=[rstd])
        em.op("act", lambda e: e.activation(out=rstd[:], in_=rstd[:], func=AF.Sqrt), reads=[rstd], writes=[rstd])
        em.op("dve", lambda e: e.reciprocal(out=rstd[:], in_=rstd[:]), reads=[rstd], writes=[rstd])
        if KCUT <= 3:
            em.end()
            return
        mxr = em.ring("mx", 2, [128, T], BF16, dma=True)
        for cc in range(8):
            ga = gsr.next()
            t1 = pa.next()
            t2 = pb.next()
            mx = mxr.next()
            em.dma("sp", ga[:], self.ZT[ZE["gate_a"] + cc * 128:ZE["gate_a"] + (cc + 1) * 128, :], writes=[ga], owner=ga)
            em.op("dve", lambda e: e.tensor_sub(out=t1[:], in0=uall[:, cc, :], in1=mean[:]), reads=[uall, mean], writes=[t1])
            em.op("pool", lambda e: e.tensor_mul(out=t1[:], in0=t1[:], in1=rstd[:]), reads=[t1, rstd], writes=[t1])
            em.op("act", lambda e: e.activation(out=t2[:], in_=t1[:], func=AF.Silu, scale=cv[:, 8 + cc:9 + cc], bias=cv[:, 16 + cc:17 + cc]),
                  reads=[t1, cv], writes=[t2])
            em.op("dve", lambda e: e.tensor_mul(out=mx[:], in0=t2[:], in1=ga[:]), reads=[t2, ga], writes=[mx])
            em.dma("sp", self.MIXT[cc * 128:(cc + 1) * 128, :], mx[:], reads=[mx], owner=mx)
        em.end()

    def stage_attn(self, e_):
        em = self.em
        ZE = self.ZE
        ZT = self.ZT
        em.begin()
        cnT = em.sb("cnT", [128, 2, T], BF16)
        cnk = em.sb("cnk", [128, NT, 256], BF16)
        wukT = em.sb("wukT", [128, 8, 256], BF16)
        wuvb = em.sb("wuvb", [128, 8, 2, 128], BF16)
        kiT2 = em.sb("kiT2", [128, T], BF16, dma=True)
        wiTok = em.sb("wiTok", [128, NT, 16], F32)
        kvg = em.sb("kvg", [128, 2], F32)
        stg = em.sb("stg", [128, 128], F32, dma=True)

        prep = ExitStack()
        stage_outer = em.stage
        em.stage = prep
        pA = em.ps("pA", [128, 512], F32)
        pB = em.ps("pB", [128, 512], F32)
        pT = em.ps("pT", [128, 128], F32)
        pTb = em.ps("pTb", [128, 2, 128], BF16)
        big = em.sb("big", [128, 2, T], F32, dma=True)
        big2 = em.sb("big2", [128, T], F32)
        rsd = em.sb("rsd", [128, T], F32)
        self.vecT(kvg, 0, self.kv_g[e_].rearrange("(cc p) -> cc p", p=128), 2, pT, stg)
        em.dma("sp", big[:], ZT[ZE["c"]:ZE["c"] + 256, :].rearrange("(cc p) t -> p cc t", p=128), writes=[big], owner=big)
        em.op("act", lambda e: e.activation(out=big2[:], in_=big[:, 0, :], func=AF.Square), reads=[big], writes=[big2])
        em.op("act", lambda e: e.activation(out=rsd[:], in_=big[:, 1, :], func=AF.Square), reads=[big], writes=[rsd])
        em.op("dve", lambda e: e.tensor_add(out=big2[:], in0=big2[:], in1=rsd[:]), reads=[big2, rsd], writes=[big2])
        avg = self.avg[256]
        for (c0, n) in CCH:
            em.op("pe", lambda e: e.matmul(pA[:, 0:n], avg[:], big2[:, c0:c0 + n], start=True, stop=True),
                  reads=[avg, big2], writes=[pA])
            em.op("dve", lambda e: e.tensor_scalar(out=rsd[:, c0:c0 + n], in0=pA[:, 0:n], scalar1=RMS_EPS, scalar2=None, op0=ALU.add),
                  reads=[pA], writes=[rsd])
        em.op("act", lambda e: e.activation(out=rsd[:], in_=rsd[:], func=AF.Sqrt), reads=[rsd], writes=[rsd])
        em.op("dve", lambda e: e.reciprocal(out=rsd[:], in_=rsd[:]), reads=[rsd], writes=[rsd])
        for cc in range(2):
            em.op("dve", lambda e, cc=cc: e.scalar_tensor_tensor(out=cnT[:, cc, :], in0=big[:, cc, :], scalar=kvg[:, cc:cc + 1],
                                                                 in1=rsd[:], op0=ALU.mult, op1=ALU.mult),
                  reads=[big, kvg, rsd], writes=[cnT])
        for i in range(NT):
            r0, n = trng(i)
            for cc in range(2):
                em.op("pe", lambda e, cc=cc: e.transpose(pTb[0:n, cc, :], cnT[:, cc, r0:r0 + n], self.idb[:, :]),
                      reads=[cnT, self.idb], writes=[pTb])
            em.op("act", lambda e: e.activation(out=cnk[0:n, i, :], in_=pTb[0:n, :, :], func=AF.Copy), reads=[pTb], writes=[cnk])
        wld = em.ring("wld", 2, [128, 2, 128], F32, dma=True)
        for h in range(8):
            w = wld.next()
            em.dma("sp", w[:], self.w_uk[e_, h].rearrange("(cc p) d -> p cc d", p=128), writes=[w], owner=w)
            for cc in range(2):
                em.op("pe", lambda e, cc=cc: e.transpose(pT[:, :], w[:, cc, :], self.idf[:, :]), reads=[w, self.idf], writes=[pT])
                em.op("act", lambda e, cc=cc: e.activation(out=wukT[:, h, cc * 128:(cc + 1) * 128], in_=pT[:, :], func=AF.Copy,
                                                           scale=128.0 ** -0.5), reads=[pT], writes=[wukT])
            w2 = wld.next()
            em.dma("sp", w2[:], self.w_uv[e_, h].rearrange("(cc p) d -> p cc d", p=128), writes=[w2], owner=w2)
            em.op("pool", lambda e: e.tensor_copy(wuvb[:, h, :, :], w2[:]), reads=[w2], writes=[wuvb])
        em.dma("sp", kiT2[:], self.ZB[2048:2176, :], writes=[kiT2], owner=kiT2)
        wiT = em.sb("wiT", [16, T], F32, dma=True)
        em.dma("sp", wiT[:], ZT[ZE["wi"]:ZE["wi"] + 16, :], writes=[wiT], owner=wiT)
        for i in range(NT):
            r0, n = trng(i)
            em.op("pe", lambda e: e.transpose(pT[0:n, 0:16], wiT[0:16, r0:r0 + n], self.idf[0:16, 0:16]),
                  reads=[wiT, self.idf], writes=[pT])
            em.op("act", lambda e: e.activation(out=wiTok[0:n, i, :], in_=pT[0:n, 0:16], func=AF.Copy), reads=[pT], writes=[wiTok])
        em.barrier()
        prep.close()
        em.stage = stage_outer

        import os
        AQT = int(os.environ.get("AQT", "99"))
        ASUB = int(os.environ.get("ASUB", "99"))
        pdot = em.psring("pdot", 2, [128, 512], F32)
        pmt = em.ps("pmt", [128, 8, 128], BF16)
        plog = em.psring("plog", 2, [128, 4, 128], F32)
        pso_r = em.psring("pso", 1, [128, 3, 128], F32)
        psb = em.ps("psb", [128, 128], F32)
        pql = em.ps("pql", [128, 2, 128], F32)
        qi_r = em.ring("qiT", 2, [128, 8, 128], BF16, dma=True)
        qh_r = em.ring("qhT", 2, [128, 8, 128], BF16, dma=True)
        ql_r = em.ring("qlat", 2, [128, 8, 2, 128], BF16)
        score_r = em.ring("score", 2, [128, T], F32)
        work_r = em.ring("work", 1, [128, T], F32)
        m8 = em.sb("m8", [128, 8], F32)
        NIT = 24
        blo = em.sb("blo", [128, 1], F32)
        brg = em.sb("brg", [128, 1], F32)
        bthr = em.sb("bthr", [128, 1], F32)
        bcnt = em.sb("bcnt", [128, 1], F32)
        btq = em.sb("btq", [128, 1], F32)
        stab = em.sb("stab", [128, NIT], F32)
        pw2 = em.sb("pw2", [128, NIT], F32)
        for k in range(NIT):
            em.op("pool", lambda e, k=k: e.memset(pw2[:, k:k + 1], 2.0 ** -(k + 1)), writes=[pw2])
        rl_r = em.ring("rl", 3, [128, 512], F32)
        mask_r = em.ring("mask", 2, [128, T], BF16)
        maskT_r = em.ring("maskT", 2, [128, NT, 128], BF16)
        cm_r = em.ring("cm", 2, [128, 8, 2, 128], F32)
        ex_r = em.ring("ex", 2, [128, 4, 128], F32)
        pt_r = em.ring("ptile", 10, [128, 4, 128], BF16)
        osb_r = em.ring("osb", 2, [128, 2, 128], BF16)
        rden_r = em.ring("rden", 2, [128, 128], F32)
        dsb_r = em.ring("dsb", 2, [128, 128], F32)
        gb_r = em.ring("gb", 2, [128, 8, 128], F32, dma=True)
        tb_r = em.ring("tb", 2, [128, 128], F32)
        mixb_r = em.ring("mixb", 2, [128, 8, 128], BF16, dma=True)
        EB = self.EB
        def pre(qt):
            q0, nq = trng(qt)
            nk = q0 + nq
            score = score_r.next()
            gb = gb_r.next()
            em.dma("sp", gb[:, :, 0:nq], ZT[ZE["gate_b"]:ZE["gate_b"] + 1024, q0:q0 + nq].rearrange("(h p) t -> p h t", p=128),
                   writes=[gb], owner=gb)
            qiT = qi_r.next()
            em.dma("sp", qiT[:, :, 0:nq], self.ZB[1024:2048, q0:q0 + nq].rearrange("(c p) t -> p c t", p=128),
                   writes=[qiT], owner=qiT)
            qhT = qh_r.next()
            em.dma("sp", qhT[:, :, 0:nq], self.ZB[0:1024, q0:q0 + nq].rearrange("(h p) t -> p h t", p=128),
                   writes=[qhT], owner=qhT)
            qlat = ql_r.next()
            for h in range(8):
                for cc in range(2):
                    em.op("pe", lambda e, h=h, cc=cc: e.matmul(pql[:, cc, 0:nq], wukT[:, h, cc * 128:(cc + 1) * 128], qhT[:, h, 0:nq],
                                                               start=True, stop=True), reads=[wukT, qhT], writes=[pql])
                em.op("act", lambda e, h=h: e.activation(out=qlat[:, h, :, 0:nq], in_=pql[:, :, 0:nq], func=AF.Copy),
                      reads=[pql], writes=[qlat])
            yield
            kch = [(k0, min(512, nk - k0)) for k0 in range(0, nk, 512)]
            for h16 in range(16):
                c_, po = h16 // 2, (h16 % 2) * 64
                for (k0, n) in kch:
                    ps = pdot.next()
                    rl = rl_r.next()
                    em.op("pe", lambda e: e.matmul(ps[0:nq, 0:n], qiT[po:po + 64, c_, 0:nq], kiT2[po:po + 64, k0:k0 + n],
                                                   start=True, stop=True), reads=[qiT, kiT2], writes=[ps])
                    em.op("act", lambda e: e.activation(out=rl[0:nq, 0:n], in_=ps[0:nq, 0:n], func=AF.Relu), reads=[ps], writes=[rl])
                    if h16 == 0:
                        em.op("dve", lambda e: e.tensor_scalar(out=score[0:nq, k0:k0 + n], in0=rl[0:nq, 0:n],
                                                               scalar1=wiTok[0:nq, qt, 0:1], scalar2=None, op0=ALU.mult),
                              reads=[rl, wiTok], writes=[score])
                    else:
                        em.op("dve", lambda e: e.scalar_tensor_tensor(out=score[0:nq, k0:k0 + n], in0=rl[0:nq, 0:n],
                                                                      scalar=wiTok[0:nq, qt, h16:h16 + 1],
                                                                      in1=score[0:nq, k0:k0 + n], op0=ALU.mult, op1=ALU.add),
                              reads=[rl, wiTok, score], writes=[score])
                yield
            mask = mask_r.next()
            if qt >= 1:
                em.op("pool", lambda e: e.memset(score[0:64, nk - 64:nk], NEG), reads=[], writes=[score])
            if nk > TOPK and nk - 64 < TOPK:
                work = work_r.next()
                src = score
                for r in range(TOPK // 8):
                    em.op("dve", lambda e, src=src: e.max(out=m8[0:nq, :], in_=src[0:nq, 0:nk]), reads=[src], writes=[m8])
                    em.op("dve", lambda e, src=src: e.match_replace(out=work[0:nq, 0:nk], in_to_replace=m8[0:nq, :],
                                                                    in_values=src[0:nq, 0:nk], imm_value=NEG2),
                          reads=[src, m8], writes=[work])
                    src = work
                    yield
                em.op("dve", lambda e: e.tensor_scalar(out=mask[0:nq, 0:nk], in0=work[0:nq, 0:nk], scalar1=-2.0e38, scalar2=None,
                                                       op0=ALU.is_lt), reads=[work], writes=[mask])
            elif nk > TOPK:
                nlo = nk - 64
                em.op("dve", lambda e: e.max(out=m8[0:nq, :], in_=score[0:nq, 0:nk]), reads=[score], writes=[m8])
                em.op("dve", lambda e: e.tensor_reduce(out=blo[0:nq, :], in_=score[0:nq, 0:nlo], axis=AX.X, op=ALU.min),
                      reads=[score], writes=[blo])
                em.op("dve", lambda e: e.tensor_sub(out=brg[0:nq, :], in0=m8[0:nq, 0:1], in1=blo[0:nq, :]), reads=[m8, blo], writes=[brg])
                em.op("dve", lambda e: e.tensor_scalar(out=stab[0:nq, :], in0=pw2[0:nq, :], scalar1=brg[0:nq, 0:1], scalar2=None, op0=ALU.mult),
                      reads=[pw2, brg], writes=[stab])
                yield
                for k in range(NIT):
                    em.op("dve", lambda e, k=k: e.tensor_add(out=bthr[0:nq, :], in0=blo[0:nq, :], in1=stab[0:nq, k:k + 1]),
                          reads=[blo, stab], writes=[bthr])
                    em.op("dve", lambda e: e.tensor_scalar(out=mask[0:nq, 0:nk], in0=score[0:nq, 0:nk], scalar1=bthr[0:nq, 0:1], scalar2=0.0,
                                                           op0=ALU.is_ge, op1=ALU.add, accum_out=bcnt[0:nq, 0:1]),
                          reads=[score, bthr], writes=[mask, bcnt])
                    em.op("dve", lambda e, k=k: e.scalar_tensor_tensor(out=btq[0:nq, :], in0=bcnt[0:nq, :], scalar=TOPK - 0.5,
                                                                       in1=stab[0:nq, k:k + 1], op0=ALU.is_ge, op1=ALU.mult),
                          reads=[bcnt, stab], writes=[btq])
                    em.op("dve", lambda e: e.tensor_add(out=blo[0:nq, :], in0=blo[0:nq, :], in1=btq[0:nq, :]), reads=[blo, btq], writes=[blo])
                    yield
                em.op("dve", lambda e: e.tensor_scalar(out=mask[0:nq, 0:nk], in0=score[0:nq, 0:nk], scalar1=blo[0:nq, 0:1], scalar2=None,
                                                       op0=ALU.is_ge), reads=[score, blo], writes=[mask])
            else:
                em.op("dve", lambda e: e.tensor_scalar(out=mask[0:nq, 0:nk], in0=score[0:nq, 0:nk], scalar1=-1.0e29, scalar2=None,
                                                       op0=ALU.is_gt), reads=[score], writes=[mask])
            if qt >= 1:
                em.op("pool", lambda e: e.memset(mask[0:64, nk - 64:nk], 0.0), writes=[mask])
            maskT = maskT_r.next()
            for kb0 in range(0, qt + 1, 8):
                kbs = list(range(kb0, min(qt + 1, kb0 + 8)))
                for kb in kbs:
                    k0, nkb = trng(kb)
                    em.op("pe", lambda e, kb=kb, k0=k0, nkb=nkb: e.transpose(pmt[0:nkb, kb - kb0, 0:nq], mask[0:nq, k0:k0 + nkb],
                                                                            self.idb[0:nq, 0:nq]),
                          reads=[mask, self.idb], writes=[pmt])
                if kb0 == 0:
                    em.op("act", lambda e: e.activation(out=maskT[0:16, 0, 0:nq], in_=pmt[0:16, 0, 0:nq], func=AF.Copy),
                          reads=[pmt], writes=[maskT])
                    if len(kbs) > 1:
                        em.op("act", lambda e: e.activation(out=maskT[:, 1:len(kbs), 0:nq], in_=pmt[:, 1:len(kbs), 0:nq], func=AF.Copy),
                              reads=[pmt], writes=[maskT])
                else:
                    em.op("act", lambda e: e.activation(out=maskT[:, kb0:kb0 + len(kbs), 0:nq], in_=pmt[:, 0:len(kbs), 0:nq], func=AF.Copy),
                          reads=[pmt], writes=[maskT])
            yield
            cm = cm_r.next()
            if qt == 0:
                near = {0: (0, 0, 16)}
            elif qt == 1:
                near = {1: (0, 0, 128), 0: (1, 2, 16)}
            else:
                near = {qt: (0, 0, 128), qt - 1: (1, 1, 128)}
            for kb, (slot, ty, rows) in near.items():
                for h in range(8):
                    em.op("pool", lambda e, kb=kb, slot=slot, ty=ty, rows=rows, h=h: e.tensor_tensor(
                        out=cm[0:rows, h, slot, 0:nq], in0=EB[0:rows, ty, h, 0:nq], in1=maskT[0:rows, kb, 0:nq], op=ALU.mult),
                        reads=[EB, maskT], writes=[cm])
            self._pre[qt] = dict(gb=gb, qlat=qlat, maskT=maskT, cm=cm, near=near)
            yield

        def head(qt, h, st, mixb):
            q0, nq = trng(qt)
            gb, qlat, maskT, cm, near = st["gb"], st["qlat"], st["maskT"], st["cm"], st["near"]
            groups = [[0]] + [list(range(a, min(qt + 1, a + 4))) for a in range(1, qt + 1, 4)]
            pso = pso_r.next()
            ptiles = {}
            for grp in groups:
                pl = plog.next()
                rows = 16 if grp[0] == 0 else 128
                for gi, kb in enumerate(grp):
                    k0, nkb = trng(kb)
                    for cc in range(2):
                        em.op("pe", lambda e, gi=gi, k0=k0, nkb=nkb, cc=cc: e.matmul(
                            pl[0:nkb, gi, 0:nq], cnT[:, cc, k0:k0 + nkb], qlat[:, h, cc, 0:nq],
                            start=(cc == 0), stop=(cc == 1)), reads=[cnT, qlat], writes=[pl])
                ex = ex_r.next()
                g_n = len(grp)
                em.op("act", lambda e, rows=rows, g_n=g_n: e.activation(out=ex[0:rows, 0:g_n, 0:nq], in_=pl[0:rows, 0:g_n, 0:nq],
                                                                        func=AF.Exp, bias=self.bfar[0:rows, h:h + 1]),
                      reads=[pl, self.bfar], writes=[ex])
                ptile = pt_r.next()
                far = [gi for gi, kb in enumerate(grp) if kb not in near]
                if far:
                    a, b = far[0], far[-1] + 1
                    kba = grp[a]
                    em.op("dve", lambda e, a=a, b=b, kba=kba, rows=rows: e.tensor_tensor(
                        out=ptile[0:rows, a:b, 0:nq], in0=ex[0:rows, a:b, 0:nq], in1=maskT[0:rows, kba:kba + (b - a), 0:nq], op=ALU.mult),
                        reads=[ex, maskT], writes=[ptile])
                for gi, kb in enumerate(grp):
                    if kb in near:
                        slot, ty, rws = near[kb]
                        em.op("dve", lambda e, gi=gi, slot=slot, rws=rws: e.tensor_tensor(
                            out=ptile[0:rws, gi, 0:nq], in0=ex[0:rws, gi, 0:nq], in1=cm[0:rws, h, slot, 0:nq], op=ALU.mult),
                            reads=[ex, cm], writes=[ptile])
                for gi, kb in enumerate(grp):
                    ptiles[kb] = (ptile, gi)
            for part in range(3):
                for kb in range(qt + 1):
                    k0, nkb = trng(kb)
                    ptile, gi = ptiles[kb]
                    if part < 2:
                        lhs = cnk[0:nkb, kb, part * 128:(part + 1) * 128]
                        rd = [cnk, ptile]
                    else:
                        lhs = self.onesb[0:nkb, :]
                        rd = [self.onesb, ptile]
                    em.op("pe", lambda e, lhs=lhs, ptile=ptile, gi=gi, nkb=nkb, kb=kb, part=part: e.matmul(
                        pso[:, part, 0:nq], lhs, ptile[0:nkb, gi, 0:nq], start=(kb == 0), stop=(kb == qt)),
                        reads=rd, writes=[pso])
            osb = osb_r.next()
            rden = rden_r.next()
            dsb = dsb_r.next()
            em.op("act", lambda e: e.activation(out=osb[:, :, 0:nq], in_=pso[:, 0:2, 0:nq], func=AF.Copy), reads=[pso], writes=[osb])
            em.op("act", lambda e: e.activation(out=dsb[:, 0:nq], in_=pso[:, 2, 0:nq], func=AF.Ln), reads=[pso], writes=[dsb])
            for cc in range(2):
                em.op("pe", lambda e, cc=cc: e.matmul(psb[:, 0:nq], wuvb[:, h, cc, :], osb[:, cc, 0:nq], start=(cc == 0), stop=(cc == 1)),
                      reads=[wuvb, osb], writes=[psb])
            tb = tb_r.next()
            em.op("act", lambda e: e.activation(out=rden[:, 0:nq], in_=dsb[:, 0:nq], func=AF.Exp, scale=-1.0), reads=[dsb], writes=[rden])
            em.op("dve", lambda e: e.tensor_tensor(out=tb[:, 0:nq], in0=psb[:, 0:nq], in1=rden[:, 0:nq], op=ALU.mult),
                  reads=[psb, rden], writes=[tb])
            em.op("pool", lambda e: e.tensor_tensor(out=mixb[:, h, 0:nq], in0=tb[:, 0:nq], in1=gb[:, h, 0:nq], op=ALU.mult),
                  reads=[tb, gb], writes=[mixb])

        self._pre = {}
        for _ in pre(0):
            pass
        nqt = min(NT, AQT)
        for qt in range(nqt):
            q0, nq = trng(qt)
            st = self._pre.pop(qt)
            gen = pre(qt + 1) if qt + 1 < nqt else None
            nk1 = trng(qt + 1)[0] + trng(qt + 1)[1] if gen is not None else 0
            nsteps = 0 if gen is None else (3 + 16 + (0 if nk1 <= TOPK else (TOPK // 8 if nk1 - 64 < TOPK else NIT + 1)))
            per_head = (nsteps + 7) // 8
            mixb = mixb_r.next()
            for h in range(8):
                head(qt, h, st, mixb)
                if gen is not None:
                    for _ in range(per_head):
                        if next(gen, "done") == "done":
                            gen = None
                            break
            if gen is not None:
                for _ in gen:
                    pass
            em.dma("sp", self.MIXT[1024:2048, q0:q0 + nq].rearrange("(h p) t -> p h t", p=128), mixb[:, :, 0:nq], reads=[mixb], owner=mixb)
        em.end()

    def stage_rec(self, l):
        em = self.em
        ZT = self.ZT
        li = l // 2
        G = 4
        em.begin()
        stg = em.sb("stg", [128, 128], F32, dma=True)
        rng = em.sb("rng", [128, 16], F32)
        epsb = em.sb("epsb", [128, 1], F32)
        em.op("pool", lambda e: e.memset(epsb[:], RMS_EPS), writes=[epsb])
        self.rmask = em.sb("rmask", [128, T], F32)
        em.op("pool", lambda e: e.memset(self.rmask[:], 1.0), writes=[self.rmask])
        em.op("pool", lambda e: e.memset(self.rmask[:, 0:1], 0.0), writes=[self.rmask])
        em.op("pool", lambda e: e.memset(self.rmask[:, 16:T].rearrange("p (c j) -> p c j", j=64)[:, :, 0:1], 0.0),
              writes=[self.rmask])
        ptk = em.ps("ptk", [128, 8, 128], BF16)
        pn = em.ps("pn", [128, 512], F32)
        self.vecT(rng, 0, self.rec_g[li].rearrange("(c p) -> c p", p=128), 16, pn, stg)
        qs_r = em.ring("qs", 2, [128, T], F32, dma=True)
        sg_r = em.ring("sg", 2, [128, T], F32, dma=True)
        fb_r = em.ring("fb", 1, [128, T], F32)
        b_r = em.ring("b", 1, [128, T], F32)
        d_r = em.ring("d1", 1, [128, T], F32)
        kk_r = em.ring("kk", 1, [128, T], F32)
        mo_r = em.ring("mo", 1, [128, T], BF16, dma=True)
        qt_r = em.ring("qtl", G, [128, T], BF16)
        kt_r = em.ring("ktl", G, [128, T], BF16)
        ktok_r = em.ring("ktok", G, [128, NT, 128], BF16)
        vtok_r = em.ring("vtok", G, [128, NT, 128], BF16, dma=True)
        sc_r = em.ring("sc", G, [128, 4, 33], F32)
        bl_r = em.ring("bl", G, [128, 33], F32)
        oT_r = em.ring("oT", G, [128, T], F32)
        S_rs = [em.ring("S%d" % g, 2, [128, 128], F32) for g in range(G)]
        Sb_r = em.ring("Sb", 2 * G, [128, 128], BF16)
        St_r = em.ring("St", 2 * G, [128, 128], F32)
        am_r = em.ring("am", 2 * G, [128, 128], BF16)
        hbank = [em.ps_views("ph%d" % g, 4, [128], F32) for g in range(G)]
        lb = self.lbv

        def pre_head(h, g):
            qs = qs_r.next(); sg = sg_r.next()
            fb = fb_r.next(); b = b_r.next(); d1 = d_r.next(); kk = kk_r.next()
            qtl = qt_r.next(); ktl = kt_r.next(); ktok = ktok_r.next(); vtok = vtok_r.next()
            sc = sc_r.next(); bl = bl_r.next(); oT = oT_r.next()
            em.dma("sp", qs[:], ZT[h * 128:(h + 1) * 128, :], writes=[qs], owner=qs)
            em.dma("sp", sg[:], ZT[2048 + h * 128:2048 + (h + 1) * 128, :], writes=[sg], owner=sg)
            em.dma("sp", vtok[0:16, 0, :], self.VTOK[0:16, h * 128:(h + 1) * 128], writes=[vtok], owner=vtok)
            em.dma("sp", vtok[:, 1:NT, :], self.VTOK[16:T, h * 128:(h + 1) * 128].rearrange("(i p) v -> p i v", p=128),
                   writes=[vtok], owner=vtok)
            em.op("dve", lambda e: e.tensor_scalar(out=fb[:], in0=sg[:], scalar1=self.oml[:, li, h:h + 1], scalar2=lb[:, li, h:h + 1],
                                                   op0=ALU.mult, op1=ALU.add), reads=[sg, self.oml, lb], writes=[fb])
            em.op("act", lambda e: e.activation(out=fb[:], in_=fb[:], func=AF.Ln), reads=[fb], writes=[fb])
            em.op("dve", lambda e: e.tensor_scalar(out=kk[:], in0=sg[:], scalar1=self.noml[:, li, h:h + 1], scalar2=self.oml[:, li, h:h + 1],
                                                   op0=ALU.mult, op1=ALU.add), reads=[sg, self.noml, self.oml], writes=[kk])
            em.op("dve", lambda e: e.tensor_tensor_scan(b[:], self.rmask[:], fb[:], 0.0, ALU.mult, ALU.add),
                  reads=[self.rmask, fb], writes=[b])
            em.op("pool", lambda e: e.tensor_copy(sc[:, 0, 0:1], b[:, 8:9]), reads=[b], writes=[sc])
            em.op("pool", lambda e: e.tensor_copy(sc[:, 0, 1:33], b[:, 48:T:64]), reads=[b], writes=[sc])
            em.op("pool", lambda e: e.tensor_copy(bl[:, 0:1], b[:, 15:16]), reads=[b], writes=[bl])
            em.op("pool", lambda e: e.tensor_copy(bl[:, 1:33], b[:, 79:T:64]), reads=[b], writes=[bl])
            em.op("dve", lambda e: e.tensor_sub(out=d1[:, 0:16], in0=b[:, 0:16], in1=sc[:, 0, 0:1].to_broadcast([128, 16])),
                  reads=[b, sc], writes=[d1])
            em.op("dve", lambda e: e.tensor_sub(out=d1[:, 16:T].rearrange("p (c j) -> p c j", j=64),
                                                in0=b[:, 16:T].rearrange("p (c j) -> p c j", j=64),
                                                in1=sc[:, 0, 1:33].unsqueeze(2).to_broadcast([128, 32, 64])),
                  reads=[b, sc], writes=[d1])
            em.op("act", lambda e: e.activation(out=sc[:, 1, :], in_=sc[:, 0, :], func=AF.Exp), reads=[sc], writes=[sc])
            em.op("act", lambda e: e.activation(out=sc[:, 3, :], in_=bl[:], func=AF.Exp), reads=[bl, sc], writes=[sc])
            em.op("dve", lambda e: e.tensor_sub(out=bl[:], in0=bl[:], in1=sc[:, 0, :]), reads=[bl, sc], writes=[bl])
            em.op("act", lambda e: e.activation(out=sc[:, 2, :], in_=bl[:], func=AF.Exp), reads=[bl, sc], writes=[sc])
            em.op("act", lambda e: e.activation(out=fb[:], in_=d1[:], func=AF.Exp), reads=[d1, fb], writes=[fb])
            em.op("dve", lambda e: e.tensor_mul(out=qtl[:], in0=qs[:], in1=fb[:]), reads=[qs, fb], writes=[qtl])
            em.op("act", lambda e: e.activation(out=d1[:], in_=d1[:], func=AF.Exp, scale=-1.0), reads=[d1], writes=[d1])
            em.op("pool", lambda e: e.tensor_mul(out=ktl[:], in0=kk[:], in1=d1[:]), reads=[kk, d1], writes=[ktl])
            for i0 in range(0, NT, 8):
                ii = list(range(i0, min(NT, i0 + 8)))
                for i in ii:
                    r0, n = trng(i)
                    em.op("pe", lambda e, i=i, r0=r0, n=n: e.transpose(ptk[0:n, i - i0, :], ktl[:, r0:r0 + n], self.idb[:, :]),
                          reads=[ktl, self.idb], writes=[ptk])
                if i0 == 0:
                    em.op("act", lambda e: e.activation(out=ktok[0:16, 0, :], in_=ptk[0:16, 0, :], func=AF.Copy), reads=[ptk], writes=[ktok])
                    em.op("act", lambda e: e.activation(out=ktok[:, 1:8, :], in_=ptk[:, 1:8, :], func=AF.Copy), reads=[ptk], writes=[ktok])
                else:
                    em.op("act", lambda e, i0=i0, m=len(ii): e.activation(out=ktok[:, i0:i0 + m, :], in_=ptk[:, 0:m, :], func=AF.Copy),
                          reads=[ptk], writes=[ktok])
            Sb = Sb_r.next()
            St = St_r.next()
            em.op("pool", lambda e: e.memset(Sb[:], 0.0), writes=[Sb])
            em.op("pool", lambda e: e.memset(St[:], 0.0), writes=[St])
            return dict(h=h, g=g, qtl=qtl, ktl=ktl, ktok=ktok, vtok=vtok, sc=sc, oT=oT, Sb=Sb, St=St, pss=None)

        def tile_part(c, i):
            r0, n = trng(i)
            qtl, ktl, vtok, oT = c["qtl"], c["ktl"], c["vtok"], c["oT"]
            psa = hbank[c["g"]][0]
            am = am_r.next()
            pso = hbank[c["g"]][0]
            em.op("pe", lambda e: e.matmul(psa[0:n, 0:n], ktl[:, r0:r0 + n], qtl[:, r0:r0 + n], start=True, stop=True),
                  reads=[ktl, qtl], writes=[psa])
            em.op("dve", lambda e: e.tensor_tensor(out=am[0:n, 0:n], in0=psa[0:n, 0:n], in1=self.cmask[0:n, 0:n], op=ALU.mult),
                  reads=[psa, self.cmask], writes=[am])
            em.op("pe", lambda e: e.matmul(pso[:, 0:n], vtok[0:n, i, :], am[0:n, 0:n], start=True, stop=True),
                  reads=[vtok, am], writes=[pso])
            em.op("act", lambda e: e.activation(out=oT[:, r0:r0 + n], in_=pso[:, 0:n], func=AF.Copy), reads=[pso], writes=[oT])

        def chunk_list():
            out = []
            for i in range(NT):
                r0, n = trng(i)
                for ci, (p0, ncx) in enumerate([(0, 16)] if i == 0 else [(0, 64), (64, 64)]):
                    j = 0 if i == 0 else 1 + 2 * (i - 1) + ci
                    out.append((i, j, p0, ncx, r0 + p0))
            return out

        CH = chunk_list()

        def emit_pss(c, idx):
            i, j, p0, ncx, c0 = CH[idx]
            pss = hbank[c["g"]][1 + (idx % 2)]
            em.op("pe", lambda e: e.matmul(pss[:], c["ktok"][p0:p0 + ncx, i, :], c["vtok"][p0:p0 + ncx, i, :], start=True, stop=True),
                  reads=[c["ktok"], c["vtok"]], writes=[pss])

        def chunk_pe(c, idx):
            i, j, p0, ncx, c0 = CH[idx]
            psi = hbank[c["g"]][3]
            Sb = c["Sb"]
            em.op("pe", lambda e: e.matmul(psi[:, 0:ncx], Sb[:], c["qtl"][:, c0:c0 + ncx], start=True, stop=True),
                  reads=[Sb, c["qtl"]], writes=[psi])
            if idx + 1 < len(CH):
                emit_pss(c, idx + 1)

        def chunk_dve(c, idx):
            i, j, p0, ncx, c0 = CH[idx]
            sc, oT = c["sc"], c["oT"]
            pss = hbank[c["g"]][1 + (idx % 2)]
            psi = hbank[c["g"]][3]
            St = c["St"]
            if idx + 1 < len(CH):
                jn = CH[idx + 1][1]
                S2 = S_rs[c["g"]].next()
                em.op("dve", lambda e: e.scalar_tensor_tensor(out=S2[:], in0=pss[:], scalar=sc[:, 2, j:j + 1], in1=St[:],
                                                              op0=ALU.mult, op1=ALU.add), reads=[pss, sc, St], writes=[S2])
                Sb2 = Sb_r.next()
                St2 = St_r.next()
                em.op("dve", lambda e: e.tensor_scalar(out=Sb2[:], in0=S2[:], scalar1=sc[:, 1, jn:jn + 1], scalar2=None, op0=ALU.mult),
                      reads=[S2, sc], writes=[Sb2])
                em.op("dve", lambda e: e.tensor_scalar(out=St2[:], in0=S2[:], scalar1=sc[:, 3, jn:jn + 1], scalar2=None, op0=ALU.mult),
                      reads=[S2, sc], writes=[St2])
                c["Sb"], c["St"] = Sb2, St2
            em.op("dve", lambda e: e.tensor_add(out=oT[:, c0:c0 + ncx], in0=oT[:, c0:c0 + ncx], in1=psi[:, 0:ncx]),
                  reads=[oT, psi], writes=[oT])

        def post_head(c):
            h, oT = c["h"], c["oT"]
            gs = qs_r.next(); d1 = d_r.next(); kk = kk_r.next()
            em.dma("sp", gs[:], ZT[4096 + h * 128:4096 + (h + 1) * 128, :], writes=[gs], owner=gs)
            em.op("act", lambda e: e.activation(out=d1[:], in_=oT[:], func=AF.Square), reads=[oT], writes=[d1])
            avg = self.avg[128]
            for (c0, n) in CCH:
                em.op("pe", lambda e: e.matmul(pn[:, 0:n], avg[:], d1[:, c0:c0 + n], start=True, stop=True), reads=[avg, d1], writes=[pn])
                em.op("act", lambda e: e.activation(out=kk[:, c0:c0 + n], in_=pn[:, 0:n], func=AF.Ln, bias=epsb[:, 0:1]),
                      reads=[pn, epsb], writes=[kk])
            em.op("act", lambda e: e.activation(out=kk[:], in_=kk[:], func=AF.Exp, scale=-0.5), reads=[kk], writes=[kk])
            em.op("dve", lambda e: e.tensor_mul(out=kk[:], in0=kk[:], in1=oT[:]), reads=[kk, oT], writes=[kk])
            mo = mo_r.next()
            em.op("dve", lambda e: e.scalar_tensor_tensor(out=mo[:], in0=kk[:], scalar=rng[:, h:h + 1], in1=gs[:], op0=ALU.mult, op1=ALU.mult),
                  reads=[kk, rng, gs], writes=[mo])
            em.dma("sp", self.MIXT[h * 128:(h + 1) * 128, :], mo[:], reads=[mo], owner=mo)

        for g0 in range(0, 16, G):
            ctx = [pre_head(g0 + g, g) for g in range(G)]
            for c in ctx:
                emit_pss(c, 0)
            idx = 0
            for i in range(NT):
                for c in ctx:
                    tile_part(c, i)
                for _ in ([0] if i == 0 else [0, 1]):
                    for c in ctx:
                        chunk_pe(c, idx)
                    for c in ctx:
                        chunk_dve(c, idx)
                    idx += 1
            for c in ctx:
                post_head(c)
        em.end()

    def stage_out(self, s, l, W, last):
        em = self.em
        em.begin()
        Wb = em.sb("Wb", [128, 16, D], BF16)
        if last:
            self.fgain = em.sb("fgain", [128, D], F32, dma=True)
            em.dma("sp", self.fgain[:], self.final_gain.partition_broadcast(128), writes=[self.fgain], owner=self.fgain)
        w32r = em.ring("wo32", 2, [128, D], F32, dma=True)
        for k in range(16):
            w32 = w32r.next()
            em.dma("sp", w32[:], W[k * 128:(k + 1) * 128, :], writes=[w32], owner=w32)
            em.op("pool", lambda e, k=k: e.tensor_copy(Wb[:, k, :], w32[:]), reads=[w32], writes=[Wb])
        mtr = em.ring("mt", 2, [128, 16, 128], BF16, dma=True)
        htr = em.ring("ht", 2, [128, D], F32, dma=True)
        hnr = em.ring("hn", 2, [128, D], F32, dma=True)
        pmm = em.psring("pmm", 4, [128, 512], F32)
        junk = em.sb("junk", [128, D], BF16)
        ssr = em.ring("ss", 2, [128, 2], F32)
        tmr = em.ring("tm", 2, [128, 2], F32)
        for i in range(NT):
            r0, n = trng(i)
            if last and i == 0:
                continue
            mt = mtr.next()
            ht = htr.next()
            hn = hnr.next()
            em.dma("sp", mt[:, :, 0:n], self.MIXT[:, r0:r0 + n].rearrange("(k p) t -> p k t", p=128), writes=[mt], owner=mt)
            em.dma("sp", ht[0:n, :], self.h_src(s, l, i), writes=[ht], owner=ht)
            for c in range(4):
                ps = pmm.next()
                for k in range(16):
                    em.op("pe", lambda e, k=k, c=c: e.matmul(ps[0:n, :], mt[:, k, 0:n], Wb[:, k, c * 512:(c + 1) * 512],
                                                             start=(k == 0), stop=(k == 15)), reads=[mt, Wb], writes=[ps])
                em.op("dve", lambda e, c=c: e.tensor_add(out=hn[0:n, c * 512:(c + 1) * 512], in0=ht[0:n, c * 512:(c + 1) * 512], in1=ps[0:n, :]),
                      reads=[ht, ps], writes=[hn])
            if not last:
                em.dma("sp", self.H[r0:r0 + n, :], hn[0:n, :], reads=[hn], owner=hn)
            else:
                ss = ssr.next()
                tm = tmr.next()
                em.op("pool", lambda e: e.memset(ss[:], 0.0), writes=[ss])
                em.op("act", lambda e: e.activation(out=junk[0:n, :], in_=hn[0:n, :], func=AF.Square, accum_out=ss[0:n, 0:1]),
                      reads=[hn], writes=[junk, ss])
                self.rstd_rows(ss, n, D, RMS_EPS, tm)
                em.op("dve", lambda e: e.scalar_tensor_tensor(out=ht[0:n, :], in0=hn[0:n, :], scalar=ss[0:n, 1:2], in1=self.fgain[0:n, :],
                                                              op0=ALU.mult, op1=ALU.mult), reads=[hn, ss, self.fgain], writes=[ht])
                em.dma("sp", self.out[s, r0 - 16:r0 - 16 + n, :], ht[0:n, :], reads=[ht], owner=ht)
        em.end()


_CACHE = {}


def kernel(**inputs):
    x = np.ascontiguousarray(inputs["x"], dtype=np.float32)
    if "nc" not in _CACHE:
        _CACHE["nc"] = Prog().build()
    nc = _CACHE["nc"]
    oh = _bucket_onehot()
    names = ["meta_tokens", "norm_gain", "final_norm_gain", "rel_bias_table", "w_in_even", "conv_w", "conv_b",
             "conv_ln_gain", "conv_ln_bias", "kv_norm_gain", "w_uk", "w_uv", "w_out_even", "w_in_odd", "lb_logits",
             "rec_norm_gain", "w_out_odd"]
    shared = {k: np.ascontiguousarray(inputs[k], dtype=np.float32) for k in names}
    shared["c_oh"] = oh
    in_maps = []
    for c in range(NCORES):
        m = dict(shared)
        m["x"] = x[c * SEQ_PER_CORE:(c + 1) * SEQ_PER_CORE]
        in_maps.append(m)
    res = run_bass_kernel_spmd(nc, in_maps, core_ids=list(range(NCORES)))
    return np.concatenate([r["out"] for r in res.results], axis=0)
```

```python
import math
from contextlib import ExitStack

import numpy as np
import concourse.bass as bass
import concourse.mybir as mybir
from concourse.bass_utils import run_bass_kernel_spmd

F32 = mybir.dt.float32
BF16 = mybir.dt.bfloat16
AF = mybir.ActivationFunctionType
ALU = mybir.AluOpType
AX = mybir.AxisListType

NCORES = 8
SEQ_PER_CORE = 2
D = 2048
SEQ = 2048
NMETA = 16
T = SEQ + NMETA
NT = 17
DEPTH = 4
P_EVEN = 6480
RMS_EPS = 1e-6
LN_EPS = 1e-5
NEG = -1.0e30
NEG2 = -3.0e38
TOPK = 256


def trng(i):
    return (0, 16) if i == 0 else (16 + 128 * (i - 1), 128)


CCH = [(0, 16)] + [(16 + 512 * j, 512) for j in range(4)]


class Tk:
    __slots__ = ("name", "t", "w", "r", "dkey", "bank")

    def __init__(self, name, t=None):
        self.name = name
        self.t = t
        self.w = {}
        self.r = {}
        self.dkey = None
        self.bank = None

    def __getitem__(self, idx):
        return self.t[idx]


class Ring:
    def __init__(self, bufs):
        self.bufs = bufs
        self.i = 0

    def next(self):
        b = self.bufs[self.i % len(self.bufs)]
        self.i += 1
        return b


import os as _os
NOSELF = _os.environ.get("NOSELF", "0") == "1"


class Em:
    ENG = ("pe", "act", "dve", "pool", "sp")

    def __init__(self, nc, n_dsem=78):
        self.nc = nc
        self.top = ExitStack()
        self.eng = dict(pe=nc.tensor, act=nc.scalar, dve=nc.vector, pool=nc.gpsimd, sp=nc.sync)
        self.sems = {}
        self.val = {}
        for e in self.ENG:
            self.sems[e] = self.top.enter_context(nc.semaphore("es_" + e))
            self.val[e] = 0
        self.bar = self.top.enter_context(nc.semaphore("bar"))
        self.nbar = 0
        self.free_d = []
        for i in range(n_dsem):
            k = "D%d" % i
            self.sems[k] = self.top.enter_context(nc.semaphore("ds_%d" % i))
            self.val[k] = 0
            self.free_d.append(k)
        self.free_sw = []
        for i in range(10):
            k = "DS%d" % i
            self.sems[k] = self.top.enter_context(nc.semaphore("dsw_%d" % i))
            self.val[k] = 0
            self.free_sw.append(k)
        self.stage_sw = []
        self.seen = {e: {} for e in self.ENG}
        self.stage = None
        self.stage_d = []
        self.uid = 0
        self.nins = 0
        self.reg = {}

    def begin(self):
        self.stage = ExitStack()
        self.stage_d = []

    def end(self):
        self.barrier()
        self.stage.close()
        self.stage = None
        self.free_d.extend(self.stage_d)
        self.stage_d = []
        self.free_sw.extend(self.stage_sw)
        self.stage_sw = []

    def _nm(self, name):
        self.uid += 1
        return "%s_%d" % (name, self.uid)

    def sb(self, name, shape, dt=F32, dma=False, top=False):
        st = self.top if top else self.stage
        nm = self._nm(name)
        t = st.enter_context(self.nc.sbuf_tensor(nm, list(shape), dt))
        self.reg[name] = nm
        tk = Tk(name, t)
        if dma == "sw":
            k = self.free_sw.pop()
            tk.dkey = k
            if not top:
                self.stage_sw.append(k)
        elif dma:
            k = self.free_d.pop()
            tk.dkey = k
            if not top:
                self.stage_d.append(k)
        return tk

    def ring(self, name, n, shape, dt=F32, dma=False):
        return Ring([self.sb("%s%d" % (name, i), shape, dt, dma=dma) for i in range(n)])

    def ps(self, name, shape, dt=F32):
        t = self.stage.enter_context(self.nc.psum_tensor(self._nm(name), list(shape), dt))
        return Tk(name, t)

    def ps_views(self, name, n, sub_shape, dt=F32):
        t = self.stage.enter_context(self.nc.psum_tensor(self._nm(name), [128, n] + list(sub_shape), dt))
        bank = Tk(name + "_bank")
        views = [Tk("%s%d" % (name, i), t[:, i]) for i in range(n)]
        for v in views:
            v.bank = bank
        return views

    def psring(self, name, n, shape, dt=F32):
        return Ring([self.ps("%s%d" % (name, i), shape, dt) for i in range(n)])

    def _wait(self, eng, toks):
        need = {}
        for d in toks:
            for k, v in d.items():
                if v > need.get(k, 0):
                    need[k] = v
        seen = self.seen[eng]
        for k, v in need.items():
            if k == eng and (eng == "pe" or NOSELF):
                continue
            if seen.get(k, 0) >= v:
                continue
            self.eng[eng].wait_ge(self.sems[k], v)
            seen[k] = v

    @staticmethod
    def _deps(reads, writes):
        toks = []
        for t in reads:
            toks.append(t.w)
            if t.bank is not None:
                toks.append(t.bank.w)
        for t in writes:
            toks.append(t.w)
            toks.append(t.r)
            if t.bank is not None:
                toks.append(t.bank.r)
        return toks

    @staticmethod
    def _mark(k, v, reads, writes):
        for t in reads:
            if t.r.get(k, 0) < v:
                t.r[k] = v
            if t.bank is not None and t.bank.r.get(k, 0) < v:
                t.bank.r[k] = v
        for t in writes:
            t.w = {k: v}
            t.r = {}
            if t.bank is not None:
                t.bank.w = {k: v}

    def op(self, eng, fn, reads=(), writes=()):
        self._wait(eng, self._deps(reads, writes))
        ins = fn(self.eng[eng])
        self.val[eng] += 1
        ins.then_inc(self.sems[eng], 1)
        self._mark(eng, self.val[eng], reads, writes)
        self.nins += 1
        return ins

    def dma(self, q, out, in_, reads=(), writes=(), owner=None, **kw):
        self._wait(q, self._deps(reads, writes))
        ins = self.eng[q].dma_start(out=out, in_=in_, **kw)
        k = owner.dkey
        self.val[k] += 16
        ins.then_inc(self.sems[k], 16)
        self._mark(k, self.val[k], reads, writes)
        self.nins += 1
        return ins

    def barrier(self):
        self.nbar += 1
        for e in self.ENG:
            g = self.eng[e]
            if self.val[e] > 0:
                g.wait_ge(self.sems[e], self.val[e])
            if e == "sp":
                for k, v in self.val.items():
                    if k[0] == "D" and v > 0 and self.seen["sp"].get(k, 0) < v:
                        g.wait_ge(self.sems[k], v)
            g.sem_inc(self.bar, 1)
        for e in self.ENG:
            self.eng[e].wait_ge(self.bar, 5 * self.nbar)
        for e in self.ENG:
            for k, v in self.val.items():
                self.seen[e][k] = v

    def close(self):
        self.top.close()


def _t5_bucket_np(rel):
    rel = np.asarray(rel, dtype=np.int32)
    nb = 16
    ret = np.where(rel > 0, nb, 0).astype(np.int32)
    n = np.abs(rel)
    max_exact = nb // 2
    nf = np.maximum(n, 1).astype(np.float32)
    large = max_exact + (np.log(nf / np.float32(max_exact)) / np.float32(math.log(128 / max_exact))
                         * np.float32(nb - max_exact)).astype(np.int32)
    large = np.minimum(large, nb - 1)
    return ret + np.where(n < max_exact, n, large)


def _bucket_onehot():
    rel = 255 - np.arange(512)
    b = _t5_bucket_np(rel)
    oh = np.zeros((32, 512), np.float32)
    oh[b, np.arange(512)] = 1.0
    return oh


class Prog:
    def __init__(self, nseq=SEQ_PER_CORE, layers=DEPTH, dbg=False, stop=10 ** 9):
        self.stop = stop
        self.nstage = 0
        self.nseq = nseq
        self.layers = layers
        self.dbg = dbg if dbg else ()
        ne = max(1, (layers + 1) // 2)
        no = layers // 2
        od = (lambda *sh: [max(no, 1)] + ([1] * len(sh) if no == 0 else list(sh)))
        nc = bass.Bass("TRN2", target_bir_lowering=False)
        self.nc = nc
        dt = nc.dram_tensor
        I = "ExternalInput"
        self.x = dt("x", [nseq, SEQ, D], F32, kind=I).ap()
        self.meta = dt("meta_tokens", [NMETA, D], F32, kind=I).ap()
        self.norm_gain = dt("norm_gain", [4, D], F32, kind=I).ap()
        self.final_gain = dt("final_norm_gain", [D], F32, kind=I).ap()
        self.relb = dt("rel_bias_table", [32, 8], F32, kind=I).ap()
        self.w_in_even = dt("w_in_even", [ne, D, P_EVEN], F32, kind=I).ap()
        self.conv_w = dt("conv_w", [2, 31, 1024], F32, kind=I).ap()
        self.conv_b = dt("conv_b", [2, 1024], F32, kind=I).ap()
        self.ln_g = dt("conv_ln_gain", [2, 1024], F32, kind=I).ap()
        self.ln_b = dt("conv_ln_bias", [2, 1024], F32, kind=I).ap()
        self.kv_g = dt("kv_norm_gain", [2, 256], F32, kind=I).ap()
        self.w_uk = dt("w_uk", [ne, 8, 256, 128], F32, kind=I).ap()
        self.w_uv = dt("w_uv", [ne, 8, 256, 128], F32, kind=I).ap()
        self.w_out_even = dt("w_out_even", [ne, D, D], F32, kind=I).ap()
        self.w_in_odd = dt("w_in_odd", od(D, 4 * D), F32, kind=I).ap()
        self.lb_logits = dt("lb_logits", [4, D], F32, kind=I).ap()
        self.rec_g = dt("rec_norm_gain", [2, D], F32, kind=I).ap()
        self.w_out_odd = dt("w_out_odd", od(D, D), F32, kind=I).ap()
        self.c_oh = dt("c_oh", [32, 512], F32, kind=I).ap()
        self.out = dt("out", [nseq, SEQ, D], F32, kind="ExternalOutput").ap()
        sk = lambda n: "ExternalOutput" if n in self.dbg else "Internal"
        self.H = dt("Hs", [T, D], F32, kind=sk("Hs")).ap()
        self.ZT = dt("ZTs", [4 * D, T], F32, kind=sk("ZTs")).ap()
        self.VTOK = dt("VTOKs", [T, D], BF16, kind=sk("VTOKs")).ap()
        self.ZB = dt("ZBs", [2176, T], BF16, kind=sk("ZBs")).ap()
        self.MIXT = dt("MIXTs", [D, T], BF16, kind=sk("MIXTs")).ap()
        self.FD = dt("FDs", [8, 512], F32, kind=sk("FDs")).ap()
        self.em = Em(nc)

    def build(self):
        em = self.em

        def run(f, *a, **k):
            if self.nstage < self.stop:
                f(*a, **k)
            self.nstage += 1

        run(self.setup_consts)
        for s in range(self.nseq):
            for l in range(self.layers):
                last = (l == self.layers - 1)
                if l % 2 == 0:
                    run(self.stage_norm_proj, s, l, even=True)
                    run(self.stage_conv, l // 2)
                    run(self.stage_attn, l // 2)
                    run(self.stage_out, s, l, self.w_out_even[l // 2], last)
                else:
                    run(self.stage_norm_proj, s, l, even=False)
                    run(self.stage_rec, l)
                    run(self.stage_out, s, l, self.w_out_odd[l // 2], last)
        em.close()
        return self.nc

    def vecT(self, dst, dst_cols, src_rows_ap, nrows, ps, stg):
        em = self.em
        em.dma("sp", stg[0:nrows, :], src_rows_ap, writes=[stg], owner=stg)
        em.op("pe", lambda e: e.transpose(ps[:, 0:nrows], stg[0:nrows, :], self.idf[0:nrows, 0:nrows]),
              reads=[stg, self.idf], writes=[ps])
        em.op("act", lambda e: e.activation(out=dst[:, dst_cols:dst_cols + nrows], in_=ps[:, 0:nrows], func=AF.Copy),
              reads=[ps], writes=[dst])

    def setup_consts(self):
        em = self.em
        nc = self.nc
        self.idf = em.sb("idf", [128, 128], F32, top=True)
        self.idb = em.sb("idb", [128, 128], BF16, top=True)
        self.onesb = em.sb("onesb", [128, 128], BF16, top=True)
        self.avg = {}
        for n in (128, 256, 1024):
            self.avg[n] = em.sb("avg%d" % n, [128, 128], F32, top=True)
        self.cmask = em.sb("cmask", [128, 128], F32, top=True)
        self.EB = em.sb("EB", [128, 3, 8, 128], F32, top=True)
        self.bfar = em.sb("bfar", [128, 8], F32, top=True, dma=True)
        self.lbv = em.sb("lbv", [128, 2, 16], F32, top=True)
        self.oml = em.sb("oml", [128, 2, 16], F32, top=True)
        self.noml = em.sb("noml", [128, 2, 16], F32, top=True)

        em.begin()
        P = lambda f, **k: em.op("pool", f, **k)
        P(lambda e: e.memset(self.idf[:], 1.0), writes=[self.idf])
        P(lambda e: e.affine_select(out=self.idf[:], in_=self.idf[:], pattern=[[-1, 128]], compare_op=ALU.is_equal,
                                    fill=0.0, base=0, channel_multiplier=1), reads=[self.idf], writes=[self.idf])
        P(lambda e: e.tensor_copy(self.idb[:], self.idf[:]), reads=[self.idf], writes=[self.idb])
        P(lambda e: e.memset(self.onesb[:], 1.0), writes=[self.onesb])
        for n in (128, 256, 1024):
            P(lambda e, n=n: e.memset(self.avg[n][:], 1.0 / n), writes=[self.avg[n]])
        P(lambda e: e.memset(self.cmask[:], 1.0), writes=[self.cmask])
        P(lambda e: e.affine_select(out=self.cmask[:], in_=self.cmask[:], pattern=[[1, 128]], compare_op=ALU.is_ge,
                                    fill=0.0, base=0, channel_multiplier=-1), reads=[self.cmask], writes=[self.cmask])
        P(lambda e: e.memset(self.cmask[0:64, 64:128], 0.0), writes=[self.cmask])

        anti = em.sb("anti", [128, 128], F32)
        P(lambda e: e.memset(anti[:], 1.0), writes=[anti])
        P(lambda e: e.affine_select(out=anti[:], in_=anti[:], pattern=[[1, 128]], compare_op=ALU.is_equal,
                                    fill=0.0, base=-127, channel_multiplier=1), reads=[anti], writes=[anti])
        tab = em.sb("tab", [32, 8], F32, dma=True)
        oh = em.sb("oh", [32, 512], F32, dma=True)
        fsb = em.sb("fsb", [8, 512], F32, dma=True)
        nbfar = em.sb("nbfar", [128, 8], F32)
        psA = em.ps("psA", [128, 512], F32)
        psB = em.ps("psB", [128, 128], F32)
        em.dma("sp", tab[:], self.relb[:, :], writes=[tab], owner=tab)
        em.dma("sp", oh[:], self.c_oh[:, :], writes=[oh], owner=oh)
        em.dma("sp", self.bfar[:], bass.AP(tensor=self.relb.tensor, offset=15 * 8, ap=[[0, 128], [1, 8]]),
               writes=[self.bfar], owner=self.bfar)
        em.op("dve", lambda e: e.tensor_scalar(out=nbfar[:], in0=self.bfar[:], scalar1=-1.0, scalar2=None, op0=ALU.mult),
              reads=[self.bfar], writes=[nbfar])
        em.op("pe", lambda e: e.matmul(psA[0:8, :], tab[0:32, 0:8], oh[0:32, :], start=True, stop=True),
              reads=[tab, oh], writes=[psA])
        em.op("act", lambda e: e.activation(out=fsb[:], in_=psA[0:8, :], func=AF.Copy), reads=[psA], writes=[fsb])
        fd_tk = Tk("FD")
        em.dma("sp", self.FD[:, :], fsb[:], reads=[fsb], writes=[fd_tk], owner=fsb)
        hk = em.ring("hk", 2, [128, 128], F32, dma=True)
        for ty, c0 in enumerate((128, 256, 144)):
            for h in range(8):
                hb = hk.next()
                src = bass.AP(tensor=self.FD.tensor, offset=h * 512 + c0, ap=[[1, 128], [1, 128]])
                em.dma("sp", hb[:], src, reads=[fd_tk], writes=[hb], owner=hb)
                em.op("pe", lambda e, hb=hb: e.matmul(psB[:], anti[:], hb[:], start=True, stop=True),
                      reads=[anti, hb], writes=[psB])
                em.op("act", lambda e, ty=ty, h=h: e.activation(out=self.EB[:, ty, h, :], in_=psB[:], func=AF.Exp,
                                                                bias=nbfar[:, h:h + 1]),
                      reads=[psB, nbfar], writes=[self.EB])

        stg = em.sb("stg", [128, 128], F32, dma=True)
        lbT = em.sb("lbT", [128, 64], F32)
        self.vecT(lbT, 0, self.lb_logits.rearrange("l (c p) -> (l c) p", p=128), 64, psB, stg)
        mx = em.sb("mx", [128, 16], F32)
        ex = em.sb("ex", [128, 4, 16], F32)
        sm = em.sb("sm", [128, 16], F32)
        rs = em.sb("rs", [128, 16], F32)
        c1 = em.sb("c1", [128, 16], F32)
        V = lambda f, **k: em.op("dve", f, **k)
        V(lambda e: e.tensor_max(out=mx[:], in0=lbT[:, 0:16], in1=lbT[:, 16:32]), reads=[lbT], writes=[mx])
        V(lambda e: e.tensor_max(out=mx[:], in0=mx[:], in1=lbT[:, 32:48]), reads=[lbT, mx], writes=[mx])
        V(lambda e: e.tensor_max(out=mx[:], in0=mx[:], in1=lbT[:, 48:64]), reads=[lbT, mx], writes=[mx])
        for l in range(4):
            V(lambda e, l=l: e.tensor_sub(out=ex[:, l, :], in0=lbT[:, 16 * l:16 * l + 16], in1=mx[:]),
              reads=[lbT, mx], writes=[ex])
        em.op("act", lambda e: e.activation(out=ex[:], in_=ex[:], func=AF.Exp), reads=[ex], writes=[ex])
        V(lambda e: e.tensor_add(out=sm[:], in0=ex[:, 0, :], in1=ex[:, 1, :]), reads=[ex], writes=[sm])
        V(lambda e: e.tensor_add(out=sm[:], in0=sm[:], in1=ex[:, 2, :]), reads=[ex, sm], writes=[sm])
        V(lambda e: e.tensor_add(out=sm[:], in0=sm[:], in1=ex[:, 3, :]), reads=[ex, sm], writes=[sm])
        V(lambda e: e.reciprocal(out=rs[:], in_=sm[:]), reads=[sm], writes=[rs])
        V(lambda e: e.tensor_mul(out=self.lbv[:, 0, :], in0=ex[:, 1, :], in1=rs[:]), reads=[ex, rs], writes=[self.lbv])
        V(lambda e: e.tensor_add(out=c1[:], in0=ex[:, 1, :], in1=ex[:, 2, :]), reads=[ex], writes=[c1])
        V(lambda e: e.tensor_add(out=c1[:], in0=c1[:], in1=ex[:, 3, :]), reads=[ex, c1], writes=[c1])
        V(lambda e: e.tensor_mul(out=self.lbv[:, 1, :], in0=c1[:], in1=rs[:]), reads=[c1, rs], writes=[self.lbv])
        V(lambda e: e.tensor_scalar(out=self.oml[:], in0=self.lbv[:], scalar1=-1.0, scalar2=1.0, op0=ALU.mult, op1=ALU.add),
          reads=[self.lbv], writes=[self.oml])
        V(lambda e: e.tensor_scalar(out=self.noml[:], in0=self.oml[:], scalar1=-1.0, scalar2=None, op0=ALU.mult),
          reads=[self.oml], writes=[self.noml])
        em.end()

    def h_src(self, s, l, i):
        r0, n = trng(i)
        if l == 0:
            if i == 0:
                return self.meta[:, :]
            return self.x[s, r0 - 16:r0 - 16 + n, :]
        return self.H[r0:r0 + n, :]

    def rstd_rows(self, ss, n, dim, eps, tmp):
        em = self.em
        em.op("dve", lambda e: e.tensor_scalar(out=tmp[0:n, 0:1], in0=ss[0:n, 0:1], scalar1=1.0 / dim, scalar2=eps,
                                               op0=ALU.mult, op1=ALU.add), reads=[ss], writes=[tmp])
        em.op("act", lambda e: e.activation(out=tmp[0:n, 1:2], in_=tmp[0:n, 0:1], func=AF.Sqrt), reads=[tmp], writes=[tmp])
        em.op("dve", lambda e: e.reciprocal(out=ss[0:n, 1:2], in_=tmp[0:n, 1:2]), reads=[tmp], writes=[ss])

    def stage_norm_proj(self, s, l, even):
        em = self.em
        em.begin()
        hnT = em.sb("hnT", [128, 16, T], BF16)
        outer = em.stage
        nrm = ExitStack()
        em.stage = nrm
        gbc = em.sb("gbc", [128, D], F32, dma=True)
        em.dma("sp", gbc[:], self.norm_gain[l, :].partition_broadcast(128), writes=[gbc], owner=gbc)
        htr = em.ring("ht", 4, [128, D], F32, dma=True)
        hsr = em.ring("hs", 3, [128, D], BF16)
        junk = em.sb("junk", [128, D], BF16)
        ssr = em.ring("ss", 4, [128, 2], F32)
        tmr = em.ring("tm", 4, [128, 2], F32)
        ptr = em.psring("ptr", 2, [128, 16, 128], BF16)
        for i in range(NT):
            r0, n = trng(i)
            ht = htr.next()
            hs = hsr.next()
            ss = ssr.next()
            tm = tmr.next()
            pt = ptr.next()
            em.dma("sp", ht[0:n, :], self.h_src(s, l, i), writes=[ht], owner=ht)
            em.op("pool", lambda e: e.memset(ss[:], 0.0), writes=[ss])
            em.op("act", lambda e: e.activation(out=junk[0:n, :], in_=ht[0:n, :], func=AF.Square, accum_out=ss[0:n, 0:1]),
                  reads=[ht], writes=[junk, ss])
            self.rstd_rows(ss, n, D, RMS_EPS, tm)
            em.op("dve", lambda e: e.scalar_tensor_tensor(out=hs[0:n, :], in0=ht[0:n, :], scalar=ss[0:n, 1:2],
                                                          in1=gbc[0:n, :], op0=ALU.mult, op1=ALU.mult),
                  reads=[ht, ss, gbc], writes=[hs])
            for k in range(16):
                em.op("pe", lambda e, k=k: e.transpose(pt[:, k, 0:n], hs[0:n, k * 128:(k + 1) * 128], self.idb[0:n, 0:n]),
                      reads=[hs, self.idb], writes=[pt])
            em.op("act", lambda e: e.activation(out=hnT[:, :, r0:r0 + n], in_=pt[:, :, 0:n], func=AF.Copy),
                  reads=[pt], writes=[hnT])
        self.hnT = hnT
        em.barrier()
        nrm.close()
        em.stage = outer
        if even:
            self.proj_even(l // 2)
        else:
            self.proj_odd(l // 2)
        em.end()

    def proj_fm(self, W, chunks, wtr32, wtr, pmm, otr, otbr=None):
        em = self.em
        hnT = self.hnT
        blocks = []
        for ch in chunks:
            col0, ncols, func, scale, dst0, dup = ch
            if (blocks and not dup and ncols == 128 and len(blocks[-1]) < 4 and not blocks[-1][-1][5]
                    and blocks[-1][-1][1] == 128 and blocks[-1][-1][0] + 128 == col0):
                blocks[-1].append(ch)
            else:
                blocks.append([ch])
        for blk in blocks:
            w32 = wtr32.next()
            wb = wtr.next()
            bcol0 = blk[0][0]
            bn = sum(c[1] for c in blk)
            em.dma("sp", w32[:, :, 0:bn], W[:, bcol0:bcol0 + bn].rearrange("(kc p) m -> p kc m", p=128),
                   writes=[w32], owner=w32)
            if blk[0][5]:
                em.dma("sp", w32[:, :, bn:2 * bn], W[:, bcol0:bcol0 + bn].rearrange("(kc p) m -> p kc m", p=128),
                       writes=[w32], owner=w32)
            for bi, (col0, ncols, func, scale, dst0, dup) in enumerate(blk):
                o = col0 - bcol0
                nm = ncols * (2 if dup else 1)
                em.op("pool", lambda e, o=o, nm=nm: e.tensor_copy(wb[:, :, o:o + nm], w32[:, :, o:o + nm]), reads=[w32], writes=[wb])
            for bi, (col0, ncols, func, scale, dst0, dup) in enumerate(blk):
                o = col0 - bcol0
                nm = ncols * (2 if dup else 1)
                tobf = dst0 < 0
                ot = otbr.next() if tobf else otr.next()
                for (c0, n) in CCH:
                    ps = pmm.next()
                    for k in range(16):
                        em.op("pe", lambda e, k=k: e.matmul(ps[0:nm, 0:n], wb[:, k, o:o + nm], hnT[:, k, c0:c0 + n],
                                                            start=(k == 0), stop=(k == 15)),
                              reads=[wb, hnT], writes=[ps])
                    em.op("act", lambda e: e.activation(out=ot[0:nm, c0:c0 + n], in_=ps[0:nm, 0:n], func=func, scale=scale),
                          reads=[ps], writes=[ot])
                if tobf:
                    r0 = -dst0 - 1
                    em.dma("act", self.ZB[r0:r0 + nm, :], ot[0:nm, :], reads=[ot], owner=ot)
                else:
                    em.dma("act", self.ZT[dst0:dst0 + nm, :], ot[0:nm, :], reads=[ot], owner=ot)

    ZE = dict(glu_v=0, glu_g=1024, gate_a=2048, q=3072, c=4096, gate_b=4352, qi=5376, ki=6400, wi=6528)

    def proj_even(self, e_):
        em = self.em
        W = self.w_in_even[e_]
        chunks = []
        for m in range(50):
            col0 = m * 128
            if col0 < 1024:
                f, sc = AF.Copy, 1.0
            elif col0 < 2048:
                f, sc = AF.Sigmoid, 1.0
            elif col0 < 3072:
                f, sc = AF.Silu, 1.0
            elif col0 < 4352:
                f, sc = AF.Copy, 1.0
            elif col0 < 5376:
                f, sc = AF.Silu, 1.0
            else:
                f, sc = AF.Copy, 0.125
            dst = col0
            if 3072 <= col0 < 4096:
                dst = -(col0 - 3072) - 1
            elif 5376 <= col0 < 6400:
                dst = -(1024 + col0 - 5376) - 1
            chunks.append((col0, 128, f, sc, dst, False))
        chunks.append((6400, 64, AF.Copy, 1.0, -2048 - 1, True))
        chunks.append((6464, 16, AF.Copy, 0.25, 6528, False))
        wtr32 = em.ring("w32", 2, [128, 16, 512], F32, dma=True)
        wtr = em.ring("wb", 2, [128, 16, 512], BF16)
        pmm = em.psring("pmm", 4, [128, 512], F32)
        otr = em.ring("ot", 2, [128, T], F32, dma=True)
        otbr = em.ring("otb", 2, [128, T], BF16, dma=True)
        self.proj_fm(W, chunks, wtr32, wtr, pmm, otr, otbr)

    def proj_odd(self, o_):
        em = self.em
        W = self.w_in_odd[o_]
        chunks = []
        for m in range(16):
            chunks.append((m * 128, 128, AF.Silu, 1.0, m * 128, False))
        for m in range(16):
            chunks.append((2048 + m * 128, 128, AF.Sigmoid, 1.0, 2048 + m * 128, False))
        for m in range(16):
            chunks.append((6144 + m * 128, 128, AF.Silu, 1.0, 4096 + m * 128, False))
        wtr32 = em.ring("w32", 2, [128, 16, 512], F32, dma=True)
        wtr = em.ring("wb", 2, [128, 16, 512], BF16)
        pmm = em.psring("pmm", 4, [128, 512], F32)
        otr = em.ring("ot", 2, [128, T], F32, dma=True)
        self.proj_fm(W, chunks, wtr32, wtr, pmm, otr)
        hnT = self.hnT
        vo = em.ring("vo", 2, [128, 512], BF16, dma=True)
        for g in range(4):
            w32 = wtr32.next()
            wb = wtr.next()
            em.dma("sp", w32[:], W[:, 4096 + g * 512:4096 + (g + 1) * 512].rearrange("(kc p) m -> p kc m", p=128),
                   writes=[w32], owner=w32)
            for k4 in range(4):
                em.op("pool", lambda e, k4=k4: e.tensor_copy(wb[:, 4 * k4:4 * k4 + 4, :], w32[:, 4 * k4:4 * k4 + 4, :]), reads=[w32], writes=[wb])
            for i in range(NT):
                r0, n = trng(i)
                ps = pmm.next()
                for k in range(16):
                    em.op("pe", lambda e, k=k: e.matmul(ps[0:n, :], hnT[:, k, r0:r0 + n], wb[:, k, :],
                                                        start=(k == 0), stop=(k == 15)),
                          reads=[wb, hnT], writes=[ps])
                v = vo.next()
                em.op("act", lambda e: e.activation(out=v[0:n, :], in_=ps[0:n, :], func=AF.Copy), reads=[ps], writes=[v])
                em.dma("act", self.VTOK[r0:r0 + n, g * 512:(g + 1) * 512], v[0:n, :], reads=[v], owner=v)

    def stage_conv(self, e_):
        em = self.em
        ZE = self.ZE
        em.begin()
        stg = em.sb("stg", [128, 128], F32, dma=True)
        pst = em.ps("pst", [128, 128], F32)
        cw = em.sb("cw", [128, 248], F32)
        cv = em.sb("cv", [128, 24], F32)
        cwr = self.conv_w[e_].rearrange("j (cc p) -> (j cc) p", p=128)
        self.vecT(cw, 0, cwr[0:124, :], 124, pst, stg)
        self.vecT(cw, 124, cwr[124:248, :], 124, pst, stg)
        self.vecT(cv, 0, self.conv_b[e_].rearrange("(cc p) -> cc p", p=128), 8, pst, stg)
        self.vecT(cv, 8, self.ln_g[e_].rearrange("(cc p) -> cc p", p=128), 8, pst, stg)
        self.vecT(cv, 16, self.ln_b[e_].rearrange("(cc p) -> cc p", p=128), 8, pst, stg)
        import os
        KCUT = int(os.environ.get("KCUT", "99"))
        if KCUT <= 1:
            em.end()
            return
        uall = em.sb("uall", [128, 8, T], F32)
        acc1 = em.sb("acc1", [128, T], F32)
        acc2 = em.sb("acc2", [128, T], F32)
        gsr = em.ring("gs", 1, [128, T], F32, dma=True)
        upr = em.ring("up", 2, [128, 30 + T], F32, dma=True)
        pa = em.ring("pa", 1, [128, T], F32)
        pb = em.ring("pb", 1, [128, T], F32)
        tmpr = em.ring("ctmp", 2, [128, T], F32)
        dgr = em.ring("dg", 3, [128, 128], F32)
        pcs = [em.ps("pc%d" % i, [128, 512], F32) for i in range(5)]
        sq = em.ring("sq", 1, [128, T], F32)
        for b in upr.bufs:
            em.op("pool", lambda e, b=b: e.memset(b[:, 0:30], 0.0), writes=[b])
        def load_u(cc):
            gs = gsr.next()
            up = upr.next()
            em.dma("sp", up[:, 30:30 + T], self.ZT[ZE["glu_v"] + cc * 128:ZE["glu_v"] + (cc + 1) * 128, :], writes=[up], owner=up)
            em.dma("sp", gs[:], self.ZT[ZE["glu_g"] + cc * 128:ZE["glu_g"] + (cc + 1) * 128, :], writes=[gs], owner=gs)
            em.op("pool", lambda e: e.tensor_tensor(out=up[:, 30:30 + T], in0=up[:, 30:30 + T], in1=gs[:], op=ALU.mult),
                  reads=[up, gs], writes=[up])
            return up

        up_next = load_u(0)
        for cc in range(8):
            up = up_next
            A = pa.next()
            B = pb.next()
            w = lambda j: cw[:, j * 8 + cc:j * 8 + cc + 1]
            em.op("dve", lambda e: e.tensor_scalar(out=A[:], in0=up[:, 0:T], scalar1=w(0), scalar2=cv[:, cc:cc + 1],
                                                   op0=ALU.mult, op1=ALU.add), reads=[up, cw, cv], writes=[A])
            for j in range(1, 10):
                em.op("dve", lambda e, j=j: e.scalar_tensor_tensor(out=A[:], in0=up[:, j:j + T], scalar=w(j), in1=A[:],
                                                                   op0=ALU.mult, op1=ALU.add), reads=[up, cw, A], writes=[A])
            em.op("act", lambda e: e.activation(out=B[:], in_=up[:, 10:10 + T], func=AF.Copy, scale=w(10)),
                  reads=[up, cw], writes=[B])
            for j in range(11, 17):
                tp = tmpr.next()
                em.op("act", lambda e, j=j, tp=tp: e.activation(out=tp[:], in_=up[:, j:j + T], func=AF.Copy, scale=w(j)),
                      reads=[up, cw], writes=[tp])
                em.op("pool", lambda e, tp=tp: e.tensor_add(out=B[:], in0=B[:], in1=tp[:]), reads=[tp, B], writes=[B])
            if cc + 1 < 8:
                up_next = load_u(cc + 1)
            for j in range(17, 31):
                dg = dgr.next()
                em.op("act", lambda e, j=j, dg=dg: e.activation(out=dg[:], in_=self.idf[:], func=AF.Copy, scale=w(j)),
                      reads=[self.idf, cw], writes=[dg])
                for ci, (c0, n) in enumerate(CCH):
                    em.op("pe", lambda e, j=j, dg=dg, ci=ci, c0=c0, n=n: e.matmul(pcs[ci][:, 0:n], dg[:], up[:, j + c0:j + c0 + n],
                                                                               start=(j == 17), stop=(j == 30)),
                          reads=[dg, up], writes=[pcs[ci]])
            em.op("dve", lambda e: e.tensor_add(out=uall[:, cc, :], in0=A[:], in1=B[:]), reads=[A, B], writes=[uall])
            for ci, (c0, n) in enumerate(CCH):
                em.op("dve", lambda e, ci=ci, c0=c0, n=n: e.tensor_add(out=uall[:, cc, c0:c0 + n], in0=uall[:, cc, c0:c0 + n], in1=pcs[ci][:, 0:n]),
                      reads=[uall, pcs[ci]], writes=[uall])
            s2 = sq.next()
            em.op("act", lambda e: e.activation(out=s2[:], in_=uall[:, cc, :], func=AF.Square), reads=[uall], writes=[s2])
            if cc == 0:
                em.op("pool", lambda e: e.tensor_copy(acc1[:], uall[:, cc, :]), reads=[uall], writes=[acc1])
                em.op("pool", lambda e: e.tensor_copy(acc2[:], s2[:]), reads=[s2], writes=[acc2])
            else:
                em.op("pool", lambda e: e.tensor_add(out=acc1[:], in0=acc1[:], in1=uall[:, cc, :]), reads=[uall, acc1], writes=[acc1])
                em.op("pool", lambda e: e.tensor_add(out=acc2[:], in0=acc2[:], in1=s2[:]), reads=[s2, acc2], writes=[acc2])
        if KCUT <= 2:
            em.end()
            return
        mean = em.sb("mean", [128, T], F32)
        rstd = em.sb("rstd", [128, T], F32)
        p1 = em.ps("p1", [128, 512], F32)
        p2 = em.ps("p2", [128, 512], F32)
        avg = self.avg[1024]
        for (c0, n) in CCH:
            em.op("pe", lambda e: e.matmul(p1[:, 0:n], avg[:], acc1[:, c0:c0 + n], start=True, stop=True),
                  reads=[avg, acc1], writes=[p1])
            em.op("pe", lambda e: e.matmul(p2[:, 0:n], avg[:], acc2[:, c0:c0 + n], start=True, stop=True),
                  reads=[avg, acc2], writes=[p2])
            em.op("act", lambda e: e.activation(out=mean[:, c0:c0 + n], in_=p1[:, 0:n], func=AF.Copy), reads=[p1], writes=[mean])
            em.op("dve", lambda e: e.tensor_tensor(out=rstd[:, c0:c0 + n], in0=mean[:, c0:c0 + n], in1=mean[:, c0:c0 + n], op=ALU.mult),
                  reads=[mean], writes=[rstd])
            em.op("dve", lambda e: e.tensor_sub(out=rstd[:, c0:c0 + n], in0=p2[:, 0:n], in1=rstd[:, c0:c0 + n]),
                  reads=[p2, rstd], writes=[rstd])
            em.op("dve", lambda e: e.tensor_scalar(out=rstd[:, c0:c0 + n], in0=rstd[:, c0:c0 + n], scalar1=LN_EPS, scalar2=None, op0=ALU.add),
                  reads=[rstd], writes=[rstd])
        em.op("act", lambda e: e.activation(out=rstd[:], in_=rstd[:], func=AF.Sqrt), reads=[rstd], writes=[rstd])
        em.op("dve", lambda e: e.reciprocal(out=rstd[:], in_=rstd[:]), reads=[rstd], writes=[rstd])
        if KCUT <= 3:
            em.end()
            return
        mxr = em.ring("mx", 2, [128, T], BF16, dma=True)
        for cc in range(8):
            ga = gsr.next()
            t1 = pa.next()
            t2 = pb.next()
            mx = mxr.next()
            em.dma("sp", ga[:], self.ZT[ZE["gate_a"] + cc * 128:ZE["gate_a"] + (cc + 1) * 128, :], writes=[ga], owner=ga)
            em.op("dve", lambda e: e.tensor_sub(out=t1[:], in0=uall[:, cc, :], in1=mean[:]), reads=[uall, mean], writes=[t1])
            em.op("pool", lambda e: e.tensor_mul(out=t1[:], in0=t1[:], in1=rstd[:]), reads=[t1, rstd], writes=[t1])
            em.op("act", lambda e: e.activation(out=t2[:], in_=t1[:], func=AF.Silu, scale=cv[:, 8 + cc:9 + cc], bias=cv[:, 16 + cc:17 + cc]),
                  reads=[t1, cv], writes=[t2])
            em.op("dve", lambda e: e.tensor_mul(out=mx[:], in0=t2[:], in1=ga[:]), reads=[t2, ga], writes=[mx])
            em.dma("sp", self.MIXT[cc * 128:(cc + 1) * 128, :], mx[:], reads=[mx], owner=mx)
        em.end()

    def stage_attn(self, e_):
        em = self.em
        ZE = self.ZE
        ZT = self.ZT
        em.begin()
        cnT = em.sb("cnT", [128, 2, T], BF16)
        cnk = em.sb("cnk", [128, NT, 256], BF16)
        wukT = em.sb("wukT", [128, 8, 256], BF16)
        wuvb = em.sb("wuvb", [128, 8, 2, 128], BF16)
        kiT2 = em.sb("kiT2", [128, T], BF16, dma=True)
        wiTok = em.sb("wiTok", [128, NT, 16], F32)
        kvg = em.sb("kvg", [128, 2], F32)
        stg = em.sb("stg", [128, 128], F32, dma=True)

        prep = ExitStack()
        stage_outer = em.stage
        em.stage = prep
        pA = em.ps("pA", [128, 512], F32)
        pB = em.ps("pB", [128, 512], F32)
        pT = em.ps("pT", [128, 128], F32)
        pTb = em.ps("pTb", [128, 2, 128], BF16)
        big = em.sb("big", [128, 2, T], F32, dma=True)
        big2 = em.sb("big2", [128, T], F32)
        rsd = em.sb("rsd", [128, T], F32)
        self.vecT(kvg, 0, self.kv_g[e_].rearrange("(cc p) -> cc p", p=128), 2, pT, stg)
        em.dma("sp", big[:], ZT[ZE["c"]:ZE["c"] + 256, :].rearrange("(cc p) t -> p cc t", p=128), writes=[big], owner=big)
        em.op("act", lambda e: e.activation(out=big2[:], in_=big[:, 0, :], func=AF.Square), reads=[big], writes=[big2])
        em.op("act", lambda e: e.activation(out=rsd[:], in_=big[:, 1, :], func=AF.Square), reads=[big], writes=[rsd])
        em.op("dve", lambda e: e.tensor_add(out=big2[:], in0=big2[:], in1=rsd[:]), reads=[big2, rsd], writes=[big2])
        avg = self.avg[256]
        for (c0, n) in CCH:
            em.op("pe", lambda e: e.matmul(pA[:, 0:n], avg[:], big2[:, c0:c0 + n], start=True, stop=True),
                  reads=[avg, big2], writes=[pA])
            em.op("dve", lambda e: e.tensor_scalar(out=rsd[:, c0:c0 + n], in0=pA[:, 0:n], scalar1=RMS_EPS, scalar2=None, op0=ALU.add),
                  reads=[pA], writes=[rsd])
        em.op("act", lambda e: e.activation(out=rsd[:], in_=rsd[:], func=AF.Sqrt), reads=[rsd], writes=[rsd])
        em.op("dve", lambda e: e.reciprocal(out=rsd[:], in_=rsd[:]), reads=[rsd], writes=[rsd])
        for cc in range(2):
            em.op("dve", lambda e, cc=cc: e.scalar_tensor_tensor(out=cnT[:, cc, :], in0=big[:, cc, :], scalar=kvg[:, cc:cc + 1],
                                                                 in1=rsd[:], op0=ALU.mult, op1=ALU.mult),
                  reads=[big, kvg, rsd], writes=[cnT])
        for i in range(NT):
            r0, n = trng(i)
            for cc in range(2):
                em.op("pe", lambda e, cc=cc: e.transpose(pTb[0:n, cc, :], cnT[:, cc, r0:r0 + n], self.idb[:, :]),
                      reads=[cnT, self.idb], writes=[pTb])
            em.op("act", lambda e: e.activation(out=cnk[0:n, i, :], in_=pTb[0:n, :, :], func=AF.Copy), reads=[pTb], writes=[cnk])
        wld = em.ring("wld", 2, [128, 2, 128], F32, dma=True)
        for h in range(8):
            w = wld.next()
            em.dma("sp", w[:], self.w_uk[e_, h].rearrange("(cc p) d -> p cc d", p=128), writes=[w], owner=w)
            for cc in range(2):
                em.op("pe", lambda e, cc=cc: e.transpose(pT[:, :], w[:, cc, :], self.idf[:, :]), reads=[w, self.idf], writes=[pT])
                em.op("act", lambda e, cc=cc: e.activation(out=wukT[:, h, cc * 128:(cc + 1) * 128], in_=pT[:, :], func=AF.Copy,
                                                           scale=128.0 ** -0.5), reads=[pT], writes=[wukT])
            w2 = wld.next()
            em.dma("sp", w2[:], self.w_uv[e_, h].rearrange("(cc p) d -> p cc d", p=128), writes=[w2], owner=w2)
            em.op("pool", lambda e: e.tensor_copy(wuvb[:, h, :, :], w2[:]), reads=[w2], writes=[wuvb])
        em.dma("sp", kiT2[:], self.ZB[2048:2176, :], writes=[kiT2], owner=kiT2)
        wiT = em.sb("wiT", [16, T], F32, dma=True)
        em.dma("sp", wiT[:], ZT[ZE["wi"]:ZE["wi"] + 16, :], writes=[wiT], owner=wiT)
        for i in range(NT):
            r0, n = trng(i)
            em.op("pe", lambda e: e.transpose(pT[0:n, 0:16], wiT[0:16, r0:r0 + n], self.idf[0:16, 0:16]),
                  reads=[wiT, self.idf], writes=[pT])
            em.op("act", lambda e: e.activation(out=wiTok[0:n, i, :], in_=pT[0:n, 0:16], func=AF.Copy), reads=[pT], writes=[wiTok])
        em.barrier()
        prep.close()
        em.stage = stage_outer

        import os
        AQT = int(os.environ.get("AQT", "99"))
        ASUB = int(os.environ.get("ASUB", "99"))
        pdot = em.psring("pdot", 2, [128, 512], F32)
        pmt = em.ps("pmt", [128, 8, 128], BF16)
        plog = em.psring("plog", 2, [128, 4, 128], F32)
        pso_r = em.psring("pso", 1, [128, 3, 128], F32)
        psb = em.ps("psb", [128, 128], F32)
        pql = em.ps("pql", [128, 2, 128], F32)
        qi_r = em.ring("qiT", 2, [128, 8, 128], BF16, dma=True)
        qh_r = em.ring("qhT", 2, [128, 8, 128], BF16, dma=True)
        ql_r = em.ring("qlat", 2, [128, 8, 2, 128], BF16)
        score_r = em.ring("score", 2, [128, T], F32)
        work_r = em.ring("work", 1, [128, T], F32)
        m8 = em.sb("m8", [128, 8], F32)
        NIT = 24
        blo = em.sb("blo", [128, 1], F32)
        brg = em.sb("brg", [128, 1], F32)
        bthr = em.sb("bthr", [128, 1], F32)
        bcnt = em.sb("bcnt", [128, 1], F32)
        btq = em.sb("btq", [128, 1], F32)
        stab = em.sb("stab", [128, NIT], F32)
        pw2 = em.sb("pw2", [128, NIT], F32)
        for k in range(NIT):
            em.op("pool", lambda e, k=k: e.memset(pw2[:, k:k + 1], 2.0 ** -(k + 1)), writes=[pw2])
        rl_r = em.ring("rl", 3, [128, 512], F32)
        mask_r = em.ring("mask", 2, [128, T], BF16)
        maskT_r = em.ring("maskT", 2, [128, NT, 128], BF16)
        cm_r = em.ring("cm", 2, [128, 8, 2, 128], F32)
        ex_r = em.ring("ex", 2, [128, 4, 128], F32)
        pt_r = em.ring("ptile", 12, [128, 4, 128], BF16)
        osb_r = em.ring("osb", 2, [128, 2, 128], BF16)
        rden_r = em.ring("rden", 2, [128, 128], F32)
        dsb_r = em.ring("dsb", 2, [128, 128], F32)
        gb_r = em.ring("gb", 2, [128, 8, 128], F32, dma=True)
        tb_r = em.ring("tb", 2, [128, 128], F32)
        mixb_r = em.ring("mixb", 2, [128, 8, 128], BF16, dma=True)
        EB = self.EB
        def pre(qt):
            q0, nq = trng(qt)
            nk = q0 + nq
            score = score_r.next()
            gb = gb_r.next()
            em.dma("sp", gb[:, :, 0:nq], ZT[ZE["gate_b"]:ZE["gate_b"] + 1024, q0:q0 + nq].rearrange("(h p) t -> p h t", p=128),
                   writes=[gb], owner=gb)
            qiT = qi_r.next()
            em.dma("sp", qiT[:, :, 0:nq], self.ZB[1024:2048, q0:q0 + nq].rearrange("(c p) t -> p c t", p=128),
                   writes=[qiT], owner=qiT)
            qhT = qh_r.next()
            em.dma("sp", qhT[:, :, 0:nq], self.ZB[0:1024, q0:q0 + nq].rearrange("(h p) t -> p h t", p=128),
                   writes=[qhT], owner=qhT)
            qlat = ql_r.next()
            for h in range(8):
                for cc in range(2):
                    em.op("pe", lambda e, h=h, cc=cc: e.matmul(pql[:, cc, 0:nq], wukT[:, h, cc * 128:(cc + 1) * 128], qhT[:, h, 0:nq],
                                                               start=True, stop=True), reads=[wukT, qhT], writes=[pql])
                em.op("act", lambda e, h=h: e.activation(out=qlat[:, h, :, 0:nq], in_=pql[:, :, 0:nq], func=AF.Copy),
                      reads=[pql], writes=[qlat])
            yield
            kch = [(k0, min(512, nk - k0)) for k0 in range(0, nk, 512)]
            for h16 in range(16):
                c_, po = h16 // 2, (h16 % 2) * 64
                for (k0, n) in kch:
                    ps = pdot.next()
                    rl = rl_r.next()
                    em.op("pe", lambda e: e.matmul(ps[0:nq, 0:n], qiT[po:po + 64, c_, 0:nq], kiT2[po:po + 64, k0:k0 + n],
                                                   start=True, stop=True), reads=[qiT, kiT2], writes=[ps])
                    em.op("act", lambda e: e.activation(out=rl[0:nq, 0:n], in_=ps[0:nq, 0:n], func=AF.Relu), reads=[ps], writes=[rl])
                    if h16 == 0:
                        em.op("dve", lambda e: e.tensor_scalar(out=score[0:nq, k0:k0 + n], in0=rl[0:nq, 0:n],
                                                               scalar1=wiTok[0:nq, qt, 0:1], scalar2=None, op0=ALU.mult),
                              reads=[rl, wiTok], writes=[score])
                    else:
                        em.op("dve", lambda e: e.scalar_tensor_tensor(out=score[0:nq, k0:k0 + n], in0=rl[0:nq, 0:n],
                                                                      scalar=wiTok[0:nq, qt, h16:h16 + 1],
                                                                      in1=score[0:nq, k0:k0 + n], op0=ALU.mult, op1=ALU.add),
                              reads=[rl, wiTok, score], writes=[score])
                yield
            mask = mask_r.next()
            if qt >= 1:
                em.op("pool", lambda e: e.memset(score[0:64, nk - 64:nk], NEG), reads=[], writes=[score])
            if nk > TOPK and nk - 64 < TOPK:
                work = work_r.next()
                src = score
                for r in range(TOPK // 8):
                    em.op("dve", lambda e, src=src: e.max(out=m8[0:nq, :], in_=src[0:nq, 0:nk]), reads=[src], writes=[m8])
                    em.op("dve", lambda e, src=src: e.match_replace(out=work[0:nq, 0:nk], in_to_replace=m8[0:nq, :],
                                                                    in_values=src[0:nq, 0:nk], imm_value=NEG2),
                          reads=[src, m8], writes=[work])
                    src = work
                    yield
                em.op("dve", lambda e: e.tensor_scalar(out=mask[0:nq, 0:nk], in0=work[0:nq, 0:nk], scalar1=-2.0e38, scalar2=None,
                                                       op0=ALU.is_lt), reads=[work], writes=[mask])
            elif nk > TOPK:
                nlo = nk - 64
                em.op("dve", lambda e: e.max(out=m8[0:nq, :], in_=score[0:nq, 0:nk]), reads=[score], writes=[m8])
                em.op("dve", lambda e: e.tensor_reduce(out=blo[0:nq, :], in_=score[0:nq, 0:nlo], axis=AX.X, op=ALU.min),
                      reads=[score], writes=[blo])
                em.op("dve", lambda e: e.tensor_sub(out=brg[0:nq, :], in0=m8[0:nq, 0:1], in1=blo[0:nq, :]), reads=[m8, blo], writes=[brg])
                em.op("dve", lambda e: e.tensor_scalar(out=stab[0:nq, :], in0=pw2[0:nq, :], scalar1=brg[0:nq, 0:1], scalar2=None, op0=ALU.mult),
                      reads=[pw2, brg], writes=[stab])
                yield
                for k in range(NIT):
                    em.op("dve", lambda e, k=k: e.tensor_add(out=bthr[0:nq, :], in0=blo[0:nq, :], in1=stab[0:nq, k:k + 1]),
                          reads=[blo, stab], writes=[bthr])
                    em.op("dve", lambda e: e.tensor_scalar(out=mask[0:nq, 0:nk], in0=score[0:nq, 0:nk], scalar1=bthr[0:nq, 0:1], scalar2=0.0,
                                                           op0=ALU.is_ge, op1=ALU.add, accum_out=bcnt[0:nq, 0:1]),
                          reads=[score, bthr], writes=[mask, bcnt])
                    em.op("dve", lambda e, k=k: e.scalar_tensor_tensor(out=btq[0:nq, :], in0=bcnt[0:nq, :], scalar=TOPK - 0.5,
                                                                       in1=stab[0:nq, k:k + 1], op0=ALU.is_ge, op1=ALU.mult),
                          reads=[bcnt, stab], writes=[btq])
                    em.op("dve", lambda e: e.tensor_add(out=blo[0:nq, :], in0=blo[0:nq, :], in1=btq[0:nq, :]), reads=[blo, btq], writes=[blo])
                    yield
                em.op("dve", lambda e: e.tensor_scalar(out=mask[0:nq, 0:nk], in0=score[0:nq, 0:nk], scalar1=blo[0:nq, 0:1], scalar2=None,
                                                       op0=ALU.is_ge), reads=[score, blo], writes=[mask])
            else:
                em.op("dve", lambda e: e.tensor_scalar(out=mask[0:nq, 0:nk], in0=score[0:nq, 0:nk], scalar1=-1.0e29, scalar2=None,
                                                       op0=ALU.is_gt), reads=[score], writes=[mask])
            if qt >= 1:
                em.op("pool", lambda e: e.memset(mask[0:64, nk - 64:nk], 0.0), writes=[mask])
            maskT = maskT_r.next()
            for kb0 in range(0, qt + 1, 8):
                kbs = list(range(kb0, min(qt + 1, kb0 + 8)))
                for kb in kbs:
                    k0, nkb = trng(kb)
                    em.op("pe", lambda e, kb=kb, k0=k0, nkb=nkb: e.transpose(pmt[0:nkb, kb - kb0, 0:nq], mask[0:nq, k0:k0 + nkb],
                                                                            self.idb[0:nq, 0:nq]),
                          reads=[mask, self.idb], writes=[pmt])
                if kb0 == 0:
                    em.op("act", lambda e: e.activation(out=maskT[0:16, 0, 0:nq], in_=pmt[0:16, 0, 0:nq], func=AF.Copy),
                          reads=[pmt], writes=[maskT])
                    if len(kbs) > 1:
                        em.op("act", lambda e: e.activation(out=maskT[:, 1:len(kbs), 0:nq], in_=pmt[:, 1:len(kbs), 0:nq], func=AF.Copy),
                              reads=[pmt], writes=[maskT])
                else:
                    em.op("act", lambda e: e.activation(out=maskT[:, kb0:kb0 + len(kbs), 0:nq], in_=pmt[:, 0:len(kbs), 0:nq], func=AF.Copy),
                          reads=[pmt], writes=[maskT])
            yield
            cm = cm_r.next()
            if qt == 0:
                near = {0: (0, 0, 16)}
            elif qt == 1:
                near = {1: (0, 0, 128), 0: (1, 2, 16)}
            else:
                near = {qt: (0, 0, 128), qt - 1: (1, 1, 128)}
            for kb, (slot, ty, rows) in near.items():
                for h in range(8):
                    em.op("pool", lambda e, kb=kb, slot=slot, ty=ty, rows=rows, h=h: e.tensor_tensor(
                        out=cm[0:rows, h, slot, 0:nq], in0=EB[0:rows, ty, h, 0:nq], in1=maskT[0:rows, kb, 0:nq], op=ALU.mult),
                        reads=[EB, maskT], writes=[cm])
            self._pre[qt] = dict(gb=gb, qlat=qlat, maskT=maskT, cm=cm, near=near)
            yield

        def head_a(qt, h, st):
            q0, nq = trng(qt)
            gb, qlat, maskT, cm, near = st["gb"], st["qlat"], st["maskT"], st["cm"], st["near"]
            groups = [[0]] + [list(range(a, min(qt + 1, a + 4))) for a in range(1, qt + 1, 4)]
            ptiles = {}
            for grp in groups:
                pl = plog.next()
                rows = 16 if grp[0] == 0 else 128
                for gi, kb in enumerate(grp):
                    k0, nkb = trng(kb)
                    for cc in range(2):
                        em.op("pe", lambda e, gi=gi, k0=k0, nkb=nkb, cc=cc: e.matmul(
                            pl[0:nkb, gi, 0:nq], cnT[:, cc, k0:k0 + nkb], qlat[:, h, cc, 0:nq],
                            start=(cc == 0), stop=(cc == 1)), reads=[cnT, qlat], writes=[pl])
                ex = ex_r.next()
                g_n = len(grp)
                em.op("act", lambda e, rows=rows, g_n=g_n: e.activation(out=ex[0:rows, 0:g_n, 0:nq], in_=pl[0:rows, 0:g_n, 0:nq],
                                                                        func=AF.Exp, bias=self.bfar[0:rows, h:h + 1]),
                      reads=[pl, self.bfar], writes=[ex])
                ptile = pt_r.next()
                far = [gi for gi, kb in enumerate(grp) if kb not in near]
                if far:
                    a, b = far[0], far[-1] + 1
                    kba = grp[a]
                    em.op("dve", lambda e, a=a, b=b, kba=kba, rows=rows: e.tensor_tensor(
                        out=ptile[0:rows, a:b, 0:nq], in0=ex[0:rows, a:b, 0:nq], in1=maskT[0:rows, kba:kba + (b - a), 0:nq], op=ALU.mult),
                        reads=[ex, maskT], writes=[ptile])
                for gi, kb in enumerate(grp):
                    if kb in near:
                        slot, ty, rws = near[kb]
                        em.op("dve", lambda e, gi=gi, slot=slot, rws=rws: e.tensor_tensor(
                            out=ptile[0:rws, gi, 0:nq], in0=ex[0:rws, gi, 0:nq], in1=cm[0:rws, h, slot, 0:nq], op=ALU.mult),
                            reads=[ex, cm], writes=[ptile])
                for gi, kb in enumerate(grp):
                    ptiles[kb] = (ptile, gi)
            return ptiles

        def head_b(qt, h, st, mixb, ptiles):
            q0, nq = trng(qt)
            gb = st["gb"]
            pso = pso_r.next()
            for part in range(3):
                for kb in range(qt + 1):
                    k0, nkb = trng(kb)
                    ptile, gi = ptiles[kb]
                    if part < 2:
                        lhs = cnk[0:nkb, kb, part * 128:(part + 1) * 128]
                        rd = [cnk, ptile]
                    else:
                        lhs = self.onesb[0:nkb, :]
                        rd = [self.onesb, ptile]
                    em.op("pe", lambda e, lhs=lhs, ptile=ptile, gi=gi, nkb=nkb, kb=kb, part=part: e.matmul(
                        pso[:, part, 0:nq], lhs, ptile[0:nkb, gi, 0:nq], start=(kb == 0), stop=(kb == qt)),
                        reads=rd, writes=[pso])
            osb = osb_r.next()
            rden = rden_r.next()
            dsb = dsb_r.next()
            em.op("act", lambda e: e.activation(out=osb[:, :, 0:nq], in_=pso[:, 0:2, 0:nq], func=AF.Copy), reads=[pso], writes=[osb])
            em.op("act", lambda e: e.activation(out=dsb[:, 0:nq], in_=pso[:, 2, 0:nq], func=AF.Ln), reads=[pso], writes=[dsb])
            for cc in range(2):
                em.op("pe", lambda e, cc=cc: e.matmul(psb[:, 0:nq], wuvb[:, h, cc, :], osb[:, cc, 0:nq], start=(cc == 0), stop=(cc == 1)),
                      reads=[wuvb, osb], writes=[psb])
            tb = tb_r.next()
            em.op("act", lambda e: e.activation(out=rden[:, 0:nq], in_=dsb[:, 0:nq], func=AF.Exp, scale=-1.0), reads=[dsb], writes=[rden])
            em.op("dve", lambda e: e.tensor_tensor(out=tb[:, 0:nq], in0=psb[:, 0:nq], in1=rden[:, 0:nq], op=ALU.mult),
                  reads=[psb, rden], writes=[tb])
            em.op("pool", lambda e: e.tensor_tensor(out=mixb[:, h, 0:nq], in0=tb[:, 0:nq], in1=gb[:, h, 0:nq], op=ALU.mult),
                  reads=[tb, gb], writes=[mixb])

        self._pre = {}
        for _ in pre(0):
            pass
        nqt = min(NT, AQT)
        for qt in range(nqt):
            q0, nq = trng(qt)
            st = self._pre.pop(qt)
            gen = pre(qt + 1) if qt + 1 < nqt else None
            nk1 = trng(qt + 1)[0] + trng(qt + 1)[1] if gen is not None else 0
            nsteps = 0 if gen is None else (3 + 16 + (0 if nk1 <= TOPK else (TOPK // 8 if nk1 - 64 < TOPK else NIT + 1)))
            per_head = (nsteps + 7) // 8
            mixb = mixb_r.next()
            pt_next = head_a(qt, 0, st)
            for h in range(8):
                pt_cur = pt_next
                if h + 1 < 8:
                    pt_next = head_a(qt, h + 1, st)
                head_b(qt, h, st, mixb, pt_cur)
                if gen is not None:
                    for _ in range(per_head):
                        if next(gen, "done") == "done":
                            gen = None
                            break
            if gen is not None:
                for _ in gen:
                    pass
            em.dma("sp", self.MIXT[1024:2048, q0:q0 + nq].rearrange("(h p) t -> p h t", p=128), mixb[:, :, 0:nq], reads=[mixb], owner=mixb)
        em.end()

    def stage_rec(self, l):
        em = self.em
        ZT = self.ZT
        li = l // 2
        G = 4
        em.begin()
        stg = em.sb("stg", [128, 128], F32, dma=True)
        rng = em.sb("rng", [128, 16], F32)
        epsb = em.sb("epsb", [128, 1], F32)
        em.op("pool", lambda e: e.memset(epsb[:], RMS_EPS), writes=[epsb])
        self.rmask = em.sb("rmask", [128, T], F32)
        em.op("pool", lambda e: e.memset(self.rmask[:], 1.0), writes=[self.rmask])
        em.op("pool", lambda e: e.memset(self.rmask[:, 0:1], 0.0), writes=[self.rmask])
        em.op("pool", lambda e: e.memset(self.rmask[:, 16:T].rearrange("p (c j) -> p c j", j=64)[:, :, 0:1], 0.0),
              writes=[self.rmask])
        ptk = em.ps("ptk", [128, 8, 128], BF16)
        pn = em.ps("pn", [128, 512], F32)
        self.vecT(rng, 0, self.rec_g[li].rearrange("(c p) -> c p", p=128), 16, pn, stg)
        qs_r = em.ring("qs", 2, [128, T], F32, dma=True)
        sg_r = em.ring("sg", 2, [128, T], F32, dma=True)
        fb_r = em.ring("fb", 1, [128, T], F32)
        b_r = em.ring("b", 1, [128, T], F32)
        d_r = em.ring("d1", 1, [128, T], F32)
        kk_r = em.ring("kk", 1, [128, T], F32)
        mo_r = em.ring("mo", 1, [128, T], BF16, dma=True)
        qt_r = em.ring("qtl", G, [128, T], BF16)
        kt_r = em.ring("ktl", G, [128, T], BF16)
        ktok_r = em.ring("ktok", G, [128, NT, 128], BF16)
        vtok_r = em.ring("vtok", G, [128, NT, 128], BF16, dma=True)
        sc_r = em.ring("sc", G, [128, 4, 33], F32)
        bl_r = em.ring("bl", G, [128, 33], F32)
        oT_r = em.ring("oT", G, [128, T], F32)
        S_rs = [em.ring("S%d" % g, 2, [128, 128], F32) for g in range(G)]
        Sb_r = em.ring("Sb", 2 * G, [128, 128], BF16)
        St_r = em.ring("St", 2 * G, [128, 128], F32)
        am_r = em.ring("am", 2 * G, [128, 128], BF16)
        hbank = [em.ps_views("ph%d" % g, 4, [128], F32) for g in range(G)]
        lb = self.lbv

        def pre_head(h, g):
            qs = qs_r.next(); sg = sg_r.next()
            fb = fb_r.next(); b = b_r.next(); d1 = d_r.next(); kk = kk_r.next()
            qtl = qt_r.next(); ktl = kt_r.next(); ktok = ktok_r.next(); vtok = vtok_r.next()
            sc = sc_r.next(); bl = bl_r.next(); oT = oT_r.next()
            em.dma("sp", qs[:], ZT[h * 128:(h + 1) * 128, :], writes=[qs], owner=qs)
            em.dma("sp", sg[:], ZT[2048 + h * 128:2048 + (h + 1) * 128, :], writes=[sg], owner=sg)
            em.dma("sp", vtok[0:16, 0, :], self.VTOK[0:16, h * 128:(h + 1) * 128], writes=[vtok], owner=vtok)
            em.dma("sp", vtok[:, 1:NT, :], self.VTOK[16:T, h * 128:(h + 1) * 128].rearrange("(i p) v -> p i v", p=128),
                   writes=[vtok], owner=vtok)
            em.op("dve", lambda e: e.tensor_scalar(out=fb[:], in0=sg[:], scalar1=self.oml[:, li, h:h + 1], scalar2=lb[:, li, h:h + 1],
                                                   op0=ALU.mult, op1=ALU.add), reads=[sg, self.oml, lb], writes=[fb])
            em.op("act", lambda e: e.activation(out=fb[:], in_=fb[:], func=AF.Ln), reads=[fb], writes=[fb])
            em.op("dve", lambda e: e.tensor_scalar(out=kk[:], in0=sg[:], scalar1=self.noml[:, li, h:h + 1], scalar2=self.oml[:, li, h:h + 1],
                                                   op0=ALU.mult, op1=ALU.add), reads=[sg, self.noml, self.oml], writes=[kk])
            em.op("dve", lambda e: e.tensor_tensor_scan(b[:], self.rmask[:], fb[:], 0.0, ALU.mult, ALU.add),
                  reads=[self.rmask, fb], writes=[b])
            em.op("pool", lambda e: e.tensor_copy(sc[:, 0, 0:1], b[:, 8:9]), reads=[b], writes=[sc])
            em.op("pool", lambda e: e.tensor_copy(sc[:, 0, 1:33], b[:, 48:T:64]), reads=[b], writes=[sc])
            em.op("pool", lambda e: e.tensor_copy(bl[:, 0:1], b[:, 15:16]), reads=[b], writes=[bl])
            em.op("pool", lambda e: e.tensor_copy(bl[:, 1:33], b[:, 79:T:64]), reads=[b], writes=[bl])
            em.op("dve", lambda e: e.tensor_sub(out=d1[:, 0:16], in0=b[:, 0:16], in1=sc[:, 0, 0:1].to_broadcast([128, 16])),
                  reads=[b, sc], writes=[d1])
            em.op("dve", lambda e: e.tensor_sub(out=d1[:, 16:T].rearrange("p (c j) -> p c j", j=64),
                                                in0=b[:, 16:T].rearrange("p (c j) -> p c j", j=64),
                                                in1=sc[:, 0, 1:33].unsqueeze(2).to_broadcast([128, 32, 64])),
                  reads=[b, sc], writes=[d1])
            em.op("act", lambda e: e.activation(out=sc[:, 1, :], in_=sc[:, 0, :], func=AF.Exp), reads=[sc], writes=[sc])
            em.op("act", lambda e: e.activation(out=sc[:, 3, :], in_=bl[:], func=AF.Exp), reads=[bl, sc], writes=[sc])
            em.op("dve", lambda e: e.tensor_sub(out=bl[:], in0=bl[:], in1=sc[:, 0, :]), reads=[bl, sc], writes=[bl])
            em.op("act", lambda e: e.activation(out=sc[:, 2, :], in_=bl[:], func=AF.Exp), reads=[bl, sc], writes=[sc])
            em.op("act", lambda e: e.activation(out=fb[:], in_=d1[:], func=AF.Exp), reads=[d1, fb], writes=[fb])
            em.op("dve", lambda e: e.tensor_mul(out=qtl[:], in0=qs[:], in1=fb[:]), reads=[qs, fb], writes=[qtl])
            em.op("act", lambda e: e.activation(out=d1[:], in_=d1[:], func=AF.Exp, scale=-1.0), reads=[d1], writes=[d1])
            em.op("pool", lambda e: e.tensor_mul(out=ktl[:], in0=kk[:], in1=d1[:]), reads=[kk, d1], writes=[ktl])
            for i0 in range(0, NT, 8):
                ii = list(range(i0, min(NT, i0 + 8)))
                for i in ii:
                    r0, n = trng(i)
                    em.op("pe", lambda e, i=i, r0=r0, n=n: e.transpose(ptk[0:n, i - i0, :], ktl[:, r0:r0 + n], self.idb[:, :]),
                          reads=[ktl, self.idb], writes=[ptk])
                if i0 == 0:
                    em.op("act", lambda e: e.activation(out=ktok[0:16, 0, :], in_=ptk[0:16, 0, :], func=AF.Copy), reads=[ptk], writes=[ktok])
                    em.op("act", lambda e: e.activation(out=ktok[:, 1:8, :], in_=ptk[:, 1:8, :], func=AF.Copy), reads=[ptk], writes=[ktok])
                else:
                    em.op("act", lambda e, i0=i0, m=len(ii): e.activation(out=ktok[:, i0:i0 + m, :], in_=ptk[:, 0:m, :], func=AF.Copy),
                          reads=[ptk], writes=[ktok])
            Sb = Sb_r.next()
            St = St_r.next()
            em.op("pool", lambda e: e.memset(Sb[:], 0.0), writes=[Sb])
            em.op("pool", lambda e: e.memset(St[:], 0.0), writes=[St])
            return dict(h=h, g=g, qtl=qtl, ktl=ktl, ktok=ktok, vtok=vtok, sc=sc, oT=oT, Sb=Sb, St=St, pss=None)

        def tile_part(c, i):
            r0, n = trng(i)
            qtl, ktl, vtok, oT = c["qtl"], c["ktl"], c["vtok"], c["oT"]
            psa = hbank[c["g"]][0]
            am = am_r.next()
            pso = hbank[c["g"]][0]
            em.op("pe", lambda e: e.matmul(psa[0:n, 0:n], ktl[:, r0:r0 + n], qtl[:, r0:r0 + n], start=True, stop=True),
                  reads=[ktl, qtl], writes=[psa])
            em.op("dve", lambda e: e.tensor_tensor(out=am[0:n, 0:n], in0=psa[0:n, 0:n], in1=self.cmask[0:n, 0:n], op=ALU.mult),
                  reads=[psa, self.cmask], writes=[am])
            em.op("pe", lambda e: e.matmul(pso[:, 0:n], vtok[0:n, i, :], am[0:n, 0:n], start=True, stop=True),
                  reads=[vtok, am], writes=[pso])
            em.op("act", lambda e: e.activation(out=oT[:, r0:r0 + n], in_=pso[:, 0:n], func=AF.Copy), reads=[pso], writes=[oT])

        def chunk_list():
            out = []
            for i in range(NT):
                r0, n = trng(i)
                for ci, (p0, ncx) in enumerate([(0, 16)] if i == 0 else [(0, 64), (64, 64)]):
                    j = 0 if i == 0 else 1 + 2 * (i - 1) + ci
                    out.append((i, j, p0, ncx, r0 + p0))
            return out

        CH = chunk_list()

        def emit_pss(c, idx):
            i, j, p0, ncx, c0 = CH[idx]
            pss = hbank[c["g"]][1 + (idx % 2)]
            em.op("pe", lambda e: e.matmul(pss[:], c["ktok"][p0:p0 + ncx, i, :], c["vtok"][p0:p0 + ncx, i, :], start=True, stop=True),
                  reads=[c["ktok"], c["vtok"]], writes=[pss])

        def chunk_pe(c, idx):
            i, j, p0, ncx, c0 = CH[idx]
            psi = hbank[c["g"]][3]
            Sb = c["Sb"]
            em.op("pe", lambda e: e.matmul(psi[:, 0:ncx], Sb[:], c["qtl"][:, c0:c0 + ncx], start=True, stop=True),
                  reads=[Sb, c["qtl"]], writes=[psi])
            if idx + 1 < len(CH):
                emit_pss(c, idx + 1)

        def chunk_dve(c, idx):
            i, j, p0, ncx, c0 = CH[idx]
            sc, oT = c["sc"], c["oT"]
            pss = hbank[c["g"]][1 + (idx % 2)]
            psi = hbank[c["g"]][3]
            St = c["St"]
            if idx + 1 < len(CH):
                jn = CH[idx + 1][1]
                S2 = S_rs[c["g"]].next()
                em.op("dve", lambda e: e.scalar_tensor_tensor(out=S2[:], in0=pss[:], scalar=sc[:, 2, j:j + 1], in1=St[:],
                                                              op0=ALU.mult, op1=ALU.add), reads=[pss, sc, St], writes=[S2])
                Sb2 = Sb_r.next()
                St2 = St_r.next()
                em.op("dve", lambda e: e.tensor_scalar(out=Sb2[:], in0=S2[:], scalar1=sc[:, 1, jn:jn + 1], scalar2=None, op0=ALU.mult),
                      reads=[S2, sc], writes=[Sb2])
                em.op("dve", lambda e: e.tensor_scalar(out=St2[:], in0=S2[:], scalar1=sc[:, 3, jn:jn + 1], scalar2=None, op0=ALU.mult),
                      reads=[S2, sc], writes=[St2])
                c["Sb"], c["St"] = Sb2, St2
            em.op("dve", lambda e: e.tensor_add(out=oT[:, c0:c0 + ncx], in0=oT[:, c0:c0 + ncx], in1=psi[:, 0:ncx]),
                  reads=[oT, psi], writes=[oT])

        def post_head(c):
            h, oT = c["h"], c["oT"]
            gs = qs_r.next(); d1 = d_r.next(); kk = kk_r.next()
            em.dma("sp", gs[:], ZT[4096 + h * 128:4096 + (h + 1) * 128, :], writes=[gs], owner=gs)
            em.op("act", lambda e: e.activation(out=d1[:], in_=oT[:], func=AF.Square), reads=[oT], writes=[d1])
            avg = self.avg[128]
            for (c0, n) in CCH:
                em.op("pe", lambda e: e.matmul(pn[:, 0:n], avg[:], d1[:, c0:c0 + n], start=True, stop=True), reads=[avg, d1], writes=[pn])
                em.op("act", lambda e: e.activation(out=kk[:, c0:c0 + n], in_=pn[:, 0:n], func=AF.Ln, bias=epsb[:, 0:1]),
                      reads=[pn, epsb], writes=[kk])
            em.op("act", lambda e: e.activation(out=kk[:], in_=kk[:], func=AF.Exp, scale=-0.5), reads=[kk], writes=[kk])
            em.op("dve", lambda e: e.tensor_mul(out=kk[:], in0=kk[:], in1=oT[:]), reads=[kk, oT], writes=[kk])
            mo = mo_r.next()
            em.op("dve", lambda e: e.scalar_tensor_tensor(out=mo[:], in0=kk[:], scalar=rng[:, h:h + 1], in1=gs[:], op0=ALU.mult, op1=ALU.mult),
                  reads=[kk, rng, gs], writes=[mo])
            em.dma("sp", self.MIXT[h * 128:(h + 1) * 128, :], mo[:], reads=[mo], owner=mo)

        for g0 in range(0, 16, G):
            ctx = [pre_head(g0 + g, g) for g in range(G)]
            for c in ctx:
                emit_pss(c, 0)
            idx = 0
            for i in range(NT):
                for c in ctx:
                    tile_part(c, i)
                for _ in ([0] if i == 0 else [0, 1]):
                    for c in ctx:
                        chunk_pe(c, idx)
                    for c in ctx:
                        chunk_dve(c, idx)
                    idx += 1
            for c in ctx:
                post_head(c)
        em.end()

    def stage_out(self, s, l, W, last):
        em = self.em
        em.begin()
        Wb = em.sb("Wb", [128, 16, D], BF16)
        if last:
            self.fgain = em.sb("fgain", [128, D], F32, dma=True)
            em.dma("sp", self.fgain[:], self.final_gain.partition_broadcast(128), writes=[self.fgain], owner=self.fgain)
        w32r = em.ring("wo32", 2, [128, D], F32, dma=True)
        for k in range(16):
            w32 = w32r.next()
            em.dma("sp", w32[:], W[k * 128:(k + 1) * 128, :], writes=[w32], owner=w32)
            em.op("pool", lambda e, k=k: e.tensor_copy(Wb[:, k, :], w32[:]), reads=[w32], writes=[Wb])
        mtr = em.ring("mt", 2, [128, 16, 128], BF16, dma=True)
        htr = em.ring("ht", 2, [128, D], F32, dma=True)
        hnr = em.ring("hn", 2, [128, D], F32, dma=True)
        pmm = em.psring("pmm", 4, [128, 512], F32)
        junk = em.sb("junk", [128, D], BF16)
        ssr = em.ring("ss", 2, [128, 2], F32)
        tmr = em.ring("tm", 2, [128, 2], F32)
        for i in range(NT):
            r0, n = trng(i)
            if last and i == 0:
                continue
            mt = mtr.next()
            ht = htr.next()
            hn = hnr.next()
            em.dma("sp", mt[:, :, 0:n], self.MIXT[:, r0:r0 + n].rearrange("(k p) t -> p k t", p=128), writes=[mt], owner=mt)
            em.dma("sp", ht[0:n, :], self.h_src(s, l, i), writes=[ht], owner=ht)
            for c in range(4):
                ps = pmm.next()
                for k in range(16):
                    em.op("pe", lambda e, k=k, c=c: e.matmul(ps[0:n, :], mt[:, k, 0:n], Wb[:, k, c * 512:(c + 1) * 512],
                                                             start=(k == 0), stop=(k == 15)), reads=[mt, Wb], writes=[ps])
                em.op("dve", lambda e, c=c: e.tensor_add(out=hn[0:n, c * 512:(c + 1) * 512], in0=ht[0:n, c * 512:(c + 1) * 512], in1=ps[0:n, :]),
                      reads=[ht, ps], writes=[hn])
            if not last:
                em.dma("sp", self.H[r0:r0 + n, :], hn[0:n, :], reads=[hn], owner=hn)
            else:
                ss = ssr.next()
                tm = tmr.next()
                em.op("pool", lambda e: e.memset(ss[:], 0.0), writes=[ss])
                em.op("act", lambda e: e.activation(out=junk[0:n, :], in_=hn[0:n, :], func=AF.Square, accum_out=ss[0:n, 0:1]),
                      reads=[hn], writes=[junk, ss])
                self.rstd_rows(ss, n, D, RMS_EPS, tm)
                em.op("dve", lambda e: e.scalar_tensor_tensor(out=ht[0:n, :], in0=hn[0:n, :], scalar=ss[0:n, 1:2], in1=self.fgain[0:n, :],
                                                              op0=ALU.mult, op1=ALU.mult), reads=[hn, ss, self.fgain], writes=[ht])
                em.dma("sp", self.out[s, r0 - 16:r0 - 16 + n, :], ht[0:n, :], reads=[ht], owner=ht)
        em.end()


_CACHE = {}


def kernel(**inputs):
    x = np.ascontiguousarray(inputs["x"], dtype=np.float32)
    if "nc" not in _CACHE:
        _CACHE["nc"] = Prog().build()
    nc = _CACHE["nc"]
    oh = _bucket_onehot()
    names = ["meta_tokens", "norm_gain", "final_norm_gain", "rel_bias_table", "w_in_even", "conv_w", "conv_b",
             "conv_ln_gain", "conv_ln_bias", "kv_norm_gain", "w_uk", "w_uv", "w_out_even", "w_in_odd", "lb_logits",
             "rec_norm_gain", "w_out_odd"]
    shared = {k: np.ascontiguousarray(inputs[k], dtype=np.float32) for k in names}
    shared["c_oh"] = oh
    in_maps = []
    for c in range(NCORES):
        m = dict(shared)
        m["x"] = x[c * SEQ_PER_CORE:(c + 1) * SEQ_PER_CORE]
        in_maps.append(m)
    res = run_bass_kernel_spmd(nc, in_maps, core_ids=list(range(NCORES)))
    return np.concatenate([r["out"] for r in res.results], axis=0)
```

```python
import math
from contextlib import ExitStack

import numpy as np
import concourse.bass as bass
import concourse.mybir as mybir
from concourse.bass_utils import run_bass_kernel_spmd

F32 = mybir.dt.float32
BF16 = mybir.dt.bfloat16
AF = mybir.ActivationFunctionType
ALU = mybir.AluOpType
AX = mybir.AxisListType

NCORES = 8
SEQ_PER_CORE = 2
D = 2048
SEQ = 2048
NMETA = 16
T = SEQ + NMETA
NT = 17
DEPTH = 4
P_EVEN = 6480
RMS_EPS = 1e-6
LN_EPS = 1e-5
NEG = -1.0e30
NEG2 = -3.0e38
TOPK = 256


def trng(i):
    return (0, 16) if i == 0 else (16 + 128 * (i - 1), 128)


CCH = [(0, 16)] + [(16 + 512 * j, 512) for j in range(4)]


class Tk:
    __slots__ = ("name", "t", "w", "r", "dkey", "bank")

    def __init__(self, name, t=None):
        self.name = name
        self.t = t
        self.w = {}
        self.r = {}
        self.dkey = None
        self.bank = None

    def __getitem__(self, idx):
        return self.t[idx]


class Ring:
    def __init__(self, bufs):
        self.bufs = bufs
        self.i = 0

    def next(self):
        b = self.bufs[self.i % len(self.bufs)]
        self.i += 1
        return b


class Em:
    ENG = ("pe", "act", "dve", "pool", "sp")

    def __init__(self, nc, n_dsem=78):
        self.nc = nc
        self.top = ExitStack()
        self.eng = dict(pe=nc.tensor, act=nc.scalar, dve=nc.vector, pool=nc.gpsimd, sp=nc.sync)
        self.sems = {}
        self.val = {}
        for e in self.ENG:
            self.sems[e] = self.top.enter_context(nc.semaphore("es_" + e))
            self.val[e] = 0
        self.bar = self.top.enter_context(nc.semaphore("bar"))
        self.nbar = 0
        self.free_d = []
        for i in range(n_dsem):
            k = "D%d" % i
            self.sems[k] = self.top.enter_context(nc.semaphore("ds_%d" % i))
            self.val[k] = 0
            self.free_d.append(k)
        self.free_sw = []
        for i in range(10):
            k = "DS%d" % i
            self.sems[k] = self.top.enter_context(nc.semaphore("dsw_%d" % i))
            self.val[k] = 0
            self.free_sw.append(k)
        self.stage_sw = []
        self.seen = {e: {} for e in self.ENG}
        self.stage = None
        self.stage_d = []
        self.uid = 0
        self.nins = 0
        self.reg = {}

    def begin(self):
        self.stage = ExitStack()
        self.stage_d = []

    def end(self):
        self.barrier()
        self.stage.close()
        self.stage = None
        self.free_d.extend(self.stage_d)
        self.stage_d = []
        self.free_sw.extend(self.stage_sw)
        self.stage_sw = []

    def _nm(self, name):
        self.uid += 1
        return "%s_%d" % (name, self.uid)

    def sb(self, name, shape, dt=F32, dma=False, top=False):
        st = self.top if top else self.stage
        nm = self._nm(name)
        t = st.enter_context(self.nc.sbuf_tensor(nm, list(shape), dt))
        self.reg[name] = nm
        tk = Tk(name, t)
        if dma == "sw":
            k = self.free_sw.pop()
            tk.dkey = k
            if not top:
                self.stage_sw.append(k)
        elif dma:
            k = self.free_d.pop()
            tk.dkey = k
            if not top:
                self.stage_d.append(k)
        return tk

    def ring(self, name, n, shape, dt=F32, dma=False):
        return Ring([self.sb("%s%d" % (name, i), shape, dt, dma=dma) for i in range(n)])

    def ps(self, name, shape, dt=F32):
        t = self.stage.enter_context(self.nc.psum_tensor(self._nm(name), list(shape), dt))
        return Tk(name, t)

    def ps_views(self, name, n, sub_shape, dt=F32):
        t = self.stage.enter_context(self.nc.psum_tensor(self._nm(name), [128, n] + list(sub_shape), dt))
        bank = Tk(name + "_bank")
        views = [Tk("%s%d" % (name, i), t[:, i]) for i in range(n)]
        for v in views:
            v.bank = bank
        return views

    def psring(self, name, n, shape, dt=F32):
        return Ring([self.ps("%s%d" % (name, i), shape, dt) for i in range(n)])

    def _wait(self, eng, toks):
        need = {}
        for d in toks:
            for k, v in d.items():
                if v > need.get(k, 0):
                    need[k] = v
        seen = self.seen[eng]
        for k, v in need.items():
            if k == eng and eng == "pe":
                continue
            if seen.get(k, 0) >= v:
                continue
            self.eng[eng].wait_ge(self.sems[k], v)
            seen[k] = v

    @staticmethod
    def _deps(reads, writes):
        toks = []
        for t in reads:
            toks.append(t.w)
            if t.bank is not None:
                toks.append(t.bank.w)
        for t in writes:
            toks.append(t.w)
            toks.append(t.r)
            if t.bank is not None:
                toks.append(t.bank.r)
        return toks

    @staticmethod
    def _mark(k, v, reads, writes):
        for t in reads:
            if t.r.get(k, 0) < v:
                t.r[k] = v
            if t.bank is not None and t.bank.r.get(k, 0) < v:
                t.bank.r[k] = v
        for t in writes:
            t.w = {k: v}
            t.r = {}
            if t.bank is not None:
                t.bank.w = {k: v}

    def op(self, eng, fn, reads=(), writes=()):
        self._wait(eng, self._deps(reads, writes))
        ins = fn(self.eng[eng])
        self.val[eng] += 1
        ins.then_inc(self.sems[eng], 1)
        self._mark(eng, self.val[eng], reads, writes)
        self.nins += 1
        return ins

    def dma(self, q, out, in_, reads=(), writes=(), owner=None, **kw):
        self._wait(q, self._deps(reads, writes))
        ins = self.eng[q].dma_start(out=out, in_=in_, **kw)
        k = owner.dkey
        self.val[k] += 16
        ins.then_inc(self.sems[k], 16)
        self._mark(k, self.val[k], reads, writes)
        self.nins += 1
        return ins

    def barrier(self):
        self.nbar += 1
        for e in self.ENG:
            g = self.eng[e]
            if self.val[e] > 0:
                g.wait_ge(self.sems[e], self.val[e])
            if e == "sp":
                for k, v in self.val.items():
                    if k[0] == "D" and v > 0 and self.seen["sp"].get(k, 0) < v:
                        g.wait_ge(self.sems[k], v)
            g.sem_inc(self.bar, 1)
        for e in self.ENG:
            self.eng[e].wait_ge(self.bar, 5 * self.nbar)
        for e in self.ENG:
            for k, v in self.val.items():
                self.seen[e][k] = v

    def close(self):
        self.top.close()


def _t5_bucket_np(rel):
    rel = np.asarray(rel, dtype=np.int32)
    nb = 16
    ret = np.where(rel > 0, nb, 0).astype(np.int32)
    n = np.abs(rel)
    max_exact = nb // 2
    nf = np.maximum(n, 1).astype(np.float32)
    large = max_exact + (np.log(nf / np.float32(max_exact)) / np.float32(math.log(128 / max_exact))
                         * np.float32(nb - max_exact)).astype(np.int32)
    large = np.minimum(large, nb - 1)
    return ret + np.where(n < max_exact, n, large)


def _bucket_onehot():
    rel = 255 - np.arange(512)
    b = _t5_bucket_np(rel)
    oh = np.zeros((32, 512), np.float32)
    oh[b, np.arange(512)] = 1.0
    return oh


class Prog:
    def __init__(self, nseq=SEQ_PER_CORE, layers=DEPTH, dbg=False, stop=10 ** 9):
        self.stop = stop
        self.nstage = 0
        self.nseq = nseq
        self.layers = layers
        self.dbg = dbg if dbg else ()
        ne = max(1, (layers + 1) // 2)
        no = layers // 2
        od = (lambda *sh: [max(no, 1)] + ([1] * len(sh) if no == 0 else list(sh)))
        nc = bass.Bass("TRN2", target_bir_lowering=False)
        self.nc = nc
        dt = nc.dram_tensor
        I = "ExternalInput"
        self.x = dt("x", [nseq, SEQ, D], F32, kind=I).ap()
        self.meta = dt("meta_tokens", [NMETA, D], F32, kind=I).ap()
        self.norm_gain = dt("norm_gain", [4, D], F32, kind=I).ap()
        self.final_gain = dt("final_norm_gain", [D], F32, kind=I).ap()
        self.relb = dt("rel_bias_table", [32, 8], F32, kind=I).ap()
        self.w_in_even = dt("w_in_even", [ne, D, P_EVEN], F32, kind=I).ap()
        self.conv_w = dt("conv_w", [2, 31, 1024], F32, kind=I).ap()
        self.conv_b = dt("conv_b", [2, 1024], F32, kind=I).ap()
        self.ln_g = dt("conv_ln_gain", [2, 1024], F32, kind=I).ap()
        self.ln_b = dt("conv_ln_bias", [2, 1024], F32, kind=I).ap()
        self.kv_g = dt("kv_norm_gain", [2, 256], F32, kind=I).ap()
        self.w_uk = dt("w_uk", [ne, 8, 256, 128], F32, kind=I).ap()
        self.w_uv = dt("w_uv", [ne, 8, 256, 128], F32, kind=I).ap()
        self.w_out_even = dt("w_out_even", [ne, D, D], F32, kind=I).ap()
        self.w_in_odd = dt("w_in_odd", od(D, 4 * D), F32, kind=I).ap()
        self.lb_logits = dt("lb_logits", [4, D], F32, kind=I).ap()
        self.rec_g = dt("rec_norm_gain", [2, D], F32, kind=I).ap()
        self.w_out_odd = dt("w_out_odd", od(D, D), F32, kind=I).ap()
        self.c_oh = dt("c_oh", [32, 512], F32, kind=I).ap()
        self.out = dt("out", [nseq, SEQ, D], F32, kind="ExternalOutput").ap()
        sk = lambda n: "ExternalOutput" if n in self.dbg else "Internal"
        self.H = dt("Hs", [T, D], F32, kind=sk("Hs")).ap()
        self.ZT = dt("ZTs", [4 * D, T], F32, kind=sk("ZTs")).ap()
        self.VTOK = dt("VTOKs", [T, D], BF16, kind=sk("VTOKs")).ap()
        self.ZB = dt("ZBs", [2176, T], BF16, kind=sk("ZBs")).ap()
        self.MIXT = dt("MIXTs", [D, T], BF16, kind=sk("MIXTs")).ap()
        self.FD = dt("FDs", [8, 512], F32, kind=sk("FDs")).ap()
        self.em = Em(nc)

    def build(self):
        em = self.em

        def run(f, *a, **k):
            if self.nstage < self.stop:
                f(*a, **k)
            self.nstage += 1

        run(self.setup_consts)
        for s in range(self.nseq):
            for l in range(self.layers):
                last = (l == self.layers - 1)
                if l % 2 == 0:
                    run(self.stage_norm_proj, s, l, even=True)
                    run(self.stage_conv, l // 2)
                    run(self.stage_attn, l // 2)
                    run(self.stage_out, s, l, self.w_out_even[l // 2], last)
                else:
                    run(self.stage_norm_proj, s, l, even=False)
                    run(self.stage_rec, l)
                    run(self.stage_out, s, l, self.w_out_odd[l // 2], last)
        em.close()
        return self.nc

    def vecT(self, dst, dst_cols, src_rows_ap, nrows, ps, stg):
        em = self.em
        em.dma("sp", stg[0:nrows, :], src_rows_ap, writes=[stg], owner=stg)
        em.op("pe", lambda e: e.transpose(ps[:, 0:nrows], stg[0:nrows, :], self.idf[0:nrows, 0:nrows]),
              reads=[stg, self.idf], writes=[ps])
        em.op("act", lambda e: e.activation(out=dst[:, dst_cols:dst_cols + nrows], in_=ps[:, 0:nrows], func=AF.Copy),
              reads=[ps], writes=[dst])

    def setup_consts(self):
        em = self.em
        nc = self.nc
        self.idf = em.sb("idf", [128, 128], F32, top=True)
        self.idb = em.sb("idb", [128, 128], BF16, top=True)
        self.onesb = em.sb("onesb", [128, 128], BF16, top=True)
        self.avg = {}
        for n in (128, 256, 1024):
            self.avg[n] = em.sb("avg%d" % n, [128, 128], F32, top=True)
        self.cmask = em.sb("cmask", [128, 128], F32, top=True)
        self.EB = em.sb("EB", [128, 3, 8, 128], F32, top=True)
        self.bfar = em.sb("bfar", [128, 8], F32, top=True, dma=True)
        self.lbv = em.sb("lbv", [128, 2, 16], F32, top=True)
        self.oml = em.sb("oml", [128, 2, 16], F32, top=True)
        self.noml = em.sb("noml", [128, 2, 16], F32, top=True)

        em.begin()
        P = lambda f, **k: em.op("pool", f, **k)
        P(lambda e: e.memset(self.idf[:], 1.0), writes=[self.idf])
        P(lambda e: e.affine_select(out=self.idf[:], in_=self.idf[:], pattern=[[-1, 128]], compare_op=ALU.is_equal,
                                    fill=0.0, base=0, channel_multiplier=1), reads=[self.idf], writes=[self.idf])
        P(lambda e: e.tensor_copy(self.idb[:], self.idf[:]), reads=[self.idf], writes=[self.idb])
        P(lambda e: e.memset(self.onesb[:], 1.0), writes=[self.onesb])
        for n in (128, 256, 1024):
            P(lambda e, n=n: e.memset(self.avg[n][:], 1.0 / n), writes=[self.avg[n]])
        P(lambda e: e.memset(self.cmask[:], 1.0), writes=[self.cmask])
        P(lambda e: e.affine_select(out=self.cmask[:], in_=self.cmask[:], pattern=[[1, 128]], compare_op=ALU.is_ge,
                                    fill=0.0, base=0, channel_multiplier=-1), reads=[self.cmask], writes=[self.cmask])
        P(lambda e: e.memset(self.cmask[0:64, 64:128], 0.0), writes=[self.cmask])

        anti = em.sb("anti", [128, 128], F32)
        P(lambda e: e.memset(anti[:], 1.0), writes=[anti])
        P(lambda e: e.affine_select(out=anti[:], in_=anti[:], pattern=[[1, 128]], compare_op=ALU.is_equal,
                                    fill=0.0, base=-127, channel_multiplier=1), reads=[anti], writes=[anti])
        tab = em.sb("tab", [32, 8], F32, dma=True)
        oh = em.sb("oh", [32, 512], F32, dma=True)
        fsb = em.sb("fsb", [8, 512], F32, dma=True)
        nbfar = em.sb("nbfar", [128, 8], F32)
        psA = em.ps("psA", [128, 512], F32)
        psB = em.ps("psB", [128, 128], F32)
        em.dma("sp", tab[:], self.relb[:, :], writes=[tab], owner=tab)
        em.dma("sp", oh[:], self.c_oh[:, :], writes=[oh], owner=oh)
        em.dma("sp", self.bfar[:], bass.AP(tensor=self.relb.tensor, offset=15 * 8, ap=[[0, 128], [1, 8]]),
               writes=[self.bfar], owner=self.bfar)
        em.op("dve", lambda e: e.tensor_scalar(out=nbfar[:], in0=self.bfar[:], scalar1=-1.0, scalar2=None, op0=ALU.mult),
              reads=[self.bfar], writes=[nbfar])
        em.op("pe", lambda e: e.matmul(psA[0:8, :], tab[0:32, 0:8], oh[0:32, :], start=True, stop=True),
              reads=[tab, oh], writes=[psA])
        em.op("act", lambda e: e.activation(out=fsb[:], in_=psA[0:8, :], func=AF.Copy), reads=[psA], writes=[fsb])
        fd_tk = Tk("FD")
        em.dma("sp", self.FD[:, :], fsb[:], reads=[fsb], writes=[fd_tk], owner=fsb)
        hk = em.ring("hk", 2, [128, 128], F32, dma=True)
        for ty, c0 in enumerate((128, 256, 144)):
            for h in range(8):
                hb = hk.next()
                src = bass.AP(tensor=self.FD.tensor, offset=h * 512 + c0, ap=[[1, 128], [1, 128]])
                em.dma("sp", hb[:], src, reads=[fd_tk], writes=[hb], owner=hb)
                em.op("pe", lambda e, hb=hb: e.matmul(psB[:], anti[:], hb[:], start=True, stop=True),
                      reads=[anti, hb], writes=[psB])
                em.op("act", lambda e, ty=ty, h=h: e.activation(out=self.EB[:, ty, h, :], in_=psB[:], func=AF.Exp,
                                                                bias=nbfar[:, h:h + 1]),
                      reads=[psB, nbfar], writes=[self.EB])

        stg = em.sb("stg", [128, 128], F32, dma=True)
        lbT = em.sb("lbT", [128, 64], F32)
        self.vecT(lbT, 0, self.lb_logits.rearrange("l (c p) -> (l c) p", p=128), 64, psB, stg)
        mx = em.sb("mx", [128, 16], F32)
        ex = em.sb("ex", [128, 4, 16], F32)
        sm = em.sb("sm", [128, 16], F32)
        rs = em.sb("rs", [128, 16], F32)
        c1 = em.sb("c1", [128, 16], F32)
        V = lambda f, **k: em.op("dve", f, **k)
        V(lambda e: e.tensor_max(out=mx[:], in0=lbT[:, 0:16], in1=lbT[:, 16:32]), reads=[lbT], writes=[mx])
        V(lambda e: e.tensor_max(out=mx[:], in0=mx[:], in1=lbT[:, 32:48]), reads=[lbT, mx], writes=[mx])
        V(lambda e: e.tensor_max(out=mx[:], in0=mx[:], in1=lbT[:, 48:64]), reads=[lbT, mx], writes=[mx])
        for l in range(4):
            V(lambda e, l=l: e.tensor_sub(out=ex[:, l, :], in0=lbT[:, 16 * l:16 * l + 16], in1=mx[:]),
              reads=[lbT, mx], writes=[ex])
        em.op("act", lambda e: e.activation(out=ex[:], in_=ex[:], func=AF.Exp), reads=[ex], writes=[ex])
        V(lambda e: e.tensor_add(out=sm[:], in0=ex[:, 0, :], in1=ex[:, 1, :]), reads=[ex], writes=[sm])
        V(lambda e: e.tensor_add(out=sm[:], in0=sm[:], in1=ex[:, 2, :]), reads=[ex, sm], writes=[sm])
        V(lambda e: e.tensor_add(out=sm[:], in0=sm[:], in1=ex[:, 3, :]), reads=[ex, sm], writes=[sm])
        V(lambda e: e.reciprocal(out=rs[:], in_=sm[:]), reads=[sm], writes=[rs])
        V(lambda e: e.tensor_mul(out=self.lbv[:, 0, :], in0=ex[:, 1, :], in1=rs[:]), reads=[ex, rs], writes=[self.lbv])
        V(lambda e: e.tensor_add(out=c1[:], in0=ex[:, 1, :], in1=ex[:, 2, :]), reads=[ex], writes=[c1])
        V(lambda e: e.tensor_add(out=c1[:], in0=c1[:], in1=ex[:, 3, :]), reads=[ex, c1], writes=[c1])
        V(lambda e: e.tensor_mul(out=self.lbv[:, 1, :], in0=c1[:], in1=rs[:]), reads=[c1, rs], writes=[self.lbv])
        V(lambda e: e.tensor_scalar(out=self.oml[:], in0=self.lbv[:], scalar1=-1.0, scalar2=1.0, op0=ALU.mult, op1=ALU.add),
          reads=[self.lbv], writes=[self.oml])
        V(lambda e: e.tensor_scalar(out=self.noml[:], in0=self.oml[:], scalar1=-1.0, scalar2=None, op0=ALU.mult),
          reads=[self.oml], writes=[self.noml])
        em.end()

    def h_src(self, s, l, i):
        r0, n = trng(i)
        if l == 0:
            if i == 0:
                return self.meta[:, :]
            return self.x[s, r0 - 16:r0 - 16 + n, :]
        return self.H[r0:r0 + n, :]

    def rstd_rows(self, ss, n, dim, eps, tmp):
        em = self.em
        em.op("dve", lambda e: e.tensor_scalar(out=tmp[0:n, 0:1], in0=ss[0:n, 0:1], scalar1=1.0 / dim, scalar2=eps,
                                               op0=ALU.mult, op1=ALU.add), reads=[ss], writes=[tmp])
        em.op("act", lambda e: e.activation(out=tmp[0:n, 1:2], in_=tmp[0:n, 0:1], func=AF.Sqrt), reads=[tmp], writes=[tmp])
        em.op("dve", lambda e: e.reciprocal(out=ss[0:n, 1:2], in_=tmp[0:n, 1:2]), reads=[tmp], writes=[ss])

    def stage_norm_proj(self, s, l, even):
        em = self.em
        em.begin()
        hnT = em.sb("hnT", [128, 16, T], BF16)
        outer = em.stage
        nrm = ExitStack()
        em.stage = nrm
        gbc = em.sb("gbc", [128, D], F32, dma=True)
        em.dma("sp", gbc[:], self.norm_gain[l, :].partition_broadcast(128), writes=[gbc], owner=gbc)
        htr = em.ring("ht", 4, [128, D], F32, dma=True)
        hsr = em.ring("hs", 3, [128, D], BF16)
        junk = em.sb("junk", [128, D], BF16)
        ssr = em.ring("ss", 4, [128, 2], F32)
        tmr = em.ring("tm", 4, [128, 2], F32)
        ptr = em.psring("ptr", 2, [128, 16, 128], BF16)
        for i in range(NT):
            r0, n = trng(i)
            ht = htr.next()
            hs = hsr.next()
            ss = ssr.next()
            tm = tmr.next()
            pt = ptr.next()
            em.dma("sp", ht[0:n, :], self.h_src(s, l, i), writes=[ht], owner=ht)
            em.op("pool", lambda e: e.memset(ss[:], 0.0), writes=[ss])
            em.op("act", lambda e: e.activation(out=junk[0:n, :], in_=ht[0:n, :], func=AF.Square, accum_out=ss[0:n, 0:1]),
                  reads=[ht], writes=[junk, ss])
            self.rstd_rows(ss, n, D, RMS_EPS, tm)
            em.op("dve", lambda e: e.scalar_tensor_tensor(out=hs[0:n, :], in0=ht[0:n, :], scalar=ss[0:n, 1:2],
                                                          in1=gbc[0:n, :], op0=ALU.mult, op1=ALU.mult),
                  reads=[ht, ss, gbc], writes=[hs])
            for k in range(16):
                em.op("pe", lambda e, k=k: e.transpose(pt[:, k, 0:n], hs[0:n, k * 128:(k + 1) * 128], self.idb[0:n, 0:n]),
                      reads=[hs, self.idb], writes=[pt])
            em.op("act", lambda e: e.activation(out=hnT[:, :, r0:r0 + n], in_=pt[:, :, 0:n], func=AF.Copy),
                  reads=[pt], writes=[hnT])
        self.hnT = hnT
        em.barrier()
        nrm.close()
        em.stage = outer
        if even:
            self.proj_even(l // 2)
        else:
            self.proj_odd(l // 2)
        em.end()

    def proj_fm(self, W, chunks, wtr32, wtr, pmm, otr, otbr=None):
        em = self.em
        hnT = self.hnT
        blocks = []
        for ch in chunks:
            col0, ncols, func, scale, dst0, dup = ch
            if (blocks and not dup and ncols == 128 and len(blocks[-1]) < 4 and not blocks[-1][-1][5]
                    and blocks[-1][-1][1] == 128 and blocks[-1][-1][0] + 128 == col0):
                blocks[-1].append(ch)
            else:
                blocks.append([ch])
        for blk in blocks:
            w32 = wtr32.next()
            wb = wtr.next()
            bcol0 = blk[0][0]
            bn = sum(c[1] for c in blk)
            em.dma("sp", w32[:, :, 0:bn], W[:, bcol0:bcol0 + bn].rearrange("(kc p) m -> p kc m", p=128),
                   writes=[w32], owner=w32)
            if blk[0][5]:
                em.dma("sp", w32[:, :, bn:2 * bn], W[:, bcol0:bcol0 + bn].rearrange("(kc p) m -> p kc m", p=128),
                       writes=[w32], owner=w32)
            for bi, (col0, ncols, func, scale, dst0, dup) in enumerate(blk):
                o = col0 - bcol0
                nm = ncols * (2 if dup else 1)
                em.op("pool", lambda e, o=o, nm=nm: e.tensor_copy(wb[:, :, o:o + nm], w32[:, :, o:o + nm]), reads=[w32], writes=[wb])
            for bi, (col0, ncols, func, scale, dst0, dup) in enumerate(blk):
                o = col0 - bcol0
                nm = ncols * (2 if dup else 1)
                tobf = dst0 < 0
                ot = otbr.next() if tobf else otr.next()
                for (c0, n) in CCH:
                    ps = pmm.next()
                    for k in range(16):
                        em.op("pe", lambda e, k=k: e.matmul(ps[0:nm, 0:n], wb[:, k, o:o + nm], hnT[:, k, c0:c0 + n],
                                                            start=(k == 0), stop=(k == 15)),
                              reads=[wb, hnT], writes=[ps])
                    em.op("act", lambda e: e.activation(out=ot[0:nm, c0:c0 + n], in_=ps[0:nm, 0:n], func=func, scale=scale),
                          reads=[ps], writes=[ot])
                if tobf:
                    r0 = -dst0 - 1
                    em.dma("act", self.ZB[r0:r0 + nm, :], ot[0:nm, :], reads=[ot], owner=ot)
                else:
                    em.dma("act", self.ZT[dst0:dst0 + nm, :], ot[0:nm, :], reads=[ot], owner=ot)

    ZE = dict(glu_v=0, glu_g=1024, gate_a=2048, q=3072, c=4096, gate_b=4352, qi=5376, ki=6400, wi=6528)

    def proj_even(self, e_):
        em = self.em
        W = self.w_in_even[e_]
        chunks = []
        for m in range(50):
            col0 = m * 128
            if col0 < 1024:
                f, sc = AF.Copy, 1.0
            elif col0 < 2048:
                f, sc = AF.Sigmoid, 1.0
            elif col0 < 3072:
                f, sc = AF.Silu, 1.0
            elif col0 < 4352:
                f, sc = AF.Copy, 1.0
            elif col0 < 5376:
                f, sc = AF.Silu, 1.0
            else:
                f, sc = AF.Copy, 0.125
            dst = col0
            if 3072 <= col0 < 4096:
                dst = -(col0 - 3072) - 1
            elif 5376 <= col0 < 6400:
                dst = -(1024 + col0 - 5376) - 1
            chunks.append((col0, 128, f, sc, dst, False))
        chunks.append((6400, 64, AF.Copy, 1.0, -2048 - 1, True))
        chunks.append((6464, 16, AF.Copy, 0.25, 6528, False))
        wtr32 = em.ring("w32", 2, [128, 16, 512], F32, dma=True)
        wtr = em.ring("wb", 2, [128, 16, 512], BF16)
        pmm = em.psring("pmm", 4, [128, 512], F32)
        otr = em.ring("ot", 2, [128, T], F32, dma=True)
        otbr = em.ring("otb", 2, [128, T], BF16, dma=True)
        self.proj_fm(W, chunks, wtr32, wtr, pmm, otr, otbr)

    def proj_odd(self, o_):
        em = self.em
        W = self.w_in_odd[o_]
        chunks = []
        for m in range(16):
            chunks.append((m * 128, 128, AF.Silu, 1.0, m * 128, False))
        for m in range(16):
            chunks.append((2048 + m * 128, 128, AF.Sigmoid, 1.0, 2048 + m * 128, False))
        for m in range(16):
            chunks.append((6144 + m * 128, 128, AF.Silu, 1.0, 4096 + m * 128, False))
        wtr32 = em.ring("w32", 2, [128, 16, 512], F32, dma=True)
        wtr = em.ring("wb", 2, [128, 16, 512], BF16)
        pmm = em.psring("pmm", 4, [128, 512], F32)
        otr = em.ring("ot", 2, [128, T], F32, dma=True)
        self.proj_fm(W, chunks, wtr32, wtr, pmm, otr)
        hnT = self.hnT
        vo = em.ring("vo", 2, [128, 512], BF16, dma=True)
        for g in range(4):
            w32 = wtr32.next()
            wb = wtr.next()
            em.dma("sp", w32[:], W[:, 4096 + g * 512:4096 + (g + 1) * 512].rearrange("(kc p) m -> p kc m", p=128),
                   writes=[w32], owner=w32)
            for k4 in range(4):
                em.op("pool", lambda e, k4=k4: e.tensor_copy(wb[:, 4 * k4:4 * k4 + 4, :], w32[:, 4 * k4:4 * k4 + 4, :]), reads=[w32], writes=[wb])
            for i in range(NT):
                r0, n = trng(i)
                ps = pmm.next()
                for k in range(16):
                    em.op("pe", lambda e, k=k: e.matmul(ps[0:n, :], hnT[:, k, r0:r0 + n], wb[:, k, :],
                                                        start=(k == 0), stop=(k == 15)),
                          reads=[wb, hnT], writes=[ps])
                v = vo.next()
                em.op("act", lambda e: e.activation(out=v[0:n, :], in_=ps[0:n, :], func=AF.Copy), reads=[ps], writes=[v])
                em.dma("act", self.VTOK[r0:r0 + n, g * 512:(g + 1) * 512], v[0:n, :], reads=[v], owner=v)

    def stage_conv(self, e_):
        em = self.em
        ZE = self.ZE
        em.begin()
        stg = em.sb("stg", [128, 128], F32, dma=True)
        pst = em.ps("pst", [128, 128], F32)
        cw = em.sb("cw", [128, 248], F32)
        cv = em.sb("cv", [128, 24], F32)
        cwr = self.conv_w[e_].rearrange("j (cc p) -> (j cc) p", p=128)
        self.vecT(cw, 0, cwr[0:124, :], 124, pst, stg)
        self.vecT(cw, 124, cwr[124:248, :], 124, pst, stg)
        self.vecT(cv, 0, self.conv_b[e_].rearrange("(cc p) -> cc p", p=128), 8, pst, stg)
        self.vecT(cv, 8, self.ln_g[e_].rearrange("(cc p) -> cc p", p=128), 8, pst, stg)
        self.vecT(cv, 16, self.ln_b[e_].rearrange("(cc p) -> cc p", p=128), 8, pst, stg)
        uall = em.sb("uall", [128, 8, T], F32)
        acc1 = em.sb("acc1", [128, T], F32)
        acc2 = em.sb("acc2", [128, T], F32)
        gsr = em.ring("gs", 1, [128, T], F32, dma=True)
        upr = em.ring("up", 2, [128, 30 + T], F32, dma=True)
        pa = em.ring("pa", 1, [128, T], F32)
        pb = em.ring("pb", 1, [128, T], F32)
        tmpr = em.ring("ctmp", 2, [128, T], F32)
        dgr = em.ring("dg", 3, [128, 128], F32)
        pcs = [em.ps("pc%d" % i, [128, 512], F32) for i in range(5)]
        sq = em.ring("sq", 1, [128, T], F32)
        for b in upr.bufs:
            em.op("pool", lambda e, b=b: e.memset(b[:, 0:30], 0.0), writes=[b])
        def load_u(cc):
            gs = gsr.next()
            up = upr.next()
            em.dma("sp", up[:, 30:30 + T], self.ZT[ZE["glu_v"] + cc * 128:ZE["glu_v"] + (cc + 1) * 128, :], writes=[up], owner=up)
            em.dma("sp", gs[:], self.ZT[ZE["glu_g"] + cc * 128:ZE["glu_g"] + (cc + 1) * 128, :], writes=[gs], owner=gs)
            em.op("pool", lambda e: e.tensor_tensor(out=up[:, 30:30 + T], in0=up[:, 30:30 + T], in1=gs[:], op=ALU.mult),
                  reads=[up, gs], writes=[up])
            return up

        up_next = load_u(0)
        for cc in range(8):
            up = up_next
            A = pa.next()
            B = pb.next()
            w = lambda j: cw[:, j * 8 + cc:j * 8 + cc + 1]
            em.op("dve", lambda e: e.tensor_scalar(out=A[:], in0=up[:, 0:T], scalar1=w(0), scalar2=cv[:, cc:cc + 1],
                                                   op0=ALU.mult, op1=ALU.add), reads=[up, cw, cv], writes=[A])
            for j in range(1, 10):
                em.op("dve", lambda e, j=j: e.scalar_tensor_tensor(out=A[:], in0=up[:, j:j + T], scalar=w(j), in1=A[:],
                                                                   op0=ALU.mult, op1=ALU.add), reads=[up, cw, A], writes=[A])
            em.op("act", lambda e: e.activation(out=B[:], in_=up[:, 10:10 + T], func=AF.Copy, scale=w(10)),
                  reads=[up, cw], writes=[B])
            for j in range(11, 17):
                tp = tmpr.next()
                em.op("act", lambda e, j=j, tp=tp: e.activation(out=tp[:], in_=up[:, j:j + T], func=AF.Copy, scale=w(j)),
                      reads=[up, cw], writes=[tp])
                em.op("pool", lambda e, tp=tp: e.tensor_add(out=B[:], in0=B[:], in1=tp[:]), reads=[tp, B], writes=[B])
            if cc + 1 < 8:
                up_next = load_u(cc + 1)
            for j in range(17, 31):
                dg = dgr.next()
                em.op("act", lambda e, j=j, dg=dg: e.activation(out=dg[:], in_=self.idf[:], func=AF.Copy, scale=w(j)),
                      reads=[self.idf, cw], writes=[dg])
                for ci, (c0, n) in enumerate(CCH):
                    em.op("pe", lambda e, j=j, dg=dg, ci=ci, c0=c0, n=n: e.matmul(pcs[ci][:, 0:n], dg[:], up[:, j + c0:j + c0 + n],
                                                                               start=(j == 17), stop=(j == 30)),
                          reads=[dg, up], writes=[pcs[ci]])
            em.op("dve", lambda e: e.tensor_add(out=uall[:, cc, :], in0=A[:], in1=B[:]), reads=[A, B], writes=[uall])
            for ci, (c0, n) in enumerate(CCH):
                em.op("dve", lambda e, ci=ci, c0=c0, n=n: e.tensor_add(out=uall[:, cc, c0:c0 + n], in0=uall[:, cc, c0:c0 + n], in1=pcs[ci][:, 0:n]),
                      reads=[uall, pcs[ci]], writes=[uall])
            s2 = sq.next()
            em.op("act", lambda e: e.activation(out=s2[:], in_=uall[:, cc, :], func=AF.Square), reads=[uall], writes=[s2])
            if cc == 0:
                em.op("pool", lambda e: e.tensor_copy(acc1[:], uall[:, cc, :]), reads=[uall], writes=[acc1])
                em.op("pool", lambda e: e.tensor_copy(acc2[:], s2[:]), reads=[s2], writes=[acc2])
            else:
                em.op("pool", lambda e: e.tensor_add(out=acc1[:], in0=acc1[:], in1=uall[:, cc, :]), reads=[uall, acc1], writes=[acc1])
                em.op("pool", lambda e: e.tensor_add(out=acc2[:], in0=acc2[:], in1=s2[:]), reads=[s2, acc2], writes=[acc2])
        mean = em.sb("mean", [128, T], F32)
        rstd = em.sb("rstd", [128, T], F32)
        p1 = em.ps("p1", [128, 512], F32)
        p2 = em.ps("p2", [128, 512], F32)
        avg = self.avg[1024]
        for (c0, n) in CCH:
            em.op("pe", lambda e: e.matmul(p1[:, 0:n], avg[:], acc1[:, c0:c0 + n], start=True, stop=True),
                  reads=[avg, acc1], writes=[p1])
            em.op("pe", lambda e: e.matmul(p2[:, 0:n], avg[:], acc2[:, c0:c0 + n], start=True, stop=True),
                  reads=[avg, acc2], writes=[p2])
            em.op("act", lambda e: e.activation(out=mean[:, c0:c0 + n], in_=p1[:, 0:n], func=AF.Copy), reads=[p1], writes=[mean])
            em.op("dve", lambda e: e.tensor_tensor(out=rstd[:, c0:c0 + n], in0=mean[:, c0:c0 + n], in1=mean[:, c0:c0 + n], op=ALU.mult),
                  reads=[mean], writes=[rstd])
            em.op("dve", lambda e: e.tensor_sub(out=rstd[:, c0:c0 + n], in0=p2[:, 0:n], in1=rstd[:, c0:c0 + n]),
                  reads=[p2, rstd], writes=[rstd])
            em.op("dve", lambda e: e.tensor_scalar(out=rstd[:, c0:c0 + n], in0=rstd[:, c0:c0 + n], scalar1=LN_EPS, scalar2=None, op0=ALU.add),
                  reads=[rstd], writes=[rstd])
        em.op("act", lambda e: e.activation(out=rstd[:], in_=rstd[:], func=AF.Sqrt), reads=[rstd], writes=[rstd])
        em.op("dve", lambda e: e.reciprocal(out=rstd[:], in_=rstd[:]), reads=[rstd], writes=[rstd])
        mxr = em.ring("mx", 2, [128, T], BF16, dma=True)
        for cc in range(8):
            ga = gsr.next()
            t1 = pa.next()
            t2 = pb.next()
            mx = mxr.next()
            em.dma("sp", ga[:], self.ZT[ZE["gate_a"] + cc * 128:ZE["gate_a"] + (cc + 1) * 128, :], writes=[ga], owner=ga)
            em.op("dve", lambda e: e.tensor_sub(out=t1[:], in0=uall[:, cc, :], in1=mean[:]), reads=[uall, mean], writes=[t1])
            em.op("pool", lambda e: e.tensor_mul(out=t1[:], in0=t1[:], in1=rstd[:]), reads=[t1, rstd], writes=[t1])
            em.op("act", lambda e: e.activation(out=t2[:], in_=t1[:], func=AF.Silu, scale=cv[:, 8 + cc:9 + cc], bias=cv[:, 16 + cc:17 + cc]),
                  reads=[t1, cv], writes=[t2])
            em.op("dve", lambda e: e.tensor_mul(out=mx[:], in0=t2[:], in1=ga[:]), reads=[t2, ga], writes=[mx])
            em.dma("sp", self.MIXT[cc * 128:(cc + 1) * 128, :], mx[:], reads=[mx], owner=mx)
        em.end()

    def stage_attn(self, e_):
        em = self.em
        ZE = self.ZE
        ZT = self.ZT
        em.begin()
        cnT = em.sb("cnT", [128, 2, T], BF16)
        cnk = em.sb("cnk", [128, NT, 256], BF16)
        wukT = em.sb("wukT", [128, 8, 256], BF16)
        wuvb = em.sb("wuvb", [128, 8, 2, 128], BF16)
        kiT2 = em.sb("kiT2", [128, T], BF16, dma=True)
        wiTok = em.sb("wiTok", [128, NT, 16], F32)
        kvg = em.sb("kvg", [128, 2], F32)
        stg = em.sb("stg", [128, 128], F32, dma=True)

        prep = ExitStack()
        stage_outer = em.stage
        em.stage = prep
        pA = em.ps("pA", [128, 512], F32)
        pB = em.ps("pB", [128, 512], F32)
        pT = em.ps("pT", [128, 128], F32)
        pTb = em.ps("pTb", [128, 2, 128], BF16)
        big = em.sb("big", [128, 2, T], F32, dma=True)
        big2 = em.sb("big2", [128, T], F32)
        rsd = em.sb("rsd", [128, T], F32)
        self.vecT(kvg, 0, self.kv_g[e_].rearrange("(cc p) -> cc p", p=128), 2, pT, stg)
        em.dma("sp", big[:], ZT[ZE["c"]:ZE["c"] + 256, :].rearrange("(cc p) t -> p cc t", p=128), writes=[big], owner=big)
        em.op("act", lambda e: e.activation(out=big2[:], in_=big[:, 0, :], func=AF.Square), reads=[big], writes=[big2])
        em.op("act", lambda e: e.activation(out=rsd[:], in_=big[:, 1, :], func=AF.Square), reads=[big], writes=[rsd])
        em.op("dve", lambda e: e.tensor_add(out=big2[:], in0=big2[:], in1=rsd[:]), reads=[big2, rsd], writes=[big2])
        avg = self.avg[256]
        for (c0, n) in CCH:
            em.op("pe", lambda e: e.matmul(pA[:, 0:n], avg[:], big2[:, c0:c0 + n], start=True, stop=True),
                  reads=[avg, big2], writes=[pA])
            em.op("dve", lambda e: e.tensor_scalar(out=rsd[:, c0:c0 + n], in0=pA[:, 0:n], scalar1=RMS_EPS, scalar2=None, op0=ALU.add),
                  reads=[pA], writes=[rsd])
        em.op("act", lambda e: e.activation(out=rsd[:], in_=rsd[:], func=AF.Sqrt), reads=[rsd], writes=[rsd])
        em.op("dve", lambda e: e.reciprocal(out=rsd[:], in_=rsd[:]), reads=[rsd], writes=[rsd])
        for cc in range(2):
            em.op("dve", lambda e, cc=cc: e.scalar_tensor_tensor(out=cnT[:, cc, :], in0=big[:, cc, :], scalar=kvg[:, cc:cc + 1],
                                                                 in1=rsd[:], op0=ALU.mult, op1=ALU.mult),
                  reads=[big, kvg, rsd], writes=[cnT])
        for i in range(NT):
            r0, n = trng(i)
            for cc in range(2):
                em.op("pe", lambda e, cc=cc: e.transpose(pTb[0:n, cc, :], cnT[:, cc, r0:r0 + n], self.idb[:, :]),
                      reads=[cnT, self.idb], writes=[pTb])
            em.op("act", lambda e: e.activation(out=cnk[0:n, i, :], in_=pTb[0:n, :, :], func=AF.Copy), reads=[pTb], writes=[cnk])
        wld = em.ring("wld", 2, [128, 2, 128], F32, dma=True)
        for h in range(8):
            w = wld.next()
            em.dma("sp", w[:], self.w_uk[e_, h].rearrange("(cc p) d -> p cc d", p=128), writes=[w], owner=w)
            for cc in range(2):
                em.op("pe", lambda e, cc=cc: e.transpose(pT[:, :], w[:, cc, :], self.idf[:, :]), reads=[w, self.idf], writes=[pT])
                em.op("act", lambda e, cc=cc: e.activation(out=wukT[:, h, cc * 128:(cc + 1) * 128], in_=pT[:, :], func=AF.Copy,
                                                           scale=128.0 ** -0.5), reads=[pT], writes=[wukT])
            w2 = wld.next()
            em.dma("sp", w2[:], self.w_uv[e_, h].rearrange("(cc p) d -> p cc d", p=128), writes=[w2], owner=w2)
            em.op("pool", lambda e: e.tensor_copy(wuvb[:, h, :, :], w2[:]), reads=[w2], writes=[wuvb])
        em.dma("sp", kiT2[:], self.ZB[2048:2176, :], writes=[kiT2], owner=kiT2)
        wiT = em.sb("wiT", [16, T], F32, dma=True)
        em.dma("sp", wiT[:], ZT[ZE["wi"]:ZE["wi"] + 16, :], writes=[wiT], owner=wiT)
        for i in range(NT):
            r0, n = trng(i)
            em.op("pe", lambda e: e.transpose(pT[0:n, 0:16], wiT[0:16, r0:r0 + n], self.idf[0:16, 0:16]),
                  reads=[wiT, self.idf], writes=[pT])
            em.op("act", lambda e: e.activation(out=wiTok[0:n, i, :], in_=pT[0:n, 0:16], func=AF.Copy), reads=[pT], writes=[wiTok])
        em.barrier()
        prep.close()
        em.stage = stage_outer

        pdot = em.psring("pdot", 2, [128, 512], F32)
        pmt = em.ps("pmt", [128, 8, 128], BF16)
        plog = em.psring("plog", 2, [128, 4, 128], F32)
        pso_r = em.psring("pso", 1, [128, 3, 128], F32)
        psb = em.ps("psb", [128, 128], F32)
        pql = em.ps("pql", [128, 2, 128], F32)
        qi_r = em.ring("qiT", 2, [128, 8, 128], BF16, dma=True)
        qh_r = em.ring("qhT", 2, [128, 8, 128], BF16, dma=True)
        ql_r = em.ring("qlat", 2, [128, 8, 2, 128], BF16)
        score_r = em.ring("score", 2, [128, T], F32)
        work_r = em.ring("work", 1, [128, T], F32)
        m8 = em.sb("m8", [128, 8], F32)
        NIT = 24
        blo = em.sb("blo", [128, 1], F32)
        brg = em.sb("brg", [128, 1], F32)
        bthr = em.sb("bthr", [128, 1], F32)
        bcnt = em.sb("bcnt", [128, 1], F32)
        btq = em.sb("btq", [128, 1], F32)
        stab = em.sb("stab", [128, NIT], F32)
        pw2 = em.sb("pw2", [128, NIT], F32)
        for k in range(NIT):
            em.op("pool", lambda e, k=k: e.memset(pw2[:, k:k + 1], 2.0 ** -(k + 1)), writes=[pw2])
        rl_r = em.ring("rl", 3, [128, 512], F32)
        mask_r = em.ring("mask", 2, [128, T], BF16)
        maskT_r = em.ring("maskT", 2, [128, NT, 128], BF16)
        cm_r = em.ring("cm", 2, [128, 8, 2, 128], F32)
        ex_r = em.ring("ex", 2, [128, 4, 128], F32)
        pt_r = em.ring("ptile", 12, [128, 4, 128], BF16)
        osb_r = em.ring("osb", 2, [128, 2, 128], BF16)
        rden_r = em.ring("rden", 2, [128, 128], F32)
        dsb_r = em.ring("dsb", 2, [128, 128], F32)
        gb_r = em.ring("gb", 2, [128, 8, 128], F32, dma=True)
        tb_r = em.ring("tb", 2, [128, 128], F32)
        mixb_r = em.ring("mixb", 2, [128, 8, 128], BF16, dma=True)
        EB = self.EB
        def pre(qt):
            q0, nq = trng(qt)
            nk = q0 + nq
            score = score_r.next()
            gb = gb_r.next()
            em.dma("sp", gb[:, :, 0:nq], ZT[ZE["gate_b"]:ZE["gate_b"] + 1024, q0:q0 + nq].rearrange("(h p) t -> p h t", p=128),
                   writes=[gb], owner=gb)
            qiT = qi_r.next()
            em.dma("sp", qiT[:, :, 0:nq], self.ZB[1024:2048, q0:q0 + nq].rearrange("(c p) t -> p c t", p=128),
                   writes=[qiT], owner=qiT)
            qhT = qh_r.next()
            em.dma("sp", qhT[:, :, 0:nq], self.ZB[0:1024, q0:q0 + nq].rearrange("(h p) t -> p h t", p=128),
                   writes=[qhT], owner=qhT)
            qlat = ql_r.next()
            for h in range(8):
                for cc in range(2):
                    em.op("pe", lambda e, h=h, cc=cc: e.matmul(pql[:, cc, 0:nq], wukT[:, h, cc * 128:(cc + 1) * 128], qhT[:, h, 0:nq],
                                                               start=True, stop=True), reads=[wukT, qhT], writes=[pql])
                em.op("act", lambda e, h=h: e.activation(out=qlat[:, h, :, 0:nq], in_=pql[:, :, 0:nq], func=AF.Copy),
                      reads=[pql], writes=[qlat])
            yield
            kch = [(k0, min(512, nk - k0)) for k0 in range(0, nk, 512)]
            for h16 in range(16):
                c_, po = h16 // 2, (h16 % 2) * 64
                for (k0, n) in kch:
                    ps = pdot.next()
                    rl = rl_r.next()
                    em.op("pe", lambda e: e.matmul(ps[0:nq, 0:n], qiT[po:po + 64, c_, 0:nq], kiT2[po:po + 64, k0:k0 + n],
                                                   start=True, stop=True), reads=[qiT, kiT2], writes=[ps])
                    em.op("act", lambda e: e.activation(out=rl[0:nq, 0:n], in_=ps[0:nq, 0:n], func=AF.Relu), reads=[ps], writes=[rl])
                    if h16 == 0:
                        em.op("dve", lambda e: e.tensor_scalar(out=score[0:nq, k0:k0 + n], in0=rl[0:nq, 0:n],
                                                               scalar1=wiTok[0:nq, qt, 0:1], scalar2=None, op0=ALU.mult),
                              reads=[rl, wiTok], writes=[score])
                    else:
                        em.op("dve", lambda e: e.scalar_tensor_tensor(out=score[0:nq, k0:k0 + n], in0=rl[0:nq, 0:n],
                                                                      scalar=wiTok[0:nq, qt, h16:h16 + 1],
                                                                      in1=score[0:nq, k0:k0 + n], op0=ALU.mult, op1=ALU.add),
                              reads=[rl, wiTok, score], writes=[score])
                yield
            mask = mask_r.next()
            if qt >= 1:
                em.op("pool", lambda e: e.memset(score[0:64, nk - 64:nk], NEG), reads=[], writes=[score])
            if nk > TOPK and nk - 64 < TOPK:
                work = work_r.next()
                src = score
                for r in range(TOPK // 8):
                    em.op("dve", lambda e, src=src: e.max(out=m8[0:nq, :], in_=src[0:nq, 0:nk]), reads=[src], writes=[m8])
                    em.op("dve", lambda e, src=src: e.match_replace(out=work[0:nq, 0:nk], in_to_replace=m8[0:nq, :],
                                                                    in_values=src[0:nq, 0:nk], imm_value=NEG2),
                          reads=[src, m8], writes=[work])
                    src = work
                    yield
                em.op("dve", lambda e: e.tensor_scalar(out=mask[0:nq, 0:nk], in0=work[0:nq, 0:nk], scalar1=-2.0e38, scalar2=None,
                                                       op0=ALU.is_lt), reads=[work], writes=[mask])
            elif nk > TOPK:
                nlo = nk - 64
                em.op("dve", lambda e: e.max(out=m8[0:nq, :], in_=score[0:nq, 0:nk]), reads=[score], writes=[m8])
                em.op("dve", lambda e: e.tensor_reduce(out=blo[0:nq, :], in_=score[0:nq, 0:nlo], axis=AX.X, op=ALU.min),
                      reads=[score], writes=[blo])
                em.op("dve", lambda e: e.tensor_sub(out=brg[0:nq, :], in0=m8[0:nq, 0:1], in1=blo[0:nq, :]), reads=[m8, blo], writes=[brg])
                em.op("dve", lambda e: e.tensor_scalar(out=stab[0:nq, :], in0=pw2[0:nq, :], scalar1=brg[0:nq, 0:1], scalar2=None, op0=ALU.mult),
                      reads=[pw2, brg], writes=[stab])
                yield
                for k in range(NIT):
                    em.op("dve", lambda e, k=k: e.tensor_add(out=bthr[0:nq, :], in0=blo[0:nq, :], in1=stab[0:nq, k:k + 1]),
                          reads=[blo, stab], writes=[bthr])
                    em.op("dve", lambda e: e.tensor_scalar(out=mask[0:nq, 0:nk], in0=score[0:nq, 0:nk], scalar1=bthr[0:nq, 0:1], scalar2=0.0,
                                                           op0=ALU.is_ge, op1=ALU.add, accum_out=bcnt[0:nq, 0:1]),
                          reads=[score, bthr], writes=[mask, bcnt])
                    em.op("dve", lambda e, k=k: e.scalar_tensor_tensor(out=btq[0:nq, :], in0=bcnt[0:nq, :], scalar=TOPK - 0.5,
                                                                       in1=stab[0:nq, k:k + 1], op0=ALU.is_ge, op1=ALU.mult),
                          reads=[bcnt, stab], writes=[btq])
                    em.op("dve", lambda e: e.tensor_add(out=blo[0:nq, :], in0=blo[0:nq, :], in1=btq[0:nq, :]), reads=[blo, btq], writes=[blo])
                    yield
                em.op("dve", lambda e: e.tensor_scalar(out=mask[0:nq, 0:nk], in0=score[0:nq, 0:nk], scalar1=blo[0:nq, 0:1], scalar2=None,
                                                       op0=ALU.is_ge), reads=[score, blo], writes=[mask])
            else:
                em.op("dve", lambda e: e.tensor_scalar(out=mask[0:nq, 0:nk], in0=score[0:nq, 0:nk], scalar1=-1.0e29, scalar2=None,
                                                       op0=ALU.is_gt), reads=[score], writes=[mask])
            if qt >= 1:
                em.op("pool", lambda e: e.memset(mask[0:64, nk - 64:nk], 0.0), writes=[mask])
            maskT = maskT_r.next()
            for kb0 in range(0, qt + 1, 8):
                kbs = list(range(kb0, min(qt + 1, kb0 + 8)))
                for kb in kbs:
                    k0, nkb = trng(kb)
                    em.op("pe", lambda e, kb=kb, k0=k0, nkb=nkb: e.transpose(pmt[0:nkb, kb - kb0, 0:nq], mask[0:nq, k0:k0 + nkb],
                                                                            self.idb[0:nq, 0:nq]),
                          reads=[mask, self.idb], writes=[pmt])
                if kb0 == 0:
                    em.op("act", lambda e: e.activation(out=maskT[0:16, 0, 0:nq], in_=pmt[0:16, 0, 0:nq], func=AF.Copy),
                          reads=[pmt], writes=[maskT])
                    if len(kbs) > 1:
                        em.op("act", lambda e: e.activation(out=maskT[:, 1:len(kbs), 0:nq], in_=pmt[:, 1:len(kbs), 0:nq], func=AF.Copy),
                              reads=[pmt], writes=[maskT])
                else:
                    em.op("act", lambda e: e.activation(out=maskT[:, kb0:kb0 + len(kbs), 0:nq], in_=pmt[:, 0:len(kbs), 0:nq], func=AF.Copy),
                          reads=[pmt], writes=[maskT])
            yield
            cm = cm_r.next()
            if qt == 0:
                near = {0: (0, 0, 16)}
            elif qt == 1:
                near = {1: (0, 0, 128), 0: (1, 2, 16)}
            else:
                near = {qt: (0, 0, 128), qt - 1: (1, 1, 128)}
            for kb, (slot, ty, rows) in near.items():
                for h in range(8):
                    em.op("pool", lambda e, kb=kb, slot=slot, ty=ty, rows=rows, h=h: e.tensor_tensor(
                        out=cm[0:rows, h, slot, 0:nq], in0=EB[0:rows, ty, h, 0:nq], in1=maskT[0:rows, kb, 0:nq], op=ALU.mult),
                        reads=[EB, maskT], writes=[cm])
            self._pre[qt] = dict(gb=gb, qlat=qlat, maskT=maskT, cm=cm, near=near)
            yield

        def head_a(qt, h, st):
            q0, nq = trng(qt)
            gb, qlat, maskT, cm, near = st["gb"], st["qlat"], st["maskT"], st["cm"], st["near"]
            groups = [[0]] + [list(range(a, min(qt + 1, a + 4))) for a in range(1, qt + 1, 4)]
            ptiles = {}
            for grp in groups:
                pl = plog.next()
                rows = 16 if grp[0] == 0 else 128
                for gi, kb in enumerate(grp):
                    k0, nkb = trng(kb)
                    for cc in range(2):
                        em.op("pe", lambda e, gi=gi, k0=k0, nkb=nkb, cc=cc: e.matmul(
                            pl[0:nkb, gi, 0:nq], cnT[:, cc, k0:k0 + nkb], qlat[:, h, cc, 0:nq],
                            start=(cc == 0), stop=(cc == 1)), reads=[cnT, qlat], writes=[pl])
                ex = ex_r.next()
                g_n = len(grp)
                em.op("act", lambda e, rows=rows, g_n=g_n: e.activation(out=ex[0:rows, 0:g_n, 0:nq], in_=pl[0:rows, 0:g_n, 0:nq],
                                                                        func=AF.Exp, bias=self.bfar[0:rows, h:h + 1]),
                      reads=[pl, self.bfar], writes=[ex])
                ptile = pt_r.next()
                far = [gi for gi, kb in enumerate(grp) if kb not in near]
                if far:
                    a, b = far[0], far[-1] + 1
                    kba = grp[a]
                    em.op("dve", lambda e, a=a, b=b, kba=kba, rows=rows: e.tensor_tensor(
                        out=ptile[0:rows, a:b, 0:nq], in0=ex[0:rows, a:b, 0:nq], in1=maskT[0:rows, kba:kba + (b - a), 0:nq], op=ALU.mult),
                        reads=[ex, maskT], writes=[ptile])
                for gi, kb in enumerate(grp):
                    if kb in near:
                        slot, ty, rws = near[kb]
                        em.op("dve", lambda e, gi=gi, slot=slot, rws=rws: e.tensor_tensor(
                            out=ptile[0:rws, gi, 0:nq], in0=ex[0:rws, gi, 0:nq], in1=cm[0:rws, h, slot, 0:nq], op=ALU.mult),
                            reads=[ex, cm], writes=[ptile])
                for gi, kb in enumerate(grp):
                    ptiles[kb] = (ptile, gi)
            return ptiles

        def head_b(qt, h, st, mixb, ptiles):
            q0, nq = trng(qt)
            gb = st["gb"]
            pso = pso_r.next()
            for part in range(3):
                for kb in range(qt + 1):
                    k0, nkb = trng(kb)
                    ptile, gi = ptiles[kb]
                    if part < 2:
                        lhs = cnk[0:nkb, kb, part * 128:(part + 1) * 128]
                        rd = [cnk, ptile]
                    else:
                        lhs = self.onesb[0:nkb, :]
                        rd = [self.onesb, ptile]
                    em.op("pe", lambda e, lhs=lhs, ptile=ptile, gi=gi, nkb=nkb, kb=kb, part=part: e.matmul(
                        pso[:, part, 0:nq], lhs, ptile[0:nkb, gi, 0:nq], start=(kb == 0), stop=(kb == qt)),
                        reads=rd, writes=[pso])
            osb = osb_r.next()
            rden = rden_r.next()
            dsb = dsb_r.next()
            em.op("act", lambda e: e.activation(out=osb[:, :, 0:nq], in_=pso[:, 0:2, 0:nq], func=AF.Copy), reads=[pso], writes=[osb])
            em.op("act", lambda e: e.activation(out=dsb[:, 0:nq], in_=pso[:, 2, 0:nq], func=AF.Ln), reads=[pso], writes=[dsb])
            for cc in range(2):
                em.op("pe", lambda e, cc=cc: e.matmul(psb[:, 0:nq], wuvb[:, h, cc, :], osb[:, cc, 0:nq], start=(cc == 0), stop=(cc == 1)),
                      reads=[wuvb, osb], writes=[psb])
            tb = tb_r.next()
            em.op("act", lambda e: e.activation(out=rden[:, 0:nq], in_=dsb[:, 0:nq], func=AF.Exp, scale=-1.0), reads=[dsb], writes=[rden])
            em.op("dve", lambda e: e.tensor_tensor(out=tb[:, 0:nq], in0=psb[:, 0:nq], in1=rden[:, 0:nq], op=ALU.mult),
                  reads=[psb, rden], writes=[tb])
            em.op("pool", lambda e: e.tensor_tensor(out=mixb[:, h, 0:nq], in0=tb[:, 0:nq], in1=gb[:, h, 0:nq], op=ALU.mult),
                  reads=[tb, gb], writes=[mixb])

        self._pre = {}
        for _ in pre(0):
            pass
        nqt = NT
        for qt in range(nqt):
            q0, nq = trng(qt)
            st = self._pre.pop(qt)
            gen = pre(qt + 1) if qt + 1 < nqt else None
            nk1 = trng(qt + 1)[0] + trng(qt + 1)[1] if gen is not None else 0
            nsteps = 0 if gen is None else (3 + 16 + (0 if nk1 <= TOPK else (TOPK // 8 if nk1 - 64 < TOPK else NIT + 1)))
            per_head = (nsteps + 7) // 8
            mixb = mixb_r.next()
            pt_next = head_a(qt, 0, st)
            for h in range(8):
                pt_cur = pt_next
                if h + 1 < 8:
                    pt_next = head_a(qt, h + 1, st)
                head_b(qt, h, st, mixb, pt_cur)
                if gen is not None:
                    for _ in range(per_head):
                        if next(gen, "done") == "done":
                            gen = None
                            break
            if gen is not None:
                for _ in gen:
                    pass
            em.dma("sp", self.MIXT[1024:2048, q0:q0 + nq].rearrange("(h p) t -> p h t", p=128), mixb[:, :, 0:nq], reads=[mixb], owner=mixb)
        em.end()

    def stage_rec(self, l):
        em = self.em
        ZT = self.ZT
        li = l // 2
        G = 4
        em.begin()
        stg = em.sb("stg", [128, 128], F32, dma=True)
        rng = em.sb("rng", [128, 16], F32)
        epsb = em.sb("epsb", [128, 1], F32)
        em.op("pool", lambda e: e.memset(epsb[:], RMS_EPS), writes=[epsb])
        self.rmask = em.sb("rmask", [128, T], F32)
        em.op("pool", lambda e: e.memset(self.rmask[:], 1.0), writes=[self.rmask])
        em.op("pool", lambda e: e.memset(self.rmask[:, 0:1], 0.0), writes=[self.rmask])
        em.op("pool", lambda e: e.memset(self.rmask[:, 16:T].rearrange("p (c j) -> p c j", j=64)[:, :, 0:1], 0.0),
              writes=[self.rmask])
        ptk = em.ps("ptk", [128, 8, 128], BF16)
        pn = em.ps("pn", [128, 512], F32)
        self.vecT(rng, 0, self.rec_g[li].rearrange("(c p) -> c p", p=128), 16, pn, stg)
        qs_r = em.ring("qs", 2, [128, T], F32, dma=True)
        sg_r = em.ring("sg", 2, [128, T], F32, dma=True)
        fb_r = em.ring("fb", 1, [128, T], F32)
        b_r = em.ring("b", 1, [128, T], F32)
        d_r = em.ring("d1", 1, [128, T], F32)
        kk_r = em.ring("kk", 1, [128, T], F32)
        mo_r = em.ring("mo", 1, [128, T], BF16, dma=True)
        qt_r = em.ring("qtl", G, [128, T], BF16)
        kt_r = em.ring("ktl", G, [128, T], BF16)
        ktok_r = em.ring("ktok", G, [128, NT, 128], BF16)
        vtok_r = em.ring("vtok", G, [128, NT, 128], BF16, dma=True)
        sc_r = em.ring("sc", G, [128, 4, 33], F32)
        bl_r = em.ring("bl", G, [128, 33], F32)
        oT_r = em.ring("oT", G, [128, T], F32)
        S_rs = [em.ring("S%d" % g, 2, [128, 128], F32) for g in range(G)]
        Sb_r = em.ring("Sb", 2 * G, [128, 128], BF16)
        St_r = em.ring("St", 2 * G, [128, 128], F32)
        am_r = em.ring("am", 2 * G, [128, 128], BF16)
        hbank = [em.ps_views("ph%d" % g, 4, [128], F32) for g in range(G)]
        lb = self.lbv

        def pre_head(h, g):
            qs = qs_r.next(); sg = sg_r.next()
            fb = fb_r.next(); b = b_r.next(); d1 = d_r.next(); kk = kk_r.next()
            qtl = qt_r.next(); ktl = kt_r.next(); ktok = ktok_r.next(); vtok = vtok_r.next()
            sc = sc_r.next(); bl = bl_r.next(); oT = oT_r.next()
            em.dma("sp", qs[:], ZT[h * 128:(h + 1) * 128, :], writes=[qs], owner=qs)
            em.dma("sp", sg[:], ZT[2048 + h * 128:2048 + (h + 1) * 128, :], writes=[sg], owner=sg)
            em.dma("sp", vtok[0:16, 0, :], self.VTOK[0:16, h * 128:(h + 1) * 128], writes=[vtok], owner=vtok)
            em.dma("sp", vtok[:, 1:NT, :], self.VTOK[16:T, h * 128:(h + 1) * 128].rearrange("(i p) v -> p i v", p=128),
                   writes=[vtok], owner=vtok)
            em.op("dve", lambda e: e.tensor_scalar(out=fb[:], in0=sg[:], scalar1=self.oml[:, li, h:h + 1], scalar2=lb[:, li, h:h + 1],
                                                   op0=ALU.mult, op1=ALU.add), reads=[sg, self.oml, lb], writes=[fb])
            em.op("act", lambda e: e.activation(out=fb[:], in_=fb[:], func=AF.Ln), reads=[fb], writes=[fb])
            em.op("dve", lambda e: e.tensor_scalar(out=kk[:], in0=sg[:], scalar1=self.noml[:, li, h:h + 1], scalar2=self.oml[:, li, h:h + 1],
                                                   op0=ALU.mult, op1=ALU.add), reads=[sg, self.noml, self.oml], writes=[kk])
            em.op("dve", lambda e: e.tensor_tensor_scan(b[:], self.rmask[:], fb[:], 0.0, ALU.mult, ALU.add),
                  reads=[self.rmask, fb], writes=[b])
            em.op("pool", lambda e: e.tensor_copy(sc[:, 0, 0:1], b[:, 8:9]), reads=[b], writes=[sc])
            em.op("pool", lambda e: e.tensor_copy(sc[:, 0, 1:33], b[:, 48:T:64]), reads=[b], writes=[sc])
            em.op("pool", lambda e: e.tensor_copy(bl[:, 0:1], b[:, 15:16]), reads=[b], writes=[bl])
            em.op("pool", lambda e: e.tensor_copy(bl[:, 1:33], b[:, 79:T:64]), reads=[b], writes=[bl])
            em.op("dve", lambda e: e.tensor_sub(out=d1[:, 0:16], in0=b[:, 0:16], in1=sc[:, 0, 0:1].to_broadcast([128, 16])),
                  reads=[b, sc], writes=[d1])
            em.op("dve", lambda e: e.tensor_sub(out=d1[:, 16:T].rearrange("p (c j) -> p c j", j=64),
                                                in0=b[:, 16:T].rearrange("p (c j) -> p c j", j=64),
                                                in1=sc[:, 0, 1:33].unsqueeze(2).to_broadcast([128, 32, 64])),
                  reads=[b, sc], writes=[d1])
            em.op("act", lambda e: e.activation(out=sc[:, 1, :], in_=sc[:, 0, :], func=AF.Exp), reads=[sc], writes=[sc])
            em.op("act", lambda e: e.activation(out=sc[:, 3, :], in_=bl[:], func=AF.Exp), reads=[bl, sc], writes=[sc])
            em.op("dve", lambda e: e.tensor_sub(out=bl[:], in0=bl[:], in1=sc[:, 0, :]), reads=[bl, sc], writes=[bl])
            em.op("act", lambda e: e.activation(out=sc[:, 2, :], in_=bl[:], func=AF.Exp), reads=[bl, sc], writes=[sc])
            em.op("act", lambda e: e.activation(out=fb[:], in_=d1[:], func=AF.Exp), reads=[d1, fb], writes=[fb])
            em.op("dve", lambda e: e.tensor_mul(out=qtl[:], in0=qs[:], in1=fb[:]), reads=[qs, fb], writes=[qtl])
            em.op("act", lambda e: e.activation(out=d1[:], in_=d1[:], func=AF.Exp, scale=-1.0), reads=[d1], writes=[d1])
            em.op("pool", lambda e: e.tensor_mul(out=ktl[:], in0=kk[:], in1=d1[:]), reads=[kk, d1], writes=[ktl])
            for i0 in range(0, NT, 8):
                ii = list(range(i0, min(NT, i0 + 8)))
                for i in ii:
                    r0, n = trng(i)
                    em.op("pe", lambda e, i=i, r0=r0, n=n: e.transpose(ptk[0:n, i - i0, :], ktl[:, r0:r0 + n], self.idb[:, :]),
                          reads=[ktl, self.idb], writes=[ptk])
                if i0 == 0:
                    em.op("act", lambda e: e.activation(out=ktok[0:16, 0, :], in_=ptk[0:16, 0, :], func=AF.Copy), reads=[ptk], writes=[ktok])
                    em.op("act", lambda e: e.activation(out=ktok[:, 1:8, :], in_=ptk[:, 1:8, :], func=AF.Copy), reads=[ptk], writes=[ktok])
                else:
                    em.op("act", lambda e, i0=i0, m=len(ii): e.activation(out=ktok[:, i0:i0 + m, :], in_=ptk[:, 0:m, :], func=AF.Copy),
                          reads=[ptk], writes=[ktok])
            Sb = Sb_r.next()
            St = St_r.next()
            em.op("pool", lambda e: e.memset(Sb[:], 0.0), writes=[Sb])
            em.op("pool", lambda e: e.memset(St[:], 0.0), writes=[St])
            return dict(h=h, g=g, qtl=qtl, ktl=ktl, ktok=ktok, vtok=vtok, sc=sc, oT=oT, Sb=Sb, St=St, pss=None)

        def tile_part(c, i):
            r0, n = trng(i)
            qtl, ktl, vtok, oT = c["qtl"], c["ktl"], c["vtok"], c["oT"]
            psa = hbank[c["g"]][0]
            am = am_r.next()
            pso = hbank[c["g"]][0]
            em.op("pe", lambda e: e.matmul(psa[0:n, 0:n], ktl[:, r0:r0 + n], qtl[:, r0:r0 + n], start=True, stop=True),
                  reads=[ktl, qtl], writes=[psa])
            em.op("dve", lambda e: e.tensor_tensor(out=am[0:n, 0:n], in0=psa[0:n, 0:n], in1=self.cmask[0:n, 0:n], op=ALU.mult),
                  reads=[psa, self.cmask], writes=[am])
            em.op("pe", lambda e: e.matmul(pso[:, 0:n], vtok[0:n, i, :], am[0:n, 0:n], start=True, stop=True),
                  reads=[vtok, am], writes=[pso])
            em.op("act", lambda e: e.activation(out=oT[:, r0:r0 + n], in_=pso[:, 0:n], func=AF.Copy), reads=[pso], writes=[oT])

        def chunk_list():
            out = []
            for i in range(NT):
                r0, n = trng(i)
                for ci, (p0, ncx) in enumerate([(0, 16)] if i == 0 else [(0, 64), (64, 64)]):
                    j = 0 if i == 0 else 1 + 2 * (i - 1) + ci
                    out.append((i, j, p0, ncx, r0 + p0))
            return out

        CH = chunk_list()

        def emit_pss(c, idx):
            i, j, p0, ncx, c0 = CH[idx]
            pss = hbank[c["g"]][1 + (idx % 2)]
            em.op("pe", lambda e: e.matmul(pss[:], c["ktok"][p0:p0 + ncx, i, :], c["vtok"][p0:p0 + ncx, i, :], start=True, stop=True),
                  reads=[c["ktok"], c["vtok"]], writes=[pss])

        def chunk_pe(c, idx):
            i, j, p0, ncx, c0 = CH[idx]
            psi = hbank[c["g"]][3]
            Sb = c["Sb"]
            em.op("pe", lambda e: e.matmul(psi[:, 0:ncx], Sb[:], c["qtl"][:, c0:c0 + ncx], start=True, stop=True),
                  reads=[Sb, c["qtl"]], writes=[psi])
            if idx + 1 < len(CH):
                emit_pss(c, idx + 1)

        def chunk_dve(c, idx):
            i, j, p0, ncx, c0 = CH[idx]
            sc, oT = c["sc"], c["oT"]
            pss = hbank[c["g"]][1 + (idx % 2)]
            psi = hbank[c["g"]][3]
            St = c["St"]
            if idx + 1 < len(CH):
                jn = CH[idx + 1][1]
                S2 = S_rs[c["g"]].next()
                em.op("dve", lambda e: e.scalar_tensor_tensor(out=S2[:], in0=pss[:], scalar=sc[:, 2, j:j + 1], in1=St[:],
                                                              op0=ALU.mult, op1=ALU.add), reads=[pss, sc, St], writes=[S2])
                Sb2 = Sb_r.next()
                St2 = St_r.next()
                em.op("dve", lambda e: e.tensor_scalar(out=Sb2[:], in0=S2[:], scalar1=sc[:, 1, jn:jn + 1], scalar2=None, op0=ALU.mult),
                      reads=[S2, sc], writes=[Sb2])
                em.op("dve", lambda e: e.tensor_scalar(out=St2[:], in0=S2[:], scalar1=sc[:, 3, jn:jn + 1], scalar2=None, op0=ALU.mult),
                      reads=[S2, sc], writes=[St2])
                c["Sb"], c["St"] = Sb2, St2
            em.op("dve", lambda e: e.tensor_add(out=oT[:, c0:c0 + ncx], in0=oT[:, c0:c0 + ncx], in1=psi[:, 0:ncx]),
                  reads=[oT, psi], writes=[oT])

        def post_head(c):
            h, oT = c["h"], c["oT"]
            gs = qs_r.next(); d1 = d_r.next(); kk = kk_r.next()
            em.dma("sp", gs[:], ZT[4096 + h * 128:4096 + (h + 1) * 128, :], writes=[gs], owner=gs)
            em.op("act", lambda e: e.activation(out=d1[:], in_=oT[:], func=AF.Square), reads=[oT], writes=[d1])
            avg = self.avg[128]
            for (c0, n) in CCH:
                em.op("pe", lambda e: e.matmul(pn[:, 0:n], avg[:], d1[:, c0:c0 + n], start=True, stop=True), reads=[avg, d1], writes=[pn])
                em.op("act", lambda e: e.activation(out=kk[:, c0:c0 + n], in_=pn[:, 0:n], func=AF.Ln, bias=epsb[:, 0:1]),
                      reads=[pn, epsb], writes=[kk])
            em.op("act", lambda e: e.activation(out=kk[:], in_=kk[:], func=AF.Exp, scale=-0.5), reads=[kk], writes=[kk])
            em.op("dve", lambda e: e.tensor_mul(out=kk[:], in0=kk[:], in1=oT[:]), reads=[kk, oT], writes=[kk])
            mo = mo_r.next()
            em.op("dve", lambda e: e.scalar_tensor_tensor(out=mo[:], in0=kk[:], scalar=rng[:, h:h + 1], in1=gs[:], op0=ALU.mult, op1=ALU.mult),
                  reads=[kk, rng, gs], writes=[mo])
            em.dma("sp", self.MIXT[h * 128:(h + 1) * 128, :], mo[:], reads=[mo], owner=mo)

        for g0 in range(0, 16, G):
            ctx = [pre_head(g0 + g, g) for g in range(G)]
            for c in ctx:
                emit_pss(c, 0)
            idx = 0
            for i in range(NT):
                for c in ctx:
                    tile_part(c, i)
                for _ in ([0] if i == 0 else [0, 1]):
                    for c in ctx:
                        chunk_pe(c, idx)
                    for c in ctx:
                        chunk_dve(c, idx)
                    idx += 1
            for c in ctx:
                post_head(c)
        em.end()

    def stage_out(self, s, l, W, last):
        em = self.em
        em.begin()
        Wb = em.sb("Wb", [128, 16, D], BF16)
        if last:
            self.fgain = em.sb("fgain", [128, D], F32, dma=True)
            em.dma("sp", self.fgain[:], self.final_gain.partition_broadcast(128), writes=[self.fgain], owner=self.fgain)
        w32r = em.ring("wo32", 4, [128, D], F32, dma=True)
        Wbk = [Tk("Wb%d" % k, Wb.t) for k in range(16)]
        for k in range(16):
            w32 = w32r.next()
            em.dma("sp" if k % 2 == 0 else "act", w32[:], W[k * 128:(k + 1) * 128, :], writes=[w32], owner=w32)
            if k % 2 == 0:
                em.op("pool", lambda e, k=k: e.tensor_copy(Wb[:, k, :], w32[:]), reads=[w32], writes=[Wbk[k]])
            else:
                em.op("act", lambda e, k=k: e.activation(out=Wb[:, k, :], in_=w32[:], func=AF.Copy), reads=[w32], writes=[Wbk[k]])
        mtr = em.ring("mt", 2, [128, 16, 128], BF16, dma=True)
        htr = em.ring("ht", 2, [128, D], F32, dma=True)
        hnr = em.ring("hn", 2, [128, D], F32, dma=True)
        pmm = em.psring("pmm", 4, [128, 512], F32)
        junk = em.sb("junk", [128, D], BF16)
        ssr = em.ring("ss", 2, [128, 2], F32)
        tmr = em.ring("tm", 2, [128, 2], F32)
        for i in range(NT):
            r0, n = trng(i)
            if last and i == 0:
                continue
            mt = mtr.next()
            ht = htr.next()
            hn = hnr.next()
            em.dma("sp", mt[:, :, 0:n], self.MIXT[:, r0:r0 + n].rearrange("(k p) t -> p k t", p=128), writes=[mt], owner=mt)
            em.dma("sp", ht[0:n, :], self.h_src(s, l, i), writes=[ht], owner=ht)
            for c in range(4):
                ps = pmm.next()
                for k in range(16):
                    em.op("pe", lambda e, k=k, c=c: e.matmul(ps[0:n, :], mt[:, k, 0:n], Wb[:, k, c * 512:(c + 1) * 512],
                                                             start=(k == 0), stop=(k == 15)), reads=[mt, Wbk[k]], writes=[ps])
                em.op("dve", lambda e, c=c: e.tensor_add(out=hn[0:n, c * 512:(c + 1) * 512], in0=ht[0:n, c * 512:(c + 1) * 512], in1=ps[0:n, :]),
                      reads=[ht, ps], writes=[hn])
            if not last:
                em.dma("sp", self.H[r0:r0 + n, :], hn[0:n, :], reads=[hn], owner=hn)
            else:
                ss = ssr.next()
                tm = tmr.next()
                em.op("pool", lambda e: e.memset(ss[:], 0.0), writes=[ss])
                em.op("act", lambda e: e.activation(out=junk[0:n, :], in_=hn[0:n, :], func=AF.Square, accum_out=ss[0:n, 0:1]),
                      reads=[hn], writes=[junk, ss])
                self.rstd_rows(ss, n, D, RMS_EPS, tm)
                em.op("dve", lambda e: e.scalar_tensor_tensor(out=ht[0:n, :], in0=hn[0:n, :], scalar=ss[0:n, 1:2], in1=self.fgain[0:n, :],
                                                              op0=ALU.mult, op1=ALU.mult), reads=[hn, ss, self.fgain], writes=[ht])
                em.dma("sp", self.out[s, r0 - 16:r0 - 16 + n, :], ht[0:n, :], reads=[ht], owner=ht)
        em.end()


_CACHE = {}


def kernel(**inputs):
    x = np.ascontiguousarray(inputs["x"], dtype=np.float32)
    if "nc" not in _CACHE:
        _CACHE["nc"] = Prog().build()
    nc = _CACHE["nc"]
    oh = _bucket_onehot()
    names = ["meta_tokens", "norm_gain", "final_norm_gain", "rel_bias_table", "w_in_even", "conv_w", "conv_b",
             "conv_ln_gain", "conv_ln_bias", "kv_norm_gain", "w_uk", "w_uv", "w_out_even", "w_in_odd", "lb_logits",
             "rec_norm_gain", "w_out_odd"]
    shared = {k: np.ascontiguousarray(inputs[k], dtype=np.float32) for k in names}
    shared["c_oh"] = oh
    in_maps = []
    for c in range(NCORES):
        m = dict(shared)
        m["x"] = x[c * SEQ_PER_CORE:(c + 1) * SEQ_PER_CORE]
        in_maps.append(m)
    res = run_bass_kernel_spmd(nc, in_maps, core_ids=list(range(NCORES)))
    return np.concatenate([r["out"] for r in res.results], axis=0)
```

```python
import math
from contextlib import ExitStack

import numpy as np
import concourse.bass as bass
import concourse.mybir as mybir
from concourse.bass_utils import run_bass_kernel_spmd

F32 = mybir.dt.float32
BF16 = mybir.dt.bfloat16
AF = mybir.ActivationFunctionType
ALU = mybir.AluOpType
AX = mybir.AxisListType

NCORES = 8
SEQ_PER_CORE = 2
D = 2048
SEQ = 2048
NMETA = 16
T = SEQ + NMETA
NT = 17
DEPTH = 4
P_EVEN = 6480
RMS_EPS = 1e-6
LN_EPS = 1e-5
NEG = -1.0e30
NEG2 = -3.0e38
TOPK = 256


def trng(i):
    return (0, 16) if i == 0 else (16 + 128 * (i - 1), 128)


CCH = [(0, 16)] + [(16 + 512 * j, 512) for j in range(4)]


class Tk:
    __slots__ = ("name", "t", "w", "r", "dkey", "bank")

    def __init__(self, name, t=None):
        self.name = name
        self.t = t
        self.w = {}
        self.r = {}
        self.dkey = None
        self.bank = None

    def __getitem__(self, idx):
        return self.t[idx]


class Ring:
    def __init__(self, bufs):
        self.bufs = bufs
        self.i = 0

    def next(self):
        b = self.bufs[self.i % len(self.bufs)]
        self.i += 1
        return b


class Em:
    ENG = ("pe", "act", "dve", "pool", "sp")

    def __init__(self, nc, n_dsem=78):
        self.nc = nc
        self.top = ExitStack()
        self.eng = dict(pe=nc.tensor, act=nc.scalar, dve=nc.vector, pool=nc.gpsimd, sp=nc.sync)
        self.sems = {}
        self.val = {}
        for e in self.ENG:
            self.sems[e] = self.top.enter_context(nc.semaphore("es_" + e))
            self.val[e] = 0
        self.bar = self.top.enter_context(nc.semaphore("bar"))
        self.nbar = 0
        self.free_d = []
        for i in range(n_dsem):
            k = "D%d" % i
            self.sems[k] = self.top.enter_context(nc.semaphore("ds_%d" % i))
            self.val[k] = 0
            self.free_d.append(k)
        self.free_sw = []
        for i in range(10):
            k = "DS%d" % i
            self.sems[k] = self.top.enter_context(nc.semaphore("dsw_%d" % i))
            self.val[k] = 0
            self.free_sw.append(k)
        self.stage_sw = []
        self.seen = {e: {} for e in self.ENG}
        self.stage = None
        self.stage_d = []
        self.uid = 0
        self.nins = 0
        self.reg = {}

    def begin(self):
        self.stage = ExitStack()
        self.stage_d = []

    def end(self):
        self.barrier()
        self.stage.close()
        self.stage = None
        self.free_d.extend(self.stage_d)
        self.stage_d = []
        self.free_sw.extend(self.stage_sw)
        self.stage_sw = []

    def _nm(self, name):
        self.uid += 1
        return "%s_%d" % (name, self.uid)

    def sb(self, name, shape, dt=F32, dma=False, top=False):
        st = self.top if top else self.stage
        nm = self._nm(name)
        t = st.enter_context(self.nc.sbuf_tensor(nm, list(shape), dt))
        self.reg[name] = nm
        tk = Tk(name, t)
        if dma == "sw":
            k = self.free_sw.pop()
            tk.dkey = k
            if not top:
                self.stage_sw.append(k)
        elif dma:
            k = self.free_d.pop()
            tk.dkey = k
            if not top:
                self.stage_d.append(k)
        return tk

    def ring(self, name, n, shape, dt=F32, dma=False):
        return Ring([self.sb("%s%d" % (name, i), shape, dt, dma=dma) for i in range(n)])

    def ps(self, name, shape, dt=F32):
        t = self.stage.enter_context(self.nc.psum_tensor(self._nm(name), list(shape), dt))
        return Tk(name, t)

    def ps_views(self, name, n, sub_shape, dt=F32):
        t = self.stage.enter_context(self.nc.psum_tensor(self._nm(name), [128, n] + list(sub_shape), dt))
        bank = Tk(name + "_bank")
        views = [Tk("%s%d" % (name, i), t[:, i]) for i in range(n)]
        for v in views:
            v.bank = bank
        return views

    def psring(self, name, n, shape, dt=F32):
        return Ring([self.ps("%s%d" % (name, i), shape, dt) for i in range(n)])

    def _wait(self, eng, toks):
        need = {}
        for d in toks:
            for k, v in d.items():
                if v > need.get(k, 0):
                    need[k] = v
        seen = self.seen[eng]
        for k, v in need.items():
            if k == eng and eng == "pe":
                continue
            if seen.get(k, 0) >= v:
                continue
            self.eng[eng].wait_ge(self.sems[k], v)
            seen[k] = v

    @staticmethod
    def _deps(reads, writes):
        toks = []
        for t in reads:
            toks.append(t.w)
            if t.bank is not None:
                toks.append(t.bank.w)
        for t in writes:
            toks.append(t.w)
            toks.append(t.r)
            if t.bank is not None:
                toks.append(t.bank.r)
        return toks

    @staticmethod
    def _mark(k, v, reads, writes):
        for t in reads:
            if t.r.get(k, 0) < v:
                t.r[k] = v
            if t.bank is not None and t.bank.r.get(k, 0) < v:
                t.bank.r[k] = v
        for t in writes:
            t.w = {k: v}
            t.r = {}
            if t.bank is not None:
                t.bank.w = {k: v}

    def op(self, eng, fn, reads=(), writes=()):
        self._wait(eng, self._deps(reads, writes))
        ins = fn(self.eng[eng])
        self.val[eng] += 1
        ins.then_inc(self.sems[eng], 1)
        self._mark(eng, self.val[eng], reads, writes)
        self.nins += 1
        return ins

    def dma(self, q, out, in_, reads=(), writes=(), owner=None, **kw):
        self._wait(q, self._deps(reads, writes))
        ins = self.eng[q].dma_start(out=out, in_=in_, **kw)
        k = owner.dkey
        self.val[k] += 16
        ins.then_inc(self.sems[k], 16)
        self._mark(k, self.val[k], reads, writes)
        self.nins += 1
        return ins

    def barrier(self):
        self.nbar += 1
        for e in self.ENG:
            g = self.eng[e]
            if self.val[e] > 0:
                g.wait_ge(self.sems[e], self.val[e])
            if e == "sp":
                for k, v in self.val.items():
                    if k[0] == "D" and v > 0 and self.seen["sp"].get(k, 0) < v:
                        g.wait_ge(self.sems[k], v)
            g.sem_inc(self.bar, 1)
        for e in self.ENG:
            self.eng[e].wait_ge(self.bar, 5 * self.nbar)
        for e in self.ENG:
            for k, v in self.val.items():
                self.seen[e][k] = v

    def close(self):
        self.top.close()


def _t5_bucket_np(rel):
    rel = np.asarray(rel, dtype=np.int32)
    nb = 16
    ret = np.where(rel > 0, nb, 0).astype(np.int32)
    n = np.abs(rel)
    max_exact = nb // 2
    nf = np.maximum(n, 1).astype(np.float32)
    large = max_exact + (np.log(nf / np.float32(max_exact)) / np.float32(math.log(128 / max_exact))
                         * np.float32(nb - max_exact)).astype(np.int32)
    large = np.minimum(large, nb - 1)
    return ret + np.where(n < max_exact, n, large)


def _bucket_onehot():
    rel = 255 - np.arange(512)
    b = _t5_bucket_np(rel)
    oh = np.zeros((32, 512), np.float32)
    oh[b, np.arange(512)] = 1.0
    return oh


class Prog:
    def __init__(self, nseq=SEQ_PER_CORE, layers=DEPTH, dbg=False, stop=10 ** 9):
        self.stop = stop
        self.nstage = 0
        self.nseq = nseq
        self.layers = layers
        self.dbg = dbg if dbg else ()
        ne = max(1, (layers + 1) // 2)
        no = layers // 2
        od = (lambda *sh: [max(no, 1)] + ([1] * len(sh) if no == 0 else list(sh)))
        nc = bass.Bass("TRN2", target_bir_lowering=False)
        self.nc = nc
        dt = nc.dram_tensor
        I = "ExternalInput"
        self.x = dt("x", [nseq, SEQ, D], F32, kind=I).ap()
        self.meta = dt("meta_tokens", [NMETA, D], F32, kind=I).ap()
        self.norm_gain = dt("norm_gain", [4, D], F32, kind=I).ap()
        self.final_gain = dt("final_norm_gain", [D], F32, kind=I).ap()
        self.relb = dt("rel_bias_table", [32, 8], F32, kind=I).ap()
        self.w_in_even = dt("w_in_even", [ne, D, P_EVEN], F32, kind=I).ap()
        self.conv_w = dt("conv_w", [2, 31, 1024], F32, kind=I).ap()
        self.conv_b = dt("conv_b", [2, 1024], F32, kind=I).ap()
        self.ln_g = dt("conv_ln_gain", [2, 1024], F32, kind=I).ap()
        self.ln_b = dt("conv_ln_bias", [2, 1024], F32, kind=I).ap()
        self.kv_g = dt("kv_norm_gain", [2, 256], F32, kind=I).ap()
        self.w_uk = dt("w_uk", [ne, 8, 256, 128], F32, kind=I).ap()
        self.w_uv = dt("w_uv", [ne, 8, 256, 128], F32, kind=I).ap()
        self.w_out_even = dt("w_out_even", [ne, D, D], F32, kind=I).ap()
        self.w_in_odd = dt("w_in_odd", od(D, 4 * D), F32, kind=I).ap()
        self.lb_logits = dt("lb_logits", [4, D], F32, kind=I).ap()
        self.rec_g = dt("rec_norm_gain", [2, D], F32, kind=I).ap()
        self.w_out_odd = dt("w_out_odd", od(D, D), F32, kind=I).ap()
        self.c_oh = dt("c_oh", [32, 512], F32, kind=I).ap()
        self.out = dt("out", [nseq, SEQ, D], F32, kind="ExternalOutput").ap()
        sk = lambda n: "ExternalOutput" if n in self.dbg else "Internal"
        self.H_all = dt("Hs", [nseq, T, D], F32, kind=sk("Hs")).ap()
        self.ZT = dt("ZTs", [4 * D, T], F32, kind=sk("ZTs")).ap()
        self.VTOK = dt("VTOKs", [T, D], BF16, kind=sk("VTOKs")).ap()
        self.ZB = dt("ZBs", [2176, T], BF16, kind=sk("ZBs")).ap()
        self.MIXT_all = dt("MIXTs", [nseq, D, T], BF16, kind=sk("MIXTs")).ap()
        self.FD = dt("FDs", [8, 512], F32, kind=sk("FDs")).ap()
        self.em = Em(nc)

    def build(self):
        em = self.em

        def run(f, *a, **k):
            if self.nstage < self.stop:
                f(*a, **k)
            self.nstage += 1

        def sel(s):
            self.H = self.H_all[s]
            self.MIXT = self.MIXT_all[s]

        self.sel = sel
        run(self.setup_consts)
        for l in range(self.layers):
            last = (l == self.layers - 1)
            for s in range(self.nseq):
                sel(s)
                if l % 2 == 0:
                    run(self.stage_norm_proj, s, l, even=True)
                    run(self.stage_conv, l // 2)
                    run(self.stage_attn, l // 2)
                else:
                    run(self.stage_norm_proj, s, l, even=False)
                    run(self.stage_rec, l)
            W = self.w_out_even[l // 2] if l % 2 == 0 else self.w_out_odd[l // 2]
            run(self.stage_out, list(range(self.nseq)), l, W, last)
        em.close()
        return self.nc

    def vecT(self, dst, dst_cols, src_rows_ap, nrows, ps, stg):
        em = self.em
        em.dma("sp", stg[0:nrows, :], src_rows_ap, writes=[stg], owner=stg)
        em.op("pe", lambda e: e.transpose(ps[:, 0:nrows], stg[0:nrows, :], self.idf[0:nrows, 0:nrows]),
              reads=[stg, self.idf], writes=[ps])
        em.op("act", lambda e: e.activation(out=dst[:, dst_cols:dst_cols + nrows], in_=ps[:, 0:nrows], func=AF.Copy),
              reads=[ps], writes=[dst])

    def setup_consts(self):
        em = self.em
        nc = self.nc
        self.idf = em.sb("idf", [128, 128], F32, top=True)
        self.idb = em.sb("idb", [128, 128], BF16, top=True)
        self.onesb = em.sb("onesb", [128, 128], BF16, top=True)
        self.avg = {}
        for n in (128, 256, 1024):
            self.avg[n] = em.sb("avg%d" % n, [128, 128], F32, top=True)
        self.cmask = em.sb("cmask", [128, 128], F32, top=True)
        self.EB = em.sb("EB", [128, 3, 8, 128], F32, top=True)
        self.bfar = em.sb("bfar", [128, 8], F32, top=True, dma=True)
        self.lbv = em.sb("lbv", [128, 2, 16], F32, top=True)
        self.oml = em.sb("oml", [128, 2, 16], F32, top=True)
        self.noml = em.sb("noml", [128, 2, 16], F32, top=True)

        em.begin()
        P = lambda f, **k: em.op("pool", f, **k)
        P(lambda e: e.memset(self.idf[:], 1.0), writes=[self.idf])
        P(lambda e: e.affine_select(out=self.idf[:], in_=self.idf[:], pattern=[[-1, 128]], compare_op=ALU.is_equal,
                                    fill=0.0, base=0, channel_multiplier=1), reads=[self.idf], writes=[self.idf])
        P(lambda e: e.tensor_copy(self.idb[:], self.idf[:]), reads=[self.idf], writes=[self.idb])
        P(lambda e: e.memset(self.onesb[:], 1.0), writes=[self.onesb])
        for n in (128, 256, 1024):
            P(lambda e, n=n: e.memset(self.avg[n][:], 1.0 / n), writes=[self.avg[n]])
        P(lambda e: e.memset(self.cmask[:], 1.0), writes=[self.cmask])
        P(lambda e: e.affine_select(out=self.cmask[:], in_=self.cmask[:], pattern=[[1, 128]], compare_op=ALU.is_ge,
                                    fill=0.0, base=0, channel_multiplier=-1), reads=[self.cmask], writes=[self.cmask])
        P(lambda e: e.memset(self.cmask[0:64, 64:128], 0.0), writes=[self.cmask])

        anti = em.sb("anti", [128, 128], F32)
        P(lambda e: e.memset(anti[:], 1.0), writes=[anti])
        P(lambda e: e.affine_select(out=anti[:], in_=anti[:], pattern=[[1, 128]], compare_op=ALU.is_equal,
                                    fill=0.0, base=-127, channel_multiplier=1), reads=[anti], writes=[anti])
        tab = em.sb("tab", [32, 8], F32, dma=True)
        oh = em.sb("oh", [32, 512], F32, dma=True)
        fsb = em.sb("fsb", [8, 512], F32, dma=True)
        nbfar = em.sb("nbfar", [128, 8], F32)
        psA = em.ps("psA", [128, 512], F32)
        psB = em.ps("psB", [128, 128], F32)
        em.dma("sp", tab[:], self.relb[:, :], writes=[tab], owner=tab)
        em.dma("sp", oh[:], self.c_oh[:, :], writes=[oh], owner=oh)
        em.dma("sp", self.bfar[:], bass.AP(tensor=self.relb.tensor, offset=15 * 8, ap=[[0, 128], [1, 8]]),
               writes=[self.bfar], owner=self.bfar)
        em.op("dve", lambda e: e.tensor_scalar(out=nbfar[:], in0=self.bfar[:], scalar1=-1.0, scalar2=None, op0=ALU.mult),
              reads=[self.bfar], writes=[nbfar])
        em.op("pe", lambda e: e.matmul(psA[0:8, :], tab[0:32, 0:8], oh[0:32, :], start=True, stop=True),
              reads=[tab, oh], writes=[psA])
        em.op("act", lambda e: e.activation(out=fsb[:], in_=psA[0:8, :], func=AF.Copy), reads=[psA], writes=[fsb])
        fd_tk = Tk("FD")
        em.dma("sp", self.FD[:, :], fsb[:], reads=[fsb], writes=[fd_tk], owner=fsb)
        hk = em.ring("hk", 2, [128, 128], F32, dma=True)
        for ty, c0 in enumerate((128, 256, 144)):
            for h in range(8):
                hb = hk.next()
                src = bass.AP(tensor=self.FD.tensor, offset=h * 512 + c0, ap=[[1, 128], [1, 128]])
                em.dma("sp", hb[:], src, reads=[fd_tk], writes=[hb], owner=hb)
                em.op("pe", lambda e, hb=hb: e.matmul(psB[:], anti[:], hb[:], start=True, stop=True),
                      reads=[anti, hb], writes=[psB])
                em.op("act", lambda e, ty=ty, h=h: e.activation(out=self.EB[:, ty, h, :], in_=psB[:], func=AF.Exp,
                                                                bias=nbfar[:, h:h + 1]),
                      reads=[psB, nbfar], writes=[self.EB])

        stg = em.sb("stg", [128, 128], F32, dma=True)
        lbT = em.sb("lbT", [128, 64], F32)
        self.vecT(lbT, 0, self.lb_logits.rearrange("l (c p) -> (l c) p", p=128), 64, psB, stg)
        mx = em.sb("mx", [128, 16], F32)
        ex = em.sb("ex", [128, 4, 16], F32)
        sm = em.sb("sm", [128, 16], F32)
        rs = em.sb("rs", [128, 16], F32)
        c1 = em.sb("c1", [128, 16], F32)
        V = lambda f, **k: em.op("dve", f, **k)
        V(lambda e: e.tensor_max(out=mx[:], in0=lbT[:, 0:16], in1=lbT[:, 16:32]), reads=[lbT], writes=[mx])
        V(lambda e: e.tensor_max(out=mx[:], in0=mx[:], in1=lbT[:, 32:48]), reads=[lbT, mx], writes=[mx])
        V(lambda e: e.tensor_max(out=mx[:], in0=mx[:], in1=lbT[:, 48:64]), reads=[lbT, mx], writes=[mx])
        for l in range(4):
            V(lambda e, l=l: e.tensor_sub(out=ex[:, l, :], in0=lbT[:, 16 * l:16 * l + 16], in1=mx[:]),
              reads=[lbT, mx], writes=[ex])
        em.op("act", lambda e: e.activation(out=ex[:], in_=ex[:], func=AF.Exp), reads=[ex], writes=[ex])
        V(lambda e: e.tensor_add(out=sm[:], in0=ex[:, 0, :], in1=ex[:, 1, :]), reads=[ex], writes=[sm])
        V(lambda e: e.tensor_add(out=sm[:], in0=sm[:], in1=ex[:, 2, :]), reads=[ex, sm], writes=[sm])
        V(lambda e: e.tensor_add(out=sm[:], in0=sm[:], in1=ex[:, 3, :]), reads=[ex, sm], writes=[sm])
        V(lambda e: e.reciprocal(out=rs[:], in_=sm[:]), reads=[sm], writes=[rs])
        V(lambda e: e.tensor_mul(out=self.lbv[:, 0, :], in0=ex[:, 1, :], in1=rs[:]), reads=[ex, rs], writes=[self.lbv])
        V(lambda e: e.tensor_add(out=c1[:], in0=ex[:, 1, :], in1=ex[:, 2, :]), reads=[ex], writes=[c1])
        V(lambda e: e.tensor_add(out=c1[:], in0=c1[:], in1=ex[:, 3, :]), reads=[ex, c1], writes=[c1])
        V(lambda e: e.tensor_mul(out=self.lbv[:, 1, :], in0=c1[:], in1=rs[:]), reads=[c1, rs], writes=[self.lbv])
        V(lambda e: e.tensor_scalar(out=self.oml[:], in0=self.lbv[:], scalar1=-1.0, scalar2=1.0, op0=ALU.mult, op1=ALU.add),
          reads=[self.lbv], writes=[self.oml])
        V(lambda e: e.tensor_scalar(out=self.noml[:], in0=self.oml[:], scalar1=-1.0, scalar2=None, op0=ALU.mult),
          reads=[self.oml], writes=[self.noml])
        em.end()

    def h_src(self, s, l, i):
        r0, n = trng(i)
        if l == 0:
            if i == 0:
                return self.meta[:, :]
            return self.x[s, r0 - 16:r0 - 16 + n, :]
        return self.H[r0:r0 + n, :]

    def rstd_rows(self, ss, n, dim, eps, tmp):
        em = self.em
        em.op("dve", lambda e: e.tensor_scalar(out=tmp[0:n, 0:1], in0=ss[0:n, 0:1], scalar1=1.0 / dim, scalar2=eps,
                                               op0=ALU.mult, op1=ALU.add), reads=[ss], writes=[tmp])
        em.op("act", lambda e: e.activation(out=tmp[0:n, 1:2], in_=tmp[0:n, 0:1], func=AF.Sqrt), reads=[tmp], writes=[tmp])
        em.op("dve", lambda e: e.reciprocal(out=ss[0:n, 1:2], in_=tmp[0:n, 1:2]), reads=[tmp], writes=[ss])

    def stage_norm_proj(self, s, l, even):
        em = self.em
        em.begin()
        hnT = em.sb("hnT", [128, 16, T], BF16)
        outer = em.stage
        nrm = ExitStack()
        em.stage = nrm
        gbc = em.sb("gbc", [128, D], F32, dma=True)
        em.dma("sp", gbc[:], self.norm_gain[l, :].partition_broadcast(128), writes=[gbc], owner=gbc)
        htr = em.ring("ht", 4, [128, D], F32, dma=True)
        hsr = em.ring("hs", 3, [128, D], BF16)
        junk = em.sb("junk", [128, D], BF16)
        ssr = em.ring("ss", 4, [128, 2], F32)
        tmr = em.ring("tm", 4, [128, 2], F32)
        ptr = em.psring("ptr", 2, [128, 16, 128], BF16)
        for i in range(NT):
            r0, n = trng(i)
            ht = htr.next()
            hs = hsr.next()
            ss = ssr.next()
            tm = tmr.next()
            pt = ptr.next()
            em.dma("sp", ht[0:n, :], self.h_src(s, l, i), writes=[ht], owner=ht)
            em.op("pool", lambda e: e.memset(ss[:], 0.0), writes=[ss])
            em.op("act", lambda e: e.activation(out=junk[0:n, :], in_=ht[0:n, :], func=AF.Square, accum_out=ss[0:n, 0:1]),
                  reads=[ht], writes=[junk, ss])
            self.rstd_rows(ss, n, D, RMS_EPS, tm)
            em.op("dve", lambda e: e.scalar_tensor_tensor(out=hs[0:n, :], in0=ht[0:n, :], scalar=ss[0:n, 1:2],
                                                          in1=gbc[0:n, :], op0=ALU.mult, op1=ALU.mult),
                  reads=[ht, ss, gbc], writes=[hs])
            for k in range(16):
                em.op("pe", lambda e, k=k: e.transpose(pt[:, k, 0:n], hs[0:n, k * 128:(k + 1) * 128], self.idb[0:n, 0:n]),
                      reads=[hs, self.idb], writes=[pt])
            em.op("act", lambda e: e.activation(out=hnT[:, :, r0:r0 + n], in_=pt[:, :, 0:n], func=AF.Copy),
                  reads=[pt], writes=[hnT])
        self.hnT = hnT
        em.barrier()
        nrm.close()
        em.stage = outer
        if even:
            self.proj_even(l // 2)
        else:
            self.proj_odd(l // 2)
        em.end()

    def proj_fm(self, W, chunks, wtr32, wtr, pmm, otr, otbr=None):
        em = self.em
        hnT = self.hnT
        blocks = []
        for ch in chunks:
            col0, ncols, func, scale, dst0, dup = ch
            if (blocks and not dup and ncols == 128 and len(blocks[-1]) < 4 and not blocks[-1][-1][5]
                    and blocks[-1][-1][1] == 128 and blocks[-1][-1][0] + 128 == col0):
                blocks[-1].append(ch)
            else:
                blocks.append([ch])
        for blk in blocks:
            w32 = wtr32.next()
            wb = wtr.next()
            bcol0 = blk[0][0]
            bn = sum(c[1] for c in blk)
            em.dma("sp", w32[:, :, 0:bn], W[:, bcol0:bcol0 + bn].rearrange("(kc p) m -> p kc m", p=128),
                   writes=[w32], owner=w32)
            if blk[0][5]:
                em.dma("sp", w32[:, :, bn:2 * bn], W[:, bcol0:bcol0 + bn].rearrange("(kc p) m -> p kc m", p=128),
                       writes=[w32], owner=w32)
            for bi, (col0, ncols, func, scale, dst0, dup) in enumerate(blk):
                o = col0 - bcol0
                nm = ncols * (2 if dup else 1)
                em.op("pool", lambda e, o=o, nm=nm: e.tensor_copy(wb[:, :, o:o + nm], w32[:, :, o:o + nm]), reads=[w32], writes=[wb])
            for bi, (col0, ncols, func, scale, dst0, dup) in enumerate(blk):
                o = col0 - bcol0
                nm = ncols * (2 if dup else 1)
                tobf = dst0 < 0
                ot = otbr.next() if tobf else otr.next()
                for (c0, n) in CCH:
                    ps = pmm.next()
                    for k in range(16):
                        em.op("pe", lambda e, k=k: e.matmul(ps[0:nm, 0:n], wb[:, k, o:o + nm], hnT[:, k, c0:c0 + n],
                                                            start=(k == 0), stop=(k == 15)),
                              reads=[wb, hnT], writes=[ps])
                    em.op("act", lambda e: e.activation(out=ot[0:nm, c0:c0 + n], in_=ps[0:nm, 0:n], func=func, scale=scale),
                          reads=[ps], writes=[ot])
                if tobf:
                    r0 = -dst0 - 1
                    em.dma("act", self.ZB[r0:r0 + nm, :], ot[0:nm, :], reads=[ot], owner=ot)
                else:
                    em.dma("act", self.ZT[dst0:dst0 + nm, :], ot[0:nm, :], reads=[ot], owner=ot)

    ZE = dict(glu_v=0, glu_g=1024, gate_a=2048, q=3072, c=4096, gate_b=4352, qi=5376, ki=6400, wi=6528)

    def proj_even(self, e_):
        em = self.em
        W = self.w_in_even[e_]
        chunks = []
        for m in range(50):
            col0 = m * 128
            if col0 < 1024:
                f, sc = AF.Copy, 1.0
            elif col0 < 2048:
                f, sc = AF.Sigmoid, 1.0
            elif col0 < 3072:
                f, sc = AF.Silu, 1.0
            elif col0 < 4352:
                f, sc = AF.Copy, 1.0
            elif col0 < 5376:
                f, sc = AF.Silu, 1.0
            else:
                f, sc = AF.Copy, 0.125
            dst = col0
            if 3072 <= col0 < 4096:
                dst = -(col0 - 3072) - 1
            elif 5376 <= col0 < 6400:
                dst = -(1024 + col0 - 5376) - 1
            chunks.append((col0, 128, f, sc, dst, False))
        chunks.append((6400, 64, AF.Copy, 1.0, -2048 - 1, True))
        chunks.append((6464, 16, AF.Copy, 0.25, 6528, False))
        wtr32 = em.ring("w32", 2, [128, 16, 512], F32, dma=True)
        wtr = em.ring("wb", 2, [128, 16, 512], BF16)
        pmm = em.psring("pmm", 4, [128, 512], F32)
        otr = em.ring("ot", 2, [128, T], F32, dma=True)
        otbr = em.ring("otb", 2, [128, T], BF16, dma=True)
        self.proj_fm(W, chunks, wtr32, wtr, pmm, otr, otbr)

    def proj_odd(self, o_):
        em = self.em
        W = self.w_in_odd[o_]
        chunks = []
        for m in range(16):
            chunks.append((m * 128, 128, AF.Silu, 1.0, m * 128, False))
        for m in range(16):
            chunks.append((2048 + m * 128, 128, AF.Sigmoid, 1.0, 2048 + m * 128, False))
        for m in range(16):
            chunks.append((6144 + m * 128, 128, AF.Silu, 1.0, 4096 + m * 128, False))
        wtr32 = em.ring("w32", 2, [128, 16, 512], F32, dma=True)
        wtr = em.ring("wb", 2, [128, 16, 512], BF16)
        pmm = em.psring("pmm", 4, [128, 512], F32)
        otr = em.ring("ot", 2, [128, T], F32, dma=True)
        self.proj_fm(W, chunks, wtr32, wtr, pmm, otr)
        hnT = self.hnT
        vo = em.ring("vo", 2, [128, 512], BF16, dma=True)
        for g in range(4):
            w32 = wtr32.next()
            wb = wtr.next()
            em.dma("sp", w32[:], W[:, 4096 + g * 512:4096 + (g + 1) * 512].rearrange("(kc p) m -> p kc m", p=128),
                   writes=[w32], owner=w32)
            for k4 in range(4):
                em.op("pool", lambda e, k4=k4: e.tensor_copy(wb[:, 4 * k4:4 * k4 + 4, :], w32[:, 4 * k4:4 * k4 + 4, :]), reads=[w32], writes=[wb])
            for i in range(NT):
                r0, n = trng(i)
                ps = pmm.next()
                for k in range(16):
                    em.op("pe", lambda e, k=k: e.matmul(ps[0:n, :], hnT[:, k, r0:r0 + n], wb[:, k, :],
                                                        start=(k == 0), stop=(k == 15)),
                          reads=[wb, hnT], writes=[ps])
                v = vo.next()
                em.op("act", lambda e: e.activation(out=v[0:n, :], in_=ps[0:n, :], func=AF.Copy), reads=[ps], writes=[v])
                em.dma("act", self.VTOK[r0:r0 + n, g * 512:(g + 1) * 512], v[0:n, :], reads=[v], owner=v)

    def stage_conv(self, e_):
        em = self.em
        ZE = self.ZE
        em.begin()
        stg = em.sb("stg", [128, 128], F32, dma=True)
        pst = em.ps("pst", [128, 128], F32)
        cw = em.sb("cw", [128, 248], F32)
        cv = em.sb("cv", [128, 24], F32)
        cwr = self.conv_w[e_].rearrange("j (cc p) -> (j cc) p", p=128)
        self.vecT(cw, 0, cwr[0:124, :], 124, pst, stg)
        self.vecT(cw, 124, cwr[124:248, :], 124, pst, stg)
        self.vecT(cv, 0, self.conv_b[e_].rearrange("(cc p) -> cc p", p=128), 8, pst, stg)
        self.vecT(cv, 8, self.ln_g[e_].rearrange("(cc p) -> cc p", p=128), 8, pst, stg)
        self.vecT(cv, 16, self.ln_b[e_].rearrange("(cc p) -> cc p", p=128), 8, pst, stg)
        uall = em.sb("uall", [128, 8, T], F32)
        acc1 = em.sb("acc1", [128, T], F32)
        acc2 = em.sb("acc2", [128, T], F32)
        gsr = em.ring("gs", 1, [128, T], F32, dma=True)
        upr = em.ring("up", 2, [128, 30 + T], F32, dma=True)
        pa = em.ring("pa", 1, [128, T], F32)
        pb = em.ring("pb", 1, [128, T], F32)
        tmpr = em.ring("ctmp", 2, [128, T], F32)
        dgr = em.ring("dg", 3, [128, 128], F32)
        pcs = [em.ps("pc%d" % i, [128, 512], F32) for i in range(5)]
        sq = em.ring("sq", 1, [128, T], F32)
        for b in upr.bufs:
            em.op("pool", lambda e, b=b: e.memset(b[:, 0:30], 0.0), writes=[b])
        def load_u(cc):
            gs = gsr.next()
            up = upr.next()
            em.dma("sp", up[:, 30:30 + T], self.ZT[ZE["glu_v"] + cc * 128:ZE["glu_v"] + (cc + 1) * 128, :], writes=[up], owner=up)
            em.dma("sp", gs[:], self.ZT[ZE["glu_g"] + cc * 128:ZE["glu_g"] + (cc + 1) * 128, :], writes=[gs], owner=gs)
            em.op("pool", lambda e: e.tensor_tensor(out=up[:, 30:30 + T], in0=up[:, 30:30 + T], in1=gs[:], op=ALU.mult),
                  reads=[up, gs], writes=[up])
            return up

        up_next = load_u(0)
        for cc in range(8):
            up = up_next
            A = pa.next()
            B = pb.next()
            w = lambda j: cw[:, j * 8 + cc:j * 8 + cc + 1]
            em.op("dve", lambda e: e.tensor_scalar(out=A[:], in0=up[:, 0:T], scalar1=w(0), scalar2=cv[:, cc:cc + 1],
                                                   op0=ALU.mult, op1=ALU.add), reads=[up, cw, cv], writes=[A])
            for j in range(1, 10):
                em.op("dve", lambda e, j=j: e.scalar_tensor_tensor(out=A[:], in0=up[:, j:j + T], scalar=w(j), in1=A[:],
                                                                   op0=ALU.mult, op1=ALU.add), reads=[up, cw, A], writes=[A])
            em.op("act", lambda e: e.activation(out=B[:], in_=up[:, 10:10 + T], func=AF.Copy, scale=w(10)),
                  reads=[up, cw], writes=[B])
            for j in range(11, 17):
                tp = tmpr.next()
                em.op("act", lambda e, j=j, tp=tp: e.activation(out=tp[:], in_=up[:, j:j + T], func=AF.Copy, scale=w(j)),
                      reads=[up, cw], writes=[tp])
                em.op("pool", lambda e, tp=tp: e.tensor_add(out=B[:], in0=B[:], in1=tp[:]), reads=[tp, B], writes=[B])
            if cc + 1 < 8:
                up_next = load_u(cc + 1)
            for j in range(17, 31):
                dg = dgr.next()
                em.op("act", lambda e, j=j, dg=dg: e.activation(out=dg[:], in_=self.idf[:], func=AF.Copy, scale=w(j)),
                      reads=[self.idf, cw], writes=[dg])
                for ci, (c0, n) in enumerate(CCH):
                    em.op("pe", lambda e, j=j, dg=dg, ci=ci, c0=c0, n=n: e.matmul(pcs[ci][:, 0:n], dg[:], up[:, j + c0:j + c0 + n],
                                                                               start=(j == 17), stop=(j == 30)),
                          reads=[dg, up], writes=[pcs[ci]])
            em.op("dve", lambda e: e.tensor_add(out=uall[:, cc, :], in0=A[:], in1=B[:]), reads=[A, B], writes=[uall])
            for ci, (c0, n) in enumerate(CCH):
                em.op("dve", lambda e, ci=ci, c0=c0, n=n: e.tensor_add(out=uall[:, cc, c0:c0 + n], in0=uall[:, cc, c0:c0 + n], in1=pcs[ci][:, 0:n]),
                      reads=[uall, pcs[ci]], writes=[uall])
            s2 = sq.next()
            em.op("act", lambda e: e.activation(out=s2[:], in_=uall[:, cc, :], func=AF.Square), reads=[uall], writes=[s2])
            if cc == 0:
                em.op("pool", lambda e: e.tensor_copy(acc1[:], uall[:, cc, :]), reads=[uall], writes=[acc1])
                em.op("pool", lambda e: e.tensor_copy(acc2[:], s2[:]), reads=[s2], writes=[acc2])
            else:
                em.op("pool", lambda e: e.tensor_add(out=acc1[:], in0=acc1[:], in1=uall[:, cc, :]), reads=[uall, acc1], writes=[acc1])
                em.op("pool", lambda e: e.tensor_add(out=acc2[:], in0=acc2[:], in1=s2[:]), reads=[s2, acc2], writes=[acc2])
        mean = em.sb("mean", [128, T], F32)
        rstd = em.sb("rstd", [128, T], F32)
        p1 = em.ps("p1", [128, 512], F32)
        p2 = em.ps("p2", [128, 512], F32)
        avg = self.avg[1024]
        for (c0, n) in CCH:
            em.op("pe", lambda e: e.matmul(p1[:, 0:n], avg[:], acc1[:, c0:c0 + n], start=True, stop=True),
                  reads=[avg, acc1], writes=[p1])
            em.op("pe", lambda e: e.matmul(p2[:, 0:n], avg[:], acc2[:, c0:c0 + n], start=True, stop=True),
                  reads=[avg, acc2], writes=[p2])
            em.op("act", lambda e: e.activation(out=mean[:, c0:c0 + n], in_=p1[:, 0:n], func=AF.Copy), reads=[p1], writes=[mean])
            em.op("dve", lambda e: e.tensor_tensor(out=rstd[:, c0:c0 + n], in0=mean[:, c0:c0 + n], in1=mean[:, c0:c0 + n], op=ALU.mult),
                  reads=[mean], writes=[rstd])
            em.op("dve", lambda e: e.tensor_sub(out=rstd[:, c0:c0 + n], in0=p2[:, 0:n], in1=rstd[:, c0:c0 + n]),
                  reads=[p2, rstd], writes=[rstd])
            em.op("dve", lambda e: e.tensor_scalar(out=rstd[:, c0:c0 + n], in0=rstd[:, c0:c0 + n], scalar1=LN_EPS, scalar2=None, op0=ALU.add),
                  reads=[rstd], writes=[rstd])
        em.op("act", lambda e: e.activation(out=rstd[:], in_=rstd[:], func=AF.Sqrt), reads=[rstd], writes=[rstd])
        em.op("dve", lambda e: e.reciprocal(out=rstd[:], in_=rstd[:]), reads=[rstd], writes=[rstd])
        mxr = em.ring("mx", 2, [128, T], BF16, dma=True)
        for cc in range(8):
            ga = gsr.next()
            t1 = pa.next()
            t2 = pb.next()
            mx = mxr.next()
            em.dma("sp", ga[:], self.ZT[ZE["gate_a"] + cc * 128:ZE["gate_a"] + (cc + 1) * 128, :], writes=[ga], owner=ga)
            em.op("dve", lambda e: e.tensor_sub(out=t1[:], in0=uall[:, cc, :], in1=mean[:]), reads=[uall, mean], writes=[t1])
            em.op("pool", lambda e: e.tensor_mul(out=t1[:], in0=t1[:], in1=rstd[:]), reads=[t1, rstd], writes=[t1])
            em.op("act", lambda e: e.activation(out=t2[:], in_=t1[:], func=AF.Silu, scale=cv[:, 8 + cc:9 + cc], bias=cv[:, 16 + cc:17 + cc]),
                  reads=[t1, cv], writes=[t2])
            em.op("dve", lambda e: e.tensor_mul(out=mx[:], in0=t2[:], in1=ga[:]), reads=[t2, ga], writes=[mx])
            em.dma("sp", self.MIXT[cc * 128:(cc + 1) * 128, :], mx[:], reads=[mx], owner=mx)
        em.end()

    def stage_attn(self, e_):
        em = self.em
        ZE = self.ZE
        ZT = self.ZT
        em.begin()
        cnT = em.sb("cnT", [128, 2, T], BF16)
        cnk = em.sb("cnk", [128, NT, 256], BF16)
        wukT = em.sb("wukT", [128, 8, 256], BF16)
        wuvb = em.sb("wuvb", [128, 8, 2, 128], BF16)
        kiT2 = em.sb("kiT2", [128, T], BF16, dma=True)
        wiTok = em.sb("wiTok", [128, NT, 16], F32)
        kvg = em.sb("kvg", [128, 2], F32)
        stg = em.sb("stg", [128, 128], F32, dma=True)

        prep = ExitStack()
        stage_outer = em.stage
        em.stage = prep
        pA = em.ps("pA", [128, 512], F32)
        pB = em.ps("pB", [128, 512], F32)
        pT = em.ps("pT", [128, 128], F32)
        pTb = em.ps("pTb", [128, 2, 128], BF16)
        big = em.sb("big", [128, 2, T], F32, dma=True)
        big2 = em.sb("big2", [128, T], F32)
        rsd = em.sb("rsd", [128, T], F32)
        self.vecT(kvg, 0, self.kv_g[e_].rearrange("(cc p) -> cc p", p=128), 2, pT, stg)
        em.dma("sp", big[:], ZT[ZE["c"]:ZE["c"] + 256, :].rearrange("(cc p) t -> p cc t", p=128), writes=[big], owner=big)
        em.op("act", lambda e: e.activation(out=big2[:], in_=big[:, 0, :], func=AF.Square), reads=[big], writes=[big2])
        em.op("act", lambda e: e.activation(out=rsd[:], in_=big[:, 1, :], func=AF.Square), reads=[big], writes=[rsd])
        em.op("dve", lambda e: e.tensor_add(out=big2[:], in0=big2[:], in1=rsd[:]), reads=[big2, rsd], writes=[big2])
        avg = self.avg[256]
        for (c0, n) in CCH:
            em.op("pe", lambda e: e.matmul(pA[:, 0:n], avg[:], big2[:, c0:c0 + n], start=True, stop=True),
                  reads=[avg, big2], writes=[pA])
            em.op("dve", lambda e: e.tensor_scalar(out=rsd[:, c0:c0 + n], in0=pA[:, 0:n], scalar1=RMS_EPS, scalar2=None, op0=ALU.add),
                  reads=[pA], writes=[rsd])
        em.op("act", lambda e: e.activation(out=rsd[:], in_=rsd[:], func=AF.Sqrt), reads=[rsd], writes=[rsd])
        em.op("dve", lambda e: e.reciprocal(out=rsd[:], in_=rsd[:]), reads=[rsd], writes=[rsd])
        for cc in range(2):
            em.op("dve", lambda e, cc=cc: e.scalar_tensor_tensor(out=cnT[:, cc, :], in0=big[:, cc, :], scalar=kvg[:, cc:cc + 1],
                                                                 in1=rsd[:], op0=ALU.mult, op1=ALU.mult),
                  reads=[big, kvg, rsd], writes=[cnT])
        for i in range(NT):
            r0, n = trng(i)
            for cc in range(2):
                em.op("pe", lambda e, cc=cc: e.transpose(pTb[0:n, cc, :], cnT[:, cc, r0:r0 + n], self.idb[:, :]),
                      reads=[cnT, self.idb], writes=[pTb])
            em.op("act", lambda e: e.activation(out=cnk[0:n, i, :], in_=pTb[0:n, :, :], func=AF.Copy), reads=[pTb], writes=[cnk])
        wld = em.ring("wld", 2, [128, 2, 128], F32, dma=True)
        for h in range(8):
            w = wld.next()
            em.dma("sp", w[:], self.w_uk[e_, h].rearrange("(cc p) d -> p cc d", p=128), writes=[w], owner=w)
            for cc in range(2):
                em.op("pe", lambda e, cc=cc: e.transpose(pT[:, :], w[:, cc, :], self.idf[:, :]), reads=[w, self.idf], writes=[pT])
                em.op("act", lambda e, cc=cc: e.activation(out=wukT[:, h, cc * 128:(cc + 1) * 128], in_=pT[:, :], func=AF.Copy,
                                                           scale=128.0 ** -0.5), reads=[pT], writes=[wukT])
            w2 = wld.next()
            em.dma("sp", w2[:], self.w_uv[e_, h].rearrange("(cc p) d -> p cc d", p=128), writes=[w2], owner=w2)
            em.op("pool", lambda e: e.tensor_copy(wuvb[:, h, :, :], w2[:]), reads=[w2], writes=[wuvb])
        em.dma("sp", kiT2[:], self.ZB[2048:2176, :], writes=[kiT2], owner=kiT2)
        wiT = em.sb("wiT", [16, T], F32, dma=True)
        em.dma("sp", wiT[:], ZT[ZE["wi"]:ZE["wi"] + 16, :], writes=[wiT], owner=wiT)
        for i in range(NT):
            r0, n = trng(i)
            em.op("pe", lambda e: e.transpose(pT[0:n, 0:16], wiT[0:16, r0:r0 + n], self.idf[0:16, 0:16]),
                  reads=[wiT, self.idf], writes=[pT])
            em.op("act", lambda e: e.activation(out=wiTok[0:n, i, :], in_=pT[0:n, 0:16], func=AF.Copy), reads=[pT], writes=[wiTok])
        em.barrier()
        prep.close()
        em.stage = stage_outer

        pdot = em.psring("pdot", 2, [128, 512], F32)
        pmt = em.ps("pmt", [128, 8, 128], BF16)
        plog = em.psring("plog", 2, [128, 4, 128], F32)
        pso_r = em.psring("pso", 1, [128, 3, 128], F32)
        psb = em.ps("psb", [128, 128], F32)
        pql = em.ps("pql", [128, 2, 128], F32)
        qi_r = em.ring("qiT", 2, [128, 8, 128], BF16, dma=True)
        qh_r = em.ring("qhT", 2, [128, 8, 128], BF16, dma=True)
        ql_r = em.ring("qlat", 2, [128, 8, 2, 128], BF16)
        score_r = em.ring("score", 2, [128, T], F32)
        work_r = em.ring("work", 1, [128, T], F32)
        m8 = em.sb("m8", [128, 8], F32)
        NIT = 24
        blo = em.sb("blo", [128, 1], F32)
        brg = em.sb("brg", [128, 1], F32)
        bthr = em.sb("bthr", [128, 1], F32)
        bcnt = em.sb("bcnt", [128, 1], F32)
        btq = em.sb("btq", [128, 1], F32)
        stab = em.sb("stab", [128, NIT], F32)
        pw2 = em.sb("pw2", [128, NIT], F32)
        for k in range(NIT):
            em.op("pool", lambda e, k=k: e.memset(pw2[:, k:k + 1], 2.0 ** -(k + 1)), writes=[pw2])
        rl_r = em.ring("rl", 3, [128, 512], F32)
        mask_r = em.ring("mask", 2, [128, T], BF16)
        maskT_r = em.ring("maskT", 2, [128, NT, 128], BF16)
        cm_r = em.ring("cm", 2, [128, 8, 2, 128], F32)
        ex_r = em.ring("ex", 2, [128, 4, 128], F32)
        pt_r = em.ring("ptile", 12, [128, 4, 128], BF16)
        osb_r = em.ring("osb", 2, [128, 2, 128], BF16)
        rden_r = em.ring("rden", 2, [128, 128], F32)
        dsb_r = em.ring("dsb", 2, [128, 128], F32)
        gb_r = em.ring("gb", 2, [128, 8, 128], F32, dma=True)
        tb_r = em.ring("tb", 2, [128, 128], F32)
        mixb_r = em.ring("mixb", 2, [128, 8, 128], BF16, dma=True)
        EB = self.EB
        def pre(qt):
            q0, nq = trng(qt)
            nk = q0 + nq
            score = score_r.next()
            gb = gb_r.next()
            em.dma("sp", gb[:, :, 0:nq], ZT[ZE["gate_b"]:ZE["gate_b"] + 1024, q0:q0 + nq].rearrange("(h p) t -> p h t", p=128),
                   writes=[gb], owner=gb)
            qiT = qi_r.next()
            em.dma("sp", qiT[:, :, 0:nq], self.ZB[1024:2048, q0:q0 + nq].rearrange("(c p) t -> p c t", p=128),
                   writes=[qiT], owner=qiT)
            qhT = qh_r.next()
            em.dma("sp", qhT[:, :, 0:nq], self.ZB[0:1024, q0:q0 + nq].rearrange("(h p) t -> p h t", p=128),
                   writes=[qhT], owner=qhT)
            qlat = ql_r.next()
            for h in range(8):
                for cc in range(2):
                    em.op("pe", lambda e, h=h, cc=cc: e.matmul(pql[:, cc, 0:nq], wukT[:, h, cc * 128:(cc + 1) * 128], qhT[:, h, 0:nq],
                                                               start=True, stop=True), reads=[wukT, qhT], writes=[pql])
                em.op("act", lambda e, h=h: e.activation(out=qlat[:, h, :, 0:nq], in_=pql[:, :, 0:nq], func=AF.Copy),
                      reads=[pql], writes=[qlat])
            yield
            kch = [(k0, min(512, nk - k0)) for k0 in range(0, nk, 512)]
            for h16 in range(16):
                c_, po = h16 // 2, (h16 % 2) * 64
                for (k0, n) in kch:
                    ps = pdot.next()
                    rl = rl_r.next()
                    em.op("pe", lambda e: e.matmul(ps[0:nq, 0:n], qiT[po:po + 64, c_, 0:nq], kiT2[po:po + 64, k0:k0 + n],
                                                   start=True, stop=True), reads=[qiT, kiT2], writes=[ps])
                    em.op("act", lambda e: e.activation(out=rl[0:nq, 0:n], in_=ps[0:nq, 0:n], func=AF.Relu), reads=[ps], writes=[rl])
                    if h16 == 0:
                        em.op("dve", lambda e: e.tensor_scalar(out=score[0:nq, k0:k0 + n], in0=rl[0:nq, 0:n],
                                                               scalar1=wiTok[0:nq, qt, 0:1], scalar2=None, op0=ALU.mult),
                              reads=[rl, wiTok], writes=[score])
                    else:
                        em.op("dve", lambda e: e.scalar_tensor_tensor(out=score[0:nq, k0:k0 + n], in0=rl[0:nq, 0:n],
                                                                      scalar=wiTok[0:nq, qt, h16:h16 + 1],
                                                                      in1=score[0:nq, k0:k0 + n], op0=ALU.mult, op1=ALU.add),
                              reads=[rl, wiTok, score], writes=[score])
                yield
            mask = mask_r.next()
            if qt >= 1:
                em.op("pool", lambda e: e.memset(score[0:64, nk - 64:nk], NEG), reads=[], writes=[score])
            if nk > TOPK and nk - 64 < TOPK:
                work = work_r.next()
                src = score
                for r in range(TOPK // 8):
                    em.op("dve", lambda e, src=src: e.max(out=m8[0:nq, :], in_=src[0:nq, 0:nk]), reads=[src], writes=[m8])
                    em.op("dve", lambda e, src=src: e.match_replace(out=work[0:nq, 0:nk], in_to_replace=m8[0:nq, :],
                                                                    in_values=src[0:nq, 0:nk], imm_value=NEG2),
                          reads=[src, m8], writes=[work])
                    src = work
                    yield
                em.op("dve", lambda e: e.tensor_scalar(out=mask[0:nq, 0:nk], in0=work[0:nq, 0:nk], scalar1=-2.0e38, scalar2=None,
                                                       op0=ALU.is_lt), reads=[work], writes=[mask])
            elif nk > TOPK:
                nlo = nk - 64
                em.op("dve", lambda e: e.max(out=m8[0:nq, :], in_=score[0:nq, 0:nk]), reads=[score], writes=[m8])
                em.op("dve", lambda e: e.tensor_reduce(out=blo[0:nq, :], in_=score[0:nq, 0:nlo], axis=AX.X, op=ALU.min),
                      reads=[score], writes=[blo])
                em.op("dve", lambda e: e.tensor_sub(out=brg[0:nq, :], in0=m8[0:nq, 0:1], in1=blo[0:nq, :]), reads=[m8, blo], writes=[brg])
                em.op("dve", lambda e: e.tensor_scalar(out=stab[0:nq, :], in0=pw2[0:nq, :], scalar1=brg[0:nq, 0:1], scalar2=None, op0=ALU.mult),
                      reads=[pw2, brg], writes=[stab])
                yield
                for k in range(NIT):
                    em.op("dve", lambda e, k=k: e.tensor_add(out=bthr[0:nq, :], in0=blo[0:nq, :], in1=stab[0:nq, k:k + 1]),
                          reads=[blo, stab], writes=[bthr])
                    em.op("dve", lambda e: e.tensor_scalar(out=mask[0:nq, 0:nk], in0=score[0:nq, 0:nk], scalar1=bthr[0:nq, 0:1], scalar2=0.0,
                                                           op0=ALU.is_ge, op1=ALU.add, accum_out=bcnt[0:nq, 0:1]),
                          reads=[score, bthr], writes=[mask, bcnt])
                    em.op("dve", lambda e, k=k: e.scalar_tensor_tensor(out=btq[0:nq, :], in0=bcnt[0:nq, :], scalar=TOPK - 0.5,
                                                                       in1=stab[0:nq, k:k + 1], op0=ALU.is_ge, op1=ALU.mult),
                          reads=[bcnt, stab], writes=[btq])
                    em.op("dve", lambda e: e.tensor_add(out=blo[0:nq, :], in0=blo[0:nq, :], in1=btq[0:nq, :]), reads=[blo, btq], writes=[blo])
                    yield
                em.op("dve", lambda e: e.tensor_scalar(out=mask[0:nq, 0:nk], in0=score[0:nq, 0:nk], scalar1=blo[0:nq, 0:1], scalar2=None,
                                                       op0=ALU.is_ge), reads=[score, blo], writes=[mask])
            else:
                em.op("dve", lambda e: e.tensor_scalar(out=mask[0:nq, 0:nk], in0=score[0:nq, 0:nk], scalar1=-1.0e29, scalar2=None,
                                                       op0=ALU.is_gt), reads=[score], writes=[mask])
            if qt >= 1:
                em.op("pool", lambda e: e.memset(mask[0:64, nk - 64:nk], 0.0), writes=[mask])
            maskT = maskT_r.next()
            for kb0 in range(0, qt + 1, 8):
                kbs = list(range(kb0, min(qt + 1, kb0 + 8)))
                for kb in kbs:
                    k0, nkb = trng(kb)
                    em.op("pe", lambda e, kb=kb, k0=k0, nkb=nkb: e.transpose(pmt[0:nkb, kb - kb0, 0:nq], mask[0:nq, k0:k0 + nkb],
                                                                            self.idb[0:nq, 0:nq]),
                          reads=[mask, self.idb], writes=[pmt])
                if kb0 == 0:
                    em.op("act", lambda e: e.activation(out=maskT[0:16, 0, 0:nq], in_=pmt[0:16, 0, 0:nq], func=AF.Copy),
                          reads=[pmt], writes=[maskT])
                    if len(kbs) > 1:
                        em.op("act", lambda e: e.activation(out=maskT[:, 1:len(kbs), 0:nq], in_=pmt[:, 1:len(kbs), 0:nq], func=AF.Copy),
                              reads=[pmt], writes=[maskT])
                else:
                    em.op("act", lambda e: e.activation(out=maskT[:, kb0:kb0 + len(kbs), 0:nq], in_=pmt[:, 0:len(kbs), 0:nq], func=AF.Copy),
                          reads=[pmt], writes=[maskT])
            yield
            cm = cm_r.next()
            if qt == 0:
                near = {0: (0, 0, 16)}
            elif qt == 1:
                near = {1: (0, 0, 128), 0: (1, 2, 16)}
            else:
                near = {qt: (0, 0, 128), qt - 1: (1, 1, 128)}
            for kb, (slot, ty, rows) in near.items():
                for h in range(8):
                    em.op("pool", lambda e, kb=kb, slot=slot, ty=ty, rows=rows, h=h: e.tensor_tensor(
                        out=cm[0:rows, h, slot, 0:nq], in0=EB[0:rows, ty, h, 0:nq], in1=maskT[0:rows, kb, 0:nq], op=ALU.mult),
                        reads=[EB, maskT], writes=[cm])
            self._pre[qt] = dict(gb=gb, qlat=qlat, maskT=maskT, cm=cm, near=near)
            yield

        def head_a(qt, h, st):
            q0, nq = trng(qt)
            gb, qlat, maskT, cm, near = st["gb"], st["qlat"], st["maskT"], st["cm"], st["near"]
            groups = [[0]] + [list(range(a, min(qt + 1, a + 4))) for a in range(1, qt + 1, 4)]
            ptiles = {}
            for grp in groups:
                pl = plog.next()
                rows = 16 if grp[0] == 0 else 128
                for gi, kb in enumerate(grp):
                    k0, nkb = trng(kb)
                    for cc in range(2):
                        em.op("pe", lambda e, gi=gi, k0=k0, nkb=nkb, cc=cc: e.matmul(
                            pl[0:nkb, gi, 0:nq], cnT[:, cc, k0:k0 + nkb], qlat[:, h, cc, 0:nq],
                            start=(cc == 0), stop=(cc == 1)), reads=[cnT, qlat], writes=[pl])
                ex = ex_r.next()
                g_n = len(grp)
                em.op("act", lambda e, rows=rows, g_n=g_n: e.activation(out=ex[0:rows, 0:g_n, 0:nq], in_=pl[0:rows, 0:g_n, 0:nq],
                                                                        func=AF.Exp, bias=self.bfar[0:rows, h:h + 1]),
                      reads=[pl, self.bfar], writes=[ex])
                ptile = pt_r.next()
                far = [gi for gi, kb in enumerate(grp) if kb not in near]
                if far:
                    a, b = far[0], far[-1] + 1
                    kba = grp[a]
                    em.op("dve", lambda e, a=a, b=b, kba=kba, rows=rows: e.tensor_tensor(
                        out=ptile[0:rows, a:b, 0:nq], in0=ex[0:rows, a:b, 0:nq], in1=maskT[0:rows, kba:kba + (b - a), 0:nq], op=ALU.mult),
                        reads=[ex, maskT], writes=[ptile])
                for gi, kb in enumerate(grp):
                    if kb in near:
                        slot, ty, rws = near[kb]
                        em.op("dve", lambda e, gi=gi, slot=slot, rws=rws: e.tensor_tensor(
                            out=ptile[0:rws, gi, 0:nq], in0=ex[0:rws, gi, 0:nq], in1=cm[0:rws, h, slot, 0:nq], op=ALU.mult),
                            reads=[ex, cm], writes=[ptile])
                for gi, kb in enumerate(grp):
                    ptiles[kb] = (ptile, gi)
            return ptiles

        def head_b(qt, h, st, mixb, ptiles):
            q0, nq = trng(qt)
            gb = st["gb"]
            pso = pso_r.next()
            for part in range(3):
                for kb in range(qt + 1):
                    k0, nkb = trng(kb)
                    ptile, gi = ptiles[kb]
                    if part < 2:
                        lhs = cnk[0:nkb, kb, part * 128:(part + 1) * 128]
                        rd = [cnk, ptile]
                    else:
                        lhs = self.onesb[0:nkb, :]
                        rd = [self.onesb, ptile]
                    em.op("pe", lambda e, lhs=lhs, ptile=ptile, gi=gi, nkb=nkb, kb=kb, part=part: e.matmul(
                        pso[:, part, 0:nq], lhs, ptile[0:nkb, gi, 0:nq], start=(kb == 0), stop=(kb == qt)),
                        reads=rd, writes=[pso])
            osb = osb_r.next()
            rden = rden_r.next()
            dsb = dsb_r.next()
            em.op("act", lambda e: e.activation(out=osb[:, :, 0:nq], in_=pso[:, 0:2, 0:nq], func=AF.Copy), reads=[pso], writes=[osb])
            em.op("act", lambda e: e.activation(out=dsb[:, 0:nq], in_=pso[:, 2, 0:nq], func=AF.Ln), reads=[pso], writes=[dsb])
            for cc in range(2):
                em.op("pe", lambda e, cc=cc: e.matmul(psb[:, 0:nq], wuvb[:, h, cc, :], osb[:, cc, 0:nq], start=(cc == 0), stop=(cc == 1)),
                      reads=[wuvb, osb], writes=[psb])
            tb = tb_r.next()
            em.op("act", lambda e: e.activation(out=rden[:, 0:nq], in_=dsb[:, 0:nq], func=AF.Exp, scale=-1.0), reads=[dsb], writes=[rden])
            em.op("dve", lambda e: e.tensor_tensor(out=tb[:, 0:nq], in0=psb[:, 0:nq], in1=rden[:, 0:nq], op=ALU.mult),
                  reads=[psb, rden], writes=[tb])
            em.op("pool", lambda e: e.tensor_tensor(out=mixb[:, h, 0:nq], in0=tb[:, 0:nq], in1=gb[:, h, 0:nq], op=ALU.mult),
                  reads=[tb, gb], writes=[mixb])

        self._pre = {}
        for _ in pre(0):
            pass
        nqt = NT
        for qt in range(nqt):
            q0, nq = trng(qt)
            st = self._pre.pop(qt)
            gen = pre(qt + 1) if qt + 1 < nqt else None
            nk1 = trng(qt + 1)[0] + trng(qt + 1)[1] if gen is not None else 0
            nsteps = 0 if gen is None else (3 + 16 + (0 if nk1 <= TOPK else (TOPK // 8 if nk1 - 64 < TOPK else NIT + 1)))
            per_head = (nsteps + 7) // 8
            mixb = mixb_r.next()
            pt_next = head_a(qt, 0, st)
            for h in range(8):
                pt_cur = pt_next
                if h + 1 < 8:
                    pt_next = head_a(qt, h + 1, st)
                head_b(qt, h, st, mixb, pt_cur)
                if gen is not None:
                    for _ in range(per_head):
                        if next(gen, "done") == "done":
                            gen = None
                            break
            if gen is not None:
                for _ in gen:
                    pass
            em.dma("sp", self.MIXT[1024:2048, q0:q0 + nq].rearrange("(h p) t -> p h t", p=128), mixb[:, :, 0:nq], reads=[mixb], owner=mixb)
        em.end()

    def stage_rec(self, l):
        em = self.em
        ZT = self.ZT
        li = l // 2
        G = 4
        em.begin()
        stg = em.sb("stg", [128, 128], F32, dma=True)
        rng = em.sb("rng", [128, 16], F32)
        epsb = em.sb("epsb", [128, 1], F32)
        em.op("pool", lambda e: e.memset(epsb[:], RMS_EPS), writes=[epsb])
        self.rmask = em.sb("rmask", [128, T], F32)
        em.op("pool", lambda e: e.memset(self.rmask[:], 1.0), writes=[self.rmask])
        em.op("pool", lambda e: e.memset(self.rmask[:, 0:1], 0.0), writes=[self.rmask])
        em.op("pool", lambda e: e.memset(self.rmask[:, 16:T].rearrange("p (c j) -> p c j", j=64)[:, :, 0:1], 0.0),
              writes=[self.rmask])
        ptk = em.ps("ptk", [128, 8, 128], BF16)
        pn = em.ps("pn", [128, 512], F32)
        self.vecT(rng, 0, self.rec_g[li].rearrange("(c p) -> c p", p=128), 16, pn, stg)
        qs_r = em.ring("qs", 2, [128, T], F32, dma=True)
        sg_r = em.ring("sg", 2, [128, T], F32, dma=True)
        fb_r = em.ring("fb", 1, [128, T], F32)
        b_r = em.ring("b", 1, [128, T], F32)
        d_r = em.ring("d1", 1, [128, T], F32)
        kk_r = em.ring("kk", 1, [128, T], F32)
        mo_r = em.ring("mo", 1, [128, T], BF16, dma=True)
        qt_r = em.ring("qtl", G, [128, T], BF16)
        kt_r = em.ring("ktl", G, [128, T], BF16)
        ktok_r = em.ring("ktok", G, [128, NT, 128], BF16)
        vtok_r = em.ring("vtok", G, [128, NT, 128], BF16, dma=True)
        sc_r = em.ring("sc", G, [128, 4, 33], F32)
        bl_r = em.ring("bl", G, [128, 33], F32)
        oT_r = em.ring("oT", G, [128, T], F32)
        S_rs = [em.ring("S%d" % g, 2, [128, 128], F32) for g in range(G)]
        Sb_r = em.ring("Sb", 2 * G, [128, 128], BF16)
        St_r = em.ring("St", 2 * G, [128, 128], F32)
        am_r = em.ring("am", 2 * G, [128, 128], BF16)
        hbank = [em.ps_views("ph%d" % g, 4, [128], F32) for g in range(G)]
        lb = self.lbv

        def pre_head(h, g):
            qs = qs_r.next(); sg = sg_r.next()
            fb = fb_r.next(); b = b_r.next(); d1 = d_r.next(); kk = kk_r.next()
            qtl = qt_r.next(); ktl = kt_r.next(); ktok = ktok_r.next(); vtok = vtok_r.next()
            sc = sc_r.next(); bl = bl_r.next(); oT = oT_r.next()
            em.dma("sp", qs[:], ZT[h * 128:(h + 1) * 128, :], writes=[qs], owner=qs)
            em.dma("sp", sg[:], ZT[2048 + h * 128:2048 + (h + 1) * 128, :], writes=[sg], owner=sg)
            em.dma("sp", vtok[0:16, 0, :], self.VTOK[0:16, h * 128:(h + 1) * 128], writes=[vtok], owner=vtok)
            em.dma("sp", vtok[:, 1:NT, :], self.VTOK[16:T, h * 128:(h + 1) * 128].rearrange("(i p) v -> p i v", p=128),
                   writes=[vtok], owner=vtok)
            em.op("dve", lambda e: e.tensor_scalar(out=fb[:], in0=sg[:], scalar1=self.oml[:, li, h:h + 1], scalar2=lb[:, li, h:h + 1],
                                                   op0=ALU.mult, op1=ALU.add), reads=[sg, self.oml, lb], writes=[fb])
            em.op("act", lambda e: e.activation(out=fb[:], in_=fb[:], func=AF.Ln), reads=[fb], writes=[fb])
            em.op("dve", lambda e: e.tensor_scalar(out=kk[:], in0=sg[:], scalar1=self.noml[:, li, h:h + 1], scalar2=self.oml[:, li, h:h + 1],
                                                   op0=ALU.mult, op1=ALU.add), reads=[sg, self.noml, self.oml], writes=[kk])
            em.op("dve", lambda e: e.tensor_tensor_scan(b[:], self.rmask[:], fb[:], 0.0, ALU.mult, ALU.add),
                  reads=[self.rmask, fb], writes=[b])
            em.op("pool", lambda e: e.tensor_copy(sc[:, 0, 0:1], b[:, 8:9]), reads=[b], writes=[sc])
            em.op("pool", lambda e: e.tensor_copy(sc[:, 0, 1:33], b[:, 48:T:64]), reads=[b], writes=[sc])
            em.op("pool", lambda e: e.tensor_copy(bl[:, 0:1], b[:, 15:16]), reads=[b], writes=[bl])
            em.op("pool", lambda e: e.tensor_copy(bl[:, 1:33], b[:, 79:T:64]), reads=[b], writes=[bl])
            em.op("dve", lambda e: e.tensor_sub(out=d1[:, 0:16], in0=b[:, 0:16], in1=sc[:, 0, 0:1].to_broadcast([128, 16])),
                  reads=[b, sc], writes=[d1])
            em.op("dve", lambda e: e.tensor_sub(out=d1[:, 16:T].rearrange("p (c j) -> p c j", j=64),
                                                in0=b[:, 16:T].rearrange("p (c j) -> p c j", j=64),
                                                in1=sc[:, 0, 1:33].unsqueeze(2).to_broadcast([128, 32, 64])),
                  reads=[b, sc], writes=[d1])
            em.op("act", lambda e: e.activation(out=sc[:, 1, :], in_=sc[:, 0, :], func=AF.Exp), reads=[sc], writes=[sc])
            em.op("act", lambda e: e.activation(out=sc[:, 3, :], in_=bl[:], func=AF.Exp), reads=[bl, sc], writes=[sc])
            em.op("dve", lambda e: e.tensor_sub(out=bl[:], in0=bl[:], in1=sc[:, 0, :]), reads=[bl, sc], writes=[bl])
            em.op("act", lambda e: e.activation(out=sc[:, 2, :], in_=bl[:], func=AF.Exp), reads=[bl, sc], writes=[sc])
            em.op("act", lambda e: e.activation(out=fb[:], in_=d1[:], func=AF.Exp), reads=[d1, fb], writes=[fb])
            em.op("dve", lambda e: e.tensor_mul(out=qtl[:], in0=qs[:], in1=fb[:]), reads=[qs, fb], writes=[qtl])
            em.op("act", lambda e: e.activation(out=d1[:], in_=d1[:], func=AF.Exp, scale=-1.0), reads=[d1], writes=[d1])
            em.op("pool", lambda e: e.tensor_mul(out=ktl[:], in0=kk[:], in1=d1[:]), reads=[kk, d1], writes=[ktl])
            for i0 in range(0, NT, 8):
                ii = list(range(i0, min(NT, i0 + 8)))
                for i in ii:
                    r0, n = trng(i)
                    em.op("pe", lambda e, i=i, r0=r0, n=n: e.transpose(ptk[0:n, i - i0, :], ktl[:, r0:r0 + n], self.idb[:, :]),
                          reads=[ktl, self.idb], writes=[ptk])
                if i0 == 0:
                    em.op("act", lambda e: e.activation(out=ktok[0:16, 0, :], in_=ptk[0:16, 0, :], func=AF.Copy), reads=[ptk], writes=[ktok])
                    em.op("act", lambda e: e.activation(out=ktok[:, 1:8, :], in_=ptk[:, 1:8, :], func=AF.Copy), reads=[ptk], writes=[ktok])
                else:
                    em.op("act", lambda e, i0=i0, m=len(ii): e.activation(out=ktok[:, i0:i0 + m, :], in_=ptk[:, 0:m, :], func=AF.Copy),
                          reads=[ptk], writes=[ktok])
            Sb = Sb_r.next()
            St = St_r.next()
            em.op("pool", lambda e: e.memset(Sb[:], 0.0), writes=[Sb])
            em.op("pool", lambda e: e.memset(St[:], 0.0), writes=[St])
            return dict(h=h, g=g, qtl=qtl, ktl=ktl, ktok=ktok, vtok=vtok, sc=sc, oT=oT, Sb=Sb, St=St, pss=None)

        def tile_part(c, i):
            r0, n = trng(i)
            qtl, ktl, vtok, oT = c["qtl"], c["ktl"], c["vtok"], c["oT"]
            psa = hbank[c["g"]][0]
            am = am_r.next()
            pso = hbank[c["g"]][0]
            em.op("pe", lambda e: e.matmul(psa[0:n, 0:n], ktl[:, r0:r0 + n], qtl[:, r0:r0 + n], start=True, stop=True),
                  reads=[ktl, qtl], writes=[psa])
            em.op("dve", lambda e: e.tensor_tensor(out=am[0:n, 0:n], in0=psa[0:n, 0:n], in1=self.cmask[0:n, 0:n], op=ALU.mult),
                  reads=[psa, self.cmask], writes=[am])
            em.op("pe", lambda e: e.matmul(pso[:, 0:n], vtok[0:n, i, :], am[0:n, 0:n], start=True, stop=True),
                  reads=[vtok, am], writes=[pso])
            em.op("act", lambda e: e.activation(out=oT[:, r0:r0 + n], in_=pso[:, 0:n], func=AF.Copy), reads=[pso], writes=[oT])

        def chunk_list():
            out = []
            for i in range(NT):
                r0, n = trng(i)
                for ci, (p0, ncx) in enumerate([(0, 16)] if i == 0 else [(0, 64), (64, 64)]):
                    j = 0 if i == 0 else 1 + 2 * (i - 1) + ci
                    out.append((i, j, p0, ncx, r0 + p0))
            return out

        CH = chunk_list()

        def emit_pss(c, idx):
            i, j, p0, ncx, c0 = CH[idx]
            pss = hbank[c["g"]][1 + (idx % 2)]
            em.op("pe", lambda e: e.matmul(pss[:], c["ktok"][p0:p0 + ncx, i, :], c["vtok"][p0:p0 + ncx, i, :], start=True, stop=True),
                  reads=[c["ktok"], c["vtok"]], writes=[pss])

        def chunk_pe(c, idx):
            i, j, p0, ncx, c0 = CH[idx]
            psi = hbank[c["g"]][3]
            Sb = c["Sb"]
            em.op("pe", lambda e: e.matmul(psi[:, 0:ncx], Sb[:], c["qtl"][:, c0:c0 + ncx], start=True, stop=True),
                  reads=[Sb, c["qtl"]], writes=[psi])
            if idx + 1 < len(CH):
                emit_pss(c, idx + 1)

        def chunk_dve(c, idx):
            i, j, p0, ncx, c0 = CH[idx]
            sc, oT = c["sc"], c["oT"]
            pss = hbank[c["g"]][1 + (idx % 2)]
            psi = hbank[c["g"]][3]
            St = c["St"]
            if idx + 1 < len(CH):
                jn = CH[idx + 1][1]
                S2 = S_rs[c["g"]].next()
                em.op("dve", lambda e: e.scalar_tensor_tensor(out=S2[:], in0=pss[:], scalar=sc[:, 2, j:j + 1], in1=St[:],
                                                              op0=ALU.mult, op1=ALU.add), reads=[pss, sc, St], writes=[S2])
                Sb2 = Sb_r.next()
                St2 = St_r.next()
                em.op("dve", lambda e: e.tensor_scalar(out=Sb2[:], in0=S2[:], scalar1=sc[:, 1, jn:jn + 1], scalar2=None, op0=ALU.mult),
                      reads=[S2, sc], writes=[Sb2])
                em.op("dve", lambda e: e.tensor_scalar(out=St2[:], in0=S2[:], scalar1=sc[:, 3, jn:jn + 1], scalar2=None, op0=ALU.mult),
                      reads=[S2, sc], writes=[St2])
                c["Sb"], c["St"] = Sb2, St2
            em.op("dve", lambda e: e.tensor_add(out=oT[:, c0:c0 + ncx], in0=oT[:, c0:c0 + ncx], in1=psi[:, 0:ncx]),
                  reads=[oT, psi], writes=[oT])

        def post_head(c):
            h, oT = c["h"], c["oT"]
            gs = qs_r.next(); d1 = d_r.next(); kk = kk_r.next()
            em.dma("sp", gs[:], ZT[4096 + h * 128:4096 + (h + 1) * 128, :], writes=[gs], owner=gs)
            em.op("act", lambda e: e.activation(out=d1[:], in_=oT[:], func=AF.Square), reads=[oT], writes=[d1])
            avg = self.avg[128]
            for (c0, n) in CCH:
                em.op("pe", lambda e: e.matmul(pn[:, 0:n], avg[:], d1[:, c0:c0 + n], start=True, stop=True), reads=[avg, d1], writes=[pn])
                em.op("act", lambda e: e.activation(out=kk[:, c0:c0 + n], in_=pn[:, 0:n], func=AF.Ln, bias=epsb[:, 0:1]),
                      reads=[pn, epsb], writes=[kk])
            em.op("act", lambda e: e.activation(out=kk[:], in_=kk[:], func=AF.Exp, scale=-0.5), reads=[kk], writes=[kk])
            em.op("dve", lambda e: e.tensor_mul(out=kk[:], in0=kk[:], in1=oT[:]), reads=[kk, oT], writes=[kk])
            mo = mo_r.next()
            em.op("dve", lambda e: e.scalar_tensor_tensor(out=mo[:], in0=kk[:], scalar=rng[:, h:h + 1], in1=gs[:], op0=ALU.mult, op1=ALU.mult),
                  reads=[kk, rng, gs], writes=[mo])
            em.dma("sp", self.MIXT[h * 128:(h + 1) * 128, :], mo[:], reads=[mo], owner=mo)

        for g0 in range(0, 16, G):
            ctx = [pre_head(g0 + g, g) for g in range(G)]
            for c in ctx:
                emit_pss(c, 0)
            idx = 0
            for i in range(NT):
                for c in ctx:
                    tile_part(c, i)
                for _ in ([0] if i == 0 else [0, 1]):
                    for c in ctx:
                        chunk_pe(c, idx)
                    for c in ctx:
                        chunk_dve(c, idx)
                    idx += 1
            for c in ctx:
                post_head(c)
        em.end()

    def stage_out(self, seqs, l, W, last):
        em = self.em
        em.begin()
        Wb = em.sb("Wb", [128, 16, D], BF16)
        if last:
            self.fgain = em.sb("fgain", [128, D], F32, dma=True)
            em.dma("sp", self.fgain[:], self.final_gain.partition_broadcast(128), writes=[self.fgain], owner=self.fgain)
        w32r = em.ring("wo32", 4, [128, D], F32, dma=True)
        Wbk = [Tk("Wb%d" % k, Wb.t) for k in range(16)]
        for k in range(16):
            w32 = w32r.next()
            em.dma("sp" if k % 2 == 0 else "act", w32[:], W[k * 128:(k + 1) * 128, :], writes=[w32], owner=w32)
            if k % 2 == 0:
                em.op("pool", lambda e, k=k: e.tensor_copy(Wb[:, k, :], w32[:]), reads=[w32], writes=[Wbk[k]])
            else:
                em.op("act", lambda e, k=k: e.activation(out=Wb[:, k, :], in_=w32[:], func=AF.Copy), reads=[w32], writes=[Wbk[k]])
        mtr = em.ring("mt", 2, [128, 16, 128], BF16, dma=True)
        htr = em.ring("ht", 2, [128, D], F32, dma=True)
        hnr = em.ring("hn", 2, [128, D], F32, dma=True)
        pmm = em.psring("pmm", 4, [128, 512], F32)
        junk = em.sb("junk", [128, D], BF16)
        ssr = em.ring("ss", 2, [128, 2], F32)
        tmr = em.ring("tm", 2, [128, 2], F32)
        for s, i in [(s_, i_) for s_ in seqs for i_ in range(NT)]:
            self.sel(s)
            r0, n = trng(i)
            if last and i == 0:
                continue
            mt = mtr.next()
            ht = htr.next()
            hn = hnr.next()
            em.dma("sp", mt[:, :, 0:n], self.MIXT[:, r0:r0 + n].rearrange("(k p) t -> p k t", p=128), writes=[mt], owner=mt)
            em.dma("sp", ht[0:n, :], self.h_src(s, l, i), writes=[ht], owner=ht)
            for c in range(4):
                ps = pmm.next()
                for k in range(16):
                    em.op("pe", lambda e, k=k, c=c: e.matmul(ps[0:n, :], mt[:, k, 0:n], Wb[:, k, c * 512:(c + 1) * 512],
                                                             start=(k == 0), stop=(k == 15)), reads=[mt, Wbk[k]], writes=[ps])
                em.op("dve", lambda e, c=c: e.tensor_add(out=hn[0:n, c * 512:(c + 1) * 512], in0=ht[0:n, c * 512:(c + 1) * 512], in1=ps[0:n, :]),
                      reads=[ht, ps], writes=[hn])
            if not last:
                em.dma("sp", self.H[r0:r0 + n, :], hn[0:n, :], reads=[hn], owner=hn)
            else:
                ss = ssr.next()
                tm = tmr.next()
                em.op("pool", lambda e: e.memset(ss[:], 0.0), writes=[ss])
                em.op("act", lambda e: e.activation(out=junk[0:n, :], in_=hn[0:n, :], func=AF.Square, accum_out=ss[0:n, 0:1]),
                      reads=[hn], writes=[junk, ss])
                self.rstd_rows(ss, n, D, RMS_EPS, tm)
                em.op("dve", lambda e: e.scalar_tensor_tensor(out=ht[0:n, :], in0=hn[0:n, :], scalar=ss[0:n, 1:2], in1=self.fgain[0:n, :],
                                                              op0=ALU.mult, op1=ALU.mult), reads=[hn, ss, self.fgain], writes=[ht])
                em.dma("sp", self.out[s, r0 - 16:r0 - 16 + n, :], ht[0:n, :], reads=[ht], owner=ht)
        em.end()


_CACHE = {}


def kernel(**inputs):
    x = np.ascontiguousarray(inputs["x"], dtype=np.float32)
    if "nc" not in _CACHE:
        _CACHE["nc"] = Prog().build()
    nc = _CACHE["nc"]
    oh = _bucket_onehot()
    names = ["meta_tokens", "norm_gain", "final_norm_gain", "rel_bias_table", "w_in_even", "conv_w", "conv_b",
             "conv_ln_gain", "conv_ln_bias", "kv_norm_gain", "w_uk", "w_uv", "w_out_even", "w_in_odd", "lb_logits",
             "rec_norm_gain", "w_out_odd"]
    shared = {k: np.ascontiguousarray(inputs[k], dtype=np.float32) for k in names}
    shared["c_oh"] = oh
    in_maps = []
    for c in range(NCORES):
        m = dict(shared)
        m["x"] = x[c * SEQ_PER_CORE:(c + 1) * SEQ_PER_CORE]
        in_maps.append(m)
    res = run_bass_kernel_spmd(nc, in_maps, core_ids=list(range(NCORES)))
    return np.concatenate([r["out"] for r in res.results], axis=0)
```

```python
import math
from contextlib import ExitStack

import numpy as np
import concourse.bass as bass
import concourse.mybir as mybir
from concourse.bass_utils import run_bass_kernel_spmd

F32 = mybir.dt.float32
BF16 = mybir.dt.bfloat16
AF = mybir.ActivationFunctionType
ALU = mybir.AluOpType
AX = mybir.AxisListType

NCORES = 8
SEQ_PER_CORE = 2
D = 2048
SEQ = 2048
NMETA = 16
T = SEQ + NMETA
NT = 17
DEPTH = 4
P_EVEN = 6480
RMS_EPS = 1e-6
LN_EPS = 1e-5
NEG = -1.0e30
NEG2 = -3.0e38
TOPK = 256


def trng(i):
    return (0, 16) if i == 0 else (16 + 128 * (i - 1), 128)


CCH = [(0, 16)] + [(16 + 512 * j, 512) for j in range(4)]


class Tk:
    __slots__ = ("name", "t", "w", "r", "dkey", "bank")

    def __init__(self, name, t=None):
        self.name = name
        self.t = t
        self.w = {}
        self.r = {}
        self.dkey = None
        self.bank = None

    def __getitem__(self, idx):
        return self.t[idx]


class Ring:
    def __init__(self, bufs):
        self.bufs = bufs
        self.i = 0

    def next(self):
        b = self.bufs[self.i % len(self.bufs)]
        self.i += 1
        return b


class Em:
    ENG = ("pe", "act", "dve", "pool", "sp")

    def __init__(self, nc, n_dsem=78):
        self.nc = nc
        self.top = ExitStack()
        self.eng = dict(pe=nc.tensor, act=nc.scalar, dve=nc.vector, pool=nc.gpsimd, sp=nc.sync)
        self.sems = {}
        self.val = {}
        for e in self.ENG:
            self.sems[e] = self.top.enter_context(nc.semaphore("es_" + e))
            self.val[e] = 0
        self.bar = self.top.enter_context(nc.semaphore("bar"))
        self.nbar = 0
        self.free_d = []
        for i in range(n_dsem):
            k = "D%d" % i
            self.sems[k] = self.top.enter_context(nc.semaphore("ds_%d" % i))
            self.val[k] = 0
            self.free_d.append(k)
        self.free_sw = []
        for i in range(10):
            k = "DS%d" % i
            self.sems[k] = self.top.enter_context(nc.semaphore("dsw_%d" % i))
            self.val[k] = 0
            self.free_sw.append(k)
        self.stage_sw = []
        self.seen = {e: {} for e in self.ENG}
        self.stage = None
        self.stage_d = []
        self.uid = 0
        self.nins = 0
        self.reg = {}

    def begin(self):
        self.stage = ExitStack()
        self.stage_d = []

    def end(self):
        self.barrier()
        self.stage.close()
        self.stage = None
        self.free_d.extend(self.stage_d)
        self.stage_d = []
        self.free_sw.extend(self.stage_sw)
        self.stage_sw = []

    def _nm(self, name):
        self.uid += 1
        return "%s_%d" % (name, self.uid)

    def sb(self, name, shape, dt=F32, dma=False, top=False):
        st = self.top if top else self.stage
        nm = self._nm(name)
        t = st.enter_context(self.nc.sbuf_tensor(nm, list(shape), dt))
        self.reg[name] = nm
        tk = Tk(name, t)
        if dma == "sw":
            k = self.free_sw.pop()
            tk.dkey = k
            if not top:
                self.stage_sw.append(k)
        elif dma:
            k = self.free_d.pop()
            tk.dkey = k
            if not top:
                self.stage_d.append(k)
        return tk

    def ring(self, name, n, shape, dt=F32, dma=False):
        return Ring([self.sb("%s%d" % (name, i), shape, dt, dma=dma) for i in range(n)])

    def ps(self, name, shape, dt=F32):
        t = self.stage.enter_context(self.nc.psum_tensor(self._nm(name), list(shape), dt))
        return Tk(name, t)

    def ps_views(self, name, n, sub_shape, dt=F32):
        t = self.stage.enter_context(self.nc.psum_tensor(self._nm(name), [128, n] + list(sub_shape), dt))
        bank = Tk(name + "_bank")
        views = [Tk("%s%d" % (name, i), t[:, i]) for i in range(n)]
        for v in views:
            v.bank = bank
        return views

    def psring(self, name, n, shape, dt=F32):
        return Ring([self.ps("%s%d" % (name, i), shape, dt) for i in range(n)])

    def _wait(self, eng, toks):
        need = {}
        for d in toks:
            for k, v in d.items():
                if v > need.get(k, 0):
                    need[k] = v
        seen = self.seen[eng]
        for k, v in need.items():
            if k == eng and eng == "pe":
                continue
            if seen.get(k, 0) >= v:
                continue
            self.eng[eng].wait_ge(self.sems[k], v)
            seen[k] = v

    @staticmethod
    def _deps(reads, writes):
        toks = []
        for t in reads:
            toks.append(t.w)
            if t.bank is not None:
                toks.append(t.bank.w)
        for t in writes:
            toks.append(t.w)
            toks.append(t.r)
            if t.bank is not None:
                toks.append(t.bank.r)
        return toks

    @staticmethod
    def _mark(k, v, reads, writes):
        for t in reads:
            if t.r.get(k, 0) < v:
                t.r[k] = v
            if t.bank is not None and t.bank.r.get(k, 0) < v:
                t.bank.r[k] = v
        for t in writes:
            t.w = {k: v}
            t.r = {}
            if t.bank is not None:
                t.bank.w = {k: v}

    def op(self, eng, fn, reads=(), writes=()):
        self._wait(eng, self._deps(reads, writes))
        ins = fn(self.eng[eng])
        self.val[eng] += 1
        ins.then_inc(self.sems[eng], 1)
        self._mark(eng, self.val[eng], reads, writes)
        self.nins += 1
        return ins

    def dma(self, q, out, in_, reads=(), writes=(), owner=None, **kw):
        self._wait(q, self._deps(reads, writes))
        ins = self.eng[q].dma_start(out=out, in_=in_, **kw)
        k = owner.dkey
        self.val[k] += 16
        ins.then_inc(self.sems[k], 16)
        self._mark(k, self.val[k], reads, writes)
        self.nins += 1
        return ins

    def barrier(self):
        self.nbar += 1
        for e in self.ENG:
            g = self.eng[e]
            if self.val[e] > 0:
                g.wait_ge(self.sems[e], self.val[e])
            if e == "sp":
                for k, v in self.val.items():
                    if k[0] == "D" and v > 0 and self.seen["sp"].get(k, 0) < v:
                        g.wait_ge(self.sems[k], v)
            g.sem_inc(self.bar, 1)
        for e in self.ENG:
            self.eng[e].wait_ge(self.bar, 5 * self.nbar)
        for e in self.ENG:
            for k, v in self.val.items():
                self.seen[e][k] = v

    def close(self):
        self.top.close()


def _t5_bucket_np(rel):
    rel = np.asarray(rel, dtype=np.int32)
    nb = 16
    ret = np.where(rel > 0, nb, 0).astype(np.int32)
    n = np.abs(rel)
    max_exact = nb // 2
    nf = np.maximum(n, 1).astype(np.float32)
    large = max_exact + (np.log(nf / np.float32(max_exact)) / np.float32(math.log(128 / max_exact))
                         * np.float32(nb - max_exact)).astype(np.int32)
    large = np.minimum(large, nb - 1)
    return ret + np.where(n < max_exact, n, large)


def _bucket_onehot():
    rel = 255 - np.arange(512)
    b = _t5_bucket_np(rel)
    oh = np.zeros((32, 512), np.float32)
    oh[b, np.arange(512)] = 1.0
    return oh


class Prog:
    def __init__(self, nseq=SEQ_PER_CORE, layers=DEPTH, dbg=False, stop=10 ** 9):
        self.stop = stop
        self.nstage = 0
        self.nseq = nseq
        self.layers = layers
        self.dbg = dbg if dbg else ()
        ne = max(1, (layers + 1) // 2)
        no = layers // 2
        od = (lambda *sh: [max(no, 1)] + ([1] * len(sh) if no == 0 else list(sh)))
        nc = bass.Bass("TRN2", target_bir_lowering=False)
        self.nc = nc
        dt = nc.dram_tensor
        I = "ExternalInput"
        self.x = dt("x", [nseq, SEQ, D], F32, kind=I).ap()
        self.meta = dt("meta_tokens", [NMETA, D], F32, kind=I).ap()
        self.norm_gain = dt("norm_gain", [4, D], F32, kind=I).ap()
        self.final_gain = dt("final_norm_gain", [D], F32, kind=I).ap()
        self.relb = dt("rel_bias_table", [32, 8], F32, kind=I).ap()
        self.w_in_even = dt("w_in_even", [ne, D, P_EVEN], F32, kind=I).ap()
        self.conv_w = dt("conv_w", [2, 31, 1024], F32, kind=I).ap()
        self.conv_b = dt("conv_b", [2, 1024], F32, kind=I).ap()
        self.ln_g = dt("conv_ln_gain", [2, 1024], F32, kind=I).ap()
        self.ln_b = dt("conv_ln_bias", [2, 1024], F32, kind=I).ap()
        self.kv_g = dt("kv_norm_gain", [2, 256], F32, kind=I).ap()
        self.w_uk = dt("w_uk", [ne, 8, 256, 128], F32, kind=I).ap()
        self.w_uv = dt("w_uv", [ne, 8, 256, 128], F32, kind=I).ap()
        self.w_out_even = dt("w_out_even", [ne, D, D], F32, kind=I).ap()
        self.w_in_odd = dt("w_in_odd", od(D, 4 * D), F32, kind=I).ap()
        self.lb_logits = dt("lb_logits", [4, D], F32, kind=I).ap()
        self.rec_g = dt("rec_norm_gain", [2, D], F32, kind=I).ap()
        self.w_out_odd = dt("w_out_odd", od(D, D), F32, kind=I).ap()
        self.c_oh = dt("c_oh", [32, 512], F32, kind=I).ap()
        self.out = dt("out", [nseq, SEQ, D], F32, kind="ExternalOutput").ap()
        sk = lambda n: "ExternalOutput" if n in self.dbg else "Internal"
        self.H_all = dt("Hs", [nseq, T, D], F32, kind=sk("Hs")).ap()
        self.ZT = dt("ZTs", [4 * D, T], F32, kind=sk("ZTs")).ap()
        self.VTOK = dt("VTOKs", [T, D], BF16, kind=sk("VTOKs")).ap()
        self.ZB = dt("ZBs", [2176, T], BF16, kind=sk("ZBs")).ap()
        self.MIXT_all = dt("MIXTs", [nseq, D, T], BF16, kind=sk("MIXTs")).ap()
        self.FD = dt("FDs", [8, 512], F32, kind=sk("FDs")).ap()
        self.em = Em(nc)

    def build(self):
        em = self.em

        def run(f, *a, **k):
            if self.nstage < self.stop:
                f(*a, **k)
            self.nstage += 1

        def sel(s):
            self.H = self.H_all[s]
            self.MIXT = self.MIXT_all[s]

        self.sel = sel
        run(self.setup_consts)
        for l in range(self.layers):
            last = (l == self.layers - 1)
            for s in range(self.nseq):
                sel(s)
                if l % 2 == 0:
                    run(self.stage_norm_proj, s, l, even=True)
                    run(self.stage_conv, l // 2)
                    run(self.stage_attn, l // 2)
                else:
                    run(self.stage_norm_proj, s, l, even=False)
                    run(self.stage_rec, l)
            W = self.w_out_even[l // 2] if l % 2 == 0 else self.w_out_odd[l // 2]
            run(self.stage_out, list(range(self.nseq)), l, W, last)
        em.close()
        return self.nc

    def vecT(self, dst, dst_cols, src_rows_ap, nrows, ps, stg):
        em = self.em
        em.dma("sp", stg[0:nrows, :], src_rows_ap, writes=[stg], owner=stg)
        em.op("pe", lambda e: e.transpose(ps[:, 0:nrows], stg[0:nrows, :], self.idf[0:nrows, 0:nrows]),
              reads=[stg, self.idf], writes=[ps])
        em.op("act", lambda e: e.activation(out=dst[:, dst_cols:dst_cols + nrows], in_=ps[:, 0:nrows], func=AF.Copy),
              reads=[ps], writes=[dst])

    def setup_consts(self):
        em = self.em
        nc = self.nc
        self.idf = em.sb("idf", [128, 128], F32, top=True)
        self.idb = em.sb("idb", [128, 128], BF16, top=True)
        self.onesb = em.sb("onesb", [128, 128], BF16, top=True)
        self.avg = {}
        for n in (128, 256, 1024):
            self.avg[n] = em.sb("avg%d" % n, [128, 128], F32, top=True)
        self.cmask = em.sb("cmask", [128, 128], F32, top=True)
        self.EB = em.sb("EB", [128, 3, 8, 128], F32, top=True)
        self.bfar = em.sb("bfar", [128, 8], F32, top=True, dma=True)
        self.lbv = em.sb("lbv", [128, 2, 16], F32, top=True)
        self.oml = em.sb("oml", [128, 2, 16], F32, top=True)
        self.noml = em.sb("noml", [128, 2, 16], F32, top=True)

        em.begin()
        P = lambda f, **k: em.op("pool", f, **k)
        P(lambda e: e.memset(self.idf[:], 1.0), writes=[self.idf])
        P(lambda e: e.affine_select(out=self.idf[:], in_=self.idf[:], pattern=[[-1, 128]], compare_op=ALU.is_equal,
                                    fill=0.0, base=0, channel_multiplier=1), reads=[self.idf], writes=[self.idf])
        P(lambda e: e.tensor_copy(self.idb[:], self.idf[:]), reads=[self.idf], writes=[self.idb])
        P(lambda e: e.memset(self.onesb[:], 1.0), writes=[self.onesb])
        for n in (128, 256, 1024):
            P(lambda e, n=n: e.memset(self.avg[n][:], 1.0 / n), writes=[self.avg[n]])
        P(lambda e: e.memset(self.cmask[:], 1.0), writes=[self.cmask])
        P(lambda e: e.affine_select(out=self.cmask[:], in_=self.cmask[:], pattern=[[1, 128]], compare_op=ALU.is_ge,
                                    fill=0.0, base=0, channel_multiplier=-1), reads=[self.cmask], writes=[self.cmask])
        P(lambda e: e.memset(self.cmask[0:64, 64:128], 0.0), writes=[self.cmask])

        anti = em.sb("anti", [128, 128], F32)
        P(lambda e: e.memset(anti[:], 1.0), writes=[anti])
        P(lambda e: e.affine_select(out=anti[:], in_=anti[:], pattern=[[1, 128]], compare_op=ALU.is_equal,
                                    fill=0.0, base=-127, channel_multiplier=1), reads=[anti], writes=[anti])
        tab = em.sb("tab", [32, 8], F32, dma=True)
        oh = em.sb("oh", [32, 512], F32, dma=True)
        fsb = em.sb("fsb", [8, 512], F32, dma=True)
        nbfar = em.sb("nbfar", [128, 8], F32)
        psA = em.ps("psA", [128, 512], F32)
        psB = em.ps("psB", [128, 128], F32)
        em.dma("sp", tab[:], self.relb[:, :], writes=[tab], owner=tab)
        em.dma("sp", oh[:], self.c_oh[:, :], writes=[oh], owner=oh)
        em.dma("sp", self.bfar[:], bass.AP(tensor=self.relb.tensor, offset=15 * 8, ap=[[0, 128], [1, 8]]),
               writes=[self.bfar], owner=self.bfar)
        em.op("dve", lambda e: e.tensor_scalar(out=nbfar[:], in0=self.bfar[:], scalar1=-1.0, scalar2=None, op0=ALU.mult),
              reads=[self.bfar], writes=[nbfar])
        em.op("pe", lambda e: e.matmul(psA[0:8, :], tab[0:32, 0:8], oh[0:32, :], start=True, stop=True),
              reads=[tab, oh], writes=[psA])
        em.op("act", lambda e: e.activation(out=fsb[:], in_=psA[0:8, :], func=AF.Copy), reads=[psA], writes=[fsb])
        fd_tk = Tk("FD")
        em.dma("sp", self.FD[:, :], fsb[:], reads=[fsb], writes=[fd_tk], owner=fsb)
        hk = em.ring("hk", 2, [128, 128], F32, dma=True)
        for ty, c0 in enumerate((128, 256, 144)):
            for h in range(8):
                hb = hk.next()
                src = bass.AP(tensor=self.FD.tensor, offset=h * 512 + c0, ap=[[1, 128], [1, 128]])
                em.dma("sp", hb[:], src, reads=[fd_tk], writes=[hb], owner=hb)
                em.op("pe", lambda e, hb=hb: e.matmul(psB[:], anti[:], hb[:], start=True, stop=True),
                      reads=[anti, hb], writes=[psB])
                em.op("act", lambda e, ty=ty, h=h: e.activation(out=self.EB[:, ty, h, :], in_=psB[:], func=AF.Exp,
                                                                bias=nbfar[:, h:h + 1]),
                      reads=[psB, nbfar], writes=[self.EB])

        stg = em.sb("stg", [128, 128], F32, dma=True)
        lbT = em.sb("lbT", [128, 64], F32)
        self.vecT(lbT, 0, self.lb_logits.rearrange("l (c p) -> (l c) p", p=128), 64, psB, stg)
        mx = em.sb("mx", [128, 16], F32)
        ex = em.sb("ex", [128, 4, 16], F32)
        sm = em.sb("sm", [128, 16], F32)
        rs = em.sb("rs", [128, 16], F32)
        c1 = em.sb("c1", [128, 16], F32)
        V = lambda f, **k: em.op("dve", f, **k)
        V(lambda e: e.tensor_max(out=mx[:], in0=lbT[:, 0:16], in1=lbT[:, 16:32]), reads=[lbT], writes=[mx])
        V(lambda e: e.tensor_max(out=mx[:], in0=mx[:], in1=lbT[:, 32:48]), reads=[lbT, mx], writes=[mx])
        V(lambda e: e.tensor_max(out=mx[:], in0=mx[:], in1=lbT[:, 48:64]), reads=[lbT, mx], writes=[mx])
        for l in range(4):
            V(lambda e, l=l: e.tensor_sub(out=ex[:, l, :], in0=lbT[:, 16 * l:16 * l + 16], in1=mx[:]),
              reads=[lbT, mx], writes=[ex])
        em.op("act", lambda e: e.activation(out=ex[:], in_=ex[:], func=AF.Exp), reads=[ex], writes=[ex])
        V(lambda e: e.tensor_add(out=sm[:], in0=ex[:, 0, :], in1=ex[:, 1, :]), reads=[ex], writes=[sm])
        V(lambda e: e.tensor_add(out=sm[:], in0=sm[:], in1=ex[:, 2, :]), reads=[ex, sm], writes=[sm])
        V(lambda e: e.tensor_add(out=sm[:], in0=sm[:], in1=ex[:, 3, :]), reads=[ex, sm], writes=[sm])
        V(lambda e: e.reciprocal(out=rs[:], in_=sm[:]), reads=[sm], writes=[rs])
        V(lambda e: e.tensor_mul(out=self.lbv[:, 0, :], in0=ex[:, 1, :], in1=rs[:]), reads=[ex, rs], writes=[self.lbv])
        V(lambda e: e.tensor_add(out=c1[:], in0=ex[:, 1, :], in1=ex[:, 2, :]), reads=[ex], writes=[c1])
        V(lambda e: e.tensor_add(out=c1[:], in0=c1[:], in1=ex[:, 3, :]), reads=[ex, c1], writes=[c1])
        V(lambda e: e.tensor_mul(out=self.lbv[:, 1, :], in0=c1[:], in1=rs[:]), reads=[c1, rs], writes=[self.lbv])
        V(lambda e: e.tensor_scalar(out=self.oml[:], in0=self.lbv[:], scalar1=-1.0, scalar2=1.0, op0=ALU.mult, op1=ALU.add),
          reads=[self.lbv], writes=[self.oml])
        V(lambda e: e.tensor_scalar(out=self.noml[:], in0=self.oml[:], scalar1=-1.0, scalar2=None, op0=ALU.mult),
          reads=[self.oml], writes=[self.noml])
        em.end()

    def h_src(self, s, l, i):
        r0, n = trng(i)
        if l == 0:
            if i == 0:
                return self.meta[:, :]
            return self.x[s, r0 - 16:r0 - 16 + n, :]
        return self.H[r0:r0 + n, :]

    def rstd_rows(self, ss, n, dim, eps, tmp):
        em = self.em
        em.op("dve", lambda e: e.tensor_scalar(out=tmp[0:n, 0:1], in0=ss[0:n, 0:1], scalar1=1.0 / dim, scalar2=eps,
                                               op0=ALU.mult, op1=ALU.add), reads=[ss], writes=[tmp])
        em.op("act", lambda e: e.activation(out=tmp[0:n, 1:2], in_=tmp[0:n, 0:1], func=AF.Sqrt), reads=[tmp], writes=[tmp])
        em.op("dve", lambda e: e.reciprocal(out=ss[0:n, 1:2], in_=tmp[0:n, 1:2]), reads=[tmp], writes=[ss])

    def stage_norm_proj(self, s, l, even):
        em = self.em
        em.begin()
        hnT = em.sb("hnT", [128, 16, T], BF16)
        outer = em.stage
        nrm = ExitStack()
        em.stage = nrm
        gbc = em.sb("gbc", [128, D], F32, dma=True)
        em.dma("sp", gbc[:], self.norm_gain[l, :].partition_broadcast(128), writes=[gbc], owner=gbc)
        htr = em.ring("ht", 4, [128, D], F32, dma=True)
        hsr = em.ring("hs", 3, [128, D], BF16)
        junk = em.sb("junk", [128, D], BF16)
        ssr = em.ring("ss", 4, [128, 2], F32)
        tmr = em.ring("tm", 4, [128, 2], F32)
        ptr = em.psring("ptr", 2, [128, 16, 128], BF16)
        for i in range(NT):
            r0, n = trng(i)
            ht = htr.next()
            hs = hsr.next()
            ss = ssr.next()
            tm = tmr.next()
            pt = ptr.next()
            em.dma("sp", ht[0:n, :], self.h_src(s, l, i), writes=[ht], owner=ht)
            em.op("pool", lambda e: e.memset(ss[:], 0.0), writes=[ss])
            em.op("act", lambda e: e.activation(out=junk[0:n, :], in_=ht[0:n, :], func=AF.Square, accum_out=ss[0:n, 0:1]),
                  reads=[ht], writes=[junk, ss])
            self.rstd_rows(ss, n, D, RMS_EPS, tm)
            em.op("dve", lambda e: e.scalar_tensor_tensor(out=hs[0:n, :], in0=ht[0:n, :], scalar=ss[0:n, 1:2],
                                                          in1=gbc[0:n, :], op0=ALU.mult, op1=ALU.mult),
                  reads=[ht, ss, gbc], writes=[hs])
            for k in range(16):
                em.op("pe", lambda e, k=k: e.transpose(pt[:, k, 0:n], hs[0:n, k * 128:(k + 1) * 128], self.idb[0:n, 0:n]),
                      reads=[hs, self.idb], writes=[pt])
            em.op("act", lambda e: e.activation(out=hnT[:, :, r0:r0 + n], in_=pt[:, :, 0:n], func=AF.Copy),
                  reads=[pt], writes=[hnT])
        self.hnT = hnT
        em.barrier()
        nrm.close()
        em.stage = outer
        if even:
            self.proj_even(l // 2)
        else:
            self.proj_odd(l // 2)
        em.end()

    def proj_fm(self, W, chunks, wtr32, wtr, pmm, otr, otbr=None):
        em = self.em
        hnT = self.hnT
        blocks = []
        for ch in chunks:
            col0, ncols, func, scale, dst0, dup = ch
            if (blocks and not dup and ncols == 128 and len(blocks[-1]) < 4 and not blocks[-1][-1][5]
                    and blocks[-1][-1][1] == 128 and blocks[-1][-1][0] + 128 == col0):
                blocks[-1].append(ch)
            else:
                blocks.append([ch])
        for blk in blocks:
            w32 = wtr32.next()
            wb = wtr.next()
            bcol0 = blk[0][0]
            bn = sum(c[1] for c in blk)
            em.dma("sp", w32[:, :, 0:bn], W[:, bcol0:bcol0 + bn].rearrange("(kc p) m -> p kc m", p=128),
                   writes=[w32], owner=w32)
            if blk[0][5]:
                em.dma("sp", w32[:, :, bn:2 * bn], W[:, bcol0:bcol0 + bn].rearrange("(kc p) m -> p kc m", p=128),
                       writes=[w32], owner=w32)
            for bi, (col0, ncols, func, scale, dst0, dup) in enumerate(blk):
                o = col0 - bcol0
                nm = ncols * (2 if dup else 1)
                em.op("pool", lambda e, o=o, nm=nm: e.tensor_copy(wb[:, :, o:o + nm], w32[:, :, o:o + nm]), reads=[w32], writes=[wb])
            for bi, (col0, ncols, func, scale, dst0, dup) in enumerate(blk):
                o = col0 - bcol0
                nm = ncols * (2 if dup else 1)
                tobf = dst0 < 0
                ot = otbr.next() if tobf else otr.next()
                for (c0, n) in CCH:
                    ps = pmm.next()
                    for k in range(16):
                        em.op("pe", lambda e, k=k: e.matmul(ps[0:nm, 0:n], wb[:, k, o:o + nm], hnT[:, k, c0:c0 + n],
                                                            start=(k == 0), stop=(k == 15)),
                              reads=[wb, hnT], writes=[ps])
                    em.op("act", lambda e: e.activation(out=ot[0:nm, c0:c0 + n], in_=ps[0:nm, 0:n], func=func, scale=scale),
                          reads=[ps], writes=[ot])
                if tobf:
                    r0 = -dst0 - 1
                    em.dma("act", self.ZB[r0:r0 + nm, :], ot[0:nm, :], reads=[ot], owner=ot)
                else:
                    em.dma("act", self.ZT[dst0:dst0 + nm, :], ot[0:nm, :], reads=[ot], owner=ot)

    ZE = dict(glu_v=0, glu_g=1024, gate_a=2048, q=3072, c=4096, gate_b=4352, qi=5376, ki=6400, wi=6528)

    def proj_even(self, e_):
        em = self.em
        W = self.w_in_even[e_]
        chunks = []
        for m in range(50):
            col0 = m * 128
            if col0 < 1024:
                f, sc = AF.Copy, 1.0
            elif col0 < 2048:
                f, sc = AF.Sigmoid, 1.0
            elif col0 < 3072:
                f, sc = AF.Silu, 1.0
            elif col0 < 4352:
                f, sc = AF.Copy, 1.0
            elif col0 < 5376:
                f, sc = AF.Silu, 1.0
            else:
                f, sc = AF.Copy, 0.125
            dst = col0
            if 3072 <= col0 < 4096:
                dst = -(col0 - 3072) - 1
            elif 5376 <= col0 < 6400:
                dst = -(1024 + col0 - 5376) - 1
            chunks.append((col0, 128, f, sc, dst, False))
        chunks.append((6400, 64, AF.Copy, 1.0, -2048 - 1, True))
        chunks.append((6464, 16, AF.Copy, 0.25, 6528, False))
        wtr32 = em.ring("w32", 2, [128, 16, 512], F32, dma=True)
        wtr = em.ring("wb", 2, [128, 16, 512], BF16)
        pmm = em.psring("pmm", 4, [128, 512], F32)
        otr = em.ring("ot", 2, [128, T], F32, dma=True)
        otbr = em.ring("otb", 2, [128, T], BF16, dma=True)
        self.proj_fm(W, chunks, wtr32, wtr, pmm, otr, otbr)

    def proj_odd(self, o_):
        em = self.em
        W = self.w_in_odd[o_]
        chunks = []
        for m in range(16):
            chunks.append((m * 128, 128, AF.Silu, 1.0, m * 128, False))
        for m in range(16):
            chunks.append((2048 + m * 128, 128, AF.Sigmoid, 1.0, 2048 + m * 128, False))
        for m in range(16):
            chunks.append((6144 + m * 128, 128, AF.Silu, 1.0, 4096 + m * 128, False))
        wtr32 = em.ring("w32", 2, [128, 16, 512], F32, dma=True)
        wtr = em.ring("wb", 2, [128, 16, 512], BF16)
        pmm = em.psring("pmm", 4, [128, 512], F32)
        otr = em.ring("ot", 2, [128, T], F32, dma=True)
        self.proj_fm(W, chunks, wtr32, wtr, pmm, otr)
        hnT = self.hnT
        vo = em.ring("vo", 2, [128, 512], BF16, dma=True)
        for g in range(4):
            w32 = wtr32.next()
            wb = wtr.next()
            em.dma("sp", w32[:], W[:, 4096 + g * 512:4096 + (g + 1) * 512].rearrange("(kc p) m -> p kc m", p=128),
                   writes=[w32], owner=w32)
            for k4 in range(4):
                em.op("pool", lambda e, k4=k4: e.tensor_copy(wb[:, 4 * k4:4 * k4 + 4, :], w32[:, 4 * k4:4 * k4 + 4, :]), reads=[w32], writes=[wb])
            for i in range(NT):
                r0, n = trng(i)
                ps = pmm.next()
                for k in range(16):
                    em.op("pe", lambda e, k=k: e.matmul(ps[0:n, :], hnT[:, k, r0:r0 + n], wb[:, k, :],
                                                        start=(k == 0), stop=(k == 15)),
                          reads=[wb, hnT], writes=[ps])
                v = vo.next()
                em.op("act", lambda e: e.activation(out=v[0:n, :], in_=ps[0:n, :], func=AF.Copy), reads=[ps], writes=[v])
                em.dma("act", self.VTOK[r0:r0 + n, g * 512:(g + 1) * 512], v[0:n, :], reads=[v], owner=v)

    def stage_conv(self, e_):
        em = self.em
        ZE = self.ZE
        em.begin()
        stg = em.sb("stg", [128, 128], F32, dma=True)
        pst = em.ps("pst", [128, 128], F32)
        cw = em.sb("cw", [128, 248], F32)
        cv = em.sb("cv", [128, 24], F32)
        cwr = self.conv_w[e_].rearrange("j (cc p) -> (j cc) p", p=128)
        self.vecT(cw, 0, cwr[0:124, :], 124, pst, stg)
        self.vecT(cw, 124, cwr[124:248, :], 124, pst, stg)
        self.vecT(cv, 0, self.conv_b[e_].rearrange("(cc p) -> cc p", p=128), 8, pst, stg)
        self.vecT(cv, 8, self.ln_g[e_].rearrange("(cc p) -> cc p", p=128), 8, pst, stg)
        self.vecT(cv, 16, self.ln_b[e_].rearrange("(cc p) -> cc p", p=128), 8, pst, stg)
        uall = em.sb("uall", [128, 8, T], F32)
        acc1 = em.sb("acc1", [128, T], F32)
        acc2 = em.sb("acc2", [128, T], F32)
        gsr = em.ring("gs", 1, [128, T], F32, dma=True)
        upr = em.ring("up", 2, [128, 30 + T], F32, dma=True)
        pa = em.ring("pa", 1, [128, T], F32)
        pb = em.ring("pb", 1, [128, T], F32)
        tmpr = em.ring("ctmp", 2, [128, T], F32)
        dgr = em.ring("dg", 3, [128, 128], F32)
        pcs = [em.ps("pc%d" % i, [128, 512], F32) for i in range(5)]
        sq = em.ring("sq", 1, [128, T], F32)
        for b in upr.bufs:
            em.op("pool", lambda e, b=b: e.memset(b[:, 0:30], 0.0), writes=[b])
        def load_u(cc):
            gs = gsr.next()
            up = upr.next()
            em.dma("sp", up[:, 30:30 + T], self.ZT[ZE["glu_v"] + cc * 128:ZE["glu_v"] + (cc + 1) * 128, :], writes=[up], owner=up)
            em.dma("sp", gs[:], self.ZT[ZE["glu_g"] + cc * 128:ZE["glu_g"] + (cc + 1) * 128, :], writes=[gs], owner=gs)
            em.op("pool", lambda e: e.tensor_tensor(out=up[:, 30:30 + T], in0=up[:, 30:30 + T], in1=gs[:], op=ALU.mult),
                  reads=[up, gs], writes=[up])
            return up

        up_next = load_u(0)
        for cc in range(8):
            up = up_next
            A = pa.next()
            B = pb.next()
            w = lambda j: cw[:, j * 8 + cc:j * 8 + cc + 1]
            em.op("dve", lambda e: e.tensor_scalar(out=A[:], in0=up[:, 0:T], scalar1=w(0), scalar2=cv[:, cc:cc + 1],
                                                   op0=ALU.mult, op1=ALU.add), reads=[up, cw, cv], writes=[A])
            for j in range(1, 10):
                em.op("dve", lambda e, j=j: e.scalar_tensor_tensor(out=A[:], in0=up[:, j:j + T], scalar=w(j), in1=A[:],
                                                                   op0=ALU.mult, op1=ALU.add), reads=[up, cw, A], writes=[A])
            em.op("act", lambda e: e.activation(out=B[:], in_=up[:, 10:10 + T], func=AF.Copy, scale=w(10)),
                  reads=[up, cw], writes=[B])
            for j in range(11, 17):
                tp = tmpr.next()
                em.op("act", lambda e, j=j, tp=tp: e.activation(out=tp[:], in_=up[:, j:j + T], func=AF.Copy, scale=w(j)),
                      reads=[up, cw], writes=[tp])
                em.op("pool", lambda e, tp=tp: e.tensor_add(out=B[:], in0=B[:], in1=tp[:]), reads=[tp, B], writes=[B])
            if cc + 1 < 8:
                up_next = load_u(cc + 1)
            for j in range(17, 31):
                dg = dgr.next()
                em.op("act", lambda e, j=j, dg=dg: e.activation(out=dg[:], in_=self.idf[:], func=AF.Copy, scale=w(j)),
                      reads=[self.idf, cw], writes=[dg])
                for ci, (c0, n) in enumerate(CCH):
                    em.op("pe", lambda e, j=j, dg=dg, ci=ci, c0=c0, n=n: e.matmul(pcs[ci][:, 0:n], dg[:], up[:, j + c0:j + c0 + n],
                                                                               start=(j == 17), stop=(j == 30)),
                          reads=[dg, up], writes=[pcs[ci]])
            em.op("dve", lambda e: e.tensor_add(out=uall[:, cc, :], in0=A[:], in1=B[:]), reads=[A, B], writes=[uall])
            for ci, (c0, n) in enumerate(CCH):
                em.op("dve", lambda e, ci=ci, c0=c0, n=n: e.tensor_add(out=uall[:, cc, c0:c0 + n], in0=uall[:, cc, c0:c0 + n], in1=pcs[ci][:, 0:n]),
                      reads=[uall, pcs[ci]], writes=[uall])
            s2 = sq.next()
            em.op("act", lambda e: e.activation(out=s2[:], in_=uall[:, cc, :], func=AF.Square), reads=[uall], writes=[s2])
            if cc == 0:
                em.op("pool", lambda e: e.tensor_copy(acc1[:], uall[:, cc, :]), reads=[uall], writes=[acc1])
                em.op("pool", lambda e: e.tensor_copy(acc2[:], s2[:]), reads=[s2], writes=[acc2])
            else:
                em.op("pool", lambda e: e.tensor_add(out=acc1[:], in0=acc1[:], in1=uall[:, cc, :]), reads=[uall, acc1], writes=[acc1])
                em.op("pool", lambda e: e.tensor_add(out=acc2[:], in0=acc2[:], in1=s2[:]), reads=[s2, acc2], writes=[acc2])
        mean = em.sb("mean", [128, T], F32)
        rstd = em.sb("rstd", [128, T], F32)
        p1 = em.ps("p1", [128, 512], F32)
        p2 = em.ps("p2", [128, 512], F32)
        avg = self.avg[1024]
        for (c0, n) in CCH:
            em.op("pe", lambda e: e.matmul(p1[:, 0:n], avg[:], acc1[:, c0:c0 + n], start=True, stop=True),
                  reads=[avg, acc1], writes=[p1])
            em.op("pe", lambda e: e.matmul(p2[:, 0:n], avg[:], acc2[:, c0:c0 + n], start=True, stop=True),
                  reads=[avg, acc2], writes=[p2])
            em.op("act", lambda e: e.activation(out=mean[:, c0:c0 + n], in_=p1[:, 0:n], func=AF.Copy), reads=[p1], writes=[mean])
            em.op("dve", lambda e: e.tensor_tensor(out=rstd[:, c0:c0 + n], in0=mean[:, c0:c0 + n], in1=mean[:, c0:c0 + n], op=ALU.mult),
                  reads=[mean], writes=[rstd])
            em.op("dve", lambda e: e.tensor_sub(out=rstd[:, c0:c0 + n], in0=p2[:, 0:n], in1=rstd[:, c0:c0 + n]),
                  reads=[p2, rstd], writes=[rstd])
            em.op("dve", lambda e: e.tensor_scalar(out=rstd[:, c0:c0 + n], in0=rstd[:, c0:c0 + n], scalar1=LN_EPS, scalar2=None, op0=ALU.add),
                  reads=[rstd], writes=[rstd])
        em.op("act", lambda e: e.activation(out=rstd[:], in_=rstd[:], func=AF.Sqrt), reads=[rstd], writes=[rstd])
        em.op("dve", lambda e: e.reciprocal(out=rstd[:], in_=rstd[:]), reads=[rstd], writes=[rstd])
        mxr = em.ring("mx", 2, [128, T], BF16, dma=True)
        for cc in range(8):
            ga = gsr.next()
            t1 = pa.next()
            t2 = pb.next()
            mx = mxr.next()
            em.dma("sp", ga[:], self.ZT[ZE["gate_a"] + cc * 128:ZE["gate_a"] + (cc + 1) * 128, :], writes=[ga], owner=ga)
            em.op("dve", lambda e: e.tensor_sub(out=t1[:], in0=uall[:, cc, :], in1=mean[:]), reads=[uall, mean], writes=[t1])
            em.op("pool", lambda e: e.tensor_mul(out=t1[:], in0=t1[:], in1=rstd[:]), reads=[t1, rstd], writes=[t1])
            em.op("act", lambda e: e.activation(out=t2[:], in_=t1[:], func=AF.Silu, scale=cv[:, 8 + cc:9 + cc], bias=cv[:, 16 + cc:17 + cc]),
                  reads=[t1, cv], writes=[t2])
            em.op("dve", lambda e: e.tensor_mul(out=mx[:], in0=t2[:], in1=ga[:]), reads=[t2, ga], writes=[mx])
            em.dma("sp", self.MIXT[cc * 128:(cc + 1) * 128, :], mx[:], reads=[mx], owner=mx)
        em.end()

    def stage_attn(self, e_):
        em = self.em
        ZE = self.ZE
        ZT = self.ZT
        em.begin()
        cnT = em.sb("cnT", [128, 2, T], BF16)
        cnk = em.sb("cnk", [128, NT, 256], BF16)
        wukT = em.sb("wukT", [128, 8, 256], BF16)
        wuvb = em.sb("wuvb", [128, 8, 2, 128], BF16)
        kiT2 = em.sb("kiT2", [128, T], BF16, dma=True)
        wiTok = em.sb("wiTok", [128, NT, 16], F32)
        kvg = em.sb("kvg", [128, 2], F32)
        stg = em.sb("stg", [128, 128], F32, dma=True)

        prep = ExitStack()
        stage_outer = em.stage
        em.stage = prep
        pA = em.ps("pA", [128, 512], F32)
        pB = em.ps("pB", [128, 512], F32)
        pT = em.ps("pT", [128, 128], F32)
        pTb = em.ps("pTb", [128, 2, 128], BF16)
        big = em.sb("big", [128, 2, T], F32, dma=True)
        big2 = em.sb("big2", [128, T], F32)
        rsd = em.sb("rsd", [128, T], F32)
        self.vecT(kvg, 0, self.kv_g[e_].rearrange("(cc p) -> cc p", p=128), 2, pT, stg)
        em.dma("sp", big[:], ZT[ZE["c"]:ZE["c"] + 256, :].rearrange("(cc p) t -> p cc t", p=128), writes=[big], owner=big)
        em.op("act", lambda e: e.activation(out=big2[:], in_=big[:, 0, :], func=AF.Square), reads=[big], writes=[big2])
        em.op("act", lambda e: e.activation(out=rsd[:], in_=big[:, 1, :], func=AF.Square), reads=[big], writes=[rsd])
        em.op("dve", lambda e: e.tensor_add(out=big2[:], in0=big2[:], in1=rsd[:]), reads=[big2, rsd], writes=[big2])
        avg = self.avg[256]
        for (c0, n) in CCH:
            em.op("pe", lambda e: e.matmul(pA[:, 0:n], avg[:], big2[:, c0:c0 + n], start=True, stop=True),
                  reads=[avg, big2], writes=[pA])
            em.op("dve", lambda e: e.tensor_scalar(out=rsd[:, c0:c0 + n], in0=pA[:, 0:n], scalar1=RMS_EPS, scalar2=None, op0=ALU.add),
                  reads=[pA], writes=[rsd])
        em.op("act", lambda e: e.activation(out=rsd[:], in_=rsd[:], func=AF.Sqrt), reads=[rsd], writes=[rsd])
        em.op("dve", lambda e: e.reciprocal(out=rsd[:], in_=rsd[:]), reads=[rsd], writes=[rsd])
        for cc in range(2):
            em.op("dve", lambda e, cc=cc: e.scalar_tensor_tensor(out=cnT[:, cc, :], in0=big[:, cc, :], scalar=kvg[:, cc:cc + 1],
                                                                 in1=rsd[:], op0=ALU.mult, op1=ALU.mult),
                  reads=[big, kvg, rsd], writes=[cnT])
        for i in range(NT):
            r0, n = trng(i)
            for cc in range(2):
                em.op("pe", lambda e, cc=cc: e.transpose(pTb[0:n, cc, :], cnT[:, cc, r0:r0 + n], self.idb[:, :]),
                      reads=[cnT, self.idb], writes=[pTb])
            em.op("act", lambda e: e.activation(out=cnk[0:n, i, :], in_=pTb[0:n, :, :], func=AF.Copy), reads=[pTb], writes=[cnk])
        wld = em.ring("wld", 2, [128, 2, 128], F32, dma=True)
        for h in range(8):
            w = wld.next()
            em.dma("sp", w[:], self.w_uk[e_, h].rearrange("(cc p) d -> p cc d", p=128), writes=[w], owner=w)
            for cc in range(2):
                em.op("pe", lambda e, cc=cc: e.transpose(pT[:, :], w[:, cc, :], self.idf[:, :]), reads=[w, self.idf], writes=[pT])
                em.op("act", lambda e, cc=cc: e.activation(out=wukT[:, h, cc * 128:(cc + 1) * 128], in_=pT[:, :], func=AF.Copy,
                                                           scale=128.0 ** -0.5), reads=[pT], writes=[wukT])
            w2 = wld.next()
            em.dma("sp", w2[:], self.w_uv[e_, h].rearrange("(cc p) d -> p cc d", p=128), writes=[w2], owner=w2)
            em.op("pool", lambda e: e.tensor_copy(wuvb[:, h, :, :], w2[:]), reads=[w2], writes=[wuvb])
        em.dma("sp", kiT2[:], self.ZB[2048:2176, :], writes=[kiT2], owner=kiT2)
        wiT = em.sb("wiT", [16, T], F32, dma=True)
        em.dma("sp", wiT[:], ZT[ZE["wi"]:ZE["wi"] + 16, :], writes=[wiT], owner=wiT)
        for i in range(NT):
            r0, n = trng(i)
            em.op("pe", lambda e: e.transpose(pT[0:n, 0:16], wiT[0:16, r0:r0 + n], self.idf[0:16, 0:16]),
                  reads=[wiT, self.idf], writes=[pT])
            em.op("act", lambda e: e.activation(out=wiTok[0:n, i, :], in_=pT[0:n, 0:16], func=AF.Copy), reads=[pT], writes=[wiTok])
        em.barrier()
        prep.close()
        em.stage = stage_outer

        pdot = em.psring("pdot", 2, [128, 512], F32)
        pmt = em.ps("pmt", [128, 8, 128], BF16)
        plog = em.psring("plog", 2, [128, 4, 128], F32)
        pso_r = em.psring("pso", 1, [128, 3, 128], F32)
        psb = em.ps("psb", [128, 128], F32)
        pql = em.ps("pql", [128, 2, 128], F32)
        qi_r = em.ring("qiT", 2, [128, 8, 128], BF16, dma=True)
        qh_r = em.ring("qhT", 2, [128, 8, 128], BF16, dma=True)
        ql_r = em.ring("qlat", 2, [128, 8, 2, 128], BF16)
        score_r = em.ring("score", 2, [128, T], F32)
        work_r = em.ring("work", 1, [128, T], F32)
        m8 = em.sb("m8", [128, 8], F32)
        NIT = 24
        blo = em.sb("blo", [128, 1], F32)
        brg = em.sb("brg", [128, 1], F32)
        bthr = em.sb("bthr", [128, 1], F32)
        bcnt = em.sb("bcnt", [128, 1], F32)
        btq = em.sb("btq", [128, 1], F32)
        stab = em.sb("stab", [128, NIT], F32)
        pw2 = em.sb("pw2", [128, NIT], F32)
        for k in range(NIT):
            em.op("pool", lambda e, k=k: e.memset(pw2[:, k:k + 1], 2.0 ** -(k + 1)), writes=[pw2])
        rl_r = em.ring("rl", 3, [128, 512], F32)
        mask_r = em.ring("mask", 2, [128, T], BF16)
        maskT_r = em.ring("maskT", 2, [128, NT, 128], BF16)
        cm_r = em.ring("cm", 2, [128, 8, 2, 128], F32)
        ex_r = em.ring("ex", 2, [128, 4, 128], F32)
        pt_r = em.ring("ptile", 12, [128, 4, 128], BF16)
        osb_r = em.ring("osb", 2, [128, 2, 128], BF16)
        rden_r = em.ring("rden", 2, [128, 128], F32)
        dsb_r = em.ring("dsb", 2, [128, 128], F32)
        gb_r = em.ring("gb", 2, [128, 8, 128], F32, dma=True)
        tb_r = em.ring("tb", 2, [128, 128], F32)
        mixb_r = em.ring("mixb", 2, [128, 8, 128], BF16, dma=True)
        EB = self.EB
        def pre(qt):
            q0, nq = trng(qt)
            nk = q0 + nq
            score = score_r.next()
            gb = gb_r.next()
            em.dma("sp", gb[:, :, 0:nq], ZT[ZE["gate_b"]:ZE["gate_b"] + 1024, q0:q0 + nq].rearrange("(h p) t -> p h t", p=128),
                   writes=[gb], owner=gb)
            qiT = qi_r.next()
            em.dma("sp", qiT[:, :, 0:nq], self.ZB[1024:2048, q0:q0 + nq].rearrange("(c p) t -> p c t", p=128),
                   writes=[qiT], owner=qiT)
            qhT = qh_r.next()
            em.dma("sp", qhT[:, :, 0:nq], self.ZB[0:1024, q0:q0 + nq].rearrange("(h p) t -> p h t", p=128),
                   writes=[qhT], owner=qhT)
            qlat = ql_r.next()
            for h in range(8):
                for cc in range(2):
                    em.op("pe", lambda e, h=h, cc=cc: e.matmul(pql[:, cc, 0:nq], wukT[:, h, cc * 128:(cc + 1) * 128], qhT[:, h, 0:nq],
                                                               start=True, stop=True), reads=[wukT, qhT], writes=[pql])
                em.op("act", lambda e, h=h: e.activation(out=qlat[:, h, :, 0:nq], in_=pql[:, :, 0:nq], func=AF.Copy),
                      reads=[pql], writes=[qlat])
            yield
            kch = [(k0, min(512, nk - k0)) for k0 in range(0, nk, 512)]
            for h16 in range(16):
                c_, po = h16 // 2, (h16 % 2) * 64
                for (k0, n) in kch:
                    ps = pdot.next()
                    rl = rl_r.next()
                    em.op("pe", lambda e: e.matmul(ps[0:nq, 0:n], qiT[po:po + 64, c_, 0:nq], kiT2[po:po + 64, k0:k0 + n],
                                                   start=True, stop=True), reads=[qiT, kiT2], writes=[ps])
                    em.op("act", lambda e: e.activation(out=rl[0:nq, 0:n], in_=ps[0:nq, 0:n], func=AF.Relu), reads=[ps], writes=[rl])
                    if h16 == 0:
                        em.op("dve", lambda e: e.tensor_scalar(out=score[0:nq, k0:k0 + n], in0=rl[0:nq, 0:n],
                                                               scalar1=wiTok[0:nq, qt, 0:1], scalar2=None, op0=ALU.mult),
                              reads=[rl, wiTok], writes=[score])
                    else:
                        em.op("dve", lambda e: e.scalar_tensor_tensor(out=score[0:nq, k0:k0 + n], in0=rl[0:nq, 0:n],
                                                                      scalar=wiTok[0:nq, qt, h16:h16 + 1],
                                                                      in1=score[0:nq, k0:k0 + n], op0=ALU.mult, op1=ALU.add),
                              reads=[rl, wiTok, score], writes=[score])
                yield
            mask = mask_r.next()
            if qt >= 1:
                em.op("pool", lambda e: e.memset(score[0:64, nk - 64:nk], NEG), reads=[], writes=[score])
            if nk > TOPK and nk - 64 < TOPK:
                work = work_r.next()
                src = score
                for r in range(TOPK // 8):
                    em.op("dve", lambda e, src=src: e.max(out=m8[0:nq, :], in_=src[0:nq, 0:nk]), reads=[src], writes=[m8])
                    em.op("dve", lambda e, src=src: e.match_replace(out=work[0:nq, 0:nk], in_to_replace=m8[0:nq, :],
                                                                    in_values=src[0:nq, 0:nk], imm_value=NEG2),
                          reads=[src, m8], writes=[work])
                    src = work
                    yield
                em.op("dve", lambda e: e.tensor_scalar(out=mask[0:nq, 0:nk], in0=work[0:nq, 0:nk], scalar1=-2.0e38, scalar2=None,
                                                       op0=ALU.is_lt), reads=[work], writes=[mask])
            elif nk > TOPK:
                nlo = nk - 64
                em.op("dve", lambda e: e.max(out=m8[0:nq, :], in_=score[0:nq, 0:nk]), reads=[score], writes=[m8])
                em.op("dve", lambda e: e.tensor_reduce(out=blo[0:nq, :], in_=score[0:nq, 0:nlo], axis=AX.X, op=ALU.min),
                      reads=[score], writes=[blo])
                em.op("dve", lambda e: e.tensor_sub(out=brg[0:nq, :], in0=m8[0:nq, 0:1], in1=blo[0:nq, :]), reads=[m8, blo], writes=[brg])
                em.op("dve", lambda e: e.tensor_scalar(out=stab[0:nq, :], in0=pw2[0:nq, :], scalar1=brg[0:nq, 0:1], scalar2=None, op0=ALU.mult),
                      reads=[pw2, brg], writes=[stab])
                yield
                for k in range(NIT):
                    em.op("dve", lambda e, k=k: e.tensor_add(out=bthr[0:nq, :], in0=blo[0:nq, :], in1=stab[0:nq, k:k + 1]),
                          reads=[blo, stab], writes=[bthr])
                    em.op("dve", lambda e: e.tensor_scalar(out=mask[0:nq, 0:nk], in0=score[0:nq, 0:nk], scalar1=bthr[0:nq, 0:1], scalar2=0.0,
                                                           op0=ALU.is_ge, op1=ALU.add, accum_out=bcnt[0:nq, 0:1]),
                          reads=[score, bthr], writes=[mask, bcnt])
                    em.op("dve", lambda e, k=k: e.scalar_tensor_tensor(out=btq[0:nq, :], in0=bcnt[0:nq, :], scalar=TOPK - 0.5,
                                                                       in1=stab[0:nq, k:k + 1], op0=ALU.is_ge, op1=ALU.mult),
                          reads=[bcnt, stab], writes=[btq])
                    em.op("dve", lambda e: e.tensor_add(out=blo[0:nq, :], in0=blo[0:nq, :], in1=btq[0:nq, :]), reads=[blo, btq], writes=[blo])
                    yield
                em.op("dve", lambda e: e.tensor_scalar(out=mask[0:nq, 0:nk], in0=score[0:nq, 0:nk], scalar1=blo[0:nq, 0:1], scalar2=None,
                                                       op0=ALU.is_ge), reads=[score, blo], writes=[mask])
            else:
                em.op("dve", lambda e: e.tensor_scalar(out=mask[0:nq, 0:nk], in0=score[0:nq, 0:nk], scalar1=-1.0e29, scalar2=None,
                                                       op0=ALU.is_gt), reads=[score], writes=[mask])
            if qt >= 1:
                em.op("pool", lambda e: e.memset(mask[0:64, nk - 64:nk], 0.0), writes=[mask])
            maskT = maskT_r.next()
            for kb0 in range(0, qt + 1, 8):
                kbs = list(range(kb0, min(qt + 1, kb0 + 8)))
                for kb in kbs:
                    k0, nkb = trng(kb)
                    em.op("pe", lambda e, kb=kb, k0=k0, nkb=nkb: e.transpose(pmt[0:nkb, kb - kb0, 0:nq], mask[0:nq, k0:k0 + nkb],
                                                                            self.idb[0:nq, 0:nq]),
                          reads=[mask, self.idb], writes=[pmt])
                if kb0 == 0:
                    em.op("act", lambda e: e.activation(out=maskT[0:16, 0, 0:nq], in_=pmt[0:16, 0, 0:nq], func=AF.Copy),
                          reads=[pmt], writes=[maskT])
                    if len(kbs) > 1:
                        em.op("act", lambda e: e.activation(out=maskT[:, 1:len(kbs), 0:nq], in_=pmt[:, 1:len(kbs), 0:nq], func=AF.Copy),
                              reads=[pmt], writes=[maskT])
                else:
                    em.op("act", lambda e: e.activation(out=maskT[:, kb0:kb0 + len(kbs), 0:nq], in_=pmt[:, 0:len(kbs), 0:nq], func=AF.Copy),
                          reads=[pmt], writes=[maskT])
            yield
            cm = cm_r.next()
            if qt == 0:
                near = {0: (0, 0, 16)}
            elif qt == 1:
                near = {1: (0, 0, 128), 0: (1, 2, 16)}
            else:
                near = {qt: (0, 0, 128), qt - 1: (1, 1, 128)}
            for kb, (slot, ty, rows) in near.items():
                for h in range(8):
                    em.op("pool", lambda e, kb=kb, slot=slot, ty=ty, rows=rows, h=h: e.tensor_tensor(
                        out=cm[0:rows, h, slot, 0:nq], in0=EB[0:rows, ty, h, 0:nq], in1=maskT[0:rows, kb, 0:nq], op=ALU.mult),
                        reads=[EB, maskT], writes=[cm])
            self._pre[qt] = dict(gb=gb, qlat=qlat, maskT=maskT, cm=cm, near=near)
            yield

        def head_a(qt, h, st):
            q0, nq = trng(qt)
            gb, qlat, maskT, cm, near = st["gb"], st["qlat"], st["maskT"], st["cm"], st["near"]
            groups = [[0]] + [list(range(a, min(qt + 1, a + 4))) for a in range(1, qt + 1, 4)]
            ptiles = {}
            for grp in groups:
                pl = plog.next()
                rows = 16 if grp[0] == 0 else 128
                for gi, kb in enumerate(grp):
                    k0, nkb = trng(kb)
                    for cc in range(2):
                        em.op("pe", lambda e, gi=gi, k0=k0, nkb=nkb, cc=cc: e.matmul(
                            pl[0:nkb, gi, 0:nq], cnT[:, cc, k0:k0 + nkb], qlat[:, h, cc, 0:nq],
                            start=(cc == 0), stop=(cc == 1)), reads=[cnT, qlat], writes=[pl])
                ex = ex_r.next()
                g_n = len(grp)
                em.op("act", lambda e, rows=rows, g_n=g_n: e.activation(out=ex[0:rows, 0:g_n, 0:nq], in_=pl[0:rows, 0:g_n, 0:nq],
                                                                        func=AF.Exp, bias=self.bfar[0:rows, h:h + 1]),
                      reads=[pl, self.bfar], writes=[ex])
                ptile = pt_r.next()
                far = [gi for gi, kb in enumerate(grp) if kb not in near]
                if far:
                    a, b = far[0], far[-1] + 1
                    kba = grp[a]
                    em.op("dve", lambda e, a=a, b=b, kba=kba, rows=rows: e.tensor_tensor(
                        out=ptile[0:rows, a:b, 0:nq], in0=ex[0:rows, a:b, 0:nq], in1=maskT[0:rows, kba:kba + (b - a), 0:nq], op=ALU.mult),
                        reads=[ex, maskT], writes=[ptile])
                for gi, kb in enumerate(grp):
                    if kb in near:
                        slot, ty, rws = near[kb]
                        em.op("dve", lambda e, gi=gi, slot=slot, rws=rws: e.tensor_tensor(
                            out=ptile[0:rws, gi, 0:nq], in0=ex[0:rws, gi, 0:nq], in1=cm[0:rws, h, slot, 0:nq], op=ALU.mult),
                            reads=[ex, cm], writes=[ptile])
                for gi, kb in enumerate(grp):
                    ptiles[kb] = (ptile, gi)
            return ptiles

        def head_b(qt, h, st, mixb, ptiles):
            q0, nq = trng(qt)
            gb = st["gb"]
            pso = pso_r.next()
            for part in range(3):
                for kb in range(qt + 1):
                    k0, nkb = trng(kb)
                    ptile, gi = ptiles[kb]
                    if part < 2:
                        lhs = cnk[0:nkb, kb, part * 128:(part + 1) * 128]
                        rd = [cnk, ptile]
                    else:
                        lhs = self.onesb[0:nkb, :]
                        rd = [self.onesb, ptile]
                    em.op("pe", lambda e, lhs=lhs, ptile=ptile, gi=gi, nkb=nkb, kb=kb, part=part: e.matmul(
                        pso[:, part, 0:nq], lhs, ptile[0:nkb, gi, 0:nq], start=(kb == 0), stop=(kb == qt)),
                        reads=rd, writes=[pso])
            osb = osb_r.next()
            rden = rden_r.next()
            dsb = dsb_r.next()
            em.op("act", lambda e: e.activation(out=osb[:, :, 0:nq], in_=pso[:, 0:2, 0:nq], func=AF.Copy), reads=[pso], writes=[osb])
            em.op("act", lambda e: e.activation(out=dsb[:, 0:nq], in_=pso[:, 2, 0:nq], func=AF.Ln), reads=[pso], writes=[dsb])
            for cc in range(2):
                em.op("pe", lambda e, cc=cc: e.matmul(psb[:, 0:nq], wuvb[:, h, cc, :], osb[:, cc, 0:nq], start=(cc == 0), stop=(cc == 1)),
                      reads=[wuvb, osb], writes=[psb])
            tb = tb_r.next()
            em.op("act", lambda e: e.activation(out=rden[:, 0:nq], in_=dsb[:, 0:nq], func=AF.Exp, scale=-1.0), reads=[dsb], writes=[rden])
            em.op("dve", lambda e: e.tensor_tensor(out=tb[:, 0:nq], in0=psb[:, 0:nq], in1=rden[:, 0:nq], op=ALU.mult),
                  reads=[psb, rden], writes=[tb])
            em.op("pool", lambda e: e.tensor_tensor(out=mixb[:, h, 0:nq], in0=tb[:, 0:nq], in1=gb[:, h, 0:nq], op=ALU.mult),
                  reads=[tb, gb], writes=[mixb])

        self._pre = {}
        for _ in pre(0):
            pass
        nqt = NT
        for qt in range(nqt):
            q0, nq = trng(qt)
            st = self._pre.pop(qt)
            gen = pre(qt + 1) if qt + 1 < nqt else None
            nk1 = trng(qt + 1)[0] + trng(qt + 1)[1] if gen is not None else 0
            nsteps = 0 if gen is None else (3 + 16 + (0 if nk1 <= TOPK else (TOPK // 8 if nk1 - 64 < TOPK else NIT + 1)))
            per_head = (nsteps + 7) // 8
            mixb = mixb_r.next()
            pt_next = head_a(qt, 0, st)
            for h in range(8):
                pt_cur = pt_next
                if h + 1 < 8:
                    pt_next = head_a(qt, h + 1, st)
                head_b(qt, h, st, mixb, pt_cur)
                if gen is not None:
                    for _ in range(per_head):
                        if next(gen, "done") == "done":
                            gen = None
                            break
            if gen is not None:
                for _ in gen:
                    pass
            em.dma("sp", self.MIXT[1024:2048, q0:q0 + nq].rearrange("(h p) t -> p h t", p=128), mixb[:, :, 0:nq], reads=[mixb], owner=mixb)
        em.end()

    def stage_rec(self, l):
        em = self.em
        ZT = self.ZT
        li = l // 2
        G = 4
        em.begin()
        stg = em.sb("stg", [128, 128], F32, dma=True)
        rng = em.sb("rng", [128, 16], F32)
        epsb = em.sb("epsb", [128, 1], F32)
        em.op("pool", lambda e: e.memset(epsb[:], RMS_EPS), writes=[epsb])
        self.rmask = em.sb("rmask", [128, T], F32)
        em.op("pool", lambda e: e.memset(self.rmask[:], 1.0), writes=[self.rmask])
        em.op("pool", lambda e: e.memset(self.rmask[:, 0:1], 0.0), writes=[self.rmask])
        em.op("pool", lambda e: e.memset(self.rmask[:, 16:T].rearrange("p (c j) -> p c j", j=64)[:, :, 0:1], 0.0),
              writes=[self.rmask])
        ptk = em.ps("ptk", [128, 8, 128], BF16)
        pn = em.ps("pn", [128, 512], F32)
        self.vecT(rng, 0, self.rec_g[li].rearrange("(c p) -> c p", p=128), 16, pn, stg)
        qs_r = em.ring("qs", 2, [128, T], F32, dma=True)
        sg_r = em.ring("sg", 2, [128, T], F32, dma=True)
        fb_r = em.ring("fb", 1, [128, T], F32)
        b_r = em.ring("b", 1, [128, T], F32)
        d_r = em.ring("d1", 1, [128, T], F32)
        kk_r = em.ring("kk", 1, [128, T], F32)
        mo_r = em.ring("mo", 1, [128, T], BF16, dma=True)
        qt_r = em.ring("qtl", G, [128, T], BF16)
        kt_r = em.ring("ktl", G, [128, T], BF16)
        ktok_r = em.ring("ktok", G, [128, NT, 128], BF16)
        vtok_r = em.ring("vtok", G, [128, NT, 128], BF16, dma=True)
        sc_r = em.ring("sc", G, [128, 4, 33], F32)
        bl_r = em.ring("bl", G, [128, 33], F32)
        oT_r = em.ring("oT", G, [128, T], F32)
        S_rs = [em.ring("S%d" % g, 2, [128, 128], F32) for g in range(G)]
        Sb_r = em.ring("Sb", 2 * G, [128, 128], BF16)
        St_r = em.ring("St", 2 * G, [128, 128], F32)
        am_r = em.ring("am", 2 * G, [128, 128], BF16)
        hbank = [em.ps_views("ph%d" % g, 4, [128], F32) for g in range(G)]
        lb = self.lbv

        def pre_head(h, g):
            qs = qs_r.next(); sg = sg_r.next()
            fb = fb_r.next(); b = b_r.next(); d1 = d_r.next(); kk = kk_r.next()
            qtl = qt_r.next(); ktl = kt_r.next(); ktok = ktok_r.next(); vtok = vtok_r.next()
            sc = sc_r.next(); bl = bl_r.next(); oT = oT_r.next()
            em.dma("sp", qs[:], ZT[h * 128:(h + 1) * 128, :], writes=[qs], owner=qs)
            em.dma("sp", sg[:], ZT[2048 + h * 128:2048 + (h + 1) * 128, :], writes=[sg], owner=sg)
            em.dma("sp", vtok[0:16, 0, :], self.VTOK[0:16, h * 128:(h + 1) * 128], writes=[vtok], owner=vtok)
            em.dma("sp", vtok[:, 1:NT, :], self.VTOK[16:T, h * 128:(h + 1) * 128].rearrange("(i p) v -> p i v", p=128),
                   writes=[vtok], owner=vtok)
            em.op("dve", lambda e: e.tensor_scalar(out=fb[:], in0=sg[:], scalar1=self.oml[:, li, h:h + 1], scalar2=lb[:, li, h:h + 1],
                                                   op0=ALU.mult, op1=ALU.add), reads=[sg, self.oml, lb], writes=[fb])
            em.op("act", lambda e: e.activation(out=fb[:], in_=fb[:], func=AF.Ln), reads=[fb], writes=[fb])
            em.op("dve", lambda e: e.tensor_scalar(out=kk[:], in0=sg[:], scalar1=self.noml[:, li, h:h + 1], scalar2=self.oml[:, li, h:h + 1],
                                                   op0=ALU.mult, op1=ALU.add), reads=[sg, self.noml, self.oml], writes=[kk])
            em.op("dve", lambda e: e.tensor_tensor_scan(b[:], self.rmask[:], fb[:], 0.0, ALU.mult, ALU.add),
                  reads=[self.rmask, fb], writes=[b])
            em.op("pool", lambda e: e.tensor_copy(sc[:, 0, 0:1], b[:, 8:9]), reads=[b], writes=[sc])
            em.op("pool", lambda e: e.tensor_copy(sc[:, 0, 1:33], b[:, 48:T:64]), reads=[b], writes=[sc])
            em.op("pool", lambda e: e.tensor_copy(bl[:, 0:1], b[:, 15:16]), reads=[b], writes=[bl])
            em.op("pool", lambda e: e.tensor_copy(bl[:, 1:33], b[:, 79:T:64]), reads=[b], writes=[bl])
            em.op("dve", lambda e: e.tensor_sub(out=d1[:, 0:16], in0=b[:, 0:16], in1=sc[:, 0, 0:1].to_broadcast([128, 16])),
                  reads=[b, sc], writes=[d1])
            em.op("dve", lambda e: e.tensor_sub(out=d1[:, 16:T].rearrange("p (c j) -> p c j", j=64),
                                                in0=b[:, 16:T].rearrange("p (c j) -> p c j", j=64),
                                                in1=sc[:, 0, 1:33].unsqueeze(2).to_broadcast([128, 32, 64])),
                  reads=[b, sc], writes=[d1])
            em.op("act", lambda e: e.activation(out=sc[:, 1, :], in_=sc[:, 0, :], func=AF.Exp), reads=[sc], writes=[sc])
            em.op("act", lambda e: e.activation(out=sc[:, 3, :], in_=bl[:], func=AF.Exp), reads=[bl, sc], writes=[sc])
            em.op("dve", lambda e: e.tensor_sub(out=bl[:], in0=bl[:], in1=sc[:, 0, :]), reads=[bl, sc], writes=[bl])
            em.op("act", lambda e: e.activation(out=sc[:, 2, :], in_=bl[:], func=AF.Exp), reads=[bl, sc], writes=[sc])
            em.op("act", lambda e: e.activation(out=fb[:], in_=d1[:], func=AF.Exp), reads=[d1, fb], writes=[fb])
            em.op("dve", lambda e: e.tensor_mul(out=qtl[:], in0=qs[:], in1=fb[:]), reads=[qs, fb], writes=[qtl])
            em.op("act", lambda e: e.activation(out=d1[:], in_=d1[:], func=AF.Exp, scale=-1.0), reads=[d1], writes=[d1])
            em.op("pool", lambda e: e.tensor_mul(out=ktl[:], in0=kk[:], in1=d1[:]), reads=[kk, d1], writes=[ktl])
            for i0 in range(0, NT, 8):
                ii = list(range(i0, min(NT, i0 + 8)))
                for i in ii:
                    r0, n = trng(i)
                    em.op("pe", lambda e, i=i, r0=r0, n=n: e.transpose(ptk[0:n, i - i0, :], ktl[:, r0:r0 + n], self.idb[:, :]),
                          reads=[ktl, self.idb], writes=[ptk])
                if i0 == 0:
                    em.op("act", lambda e: e.activation(out=ktok[0:16, 0, :], in_=ptk[0:16, 0, :], func=AF.Copy), reads=[ptk], writes=[ktok])
                    em.op("act", lambda e: e.activation(out=ktok[:, 1:8, :], in_=ptk[:, 1:8, :], func=AF.Copy), reads=[ptk], writes=[ktok])
                else:
                    em.op("act", lambda e, i0=i0, m=len(ii): e.activation(out=ktok[:, i0:i0 + m, :], in_=ptk[:, 0:m, :], func=AF.Copy),
                          reads=[ptk], writes=[ktok])
            Sb = Sb_r.next()
            St = St_r.next()
            em.op("pool", lambda e: e.memset(Sb[:], 0.0), writes=[Sb])
            em.op("pool", lambda e: e.memset(St[:], 0.0), writes=[St])
            return dict(h=h, g=g, qtl=qtl, ktl=ktl, ktok=ktok, vtok=vtok, sc=sc, oT=oT, Sb=Sb, St=St, pss=None)

        def tile_part(c, i):
            r0, n = trng(i)
            qtl, ktl, vtok, oT = c["qtl"], c["ktl"], c["vtok"], c["oT"]
            psa = hbank[c["g"]][0]
            am = am_r.next()
            pso = hbank[c["g"]][0]
            em.op("pe", lambda e: e.matmul(psa[0:n, 0:n], ktl[:, r0:r0 + n], qtl[:, r0:r0 + n], start=True, stop=True),
                  reads=[ktl, qtl], writes=[psa])
            em.op("dve", lambda e: e.tensor_tensor(out=am[0:n, 0:n], in0=psa[0:n, 0:n], in1=self.cmask[0:n, 0:n], op=ALU.mult),
                  reads=[psa, self.cmask], writes=[am])
            em.op("pe", lambda e: e.matmul(pso[:, 0:n], vtok[0:n, i, :], am[0:n, 0:n], start=True, stop=True),
                  reads=[vtok, am], writes=[pso])
            em.op("act", lambda e: e.activation(out=oT[:, r0:r0 + n], in_=pso[:, 0:n], func=AF.Copy), reads=[pso], writes=[oT])

        def chunk_list():
            out = []
            for i in range(NT):
                r0, n = trng(i)
                for ci, (p0, ncx) in enumerate([(0, 16)] if i == 0 else [(0, 64), (64, 64)]):
                    j = 0 if i == 0 else 1 + 2 * (i - 1) + ci
                    out.append((i, j, p0, ncx, r0 + p0))
            return out

        CH = chunk_list()

        def emit_pss(c, idx):
            i, j, p0, ncx, c0 = CH[idx]
            pss = hbank[c["g"]][1 + (idx % 2)]
            em.op("pe", lambda e: e.matmul(pss[:], c["ktok"][p0:p0 + ncx, i, :], c["vtok"][p0:p0 + ncx, i, :], start=True, stop=True),
                  reads=[c["ktok"], c["vtok"]], writes=[pss])

        def chunk_pe(c, idx):
            i, j, p0, ncx, c0 = CH[idx]
            psi = hbank[c["g"]][3]
            Sb = c["Sb"]
            em.op("pe", lambda e: e.matmul(psi[:, 0:ncx], Sb[:], c["qtl"][:, c0:c0 + ncx], start=True, stop=True),
                  reads=[Sb, c["qtl"]], writes=[psi])
            if idx + 1 < len(CH):
                emit_pss(c, idx + 1)

        def chunk_dve(c, idx):
            i, j, p0, ncx, c0 = CH[idx]
            sc, oT = c["sc"], c["oT"]
            pss = hbank[c["g"]][1 + (idx % 2)]
            psi = hbank[c["g"]][3]
            St = c["St"]
            if idx + 1 < len(CH):
                jn = CH[idx + 1][1]
                S2 = S_rs[c["g"]].next()
                em.op("dve", lambda e: e.scalar_tensor_tensor(out=S2[:], in0=pss[:], scalar=sc[:, 2, j:j + 1], in1=St[:],
                                                              op0=ALU.mult, op1=ALU.add), reads=[pss, sc, St], writes=[S2])
                Sb2 = Sb_r.next()
                St2 = St_r.next()
                em.op("dve", lambda e: e.tensor_scalar(out=Sb2[:], in0=S2[:], scalar1=sc[:, 1, jn:jn + 1], scalar2=None, op0=ALU.mult),
                      reads=[S2, sc], writes=[Sb2])
                em.op("dve", lambda e: e.tensor_scalar(out=St2[:], in0=S2[:], scalar1=sc[:, 3, jn:jn + 1], scalar2=None, op0=ALU.mult),
                      reads=[S2, sc], writes=[St2])
                c["Sb"], c["St"] = Sb2, St2
            em.op("dve", lambda e: e.tensor_add(out=oT[:, c0:c0 + ncx], in0=oT[:, c0:c0 + ncx], in1=psi[:, 0:ncx]),
                  reads=[oT, psi], writes=[oT])

        def post_head(c):
            h, oT = c["h"], c["oT"]
            gs = qs_r.next(); d1 = d_r.next(); kk = kk_r.next()
            em.dma("sp", gs[:], ZT[4096 + h * 128:4096 + (h + 1) * 128, :], writes=[gs], owner=gs)
            em.op("act", lambda e: e.activation(out=d1[:], in_=oT[:], func=AF.Square), reads=[oT], writes=[d1])
            avg = self.avg[128]
            for (c0, n) in CCH:
                em.op("pe", lambda e: e.matmul(pn[:, 0:n], avg[:], d1[:, c0:c0 + n], start=True, stop=True), reads=[avg, d1], writes=[pn])
                em.op("act", lambda e: e.activation(out=kk[:, c0:c0 + n], in_=pn[:, 0:n], func=AF.Ln, bias=epsb[:, 0:1]),
                      reads=[pn, epsb], writes=[kk])
            em.op("act", lambda e: e.activation(out=kk[:], in_=kk[:], func=AF.Exp, scale=-0.5), reads=[kk], writes=[kk])
            em.op("dve", lambda e: e.tensor_mul(out=kk[:], in0=kk[:], in1=oT[:]), reads=[kk, oT], writes=[kk])
            mo = mo_r.next()
            em.op("dve", lambda e: e.scalar_tensor_tensor(out=mo[:], in0=kk[:], scalar=rng[:, h:h + 1], in1=gs[:], op0=ALU.mult, op1=ALU.mult),
                  reads=[kk, rng, gs], writes=[mo])
            em.dma("sp", self.MIXT[h * 128:(h + 1) * 128, :], mo[:], reads=[mo], owner=mo)

        for g0 in range(0, 16, G):
            ctx = [pre_head(g0 + g, g) for g in range(G)]
            for c in ctx:
                emit_pss(c, 0)
            idx = 0
            for i in range(NT):
                for c in ctx:
                    tile_part(c, i)
                for _ in ([0] if i == 0 else [0, 1]):
                    for c in ctx:
                        chunk_pe(c, idx)
                    for c in ctx:
                        chunk_dve(c, idx)
                    idx += 1
            for c in ctx:
                post_head(c)
        em.end()

    def stage_out(self, seqs, l, W, last):
        em = self.em
        em.begin()
        Wb = em.sb("Wb", [128, 16, D], BF16)
        if last:
            self.fgain = em.sb("fgain", [128, D], F32, dma=True)
            em.dma("sp", self.fgain[:], self.final_gain.partition_broadcast(128), writes=[self.fgain], owner=self.fgain)
        w32r = em.ring("wo32", 4, [128, D], F32, dma=True)
        Wbk = [Tk("Wb%d" % k, Wb.t) for k in range(16)]
        for k in range(16):
            w32 = w32r.next()
            em.dma("sp" if k % 2 == 0 else "act", w32[:], W[k * 128:(k + 1) * 128, :], writes=[w32], owner=w32)
            if k % 2 == 0:
                em.op("pool", lambda e, k=k: e.tensor_copy(Wb[:, k, :], w32[:]), reads=[w32], writes=[Wbk[k]])
            else:
                em.op("act", lambda e, k=k: e.activation(out=Wb[:, k, :], in_=w32[:], func=AF.Copy), reads=[w32], writes=[Wbk[k]])
        mtr = em.ring("mt", 3, [128, 16, 128], BF16, dma=True)
        htr = em.ring("ht", 3, [128, D], F32, dma=True)
        hnr = em.ring("hn", 3, [128, D], F32, dma=True)
        pmm = em.psring("pmm", 4, [128, 512], F32)
        junk = em.sb("junk", [128, D], BF16)
        ssr = em.ring("ss", 2, [128, 2], F32)
        tmr = em.ring("tm", 2, [128, 2], F32)
        for s, i in [(s_, i_) for s_ in seqs for i_ in range(NT)]:
            self.sel(s)
            r0, n = trng(i)
            if last and i == 0:
                continue
            mt = mtr.next()
            ht = htr.next()
            hn = hnr.next()
            em.dma("sp", mt[:, :, 0:n], self.MIXT[:, r0:r0 + n].rearrange("(k p) t -> p k t", p=128), writes=[mt], owner=mt)
            em.dma("sp", ht[0:n, :], self.h_src(s, l, i), writes=[ht], owner=ht)
            for c in range(4):
                ps = pmm.next()
                for k in range(16):
                    em.op("pe", lambda e, k=k, c=c: e.matmul(ps[0:n, :], mt[:, k, 0:n], Wb[:, k, c * 512:(c + 1) * 512],
                                                             start=(k == 0), stop=(k == 15)), reads=[mt, Wbk[k]], writes=[ps])
                em.op("dve", lambda e, c=c: e.tensor_add(out=hn[0:n, c * 512:(c + 1) * 512], in0=ht[0:n, c * 512:(c + 1) * 512], in1=ps[0:n, :]),
                      reads=[ht, ps], writes=[hn])
            if not last:
                em.dma("act", self.H[r0:r0 + n, :], hn[0:n, :], reads=[hn], owner=hn)
            else:
                ss = ssr.next()
                tm = tmr.next()
                em.op("pool", lambda e: e.memset(ss[:], 0.0), writes=[ss])
                em.op("act", lambda e: e.activation(out=junk[0:n, :], in_=hn[0:n, :], func=AF.Square, accum_out=ss[0:n, 0:1]),
                      reads=[hn], writes=[junk, ss])
                self.rstd_rows(ss, n, D, RMS_EPS, tm)
                em.op("dve", lambda e: e.scalar_tensor_tensor(out=ht[0:n, :], in0=hn[0:n, :], scalar=ss[0:n, 1:2], in1=self.fgain[0:n, :],
                                                              op0=ALU.mult, op1=ALU.mult), reads=[hn, ss, self.fgain], writes=[ht])
                em.dma("act", self.out[s, r0 - 16:r0 - 16 + n, :], ht[0:n, :], reads=[ht], owner=ht)
        em.end()


_CACHE = {}


def kernel(**inputs):
    x = np.ascontiguousarray(inputs["x"], dtype=np.float32)
    if "nc" not in _CACHE:
        _CACHE["nc"] = Prog().build()
    nc = _CACHE["nc"]
    oh = _bucket_onehot()
    names = ["meta_tokens", "norm_gain", "final_norm_gain", "rel_bias_table", "w_in_even", "conv_w", "conv_b",
             "conv_ln_gain", "conv_ln_bias", "kv_norm_gain", "w_uk", "w_uv", "w_out_even", "w_in_odd", "lb_logits",
             "rec_norm_gain", "w_out_odd"]
    shared = {k: np.ascontiguousarray(inputs[k], dtype=np.float32) for k in names}
    shared["c_oh"] = oh
    in_maps = []
    for c in range(NCORES):
        m = dict(shared)
        m["x"] = x[c * SEQ_PER_CORE:(c + 1) * SEQ_PER_CORE]
        in_maps.append(m)
    res = run_bass_kernel_spmd(nc, in_maps, core_ids=list(range(NCORES)))
    return np.concatenate([r["out"] for r in res.results], axis=0)
```
